# Optimizing a Trainium2 kernel written in Bass

```python
import jax, jax.numpy as jnp
from jax import lax
import numpy as np

D_MODEL = 2048
BATCH = 8
SEQ = 2048
DEPTH = 1
DEC_BATCH = 128
DEC_SEQ = 1
PAST_LEN = 16384
PAGE_SIZE = 128

HEAD_DIM = 64
D_MIX = D_MODEL
D_RWKV = D_MIX // 2
D_ATTN = D_MIX - D_RWKV
N_RWKV_HEADS = D_RWKV // HEAD_DIM
N_Q_HEADS = D_ATTN // HEAD_DIM
N_KV_HEADS = 4
GROUP = N_Q_HEADS // N_KV_HEADS
WINDOW = 128
BLOCK = WINDOW
LORA_DECAY = 96
LORA_A = 96
LORA_GATE = 256
RWKV_PROJ = 3 * D_RWKV + LORA_DECAY + LORA_A + LORA_GATE
ATTN_PROJ = D_ATTN + 2 * N_KV_HEADS * HEAD_DIM
PROJ = RWKV_PROJ + ATTN_PROJ
D_FF = ((8 * D_MODEL + 3 * 256 - 1) // (3 * 256)) * 256
PLE_DIM = 256
EPS = 1e-6
GN_EPS = 64e-5
SCALE = HEAD_DIM ** -0.5
RWKV_SPLITS = [D_RWKV, 2 * D_RWKV, 3 * D_RWKV, 3 * D_RWKV + LORA_DECAY, 3 * D_RWKV + LORA_DECAY + LORA_A]
ATTN_SPLITS = [D_ATTN, D_ATTN + N_KV_HEADS * HEAD_DIM]

kernel_name = "hymba_rwkv7_swa_sink_alibi_decode_step"


def rms_norm(x, g):
    x32 = x.astype(jnp.float32)
    y = x32 * lax.rsqrt(jnp.mean(x32 * x32, axis=-1, keepdims=True) + EPS)
    return (y * g.astype(jnp.float32)).astype(x.dtype)


def alibi_slopes():
    return 2.0 ** (-8.0 * jnp.arange(1, N_Q_HEADS + 1, dtype=jnp.float32) / N_Q_HEADS)


def rwkv7_time_mix(zr, shift_prev, wkv0, mu, w0, w2, a0, a2, g2, k_k, k_a, r_k, ln_w, ln_b):
    b, t, _ = zr.shape
    z_prev = jnp.concatenate([shift_prev[:, None, :].astype(zr.dtype), zr[:, :-1]], axis=1)
    xs = zr + (z_prev - zr) * mu
    r, k, v, xw, xa, xg = jnp.split(xs, RWKV_SPLITS, axis=-1)
    w = -jax.nn.softplus(-(w0 + jnp.tanh(xw) @ w2)) - 0.5
    a = jax.nn.sigmoid(a0 + xa @ a2)
    g = jax.nn.sigmoid(xg) @ g2
    heads = lambda u: u.reshape(b, t, N_RWKV_HEADS, HEAD_DIM).astype(jnp.float32)
    kk = heads(k * k_k)
    kk = kk * lax.rsqrt(jnp.maximum(jnp.sum(kk * kk, axis=-1, keepdims=True), 1e-24))
    k_h = heads(k * (1 + (a - 1) * k_a))
    r_h, v_h, a_h = heads(r), heads(v), heads(a)
    decay = jnp.exp(-jnp.exp(heads(w)))

    def step(S, inp):
        r_t, d_t, k_t, v_t, kk_t, a_t = inp
        sa = jnp.einsum('bhvk,bhk->bhv', S, -kk_t)
        S = (S * d_t[:, :, None, :]
             + sa[..., None] * (kk_t * a_t)[:, :, None, :]
             + v_t[..., None] * k_t[:, :, None, :])
        return S, jnp.einsum('bhvk,bhk->bhv', S, r_t)

    seq_first = lambda u: jnp.swapaxes(u, 0, 1)
    s_fin, y = lax.scan(step, wkv0.astype(jnp.float32),
                        tuple(seq_first(u) for u in (r_h, decay, k_h, v_h, kk, a_h)))
    y = seq_first(y)
    mean = jnp.mean(y, axis=-1, keepdims=True)
    var = jnp.mean(jnp.square(y - mean), axis=-1, keepdims=True)
    y = ((y - mean) * lax.rsqrt(var + GN_EPS)).reshape(b, t, D_RWKV) * ln_w + ln_b
    bonus = (jnp.sum(r_h * k_h * r_k, axis=-1, keepdims=True) * v_h).reshape(b, t, D_RWKV)
    out = (y + bonus) * g
    return out.astype(zr.dtype), zr[:, -1], s_fin


def sink_softmax(scores, valid, sinks):
    scores = jnp.where(valid, scores, -jnp.inf)
    sink = sinks.astype(jnp.float32)[:, :, None, None]
    m = jnp.maximum(jnp.max(scores, axis=-1, keepdims=True), sink)
    p = jnp.exp(scores - m)
    return p / (jnp.sum(p, axis=-1, keepdims=True) + jnp.exp(sink - m))


def swa_prompt(q, k, v, sinks):
    b, t = q.shape[:2]
    nb = t // BLOCK
    qb = q.reshape(b, nb, BLOCK, N_KV_HEADS, GROUP, HEAD_DIM)

    def band(u):
        up = jnp.concatenate([jnp.zeros_like(u[:, :BLOCK]), u], axis=1)
        up = up.reshape(b, nb + 1, BLOCK, N_KV_HEADS, HEAD_DIM)
        return jnp.concatenate([up[:, :-1], up[:, 1:]], axis=2)

    kb, vb = band(k), band(v)
    s = jnp.einsum('bnqkgd,bnskd->bnkgqs', qb, kb, preferred_element_type=jnp.float32) * SCALE
    dist = jnp.arange(BLOCK)[:, None] + BLOCK - jnp.arange(2 * BLOCK)[None, :]
    key_pos = jnp.arange(nb)[:, None] * BLOCK - BLOCK + jnp.arange(2 * BLOCK)[None, :]
    valid = (dist >= 0) & (dist < WINDOW) & (key_pos >= 0)[:, None, None, None, :]
    s = s - alibi_slopes().reshape(N_KV_HEADS, GROUP, 1, 1) * dist.astype(jnp.float32)
    p = sink_softmax(s, valid, sinks.reshape(N_KV_HEADS, GROUP))
    o = jnp.einsum('bnkgqs,bnskd->bnqkgd', p.astype(vb.dtype), vb)
    return o.reshape(b, t, D_ATTN), k[:, -WINDOW:], v[:, -WINDOW:]


def swa_sample(q, k, v, k_buf, v_buf, sinks):
    b, t = q.shape[:2]
    kc = jnp.concatenate([k_buf.astype(k.dtype), k], axis=1)
    vc = jnp.concatenate([v_buf.astype(v.dtype), v], axis=1)
    qg = q.reshape(b, t, N_KV_HEADS, GROUP, HEAD_DIM)
    s = jnp.einsum('btkgd,bskd->bkgts', qg, kc, preferred_element_type=jnp.float32) * SCALE
    dist = jnp.arange(t)[:, None] + WINDOW - jnp.arange(WINDOW + t)[None, :]
    valid = (dist >= 0) & (dist < WINDOW)
    s = s - alibi_slopes().reshape(N_KV_HEADS, GROUP, 1, 1) * dist.astype(jnp.float32)
    p = sink_softmax(s, valid, sinks.reshape(N_KV_HEADS, GROUP))
    o = jnp.einsum('bkgts,bskd->btkgd', p.astype(vc.dtype), vc)
    return o.reshape(b, t, D_ATTN), kc[:, -WINDOW:], vc[:, -WINDOW:]


def decoder_layer(x, p_emb, shift_prev, wkv0, kv_buf, lp):
    b, t = x.shape[:2]
    h = rms_norm(x, lp['norm_mix'])
    z = h @ lp['w_in']
    zr, za = z[..., :RWKV_PROJ], z[..., RWKV_PROJ:]
    y_r, shift_new, wkv_new = rwkv7_time_mix(
        zr, shift_prev, wkv0, lp['mu_shift'], lp['rwkv_w0'], lp['rwkv_w2'], lp['rwkv_a0'],
        lp['rwkv_a2'], lp['rwkv_g2'], lp['rwkv_k_k'], lp['rwkv_k_a'], lp['rwkv_r_k'],
        lp['rwkv_ln_w'], lp['rwkv_ln_b'])
    q, k, v = jnp.split(za, ATTN_SPLITS, axis=-1)
    q = q.reshape(b, t, N_Q_HEADS, HEAD_DIM)
    k = k.reshape(b, t, N_KV_HEADS, HEAD_DIM)
    v = v.reshape(b, t, N_KV_HEADS, HEAD_DIM)
    if kv_buf is None:
        y_a, k_new, v_new = swa_prompt(q, k, v, lp['attn_sinks'])
    else:
        y_a, k_new, v_new = swa_sample(q, k, v, kv_buf[0], kv_buf[1], lp['attn_sinks'])
    x = x + jnp.concatenate([y_r, y_a], axis=-1) @ lp['w_out']
    hf = rms_norm(x, lp['norm_ffn'])
    x = x + (jax.nn.silu(hf @ lp['w_gate']) * (hf @ lp['w_up'])) @ lp['w_down']
    x = x + jax.nn.sigmoid(rms_norm(x, lp['norm_ple']) @ lp['ple_gate']) * (p_emb @ lp['ple_proj'])
    return x, wkv_new, shift_new, k_new, v_new


def setup_inputs(seed: int = 0) -> dict:
    key = jax.random.key(seed)
    ks = iter(jax.random.split(key, 40))
    nrm = lambda shape, scale: scale * jax.random.normal(next(ks), shape, jnp.float32)
    gain = lambda shape: 1.0 + 0.05 * jax.random.normal(next(ks), shape, jnp.float32)
    unif = lambda shape, lo, hi: jax.random.uniform(next(ks), shape, jnp.float32, lo, hi)
    return {
        'x_prompt': nrm((BATCH, SEQ, D_MODEL), 1.0),
        'x_sample': nrm((DEC_BATCH, DEC_SEQ, D_MODEL), 1.0),
        'state_wkv': nrm((DEPTH, DEC_BATCH, N_RWKV_HEADS, HEAD_DIM, HEAD_DIM), 0.3),
        'state_shift': nrm((DEPTH, DEC_BATCH, RWKV_PROJ), 1.0),
        'cache_k': nrm((DEPTH, DEC_BATCH, WINDOW, N_KV_HEADS, HEAD_DIM), 1.0),
        'cache_v': nrm((DEPTH, DEC_BATCH, WINDOW, N_KV_HEADS, HEAD_DIM), 1.0),
        'p_prompt': nrm((DEPTH, BATCH, SEQ, PLE_DIM), 1.0),
        'p_sample': nrm((DEPTH, DEC_BATCH, DEC_SEQ, PLE_DIM), 1.0),
        'norm_mix': gain((DEPTH, D_MODEL)),
        'w_in': nrm((DEPTH, D_MODEL, PROJ), D_MODEL ** -0.5),
        'mu_shift': unif((DEPTH, RWKV_PROJ), 0.0, 1.0),
        'rwkv_w0': unif((DEPTH, D_RWKV), -5.0, 0.0),
        'rwkv_w2': nrm((DEPTH, LORA_DECAY, D_RWKV), LORA_DECAY ** -0.5),
        'rwkv_a0': nrm((DEPTH, D_RWKV), 0.1),
        'rwkv_a2': nrm((DEPTH, LORA_A, D_RWKV), LORA_A ** -0.5),
        'rwkv_g2': nrm((DEPTH, LORA_GATE, D_RWKV), LORA_GATE ** -0.5),
        'rwkv_k_k': gain((DEPTH, D_RWKV)),
        'rwkv_k_a': gain((DEPTH, D_RWKV)),
        'rwkv_r_k': nrm((DEPTH, N_RWKV_HEADS, HEAD_DIM), 0.1),
        'rwkv_ln_w': gain((DEPTH, D_RWKV)),
        'rwkv_ln_b': nrm((DEPTH, D_RWKV), 0.02),
        'attn_sinks': nrm((DEPTH, N_Q_HEADS), 1.0),
        'w_out': nrm((DEPTH, D_MIX, D_MODEL), D_MIX ** -0.5),
        'norm_ffn': gain((DEPTH, D_MODEL)),
        'w_gate': nrm((DEPTH, D_MODEL, D_FF), D_MODEL ** -0.5),
        'w_up': nrm((DEPTH, D_MODEL, D_FF), D_MODEL ** -0.5),
        'w_down': nrm((DEPTH, D_FF, D_MODEL), D_FF ** -0.5),
        'norm_ple': gain((DEPTH, D_MODEL)),
        'ple_gate': nrm((DEPTH, D_MODEL, D_MODEL), D_MODEL ** -0.5),
        'ple_proj': nrm((DEPTH, PLE_DIM, D_MODEL), PLE_DIM ** -0.5),
        'final_norm': gain((D_MODEL,)),
    }


def reference(x_prompt, x_sample, state_wkv, state_shift, cache_k, cache_v, p_prompt, p_sample,
              norm_mix, w_in, mu_shift, rwkv_w0, rwkv_w2, rwkv_a0, rwkv_a2, rwkv_g2, rwkv_k_k,
              rwkv_k_a, rwkv_r_k, rwkv_ln_w, rwkv_ln_b, attn_sinks, w_out, norm_ffn, w_gate, w_up,
              w_down, norm_ple, ple_gate, ple_proj, final_norm):
    xp, xs = x_prompt, x_sample
    bp = x_prompt.shape[0]
    wkv_p, shift_p, k_p, v_p = [], [], [], []
    wkv_s, shift_s, k_s, v_s = [], [], [], []
    for i in range(DEPTH):
        lp = {
            'norm_mix': norm_mix[i], 'w_in': w_in[i], 'mu_shift': mu_shift[i],
            'rwkv_w0': rwkv_w0[i], 'rwkv_w2': rwkv_w2[i], 'rwkv_a0': rwkv_a0[i],
            'rwkv_a2': rwkv_a2[i], 'rwkv_g2': rwkv_g2[i], 'rwkv_k_k': rwkv_k_k[i],
            'rwkv_k_a': rwkv_k_a[i], 'rwkv_r_k': rwkv_r_k[i], 'rwkv_ln_w': rwkv_ln_w[i],
            'rwkv_ln_b': rwkv_ln_b[i], 'attn_sinks': attn_sinks[i], 'w_out': w_out[i],
            'norm_ffn': norm_ffn[i], 'w_gate': w_gate[i], 'w_up': w_up[i], 'w_down': w_down[i],
            'norm_ple': norm_ple[i], 'ple_gate': ple_gate[i], 'ple_proj': ple_proj[i],
        }
        shift0 = jnp.zeros((bp, RWKV_PROJ), xp.dtype)
        wkv_zero = jnp.zeros((bp, N_RWKV_HEADS, HEAD_DIM, HEAD_DIM), jnp.float32)
        xp, a1, a2, a3, a4 = decoder_layer(xp, p_prompt[i], shift0, wkv_zero, None, lp)
        wkv_p.append(a1); shift_p.append(a2); k_p.append(a3); v_p.append(a4)
        xs, b1, b2, b3, b4 = decoder_layer(xs, p_sample[i], state_shift[i], state_wkv[i],
                                           (cache_k[i], cache_v[i]), lp)
        wkv_s.append(b1); shift_s.append(b2); k_s.append(b3); v_s.append(b4)
    y_prompt = rms_norm(xp, final_norm)
    y_sample = rms_norm(xs, final_norm)
    return (y_prompt, y_sample,
            jnp.stack(wkv_p), jnp.stack(shift_p), jnp.stack(k_p), jnp.stack(v_p),
            jnp.stack(wkv_s), jnp.stack(shift_s), jnp.stack(k_s), jnp.stack(v_s))
```

```python
import bisect
import concourse.bass as bass
import concourse.mybir as mybir

F32 = mybir.dt.float32
BF16 = mybir.dt.bfloat16
AF = mybir.ActivationFunctionType
ALU = mybir.AluOpType
AX = mybir.AxisListType

SEM_ROLL = 12000


class Sched:
    def __init__(self, nc, stack):
        self.nc = nc
        self.stack = stack
        self.engs = {"pe": nc.tensor, "dve": nc.vector, "act": nc.scalar,
                     "pool": nc.gpsimd, "sp": nc.sync}
        self.sem = {}
        self.cnt = {}
        self.seq = {}
        self.sigs = {}
        self.sigseq = {}
        self.last_ins = {}
        self.last_seq = {}
        self.nsem = 0
        for e in self.engs:
            self._new_sem(e)
            self.seq[e] = 0
            self.sigs[e] = []
            self.sigseq[e] = []
            self.last_ins[e] = None
        self.semobj = {}
        for e in self.engs:
            pass
        self.seen = {e: {} for e in self.engs}
        self.lastw = {}
        self.readers = {}
        self.dma_sems = {}
        self.dma_rr = {}
        self.ndma = 0
        self.nwait = 0

    def _new_sem(self, e):
        name = "s_%s_%d" % (e, self.nsem)
        self.nsem += 1
        s = self.stack.enter_context(self.nc.semaphore(name))
        self.sem[e] = (name, s)
        self.cnt[e] = 0
        if not hasattr(self, "semh"):
            self.semh = {}
        self.semh[name] = s

    def _resolve(self, tok):
        if tok[0] == "d":
            return tok[1], tok[2]
        _, e, q = tok
        i = bisect.bisect_left(self.sigseq[e], q)
        if i < len(self.sigseq[e]):
            _, sn, v = self.sigs[e][i]
            return sn, v
        ins = self.last_ins[e]
        assert ins is not None
        if self.cnt[e] >= SEM_ROLL:
            self._new_sem(e)
        sn, s = self.sem[e]
        self.cnt[e] += 1
        ins.then_inc(s, 1)
        self.sigs[e].append((self.last_seq[e], sn, self.cnt[e]))
        self.sigseq[e].append(self.last_seq[e])
        return sn, self.cnt[e]

    def _wait(self, e, toks):
        need = {}
        for t in toks:
            sn, v = self._resolve(t)
            if self.seen[e].get(sn, 0) >= v:
                continue
            if need.get(sn, 0) < v:
                need[sn] = v
        for sn, v in need.items():
            self.engs[e].wait_ge(self.semh[sn], v)
            self.seen[e][sn] = v
            self.nwait += 1

    def _deps(self, e, r, w):
        toks = []
        for k in r:
            t = self.lastw.get(k)
            if t is not None and not (e == "pe" and t[0] == "c" and t[1] == "pe"):
                toks.append(t)
            if k.startswith("pb") or k.startswith("pt"):
                for t in self.readers.get(k, ()):
                    if t[0] == "c" and t[1] != e:
                        toks.append(t)
        for k in w:
            t = self.lastw.get(k)
            if t is not None and not (e == "pe" and t[0] == "c" and t[1] == "pe"):
                toks.append(t)
            for t in self.readers.get(k, ()):
                toks.append(t)
        return toks

    def _record(self, tok, r, w):
        for k in r:
            self.readers.setdefault(k, []).append(tok)
        for k in w:
            self.lastw[k] = tok
            self.readers[k] = []

    def op(self, e, fn, r=(), w=()):
        self._wait(e, self._deps(e, r, w))
        ins = fn()
        self.seq[e] += 1
        self.last_ins[e] = ins
        self.last_seq[e] = self.seq[e]
        self._record(("c", e, self.seq[e]), r, w)
        return ins

    def dma(self, q, out, in_, r=(), w=(), **kw):
        ring = self.dma_sems.setdefault(q, [])
        if len(ring) < 8:
            name = "d_%s_%d" % (q, len(ring))
            s = self.stack.enter_context(self.nc.semaphore(name))
            self.semh[name] = s
            ring.append([name, 0])
            slot = ring[-1]
        else:
            i = self.dma_rr.get(q, 0)
            self.dma_rr[q] = (i + 1) % len(ring)
            slot = ring[i]
        toks = self._deps(q, r, w)
        if slot[1] > 0:
            toks.append(("d", slot[0], slot[1]))
        self._wait(q, toks)
        slot[1] += 16
        ins = self.engs[q].dma_start(out=out, in_=in_, **kw)
        ins.then_inc(self.semh[slot[0]], 16)
        self.seq[q] += 1
        self._record(("d", slot[0], slot[1]), r, w)
        self.ndma += 1
        return ins

    def finish(self, e="sp"):
        toks = []
        for k, t in self.lastw.items():
            toks.append(t)
        self._wait(e, toks)

import contextlib
import numpy as np
from concourse.bass_utils import run_bass_kernel_spmd

D = 2048
KC = 16
DFF = 5632
FC = 44
PROJ = 5056
RWP = 3520
TP = 256
NPASS = 2048 // TP
NBLK = 54
EPS = 1e-6
GN_EPS = 64e-5
SCALE = 0.125
DEC = 0.6065306597126334
SLOPES = [2.0 ** (-8.0 * (h + 1) / 16) for h in range(16)]
QPERM = [0, 4, 1, 5, 2, 6, 3, 7, 8, 12, 9, 13, 10, 14, 11, 15]
PV_GMIX, PV_GFFN, PV_GPLE, PV_MU = 0, 16, 32, 48
PV_W0, PV_A0, PV_KK, PV_KA, PV_RK, PV_LNW, PV_LNB = 76, 84, 92, 100, 108, 116, 124
PV_N = 132
C_ID, C_BD, C_IH, C_HSEL, C_MS, C_MI, C_MST, C_RST, C_DM, C_SB, C_ON, C_N = 0, 128, 256, 320, 322, 386, 450, 514, 770, 1026, 1042, 1170

RW_COL = [c * 128 for c in range(8)] + [1024 + c * 128 for c in range(8)] + \
         [2048 + c * 128 for c in range(8)] + [3072, 3168, 3264, 3392]
RW_M = [128] * 24 + [96, 96, 128, 128]


def hcol(h):
    return ((h % 2) * 8 + h // 2) * 64


class _Stop(Exception):
    pass


def build_program(debug=False, stop=None):
    nc = bass.Bass("TRN2", target_bir_lowering=False)
    din = lambda n, sh: nc.dram_tensor(n, sh, F32, kind="ExternalInput").ap()
    dout = lambda n, sh: nc.dram_tensor(n, sh, F32, kind="ExternalOutput").ap()
    xp = din("xp", [2048, D]); xs_d = din("xs", [16, D])
    pp = din("pp", [2048, 256]); ps_d = din("psm", [16, 256])
    swkv = din("swkv", [16, 16, 64, 64])
    sshT = din("sshT", [RWP, 16])
    ck = din("ck", [16, 128, 256]); cv = din("cv", [16, 128, 256])
    w_in = din("w_in", [D, PROJ]); w_out = din("w_out", [D, D])
    w_gate = din("w_gate", [D, DFF]); w_up = din("w_up", [D, DFF]); w_down = din("w_down", [DFF, D])
    ple_gate = din("ple_gate", [D, D]); ple_proj = din("ple_proj", [256, D])
    w2 = din("w2", [96, 1024]); a2 = din("a2", [96, 1024]); g2 = din("g2", [256, 1024])
    pv_d = din("pv", [128, PV_N]); cst_d = din("cst", [128, C_N])
    fng = din("fng", [1, D]); esk = din("sinks", [1, 16])

    wsc = nc.dram_tensor("wsc", [NBLK, 128, KC * 512], BF16, kind="Internal").ap()
    y_p = dout("y_p", [2048, D]); y_s = dout("y_s", [16, D])
    wkv_p = dout("wkv_p", [16, 64, 64])
    sh_p = dout("sh_p", [RWP, 1])
    k_p = dout("k_p", [128, 256]); v_p = dout("v_p", [128, 256])
    wkv_s = dout("wkv_s", [16, 16, 64, 64])
    sh_s = dout("sh_s", [RWP, 16])
    k_s = dout("k_s", [16, 128, 256]); v_s = dout("v_s", [16, 128, 256])

    with contextlib.ExitStack() as st:
        S = Sched(nc, st)
        sb = lambda n, sh, dt=F32: st.enter_context(nc.sbuf_tensor("sb_" + n, sh, dt))
        V, A, P, PE = nc.vector, nc.scalar, nc.gpsimd, nc.tensor
        PF = st.enter_context(nc.psum_tensor("PF", [128, 4096], F32))
        PTb = PF[:, 3072:4096].bitcast(BF16)
        pbk = lambda b: "pb%d" % b
        bank = lambda b: PF[:, b * 512:(b + 1) * 512]

        x_sb = sb("x_sb", [128, 2, D])
        hb = sb("hb", [128, 2, D], BF16)
        hT = sb("hT", [128, KC, TP], BF16)
        wb = [sb("wb%d" % i, [128, KC, 512], BF16) for i in range(2)]
        pv = sb("pv", [128, PV_N]); cst = sb("cst", [128, C_N])
        omu = sb("omu", [128, 28]); omka = sb("omka", [128, 8])
        cstb = sb("cstb", [128, 384], BF16)
        fn_bc = sb("fn_bc", [128, D])
        esink = sb("esink", [128, 16])
        w2b = sb("w2b", [128, 1024], BF16); a2b = sb("a2b", [128, 1024], BF16)
        g2b = sb("g2b", [128, 2, 1024], BF16)
        ss = sb("ss", [128, 8]); sd = sb("sd", [128, 8]); rstd = sb("rstd", [128, 8])
        rwbig = sb("rwbig", [128, 6 * 8 * TP], BF16)
        def rwv(i):
            return rwbig[:, i * 8 * TP:(i + 1) * 8 * TP].rearrange("p (c t) -> p c t", c=8)
        Rt, Kt, At, Bt, vT, ybT = [rwv(i) for i in range(6)]
        hid = rwbig[:, 0:FC * TP].rearrange("p (c t) -> p c t", c=FC)
        gT = sb("gT", [128, 8, TP], BF16)
        dS = sb("dS", [128, 8, 16])
        tmV = sb("tmV", [128, 2, 1024], BF16); tmK = sb("tmK", [128, 2, 1024], BF16)
        tmB = sb("tmB", [128, 2, 1024], BF16)
        gC = sb("gC", [128, 8, 4])
        carry = sb("carry", [128, 28]); zlast = sb("zlast", [128, 28]); zs = sb("zs", [128, 28, 16])
        shT = sb("shT", [128, 28, 16]); msh = sb("msh", [128, 28, 16])
        txw = sb("txw", [128, TP], BF16); xaT = sb("xaT", [128, TP], BF16)
        sxg = sb("sxg", [128, 2, TP], BF16)
        tmr = [sb("tmr%d" % i, [128, TP + 1]) for i in range(3)]
        scr = sb("scr", [128, 13 * TP])
        def scv(i):
            return scr[:, i * TP:(i + 1) * TP]
        xs_r2 = sb("xs_r2", [128, TP]); xs_k2 = sb("xs_k2", [128, TP])
        xs_r, xs_k, xs_o, t_sg, t_L, t_eL, t_enL, t_eLm, t_a, t_kkr, t_sq, t_b, t_kh = [scv(i) for i in range(13)]
        t_rkr = sb("t_rkr", [128, TP], BF16)
        xs_rb = [xs_r, xs_r2]; xs_kb = [xs_k, xs_k2]
        def epv(i):
            return scr[:, i * 512:(i + 1) * 512].rearrange("p (c t) -> p c t", c=4)
        yT_sb, e_sq, e_m, e_v = [epv(i) for i in range(4)]
        chb = sb("chb", [128, 11 * 1024], BF16)
        def chv(i):
            return chb[:, i * 1024:(i + 1) * 1024]
        cY = [chv(0), chv(1)]; cYT = [chv(2), chv(3)]; cP = [chv(4), chv(5)]
        cNak, cMrb, cMrk, cW, cU = chv(6), chv(7), chv(8), chv(9), chv(10)
        chf = sb("chf", [128, 2048])
        cT1 = chf[:, 0:1024]; cYs = chf[:, 1024:2048]
        ST = sb("ST", [128, 8, 64]); STb = sb("STb", [128, 8, 128], BF16)
        qT = sb("qT", [128, 8, TP], BF16); kT = sb("kT", [128, 2, 128 + TP], BF16)
        Vall = sb("Vall", [128, 3, 4, 66], BF16)
        t_f1 = sb("t_f1", [128, 512]); t_f2 = sb("t_f2", [128, 512])
        a_sc = t_f1; ktm = t_f2
        a_PT = [sb("a_PT%d" % i, [128, 512], BF16) for i in range(2)]
        a_den = sb("a_den", [128, 4])
        wpp = sb("wpp", [128, 2, 512], BF16)
        pT_ = sb("pT_", [128, 2, TP], BF16); pe_sb = sb("pe_sb", [128, 2, 256]); pe_b = sb("pe_b", [128, 2, 256], BF16)
        x1 = x_sb[:, 1, :]
        sS = x1[:, 0:512].rearrange("p (c v) -> p c v", c=8)
        sSn = x1[:, 512:1024].rearrange("p (c v) -> p c v", c=8)
        sBlk = x1[:, 1024:2048].rearrange("p (c v) -> p c v", c=8)
        sAbd = chf[:, 0:1024].rearrange("p (c v) -> p c v", c=8)
        sRv = chf[:, 1024:1536].rearrange("p (c v) -> p c v", c=8)
        sT1 = chf[:, 1536:2048].rearrange("p (c v) -> p c v", c=8)
        chb32 = chb[:, 0:4096].bitcast(F32)
        sKV = chb32[:, 0:512]; sKn = sKV[:, 0:256]; sVn = sKV[:, 256:512]
        sPT = chb32[:, 512:768].rearrange("p (b h) -> p b h", b=16)
        sPTn = chb32[:, 768:1024].rearrange("p (b h) -> p b h", b=16)
        vf = chb32[:, 1024:1152].rearrange("p (c b) -> p c b", c=8)
        sKT = chb[:, 4096:4352].rearrange("p (k n) -> p k n", k=2)

        ident = cst[:, C_ID:C_ID + 128]; BDf = cst[:, C_BD:C_BD + 128]
        identb = cstb[:, 0:128]; BDb = cstb[:, 128:256]
        mS = cst[:, C_MS:C_MS + 64]; mI = cst[:, C_MI:C_MI + 64]; mST = cst[:, C_MST:C_MST + 64]
        rst = cst[:, C_RST:C_RST + 256]
        Dm = cst[:, C_DM:C_DM + 256]
        sbias = cst[:, C_SB:C_SB + 16]
        Ihalf = cst[:, C_IH:C_IH + 64]

        S.dma("sp", pv[:], pv_d[:, :], w=["pv"])
        S.dma("sp", cst[:], cst_d[:, :], w=["cst"])
        S.dma("sp", fn_bc[:], fng.partition_broadcast(128), w=["fn_bc"])
        S.dma("sp", esink[:], esk.partition_broadcast(128), w=["esink"])
        S.dma("pool", w2b[0:96, :], w2[:, :], w=["w2b"])
        S.dma("pool", a2b[0:96, :], a2[:, :], w=["a2b"])
        S.dma("pool", g2b[:], g2.rearrange("(k p) n -> p k n", p=128), w=["g2b"])
        S.op("dve", lambda: V.tensor_copy(cstb[:, 0:256], cst[:, 0:256]), r=["cst"], w=["cstb"])
        S.op("dve", lambda: V.tensor_scalar(omu[:], pv[:, PV_MU:PV_MU + 28], -1.0, 1.0, ALU.mult, ALU.add), r=["pv"], w=["omu"])
        S.op("dve", lambda: V.tensor_scalar(omka[:], pv[:, PV_KA:PV_KA + 8], -1.0, 1.0, ALU.mult, ALU.add), r=["pv"], w=["omka"])
        S.op("act", lambda: A.activation(esink[:], esink[:], AF.Exp), r=["esink"], w=["esink"])
        S.op("pool", lambda: P.memset(carry[:], 0.0), w=["carry%d" % i for i in range(28)])
        S.op("pool", lambda: P.memset(ST[:], 0.0), w=["ST"])
        S.op("pool", lambda: P.memset(STb[:], 0.0), w=["STb"])
        S.op("pool", lambda: P.memset(Vall[:], 1.0), w=["Vall0", "Vall1", "Vall2"])
        S.op("pool", lambda: P.memset(kT[:], 0.0), w=["kT"])
        for i_ in range(2):
            S.op("pool", lambda: P.memset(wb[i_][:], 0.0), w=["wb%d" % i_])

        def ckp(k):
            if stop == k:
                raise _Stop()
        wslot = [0]

        wblk = [0]

        def wload(W, specs, nk=KC, row0=0):
            sl = wslot[0]; wslot[0] ^= 1
            bi = wblk[0] % NBLK; first = wblk[0] < NBLK; wblk[0] += 1
            if first:
                for sp_ in specs:
                    c0, n, d0 = sp_[0:3]
                    Wm = sp_[3] if len(sp_) > 3 else W
                    src = Wm[row0:row0 + nk * 128, c0:c0 + n].rearrange("(k p) n -> p k n", p=128)
                    S.dma("pool", wb[sl][:, 0:nk, d0:d0 + n], src, w=["wb%d" % sl])
                S.dma("sp", wsc[bi], wb[sl][:, :, :].rearrange("p k n -> p (k n)"), r=["wb%d" % sl], w=["wsc%d" % bi])
            else:
                S.dma("pool", wb[sl][:, :, :].rearrange("p k n -> p (k n)"), wsc[bi], r=["wsc%d" % bi], w=["wb%d" % sl])
            return sl

        ev_rr = [0]

        def evac(out, in_, r, w, scale=None):
            ev_rr[0] ^= 1
            if ev_rr[0]:
                if scale is None:
                    S.op("act", lambda: A.activation(out, in_, AF.Copy), r=r, w=w)
                else:
                    S.op("act", lambda: A.activation(out, in_, AF.Copy, scale=scale), r=r, w=w)
            else:
                if scale is None:
                    S.op("dve", lambda: V.tensor_copy(out, in_), r=r, w=w)
                else:
                    S.op("dve", lambda: V.tensor_scalar(out, in_, scale, None, ALU.mult), r=r, w=w)

        pt_rr = [0]

        def norm_T(subs, gcol, ntok):
            for s, (t0, n) in enumerate(subs):
                S.op("act", lambda: A.activation(hb[0:n, s, :], x_sb[0:n, s, :], AF.Square, accum_out=ss[0:n, s:s + 1]),
                     r=["x%d" % s], w=["hb%d" % s, "ss"])
                S.op("act", lambda: A.activation(sd[0:n, s:s + 1], ss[0:n, s:s + 1], AF.Sqrt, bias=EPS, scale=1.0 / D),
                     r=["ss"], w=["sd"])
                S.op("dve", lambda: V.reciprocal(rstd[0:n, s:s + 1], sd[0:n, s:s + 1]), r=["sd"], w=["rstd%d" % s])
                S.op("dve", lambda: V.tensor_scalar(hb[0:n, s, :], x_sb[0:n, s, :], rstd[0:n, s:s + 1], None, ALU.mult),
                     r=["x%d" % s, "rstd%d" % s], w=["hb%d" % s])
            for kc in range(KC):
                pt_rr[0] ^= 1
                pt = PTb[:, pt_rr[0] * 1024:(pt_rr[0] + 1) * 1024]; pk = "pb%d" % (6 + pt_rr[0])
                for s, (t0, n) in enumerate(subs):
                    S.op("pe", lambda: PE.transpose(pt[:, t0:t0 + n], hb[0:n, s, kc * 128:(kc + 1) * 128], identb[0:n, 0:n]),
                         r=["hb%d" % s, "cstb"], w=[pk])
                evac(hT[:, kc, 0:ntok], pt[:, 0:ntok], r=[pk, "pv"], w=["hT%d" % kc], scale=pv[:, gcol + kc:gcol + kc + 1])

        def proj_fm(sl, d0, M, ps, pkeys, ntok):
            for kc in range(KC):
                S.op("pe", lambda: PE.matmul(ps[0:M, 0:ntok], wb[sl][:, kc, d0:d0 + M], hT[:, kc, 0:ntok],
                                             start=(kc == 0), stop=(kc == KC - 1)),
                     r=["wb%d" % sl, "hT%d" % kc], w=pkeys)

        def tshift(ci, ps, pkeys, M, ntok, sample, last, out_ap, okeys):
            mu = pv[0:M, PV_MU + ci:PV_MU + ci + 1]
            if sample:
                S.op("act", lambda: A.activation(zs[0:M, ci, :], ps[0:M, 0:16], AF.Copy), r=pkeys, w=["zs"])
                S.op("dve", lambda: V.scalar_tensor_tensor(out_ap, ps[0:M, 0:16], omu[0:M, ci:ci + 1], msh[0:M, ci, :],
                                                           ALU.mult, ALU.add), r=pkeys + ["omu", "msh"], w=okeys)
                return
            tm = tmr[ci % 3]; tk = "tmr%d" % (ci % 3)
            S.op("act", lambda: A.activation(tm[0:M, 0:1], carry[0:M, ci:ci + 1], AF.Copy), r=["carry%d" % ci], w=[tk])
            S.op("act", lambda: A.activation(tm[0:M, 1:1 + ntok], ps[0:M, 0:ntok], AF.Copy, scale=mu), r=pkeys + ["pv"], w=[tk])
            S.op("act", lambda: A.activation(carry[0:M, ci:ci + 1], tm[0:M, ntok:ntok + 1], AF.Copy), r=[tk], w=["carry%d" % ci])
            if last:
                S.op("act", lambda: A.activation(zlast[0:M, ci:ci + 1], ps[0:M, ntok - 1:ntok], AF.Copy), r=pkeys, w=["zlast"])
            S.op("dve", lambda: V.scalar_tensor_tensor(out_ap, ps[0:M, 0:ntok], omu[0:M, ci:ci + 1], tm[0:M, 0:ntok],
                                                       ALU.mult, ALU.add), r=pkeys + ["omu", tk], w=okeys)

        def prep_P(c, sl, ntok, sample, last):
            N = ntok
            b0 = 0 if c % 2 == 0 else 3
            pr, pk_, pvv = bank(b0), bank(b0 + 1), bank(b0 + 2)
            kr_, kk_, kv_ = ["pb%d" % (b0 + i) for i in range(3)]
            xs_r, xs_k = xs_rb[c % 2], xs_kb[c % 2]
            xr_k, xk_k = "xs_r%d" % (c % 2), "xs_k%d" % (c % 2)
            if sl is not None:
                proj_fm(sl, 0, 128, pr, [kr_], N)
                proj_fm(sl, 128, 128, pk_, [kk_], N)
                proj_fm(sl, 256, 128, pvv, [kv_], N)
                return
            tshift(c, pr, [kr_], 128, N, sample, last, xs_r[:, 0:N], [xr_k])
            tshift(8 + c, pk_, [kk_], 128, N, sample, last, xs_k[:, 0:N], [xk_k])
            tshift(16 + c, pvv, [kv_], 128, N, sample, last, vT[:, c, 0:N], ["vT%d" % c])

        def prep_H(c, ntok, sample, last):
            N = ntok
            xs_r, xs_k = xs_rb[c % 2], xs_kb[c % 2]
            xr_k, xk_k = "xs_r%d" % (c % 2), "xs_k%d" % (c % 2)
            cs = slice(c * 128, (c + 1) * 128)
            pc = lambda o: pv[:, o + c:o + c + 1]
            S.op("pe", lambda: PE.matmul(bank(6)[:, 0:N], w2b[0:96, cs], txw[0:96, 0:N], start=True, stop=True),
                 r=["w2b", "txw"], w=["pb6"])
            S.op("pe", lambda: PE.matmul(bank(7)[:, 0:N], a2b[0:96, cs], xaT[0:96, 0:N], start=True, stop=True),
                 r=["a2b", "xaT"], w=["pb7"])
            S.op("dve", lambda: V.tensor_scalar(t_kkr[:, 0:N], xs_k[:, 0:N], pc(PV_KK), None, ALU.mult), r=[xk_k, "pv"], w=["t_kkr"])
            S.op("act", lambda: A.activation(t_sg[:, 0:N], bank(6)[:, 0:N], AF.Sigmoid, bias=pc(PV_W0)), r=["pb6", "pv"], w=["t_sg"])
            S.op("act", lambda: A.activation(t_a[:, 0:N], bank(7)[:, 0:N], AF.Sigmoid, bias=pc(PV_A0)), r=["pb7", "pv"], w=["t_a"])
            S.op("act", lambda: A.activation(t_sq[:, 0:N], t_kkr[:, 0:N], AF.Square), r=["t_kkr"], w=["t_sq"])
            for k2 in range(2):
                S.op("pe", lambda: PE.matmul(bank(6)[:, 0:N], g2b[:, k2, cs], sxg[:, k2, 0:N], start=(k2 == 0), stop=(k2 == 1)),
                     r=["g2b", "sxg"], w=["pb6"])
            S.op("pe", lambda: PE.matmul(bank(7)[:, 0:N], BDf, t_sq[:, 0:N], start=True, stop=True), r=["cst", "t_sq"], w=["pb7"])
            S.op("dve", lambda: V.tensor_scalar(t_sg[:, 0:N], t_sg[:, 0:N], -DEC, None, ALU.mult), r=["t_sg"], w=["t_sg"])
            if sample:
                S.op("act", lambda: A.activation(dS[:, c, :], t_sg[:, 0:N], AF.Exp), r=["t_sg"], w=["dS"])
            else:
                S.op("dve", lambda: V.tensor_tensor_scan(t_L[:, 0:N], rst[:, 0:N], t_sg[:, 0:N], 0.0, ALU.mult, ALU.add),
                     r=["t_sg", "cst"], w=["t_L"])
                S.op("act", lambda: A.activation(t_eL[:, 0:N], t_L[:, 0:N], AF.Exp), r=["t_L"], w=["t_eL"])
                S.op("act", lambda: A.activation(t_enL[:, 0:N], t_L[:, 0:N], AF.Exp, scale=-1.0), r=["t_L"], w=["t_enL"])
                eLv = t_eL[:, 0:N].rearrange("p (j t) -> p j t", t=64)
                eLmv = t_eLm[:, 0:N].rearrange("p (j t) -> p j t", t=64)
                S.op("dve", lambda: V.memset(eLmv[:, :, 0:1], 1.0), w=["t_eLm"])
            S.op("dve", lambda: V.tensor_scalar(t_sq[:, 0:N], bank(7)[:, 0:N], 1e-24, None, ALU.max), r=["pb7"], w=["t_sq"])
            S.op("act", lambda: A.activation(t_sq[:, 0:N], t_sq[:, 0:N], AF.Sqrt), r=["t_sq"], w=["t_sq"])
            S.op("act", lambda: A.activation(gT[:, c, 0:N], bank(6)[:, 0:N], AF.Copy), r=["pb6"], w=["gT%d" % c])
            if not sample:
                S.op("dve", lambda: V.tensor_copy(eLmv[:, :, 1:64], eLv[:, :, 0:63]), r=["t_eL"], w=["t_eLm"])
                S.op("act", lambda: A.activation(gC[:, c, :], eLv[:, :, 63], AF.Copy), r=["t_eL"], w=["gC"])
                S.op("dve", lambda: V.tensor_tensor(Rt[:, c, 0:N], xs_r[:, 0:N], t_eL[:, 0:N], ALU.mult), r=[xr_k, "t_eL"], w=["Rt%d" % c])
            else:
                S.op("dve", lambda: V.tensor_copy(Rt[:, c, 0:N], xs_r[:, 0:N]), r=[xr_k], w=["Rt%d" % c])
            S.op("dve", lambda: V.reciprocal(t_sq[:, 0:N], t_sq[:, 0:N]), r=["t_sq"], w=["t_sq"])
            S.op("dve", lambda: V.tensor_tensor(t_kkr[:, 0:N], t_kkr[:, 0:N], t_sq[:, 0:N], ALU.mult), r=["t_kkr", "t_sq"], w=["t_kkr"])
            S.op("dve", lambda: V.tensor_tensor(t_b[:, 0:N], t_kkr[:, 0:N], t_a[:, 0:N], ALU.mult), r=["t_kkr", "t_a"], w=["t_b"])
            S.op("dve", lambda: V.tensor_scalar(t_a[:, 0:N], t_a[:, 0:N], pc(PV_KA), omka[:, c:c + 1], ALU.mult, ALU.add),
                 r=["t_a", "pv", "omka"], w=["t_a"])
            S.op("dve", lambda: V.tensor_tensor(t_kh[:, 0:N], xs_k[:, 0:N], t_a[:, 0:N], ALU.mult), r=[xk_k, "t_a"], w=["t_kh"])
            if sample:
                S.op("dve", lambda: V.tensor_scalar(At[:, c, 0:N], t_kkr[:, 0:N], -1.0, None, ALU.mult), r=["t_kkr"], w=["At%d" % c])
                S.op("dve", lambda: V.tensor_copy(Bt[:, c, 0:N], t_b[:, 0:N]), r=["t_b"], w=["Bt%d" % c])
                S.op("dve", lambda: V.tensor_copy(Kt[:, c, 0:N], t_kh[:, 0:N]), r=["t_kh"], w=["Kt%d" % c])
            else:
                S.op("dve", lambda: V.scalar_tensor_tensor(At[:, c, 0:N], t_kkr[:, 0:N], -1.0, t_eLm[:, 0:N], ALU.mult, ALU.mult),
                     r=["t_kkr", "t_eLm"], w=["At%d" % c])
                S.op("dve", lambda: V.tensor_tensor(Bt[:, c, 0:N], t_b[:, 0:N], t_enL[:, 0:N], ALU.mult), r=["t_b", "t_enL"], w=["Bt%d" % c])
                S.op("dve", lambda: V.tensor_tensor(Kt[:, c, 0:N], t_kh[:, 0:N], t_enL[:, 0:N], ALU.mult), r=["t_kh", "t_enL"], w=["Kt%d" % c])

        def prep_tail(c, ntok):
            N = ntok
            xs_r = xs_rb[c % 2]; xr_k = "xs_r%d" % (c % 2)
            pc = lambda o: pv[:, o + c:o + c + 1]
            S.op("dve", lambda: V.scalar_tensor_tensor(t_rkr[:, 0:N], xs_r[:, 0:N], pc(PV_RK), t_kh[:, 0:N], ALU.mult, ALU.mult),
                 r=[xr_k, "t_kh", "pv"], w=["t_rkr"])
            S.op("pe", lambda: PE.matmul(bank(6)[:, 0:N], BDb, t_rkr[:, 0:N], start=True, stop=True), r=["cstb", "t_rkr"], w=["pb6"])
            S.op("dve", lambda: V.tensor_tensor(ybT[:, c, 0:N], bank(6)[:, 0:N], vT[:, c, 0:N], ALU.mult), r=["pb6", "vT%d" % c], w=["ybT%d" % c])

        def epilogue(col0, n, srcs=None):
            for hf in range(2):
                if srcs is None:
                    for c4 in range(4):
                        c = hf * 4 + c4
                        S.op("pe", lambda: PE.transpose(bank(4)[:, c4 * 128:(c4 + 1) * 128], cYs[:, c * 128:(c + 1) * 128], ident),
                             r=["cYs_0", "cYs_1", "cst"], w=["pb4"])
                    src_ap, sk = bank(4)[:, 0:4 * n], "pb4"
                else:
                    src_ap, sk = srcs[hf]
                src = src_ap.rearrange("p (c t) -> p c t", c=4)
                yv = yT_sb[:, :, 0:n]; sq = e_sq[:, :, 0:n]; em = e_m[:, :, 0:n]; ev = e_v[:, :, 0:n]
                S.op("act", lambda: A.activation(yv, src, AF.Copy), r=[sk], w=["yT_sb"])
                S.op("act", lambda: A.activation(sq, src, AF.Square), r=[sk], w=["e_sq"])
                for c4 in range(4):
                    S.op("pe", lambda: PE.matmul(bank(5)[:, c4 * n:(c4 + 1) * n], BDf, yT_sb[:, c4, 0:n], start=True, stop=True),
                         r=["cst", "yT_sb"], w=["pb5"])
                m_ps = bank(5)[:, 0:4 * n].rearrange("p (c t) -> p c t", c=4)
                S.op("act", lambda: A.activation(em, m_ps, AF.Copy, scale=1.0 / 64), r=["pb5"], w=["e_m"])
                yield
                for c4 in range(4):
                    S.op("pe", lambda: PE.matmul(bank(5)[:, c4 * n:(c4 + 1) * n], BDf, e_sq[:, c4, 0:n], start=True, stop=True),
                         r=["cst", "e_sq"], w=["pb5"])
                S.op("dve", lambda: V.tensor_tensor(ev, em, em, ALU.mult), r=["e_m"], w=["e_v"])
                S.op("dve", lambda: V.scalar_tensor_tensor(ev, m_ps, 1.0 / 64, ev, ALU.mult, ALU.subtract), r=["pb5", "e_v"], w=["e_v"])
                S.op("act", lambda: A.activation(ev, ev, AF.Sqrt, bias=GN_EPS), r=["e_v"], w=["e_v"])
                S.op("dve", lambda: V.reciprocal(ev, ev), r=["e_v"], w=["e_v"])
                S.op("dve", lambda: V.tensor_tensor(yv, yv, em, ALU.subtract), r=["yT_sb", "e_m"], w=["yT_sb"])
                S.op("dve", lambda: V.tensor_tensor(yv, yv, ev, ALU.mult), r=["yT_sb", "e_v"], w=["yT_sb"])
                yield
                for c4 in range(4):
                    c = hf * 4 + c4
                    S.op("dve", lambda: V.tensor_scalar(yT_sb[:, c4, 0:n], yT_sb[:, c4, 0:n], pv[:, PV_LNW + c:PV_LNW + c + 1],
                                                        pv[:, PV_LNB + c:PV_LNB + c + 1], ALU.mult, ALU.add), r=["yT_sb", "pv"], w=["yT_sb"])
                    S.op("dve", lambda: V.tensor_tensor(yT_sb[:, c4, 0:n], yT_sb[:, c4, 0:n], ybT[:, c, col0:col0 + n], ALU.add),
                         r=["yT_sb", "ybT%d" % c], w=["yT_sb"])
                    S.op("dve", lambda: V.tensor_tensor(hT[:, c, col0:col0 + n], yT_sb[:, c4, 0:n], gT[:, c, col0:col0 + n], ALU.mult),
                         r=["yT_sb", "gT%d" % c], w=["hT%d" % c])
                yield

        def chunkA(j):
            hp = (j % 2) * 64; pq = "_%d" % (j % 2)
            tok = slice(j * 64, (j + 1) * 64)
            rows = slice(hp, hp + 64)
            A2 = PF[:, 0:1024]; B2 = PF[:, 1024:2048]
            ka, kb_ = ["pb0", "pb1"], ["pb2", "pb3"]
            def hmm(dst2, lh, rh, lh_n, rh_n, keys):
                for h in range(16):
                    c, hh = h // 2, h % 2
                    kr = slice(hh * 64, hh * 64 + 64)
                    S.op("pe", lambda: PE.matmul(dst2[rows, hcol(h):hcol(h) + 64], lh[kr, c, tok], rh[kr, c, tok], start=True, stop=True),
                         r=["%s%d" % (lh_n, c), "%s%d" % (rh_n, c)], w=keys)
            m3 = lambda m: m[rows, :].unsqueeze(1).broadcast_to([64, 16, 64])
            v3 = lambda t: t[rows, :].rearrange("p (h t) -> p h t", h=16)
            hmm(A2, Bt, At, "Bt", "At", ka)
            hmm(B2, At, Bt, "At", "Bt", kb_)
            S.op("dve", lambda: V.tensor_tensor(v3(cY[0]), v3(A2), m3(mS), ALU.mult), r=ka + ["cst"], w=["cY0" + pq])
            S.op("dve", lambda: V.tensor_tensor(v3(cYT[0]), v3(B2), m3(mST), ALU.mult), r=kb_ + ["cst"], w=["cYT0" + pq])
            yield
            hmm(A2, Kt, At, "Kt", "At", ka)
            S.op("dve", lambda: V.tensor_tensor(v3(cNak), v3(A2), m3(mS), ALU.mult), r=ka + ["cst"], w=["cNak" + pq])
            hmm(B2, Bt, Rt, "Bt", "Rt", kb_)
            S.op("dve", lambda: V.tensor_tensor(v3(cMrb), v3(B2), m3(mI), ALU.mult), r=kb_ + ["cst"], w=["cMrb" + pq])
            yield
            hmm(A2, Kt, Rt, "Kt", "Rt", ka)
            S.op("dve", lambda: V.tensor_tensor(v3(cMrk), v3(A2), m3(mI), ALU.mult), r=ka + ["cst"], w=["cMrk" + pq])
            idb3 = identb[rows, hp:hp + 64].unsqueeze(1).broadcast_to([64, 16, 64])
            S.op("dve", lambda: V.tensor_tensor(v3(cP[0]), v3(cY[0]), idb3, ALU.add), r=["cY0" + pq, "cstb"], w=["cP0" + pq])
            yield
            cur = 0
            for lvl in range(5):
                nxt = cur ^ 1
                Yc, YTc, Pc = cY[cur], cYT[cur], cP[cur]
                Yn, YTn, Pn = cY[nxt], cYT[nxt], cP[nxt]
                kY, kYT, kP = "cY%d" % cur + pq, "cYT%d" % cur + pq, "cP%d" % cur + pq
                for h in range(16):
                    hc = slice(hcol(h), hcol(h) + 64)
                    S.op("pe", lambda: PE.matmul(B2[rows, hc], Yc[rows, hc], YTc[rows, hc], start=True, stop=True), r=[kY, kYT], w=kb_)
                if lvl < 4:
                    for h in range(16):
                        hc = slice(hcol(h), hcol(h) + 64)
                        S.op("pe", lambda: PE.matmul(A2[rows, hc], YTc[rows, hc], Yc[rows, hc], start=True, stop=True), r=[kY, kYT], w=ka)
                S.op("act", lambda: A.activation(YTn[rows, :], B2[rows, :], AF.Copy), r=kb_, w=["cYT%d" % nxt + pq])
                if lvl < 4:
                    S.op("dve", lambda: V.tensor_copy(Yn[rows, :], A2[rows, :]), r=ka, w=["cY%d" % nxt + pq])
                yield
                for h in range(16):
                    hc = slice(hcol(h), hcol(h) + 64)
                    S.op("pe", lambda: PE.matmul(B2[rows, hc], YTn[rows, hc], Pc[rows, hc], start=True, stop=True), r=["cYT%d" % nxt + pq, kP], w=kb_)
                S.op("dve", lambda: V.tensor_tensor(Pn[rows, :], B2[rows, :], Pc[rows, :], ALU.add), r=kb_ + [kP], w=["cP%d" % nxt + pq])
                yield
                cur = nxt
            assert cur == 1

        def chunkB(j):
            hp = (j % 2) * 64; pq = "_%d" % (j % 2)
            s_, tok = j // 2, slice(j * 64, (j + 1) * 64)
            rows = slice(hp, hp + 64)
            TTt = cP[1]; kTT = "cP1" + pq
            C2 = PF[:, 2048:3072]; kc_ = ["pb4", "pb5"]
            for h in range(16):
                S.op("pe", lambda: PE.matmul(C2[rows, h * 64:(h + 1) * 64], cNak[rows, hcol(h):hcol(h) + 64], tmV[rows, s_, h * 64:(h + 1) * 64],
                                             start=True, stop=True), r=["cNak" + pq, "tmV"], w=kc_)
            S.op("act", lambda: A.activation(cT1[rows, :], C2[rows, :], AF.Copy), r=kc_, w=["cT1" + pq])
            yield
            for c in range(8):
                S.op("pe", lambda: PE.matmul(C2[rows, c * 128:(c + 1) * 128], At[:, c, tok], STb[:, c, :], start=True, stop=True),
                     r=["At%d" % c, "STb"], w=kc_)
            S.op("dve", lambda: V.tensor_tensor(cW[rows, :], C2[rows, :], cT1[rows, :], ALU.add), r=kc_ + ["cT1" + pq], w=["cW" + pq])
            yield
            for h in range(16):
                S.op("pe", lambda: PE.matmul(C2[rows, h * 64:(h + 1) * 64], TTt[rows, hcol(h):hcol(h) + 64], cW[rows, h * 64:(h + 1) * 64],
                                             start=True, stop=True), r=[kTT, "cW" + pq], w=kc_)
            S.op("act", lambda: A.activation(cU[rows, :], C2[rows, :], AF.Copy), r=kc_, w=["cU" + pq])
            yield
            for h in range(16):
                hs = slice(h * 64, (h + 1) * 64); hc = slice(hcol(h), hcol(h) + 64)
                S.op("pe", lambda: PE.matmul(C2[rows, hs], cMrb[rows, hc], cU[rows, hs], start=True, stop=False), r=["cMrb" + pq, "cU" + pq], w=kc_)
                S.op("pe", lambda: PE.matmul(C2[rows, hs], cMrk[rows, hc], tmV[rows, s_, hs], start=False, stop=True), r=["cMrk" + pq, "tmV"], w=kc_)
            S.op("act", lambda: A.activation(cT1[rows, :], C2[rows, :], AF.Copy), r=kc_, w=["cT1" + pq])
            yield
            for c in range(8):
                S.op("pe", lambda: PE.matmul(C2[rows, c * 128:(c + 1) * 128], Rt[:, c, tok], STb[:, c, :], start=True, stop=True),
                     r=["Rt%d" % c, "STb"], w=kc_)
            S.op("dve", lambda: V.tensor_tensor(cYs[rows, :], C2[rows, :], cT1[rows, :], ALU.add), r=kc_ + ["cT1" + pq], w=["cYs" + pq])
            yield
            SN = PF[:, 2048:2560]
            for h in range(16):
                c, hh = h // 2, h % 2
                hs = slice(h * 64, (h + 1) * 64)
                o = SN[hh * 64:hh * 64 + 64, c * 64:(c + 1) * 64]
                S.op("pe", lambda: PE.matmul(o, tmK[rows, s_, hs], tmV[rows, s_, hs], start=True, stop=False), r=["tmK", "tmV"], w=["pb4"])
                S.op("pe", lambda: PE.matmul(o, tmB[rows, s_, hs], cU[rows, hs], start=False, stop=True), r=["tmB", "cU" + pq], w=["pb4"])
            SN3 = SN.rearrange("p (c v) -> p c v", c=8)
            S.op("dve", lambda: V.tensor_tensor(ST[:], ST[:], SN3, ALU.add), r=["ST", "pb4"], w=["ST"])
            S.op("dve", lambda: V.tensor_tensor(ST[:], ST[:], gC[:, :, j:j + 1].broadcast_to([128, 8, 64]), ALU.mult), r=["ST", "gC"], w=["ST"])
            S.op("act", lambda: A.activation(STb[0:64, :, 0:64], ST[0:64, :, :], AF.Copy), r=["ST"], w=["STb"])
            S.op("act", lambda: A.activation(STb[64:128, :, 64:128], ST[64:128, :, :], AF.Copy), r=["ST"], w=["STb"])
            yield
            if j % 2 == 1:
                yield from epilogue((j // 2) * 128, 128)

        def att_block(p, n_):
            gblk = p * 2 + n_
            qs = slice(n_ * 128, (n_ + 1) * 128)
            sb_, sbk = bank(6), "pb6"
            ob, obk = bank(7), "pb7"
            for kv in range(4):
                half = (kv % 2) * 64; hr = slice(half, half + 64); kc2 = kv // 2
                kbs = [1] if gblk == 0 else [0, 1]
                for kb in kbs:
                    kcols = slice(n_ * 128 + kb * 128, n_ * 128 + kb * 128 + 128)
                    S.op("pe", lambda: PE.matmul(sb_[:, 0:512], kT[hr, kc2, kcols], qT[hr, (kv // 2) * 4:(kv // 2) * 4 + 4, qs],
                                                 start=True, stop=True), r=["kT", "qT"], w=[sbk])
                    for g in range(4):
                        h = 4 * kv + g
                        S.op("dve", lambda: V.scalar_tensor_tensor(a_sc[:, g * 128:(g + 1) * 128], Dm[:, kb * 128:(kb + 1) * 128],
                                                                   SLOPES[h] / SCALE, sb_[:, g * 128:(g + 1) * 128], ALU.mult, ALU.add),
                             r=[sbk, "cst"], w=["t_f1"])
                    S.op("act", lambda: A.activation(a_PT[kb][:, :], a_sc[:, :], AF.Exp, scale=SCALE), r=["t_f1"], w=["a_PT%d" % kb])
                    yield
                for g in range(4):
                    for ki, kb in enumerate(kbs):
                        S.op("pe", lambda: PE.matmul(ob[:, g * 66:(g + 1) * 66], a_PT[kb][:, g * 128:(g + 1) * 128], Vall[:, n_ + kb, kv, :],
                                                     start=(ki == 0), stop=(ki == len(kbs) - 1)),
                             r=["a_PT%d" % kb, "Vall%d" % (n_ + kb)], w=[obk])
                o3 = ob[:, 0:264].rearrange("p (g d) -> p g d", g=4)
                S.op("dve", lambda: V.tensor_tensor(a_den[:, :], o3[:, :, 64], esink[:, 4 * kv:4 * kv + 4], ALU.add), r=[obk, "esink"], w=["a_den"])
                S.op("dve", lambda: V.reciprocal(a_den[:, :], a_den[:, :]), r=["a_den"], w=["a_den"])
                S.op("dve", lambda: V.tensor_tensor(hb[:, n_, 1024 + kv * 256:1024 + (kv + 1) * 256].rearrange("p (g d) -> p g d", g=4),
                                                    o3[:, :, 0:64], a_den[:, :].unsqueeze(2).broadcast_to([128, 4, 64]), ALU.mult),
                     r=[obk, "a_den"], w=["hb%d" % n_])
                yield
            pt = PTb[:, 0:1024]; pk = "pb6"
            for c in range(8):
                S.op("pe", lambda: PE.transpose(pt[:, c * 128:(c + 1) * 128], hb[:, n_, 1024 + c * 128:1024 + (c + 1) * 128], identb),
                     r=["hb%d" % n_, "cstb"], w=[pk])
            for c in range(8):
                S.op("dve" if c % 2 else "act", (lambda: V.tensor_copy(hT[:, 8 + c, qs], pt[:, c * 128:(c + 1) * 128])) if c % 2 else
                     (lambda: A.activation(hT[:, 8 + c, qs], pt[:, c * 128:(c + 1) * 128], AF.Copy)), r=[pk], w=["hT%d" % (8 + c)])
            yield

        def interleave(gens):
            gens = list(gens)
            while gens:
                for g in list(gens):
                    try:
                        next(g)
                    except StopIteration:
                        gens.remove(g)

        def sample_mix():
            ONES = cst[:, C_ON:C_ON + 128]
            allk = lambda n: ["%s%d" % (n, c) for c in range(8)]
            wv = wkv_s.rearrange("b (c hh) k v -> b hh k c v", hh=2)
            sv = swkv.rearrange("b (c hh) k v -> b hh k c v", hh=2)
            bc8 = lambda ap, w_: ap.broadcast_to([128, 8, w_])
            c8 = lambda ap, w_: ap.rearrange("p (c v) -> p c v", c=8)
            W32 = chb.bitcast(F32)
            def mkset(b_sS, b_sSn, b_sBlk, b_sAbd, b_sRv, b_sT1):
                bfv = lambda base, n: base.bitcast(BF16)[:, 0:n]
                return dict(sS=c8(b_sS, 64), sSn=c8(b_sSn, 64), sT1=c8(b_sT1, 64),
                            sBlk=c8(bfv(b_sBlk, 1024), 128), sAbd=c8(bfv(b_sAbd, 1024), 128),
                            sSb=c8(b_sAbd[:, 512:768].bitcast(BF16), 64), sRv=c8(bfv(b_sRv, 512), 64))
            setA = mkset(x1[:, 0:512], x1[:, 512:1024], x1[:, 1024:2048], chf[:, 0:1024], chf[:, 1024:1536], chf[:, 1536:2048])
            setB = mkset(W32[:, 1792:2304], W32[:, 2304:2816], scr[:, 2048:3072], W32[:, 2816:3840], W32[:, 3840:4352], W32[:, 4352:4864])
            sets = [setA, setB]
            sKVb = [W32[:, 0:512], W32[:, 1152:1664]]
            sKTb = [W32[:, 4864:4992].bitcast(BF16).rearrange("p (k n) -> p k n", k=2), W32[:, 1664:1792].bitcast(BF16).rearrange("p (k n) -> p k n", k=2)]
            for i_ in range(2):
                S.op("dve", lambda: V.memset(sets[i_]["sBlk"], 0.0), w=["sBlk%d" % i_])
            for b in range(16):
                S.dma("sp", k_s[b, 0:127, :], ck[b, 1:128, :], w=["o_ks"])
                S.dma("sp", v_s[b, 0:127, :], cv[b, 1:128, :], w=["o_vs"])
            S.dma("sp", k_s[:, 127, :], ktm[0:16, 0:256], r=["ktm"], w=["o_ks"])
            S.dma("sp", v_s[:, 127, :], ktm[0:16, 256:512], r=["ktm"], w=["o_vs"])

            ckp(201)
            def rw_sample(b):
                i = b % 2; T_ = sets[i]; x = "%d" % i
                uB, vB = (4, 5) if i == 0 else (6, 7)
                for hh in range(2):
                    S.dma("sp", T_["sS"][hh * 64:(hh + 1) * 64, :, :], sv[b, hh], w=["sS" + x])
                S.op("dve", lambda: V.tensor_tensor(T_["sAbd"], BDf.unsqueeze(1).broadcast_to([128, 8, 128]), bc8(At[:, :, b:b + 1], 128), ALU.mult),
                     r=["cst"] + allk("At"), w=["sAbd" + x])
                S.op("dve", lambda: V.tensor_tensor(T_["sRv"], Ihalf.unsqueeze(1).broadcast_to([128, 8, 64]), bc8(vT[:, :, b:b + 1], 64), ALU.mult),
                     r=["cst"] + allk("vT"), w=["sRv" + x])
                yield
                S.op("act", lambda: A.activation(T_["sSb"], T_["sS"], AF.Copy), r=["sS" + x], w=["sSb" + x])
                for c in range(8):
                    S.op("pe", lambda: PE.matmul(bank(uB)[:, c * 64:(c + 1) * 64], T_["sAbd"][:, c, :], T_["sSb"][:, c, :], start=True, stop=True),
                         r=["sAbd" + x, "sSb" + x], w=["pb%d" % uB])
                for c in range(8):
                    S.op("pe", lambda: PE.matmul(bank(vB)[:, c * 64:(c + 1) * 64], BDb, T_["sRv"][:, c, :], start=True, stop=True),
                         r=["cstb", "sRv" + x], w=["pb%d" % vB])
                U3 = c8(bank(uB), 64); V3 = c8(bank(vB), 64)
                S.op("dve", lambda: V.tensor_tensor(T_["sSn"], T_["sS"], bc8(dS[:, :, b:b + 1], 64), ALU.mult), r=["sS" + x, "dS"], w=["sSn" + x])
                yield
                S.op("dve", lambda: V.tensor_tensor(T_["sT1"], U3, bc8(Bt[:, :, b:b + 1], 64), ALU.mult), r=["pb%d" % uB] + allk("Bt"), w=["sT1" + x])
                S.op("dve", lambda: V.tensor_tensor(T_["sSn"], T_["sSn"], T_["sT1"], ALU.add), r=["sSn" + x, "sT1" + x], w=["sSn" + x])
                S.op("dve", lambda: V.tensor_tensor(T_["sT1"], V3, bc8(Kt[:, :, b:b + 1], 64), ALU.mult), r=["pb%d" % vB] + allk("Kt"), w=["sT1" + x])
                S.op("dve", lambda: V.tensor_tensor(T_["sSn"], T_["sSn"], T_["sT1"], ALU.add), r=["sSn" + x, "sT1" + x], w=["sSn" + x])
                yield
                for hh in range(2):
                    S.dma("sp", wv[b, hh], T_["sSn"][hh * 64:(hh + 1) * 64, :, :], r=["sSn" + x], w=["o_wkvs"])
                S.op("act", lambda: A.activation(T_["sBlk"][0:64, :, 0:64], T_["sSn"][0:64, :, :], AF.Copy), r=["sSn" + x], w=["sBlk" + x])
                S.op("act", lambda: A.activation(T_["sBlk"][64:128, :, 64:128], T_["sSn"][64:128, :, :], AF.Copy), r=["sSn" + x], w=["sBlk" + x])
                for c in range(8):
                    S.op("pe", lambda: PE.matmul(bank(c // 4)[:, (c % 4) * 16 + b:(c % 4) * 16 + b + 1], T_["sBlk"][:, c, :], Rt[:, c, b:b + 1],
                                                 start=True, stop=True), r=["sBlk" + x, "Rt%d" % c], w=["pb%d" % (c // 4)])
                yield

            def at_sample(b):
                i = b % 2; x = "%d" % i
                kvb = sKVb[i]; Kn = kvb[:, 0:256]; Vn = kvb[:, 256:512]; KT = sKTb[i]
                tB, sB = (4, 5) if i == 0 else (6, 7)
                S.dma("sp", Kn[0:127, :], ck[b, 1:128, :], w=["sKa" + x])
                S.dma("sp", Vn[0:127, :], cv[b, 1:128, :], w=["sVa" + x])
                S.dma("sp", Kn[127:128, :], ktm[b:b + 1, 0:256], r=["ktm"], w=["sKb" + x])
                S.dma("sp", Vn[127:128, :], ktm[b:b + 1, 256:512], r=["ktm"], w=["sVb" + x])
                yield
                for k2 in range(2):
                    S.op("pe", lambda: PE.transpose(bank(tB)[:, k2 * 128:(k2 + 1) * 128], Kn[:, k2 * 128:(k2 + 1) * 128], ident),
                         r=["sKa" + x, "sKb" + x, "cst"], w=["pb%d" % tB])
                S.op("act", lambda: A.activation(KT[:, :, :], bank(tB)[:, 0:256].rearrange("p (k n) -> p k n", k=2), AF.Copy), r=["pb%d" % tB], w=["sKT" + x])
                yield
                for kv in range(4):
                    hr = slice((kv % 2) * 64, (kv % 2) * 64 + 64)
                    ob_ = sB if kv % 2 == 0 else tB
                    S.op("pe", lambda: PE.matmul(bank(ob_)[:, 256 + kv * 4:256 + kv * 4 + 4], KT[hr, kv // 2, :], qT[hr, (kv // 2) * 4:(kv // 2) * 4 + 4, b],
                                                 start=True, stop=True), r=["sKT" + x, "qT"], w=["pb%d" % ob_])
                ev_ = lambda ap, o: ap.rearrange("p (k two g) -> p k two g", k=2, two=2)[:, :, o, :]
                for o, ob_ in ((0, sB), (1, tB)):
                    S.op("dve", lambda: V.tensor_tensor(ev_(sPT[:, b, :], o), ev_(bank(ob_)[:, 256:272], o), ev_(sbias, o), ALU.add),
                         r=["pb%d" % ob_, "cst"], w=["sPT" + x])
                S.op("act", lambda: A.activation(sPT[:, b, :], sPT[:, b, :], AF.Exp, scale=SCALE), r=["sPT" + x], w=["sPT" + x])
                yield
                S.op("pe", lambda: PE.matmul(bank(sB)[:, 288:304], ONES, sPT[:, b, :], start=True, stop=True), r=["cst", "sPT" + x], w=["pb%d" % sB])
                S.op("dve", lambda: V.tensor_tensor(sPTn[:, b, :], bank(sB)[:, 288:304], esink[:, :], ALU.add), r=["pb%d" % sB, "esink"], w=["sPTn" + x])
                S.op("dve", lambda: V.reciprocal(sPTn[:, b, :], sPTn[:, b, :]), r=["sPTn" + x], w=["sPTn" + x])
                S.op("dve", lambda: V.tensor_tensor(sPTn[:, b, :], sPTn[:, b, :], sPT[:, b, :], ALU.mult), r=["sPTn" + x, "sPT" + x], w=["sPTn" + x])
                yield
                for h in range(16):
                    kv = h // 4
                    S.op("pe", lambda: PE.matmul(bank(3)[(h % 2) * 64:(h % 2) * 64 + 64, (h // 2) * 16 + b:(h // 2) * 16 + b + 1],
                                                 Vn[:, kv * 64:(kv + 1) * 64], sPTn[:, b, h:h + 1], start=True, stop=True),
                         r=["sVa" + x, "sVb" + x, "sPTn" + x], w=["pb3"])
                yield

            def pairs(fn):
                import os
                for b in range(0, 16, 2):
                    if os.environ.get("DBG_SEQ"):
                        interleave([fn(b)]); interleave([fn(b + 1)])
                    else:
                        interleave([fn(b), fn(b + 1)])
            pairs(rw_sample)
            ckp(202)
            for _ in epilogue(0, 16, [(bank(0)[:, 0:64], "pb0"), (bank(1)[:, 0:64], "pb1")]):
                pass
            for ci in range(28):
                M = RW_M[ci]
                S.dma("sp", sh_s[RW_COL[ci]:RW_COL[ci] + M, :], zs[0:M, ci, :], r=["zs"], w=["o_shs"])
            ckp(203)
            pairs(at_sample)
            ckp(204)
            for c in range(8):
                S.op("act", lambda: A.activation(hT[:, 8 + c, 0:16], bank(3)[:, c * 16:(c + 1) * 16], AF.Copy), r=["pb3"], w=["hT%d" % (8 + c)])

        def tm_proj(sl, subs, ncols, ps_of, handler, ntok):
            for s, (t0, n) in enumerate(subs):
                ps, pkeys = ps_of(s)
                for kc in range(KC):
                    S.op("pe", lambda: PE.matmul(ps[0:n, 0:ncols], hT[:, kc, t0:t0 + n], wb[sl][:, kc, 0:ncols],
                                                 start=(kc == 0), stop=(kc == KC - 1)), r=["hT%d" % kc, "wb%d" % sl], w=pkeys)
                handler(s, t0, n, ps, pkeys)

        def resid_add(W, subs, nk_list=None):
            for cb in range(4):
                sl = wload(W, [(cb * 512, 512, 0)])
                def h(s, t0, n, ps, pkeys):
                    S.op("dve", lambda: V.tensor_tensor(x_sb[0:n, s, cb * 512:(cb + 1) * 512], x_sb[0:n, s, cb * 512:(cb + 1) * 512],
                                                        ps[0:n, 0:512], ALU.add), r=pkeys + ["x%d" % s], w=["x%d" % s])
                tm_proj(sl, subs, 512, lambda s: (bank(4 + (s % 2)), ["pb%d" % (4 + (s % 2))]), h, None)

        try:
            for p in range(NPASS + 1):
                sample = (p == NPASS)
                last = (p == NPASS - 1)
                if sample:
                    subs = [(0, 16)]; ntok = 16
                else:
                    subs = [(0, 128), (128, 128)]; ntok = TP
                tok0 = p * TP
                for s, (t0, n) in enumerate(subs):
                    src = xs_d[0:16, :] if sample else xp[tok0 + t0:tok0 + t0 + n, :]
                    S.dma("sp", x_sb[0:n, s, :], src, w=["x%d" % s])
                if sample:
                    for ci in range(28):
                        M = RW_M[ci]
                        S.dma("sp", shT[0:M, ci, :], sshT[RW_COL[ci]:RW_COL[ci] + M, :], w=["shT"])
                    S.op("dve", lambda: V.memset(msh[:], 0.0), w=["msh"])
                    for ci in range(28):
                        M = RW_M[ci]
                        S.op("dve", lambda: V.tensor_scalar(msh[0:M, ci, :], shT[0:M, ci, :], pv[0:M, PV_MU + ci:PV_MU + ci + 1], None, ALU.mult),
                             r=["shT", "pv"], w=["msh"])
                ckp(1)
                norm_T(subs, PV_GMIX, ntok)
                ckp(2)
                sl = wload(w_in, [(RW_COL[24 + i], RW_M[24 + i], i * 128) for i in range(4)])
                for i in range(4):
                    ci = 24 + i; M = RW_M[ci]; b_ = 4 + (i % 2)
                    proj_fm(sl, i * 128, M, bank(b_), ["pb%d" % b_], ntok)
                    tshift(ci, bank(b_), ["pb%d" % b_], M, ntok, sample, last, xs_o[0:M, 0:ntok], ["xs_o"])
                    if i == 0:
                        S.op("act", lambda: A.activation(txw[0:96, 0:ntok], xs_o[0:96, 0:ntok], AF.Tanh), r=["xs_o"], w=["txw"])
                    elif i == 1:
                        S.op("act", lambda: A.activation(xaT[0:96, 0:ntok], xs_o[0:96, 0:ntok], AF.Copy), r=["xs_o"], w=["xaT"])
                    else:
                        S.op("act", lambda: A.activation(sxg[:, i - 2, 0:ntok], xs_o[:, 0:ntok], AF.Sigmoid), r=["xs_o"], w=["sxg"])
                ckp(3)
                def PP(c):
                    sl_ = wload(w_in, [(c * 128, 128, 0), (1024 + c * 128, 128, 128), (2048 + c * 128, 128, 256)])
                    prep_P(c, sl_, ntok, sample, last)
                def TT(c):
                    prep_P(c, None, ntok, sample, last)
                PP(0); TT(0); PP(1); TT(1)
                for c in range(8):
                    prep_H(c, ntok, sample, last)
                    if c + 2 < 8:
                        PP(c + 2)
                    prep_tail(c, ntok)
                    if c + 2 < 8:
                        TT(c + 2)
                ckp(5)
                for half in range(2):
                    sl = wload(w_in, [(RWP + QPERM[half * 8 + i] * 64, 64, i * 64) for i in range(8)])
                    for i in range(4):
                        b_ = 4 + (i % 2)
                        proj_fm(sl, i * 128, 128, bank(b_), ["pb%d" % b_], ntok)
                        evac(qT[:, half * 4 + i, 0:ntok], bank(b_)[:, 0:ntok], r=["pb%d" % b_], w=["qT"])
                sl = wload(w_in, [(RWP + 1024, 512, 0)])
                for i in range(2):
                    b_ = 4 + i
                    proj_fm(sl, i * 128, 128, bank(b_), ["pb%d" % b_], ntok)
                    evac(kT[:, i, 128:128 + ntok], bank(b_)[:, 0:ntok], r=["pb%d" % b_], w=["kT"])
                def kvh(s, t0, n, ps, pkeys):
                    if sample:
                        S.op("act", lambda: A.activation(ktm[0:n, :], ps[0:n, 0:512], AF.Copy), r=pkeys, w=["ktm"])
                        return
                    S.op("act", lambda: A.activation(Vall[0:n, 1 + s, :, 0:64], ps[0:n, 256:512].rearrange("p (k d) -> p k d", k=4), AF.Copy),
                         r=pkeys, w=["Vall%d" % (1 + s)])
                    if last and s == 1:
                        S.op("dve", lambda: V.tensor_copy(ktm[:, :], ps[:, 0:512]), r=pkeys, w=["ktm"])
                        S.dma("sp", k_p[:, :], ktm[:, 0:256], r=["ktm"], w=["o_kp"])
                        S.dma("sp", v_p[:, :], ktm[:, 256:512], r=["ktm"], w=["o_vp"])
                tm_proj(sl, subs, 512, lambda s: (bank(4 + (s % 2)), ["pb%d" % (4 + (s % 2))]), kvh, ntok)

                for s, (t0, n) in enumerate(subs):
                    src = ps_d[0:16, :] if sample else pp[tok0 + t0:tok0 + t0 + n, :]
                    S.dma("sp", pe_sb[0:n, s, :], src, w=["pe_sb"])
                    S.op("dve", lambda: V.tensor_copy(pe_b[0:n, s, :], pe_sb[0:n, s, :]), r=["pe_sb"], w=["pe_b"])
                    pt_rr[0] ^= 1
                    pt = PTb[:, pt_rr[0] * 1024:(pt_rr[0] + 1) * 1024]; pk = "pb%d" % (6 + pt_rr[0])
                    for k2 in range(2):
                        S.op("pe", lambda: PE.transpose(pt[:, k2 * 128:k2 * 128 + n], pe_b[0:n, s, k2 * 128:(k2 + 1) * 128], identb[0:n, 0:n]),
                             r=["pe_b", "cstb"], w=[pk])
                    for k2 in range(2):
                        evac(pT_[:, k2, t0:t0 + n], pt[:, k2 * 128:k2 * 128 + n], r=[pk], w=["pT_"])
                ckp(6)
                if not sample:
                    for arr, dst, an, dn in ((vT, tmV, "vT", "tmV"), (Kt, tmK, "Kt", "tmK"), (Bt, tmB, "Bt", "tmB")):
                        for s in range(2):
                            pt_rr[0] ^= 1
                            pt = PTb[:, pt_rr[0] * 1024:(pt_rr[0] + 1) * 1024]; pk = "pb%d" % (6 + pt_rr[0])
                            for c in range(8):
                                S.op("pe", lambda: PE.transpose(pt[:, c * 128:(c + 1) * 128], arr[:, c, s * 128:(s + 1) * 128], identb),
                                     r=["%s%d" % (an, c), "cstb"], w=[pk])
                            evac(dst[:, s, :], pt[:, :], r=[pk], w=[dn])
                    def att_all():
                        for n_ in range(2):
                            yield from att_block(p, n_)
                    att = att_all()
                    att_done = False
                    interleave([chunkA(0)])
                    for j in range(4):
                        live = [chunkB(j)] + ([chunkA(j + 1)] if j < 3 else [])
                        while live:
                            for g in list(live):
                                try:
                                    next(g)
                                except StopIteration:
                                    live.remove(g)
                            if not att_done:
                                try:
                                    next(att)
                                except StopIteration:
                                    att_done = True
                    if not att_done:
                        interleave([att])
                    if last:
                        S.dma("sp", wkv_p.rearrange("(c hh) k v -> hh k c v", hh=2)[0], ST[0:64, :, :], r=["ST"], w=["o_wkvp"])
                        S.dma("sp", wkv_p.rearrange("(c hh) k v -> hh k c v", hh=2)[1], ST[64:128, :, :], r=["ST"], w=["o_wkvp"])
                        for ci in range(28):
                            M = RW_M[ci]
                            S.dma("sp", sh_p[RW_COL[ci]:RW_COL[ci] + M, :], zlast[0:M, ci:ci + 1], r=["zlast"], w=["o_shp"])
                    S.op("act", lambda: A.activation(kT[:, :, 0:128], kT[:, :, TP:TP + 128], AF.Copy), r=["kT"], w=["kT"])
                    S.op("act", lambda: A.activation(Vall[:, 0, :, 0:64], Vall[:, 2, :, 0:64], AF.Copy), r=["Vall2"], w=["Vall0"])
                else:
                    sample_mix()
                ckp(10)
                resid_add(w_out, subs)
                ckp(11)
                norm_T(subs, PV_GFFN, ntok)
                for fb in range(FC // 2):
                    slg = wload(None, [(fb * 256, 256, 0, w_gate), (fb * 256, 256, 256, w_up)])
                    for i in range(2):
                        f = fb * 2 + i
                        proj_fm(slg, i * 128, 128, bank(0 + (f % 2)), ["pb%d" % (f % 2)], ntok)
                        proj_fm(slg, 256 + i * 128, 128, bank(2 + (f % 2)), ["pb%d" % (2 + (f % 2))], ntok)
                        S.op("act", lambda: A.activation(t_f1[:, 0:ntok], bank(f % 2)[:, 0:ntok], AF.Silu), r=["pb%d" % (f % 2)], w=["t_f1"])
                        S.op("dve", lambda: V.tensor_tensor(hid[:, f, 0:ntok], t_f1[:, 0:ntok], bank(2 + (f % 2))[:, 0:ntok], ALU.mult),
                             r=["t_f1", "pb%d" % (2 + (f % 2))], w=["hid"])
                for cb in range(4):
                    parts = [(0, 16), (16, 16), (32, 12)]
                    for pi, (k0, nk) in enumerate(parts):
                        sl = wload(w_down, [(cb * 512, 512, 0)], nk=nk, row0=k0 * 128)
                        for s, (t0, n) in enumerate(subs):
                            for kk_ in range(nk):
                                S.op("pe", lambda: PE.matmul(bank(4 + s)[0:n, 0:512], hid[:, k0 + kk_, t0:t0 + n], wb[sl][:, kk_, 0:512],
                                                             start=(pi == 0 and kk_ == 0), stop=(pi == 2 and kk_ == nk - 1)),
                                     r=["hid", "wb%d" % sl], w=["pb%d" % (4 + s)])
                    for s, (t0, n) in enumerate(subs):
                        S.op("dve", lambda: V.tensor_tensor(x_sb[0:n, s, cb * 512:(cb + 1) * 512], x_sb[0:n, s, cb * 512:(cb + 1) * 512],
                                                            bank(4 + s)[0:n, 0:512], ALU.add), r=["pb%d" % (4 + s), "x%d" % s], w=["x%d" % s])
                ckp(12)
                norm_T(subs, PV_GPLE, ntok)
                for cb in range(4):
                    sl = wload(ple_gate, [(cb * 512, 512, 0)])
                    S.dma("pool", wpp[:, :, :], ple_proj[:, cb * 512:(cb + 1) * 512].rearrange("(k p) n -> p k n", p=128), w=["wpp"])
                    for s, (t0, n) in enumerate(subs):
                        gb, pb_ = bank(4 + (s % 2)), bank(2 + (s % 2))
                        gk, pk2 = "pb%d" % (4 + (s % 2)), "pb%d" % (2 + (s % 2))
                        for kc in range(KC):
                            S.op("pe", lambda: PE.matmul(gb[0:n, 0:512], hT[:, kc, t0:t0 + n], wb[sl][:, kc, 0:512], start=(kc == 0), stop=(kc == KC - 1)),
                                 r=["hT%d" % kc, "wb%d" % sl], w=[gk])
                        for k2 in range(2):
                            S.op("pe", lambda: PE.matmul(pb_[0:n, 0:512], pT_[:, k2, t0:t0 + n], wpp[:, k2, :], start=(k2 == 0), stop=(k2 == 1)),
                                 r=["pT_", "wpp"], w=[pk2])
                        S.op("act", lambda: A.activation(t_f1[0:n, :], gb[0:n, 0:512], AF.Sigmoid), r=[gk], w=["t_f1"])
                        S.op("dve", lambda: V.tensor_tensor(t_f1[0:n, :], t_f1[0:n, :], pb_[0:n, 0:512], ALU.mult), r=["t_f1", pk2], w=["t_f1"])
                        S.op("dve", lambda: V.tensor_tensor(x_sb[0:n, s, cb * 512:(cb + 1) * 512], x_sb[0:n, s, cb * 512:(cb + 1) * 512], t_f1[0:n, :], ALU.add),
                             r=["t_f1", "x%d" % s], w=["x%d" % s])
                ckp(13)
                for s, (t0, n) in enumerate(subs):
                    S.op("act", lambda: A.activation(hb[0:n, s, :], x_sb[0:n, s, :], AF.Square, accum_out=ss[0:n, s:s + 1]), r=["x%d" % s], w=["hb%d" % s, "ss"])
                    S.op("act", lambda: A.activation(sd[0:n, s:s + 1], ss[0:n, s:s + 1], AF.Sqrt, bias=EPS, scale=1.0 / D), r=["ss"], w=["sd"])
                    S.op("dve", lambda: V.reciprocal(rstd[0:n, s:s + 1], sd[0:n, s:s + 1]), r=["sd"], w=["rstd%d" % s])
                    S.op("dve", lambda: V.scalar_tensor_tensor(x_sb[0:n, s, :], x_sb[0:n, s, :], rstd[0:n, s:s + 1], fn_bc[0:n, :], ALU.mult, ALU.mult),
                         r=["x%d" % s, "rstd%d" % s, "fn_bc"], w=["x%d" % s])
                    dst = y_s[0:16, :] if sample else y_p[tok0 + t0:tok0 + t0 + n, :]
                    S.dma("sp", dst, x_sb[0:n, s, :], r=["x%d" % s], w=["o_y"])
                assert wblk[0] == NBLK * (p + 1), wblk[0]
                ckp(14 + p)
        except _Stop:
            pass
        S.finish("sp")
        print("build stats: waits", S.nwait, "dmas", S.ndma, "seq", dict(S.seq), flush=True)
    return nc


def _consts():
    c = np.zeros((128, C_N), np.float32)
    c[:, C_ID:C_ID + 128] = np.eye(128)
    p = np.arange(128)
    c[:, C_BD:C_BD + 128] = (p[:, None] // 64 == p[None, :] // 64)
    c[:, C_IH:C_IH + 64] = (p[:, None] % 64 == np.arange(64)[None, :])
    c[:, C_HSEL:C_HSEL + 2] = (p[:, None] // 64 == np.arange(2)[None, :])
    s_ = (p % 64)[:, None]; t_ = np.arange(64)[None, :]
    c[:, C_MS:C_MS + 64] = (s_ < t_)
    c[:, C_MI:C_MI + 64] = (s_ <= t_)
    c[:, C_MST:C_MST + 64] = (s_ > t_)
    c[:, C_RST:C_RST + 256] = (np.arange(256) % 64 != 0)[None, :]
    j = p[:, None].astype(np.float32); i = np.arange(128)[None, :].astype(np.float32)
    BIG = 1e5
    c[:, C_DM:C_DM + 128] = np.where(j > i, -(i + 128 - j), -BIG)
    c[:, C_DM + 128:C_DM + 256] = np.where(j <= i, -(i - j), -BIG)
    sl = np.array(SLOPES, np.float32)[None, :]
    c[:, C_SB:C_SB + 16] = -sl * (127 - j) / SCALE
    c[:, C_ON:C_ON + 128] = 1.0
    return c


def _pv(inp):
    pv = np.zeros((128, PV_N), np.float32)
    f = lambda v, n: np.ascontiguousarray(np.asarray(v, np.float32).reshape(n, 128).T)
    pv[:, PV_GMIX:PV_GMIX + 16] = f(inp["norm_mix"][0], 16)
    pv[:, PV_GFFN:PV_GFFN + 16] = f(inp["norm_ffn"][0], 16)
    pv[:, PV_GPLE:PV_GPLE + 16] = f(inp["norm_ple"][0], 16)
    mu = np.asarray(inp["mu_shift"][0], np.float32)
    for ci in range(28):
        pv[0:RW_M[ci], PV_MU + ci] = mu[RW_COL[ci]:RW_COL[ci] + RW_M[ci]]
    pv[:, PV_W0:PV_W0 + 8] = f(inp["rwkv_w0"][0], 8)
    pv[:, PV_A0:PV_A0 + 8] = f(inp["rwkv_a0"][0], 8)
    pv[:, PV_KK:PV_KK + 8] = f(inp["rwkv_k_k"][0], 8)
    pv[:, PV_KA:PV_KA + 8] = f(inp["rwkv_k_a"][0], 8)
    pv[:, PV_RK:PV_RK + 8] = f(np.asarray(inp["rwkv_r_k"][0]).reshape(-1), 8)
    pv[:, PV_LNW:PV_LNW + 8] = f(inp["rwkv_ln_w"][0], 8)
    pv[:, PV_LNB:PV_LNB + 8] = f(inp["rwkv_ln_b"][0], 8)
    return pv


_NC = [None]


def kernel(**inp):
    A_ = lambda k: np.ascontiguousarray(np.asarray(inp[k], np.float32))
    if _NC[0] is None:
        _NC[0] = build_program()
    nc = _NC[0]
    shared = {
        "w_in": A_("w_in")[0], "w_out": A_("w_out")[0], "w_gate": A_("w_gate")[0], "w_up": A_("w_up")[0],
        "w_down": A_("w_down")[0], "ple_gate": A_("ple_gate")[0], "ple_proj": A_("ple_proj")[0],
        "w2": A_("rwkv_w2")[0], "a2": A_("rwkv_a2")[0], "g2": A_("rwkv_g2")[0],
        "pv": _pv(inp), "cst": _consts(), "fng": A_("final_norm").reshape(1, D),
        "sinks": A_("attn_sinks").reshape(1, 16),
    }
    xp, xs = A_("x_prompt"), A_("x_sample")
    pp, ps = A_("p_prompt"), A_("p_sample")
    swkv, ssh = A_("state_wkv"), A_("state_shift")
    ck, cv = A_("cache_k"), A_("cache_v")
    in_maps = []
    for i in range(8):
        b = slice(16 * i, 16 * i + 16)
        m = dict(shared)
        m.update({
            "xp": xp[i], "xs": np.ascontiguousarray(xs[b, 0]), "pp": pp[0, i], "psm": np.ascontiguousarray(ps[0, b, 0]),
            "swkv": np.ascontiguousarray(np.swapaxes(swkv[0, b], -1, -2)),
            "sshT": np.ascontiguousarray(ssh[0, b].T),
            "ck": np.ascontiguousarray(ck[0, b].reshape(16, 128, 256)), "cv": np.ascontiguousarray(cv[0, b].reshape(16, 128, 256)),
        })
        in_maps.append(m)
    res = run_bass_kernel_spmd(nc, in_maps, core_ids=list(range(8)))
    R = res.results
    cat = lambda k: np.stack([np.asarray(r[k], np.float32) for r in R])
    y_p = cat("y_p")
    y_s = cat("y_s").reshape(128, 1, D)
    wkv_p = np.swapaxes(cat("wkv_p"), -1, -2)[None]
    sh_p = cat("sh_p").reshape(8, RWP)[None]
    k_p = cat("k_p").reshape(8, 128, 4, 64)[None]
    v_p = cat("v_p").reshape(8, 128, 4, 64)[None]
    wkv_s = np.swapaxes(cat("wkv_s").reshape(128, 16, 64, 64), -1, -2)[None]
    sh_s = np.swapaxes(cat("sh_s"), 1, 2).reshape(128, RWP)[None]
    k_s = cat("k_s").reshape(128, 128, 4, 64)[None]
    v_s = cat("v_s").reshape(128, 128, 4, 64)[None]
    c_ = np.ascontiguousarray
    return (c_(y_p), c_(y_s), c_(wkv_p), c_(sh_p), c_(k_p), c_(v_p), c_(wkv_s), c_(sh_s), c_(k_s), c_(v_s))
```

```python
import bisect
import concourse.bass as bass
import concourse.mybir as mybir

F32 = mybir.dt.float32
BF16 = mybir.dt.bfloat16
AF = mybir.ActivationFunctionType
ALU = mybir.AluOpType
AX = mybir.AxisListType

SEM_ROLL = 12000


class Sched:
    def __init__(self, nc, stack):
        self.nc = nc
        self.stack = stack
        self.engs = {"pe": nc.tensor, "dve": nc.vector, "act": nc.scalar,
                     "pool": nc.gpsimd, "sp": nc.sync}
        self.sem = {}
        self.cnt = {}
        self.seq = {}
        self.sigs = {}
        self.sigseq = {}
        self.last_ins = {}
        self.last_seq = {}
        self.nsem = 0
        for e in self.engs:
            self._new_sem(e)
            self.seq[e] = 0
            self.sigs[e] = []
            self.sigseq[e] = []
            self.last_ins[e] = None
        self.semobj = {}
        for e in self.engs:
            pass
        self.seen = {e: {} for e in self.engs}
        self.lastw = {}
        self.readers = {}
        self.dma_sems = {}
        self.dma_rr = {}
        self.ndma = 0
        self.nwait = 0

    def _new_sem(self, e):
        name = "s_%s_%d" % (e, self.nsem)
        self.nsem += 1
        s = self.stack.enter_context(self.nc.semaphore(name))
        self.sem[e] = (name, s)
        self.cnt[e] = 0
        if not hasattr(self, "semh"):
            self.semh = {}
        self.semh[name] = s

    def _resolve(self, tok):
        if tok[0] == "d":
            return tok[1], tok[2]
        _, e, q = tok
        i = bisect.bisect_left(self.sigseq[e], q)
        if i < len(self.sigseq[e]):
            _, sn, v = self.sigs[e][i]
            return sn, v
        ins = self.last_ins[e]
        assert ins is not None
        if self.cnt[e] >= SEM_ROLL:
            self._new_sem(e)
        sn, s = self.sem[e]
        self.cnt[e] += 1
        ins.then_inc(s, 1)
        self.sigs[e].append((self.last_seq[e], sn, self.cnt[e]))
        self.sigseq[e].append(self.last_seq[e])
        return sn, self.cnt[e]

    def _wait(self, e, toks):
        need = {}
        for t in toks:
            sn, v = self._resolve(t)
            if self.seen[e].get(sn, 0) >= v:
                continue
            if need.get(sn, 0) < v:
                need[sn] = v
        for sn, v in need.items():
            self.engs[e].wait_ge(self.semh[sn], v)
            self.seen[e][sn] = v
            self.nwait += 1

    def _deps(self, e, r, w):
        toks = []
        for k in r:
            t = self.lastw.get(k)
            if t is not None and not (e == "pe" and t[0] == "c" and t[1] == "pe"):
                toks.append(t)
            if k.startswith("pb") or k.startswith("pt"):
                for t in self.readers.get(k, ()):
                    if t[0] == "c" and t[1] != e:
                        toks.append(t)
        for k in w:
            t = self.lastw.get(k)
            if t is not None and not (e == "pe" and t[0] == "c" and t[1] == "pe"):
                toks.append(t)
            for t in self.readers.get(k, ()):
                toks.append(t)
        return toks

    def _record(self, tok, r, w):
        for k in r:
            self.readers.setdefault(k, []).append(tok)
        for k in w:
            self.lastw[k] = tok
            self.readers[k] = []

    def op(self, e, fn, r=(), w=()):
        self._wait(e, self._deps(e, r, w))
        ins = fn()
        self.seq[e] += 1
        self.last_ins[e] = ins
        self.last_seq[e] = self.seq[e]
        self._record(("c", e, self.seq[e]), r, w)
        return ins

    def dma(self, q, out, in_, r=(), w=(), **kw):
        ring = self.dma_sems.setdefault(q, [])
        if len(ring) < 8:
            name = "d_%s_%d" % (q, len(ring))
            s = self.stack.enter_context(self.nc.semaphore(name))
            self.semh[name] = s
            ring.append([name, 0])
            slot = ring[-1]
        else:
            i = self.dma_rr.get(q, 0)
            self.dma_rr[q] = (i + 1) % len(ring)
            slot = ring[i]
        toks = self._deps(q, r, w)
        if slot[1] > 0:
            toks.append(("d", slot[0], slot[1]))
        self._wait(q, toks)
        slot[1] += 16
        ins = self.engs[q].dma_start(out=out, in_=in_, **kw)
        ins.then_inc(self.semh[slot[0]], 16)
        self.seq[q] += 1
        self._record(("d", slot[0], slot[1]), r, w)
        self.ndma += 1
        return ins

    def finish(self, e="sp"):
        toks = []
        for k, t in self.lastw.items():
            toks.append(t)
        self._wait(e, toks)

import contextlib
import numpy as np
from concourse.bass_utils import run_bass_kernel_spmd

D = 2048
KC = 16
DFF = 5632
FC = 44
PROJ = 5056
RWP = 3520
TP = 256
NPASS = 2048 // TP
NBLK = 54
EPS = 1e-6
GN_EPS = 64e-5
SCALE = 0.125
DEC = 0.6065306597126334
SLOPES = [2.0 ** (-8.0 * (h + 1) / 16) for h in range(16)]
QPERM = [0, 4, 1, 5, 2, 6, 3, 7, 8, 12, 9, 13, 10, 14, 11, 15]
PV_GMIX, PV_GFFN, PV_GPLE, PV_MU = 0, 16, 32, 48
PV_W0, PV_A0, PV_KK, PV_KA, PV_RK, PV_LNW, PV_LNB = 76, 84, 92, 100, 108, 116, 124
PV_N = 132
C_ID, C_BD, C_IH, C_HSEL, C_MS, C_MI, C_MST, C_RST, C_DM, C_SB, C_ON, C_N = 0, 128, 256, 320, 322, 386, 450, 514, 770, 1026, 1042, 1170

RW_COL = [c * 128 for c in range(8)] + [1024 + c * 128 for c in range(8)] + \
         [2048 + c * 128 for c in range(8)] + [3072, 3168, 3264, 3392]
RW_M = [128] * 24 + [96, 96, 128, 128]


def hcol(h):
    return ((h % 2) * 8 + h // 2) * 64


class _Stop(Exception):
    pass


def build_program(debug=False, stop=None):
    nc = bass.Bass("TRN2", target_bir_lowering=False)
    din = lambda n, sh: nc.dram_tensor(n, sh, F32, kind="ExternalInput").ap()
    dout = lambda n, sh: nc.dram_tensor(n, sh, F32, kind="ExternalOutput").ap()
    xp = din("xp", [2048, D]); xs_d = din("xs", [16, D])
    pp = din("pp", [2048, 256]); ps_d = din("psm", [16, 256])
    swkv = din("swkv", [16, 16, 64, 64])
    sshT = din("sshT", [RWP, 16])
    ck = din("ck", [16, 128, 256]); cv = din("cv", [16, 128, 256])
    w_in = din("w_in", [D, PROJ]); w_out = din("w_out", [D, D])
    w_gate = din("w_gate", [D, DFF]); w_up = din("w_up", [D, DFF]); w_down = din("w_down", [DFF, D])
    ple_gate = din("ple_gate", [D, D]); ple_proj = din("ple_proj", [256, D])
    w2 = din("w2", [96, 1024]); a2 = din("a2", [96, 1024]); g2 = din("g2", [256, 1024])
    pv_d = din("pv", [128, PV_N]); cst_d = din("cst", [128, C_N])
    fng = din("fng", [1, D]); esk = din("sinks", [1, 16])

    wsc = nc.dram_tensor("wsc", [NBLK, 128, KC * 512], BF16, kind="Internal").ap()
    y_p = dout("y_p", [2048, D]); y_s = dout("y_s", [16, D])
    wkv_p = dout("wkv_p", [16, 64, 64])
    sh_p = dout("sh_p", [RWP, 1])
    k_p = dout("k_p", [128, 256]); v_p = dout("v_p", [128, 256])
    wkv_s = dout("wkv_s", [16, 16, 64, 64])
    sh_s = dout("sh_s", [RWP, 16])
    k_s = dout("k_s", [16, 128, 256]); v_s = dout("v_s", [16, 128, 256])

    with contextlib.ExitStack() as st:
        S = Sched(nc, st)
        sb = lambda n, sh, dt=F32: st.enter_context(nc.sbuf_tensor("sb_" + n, sh, dt))
        V, A, P, PE = nc.vector, nc.scalar, nc.gpsimd, nc.tensor
        PF = st.enter_context(nc.psum_tensor("PF", [128, 4096], F32))
        PTb = PF[:, 3072:4096].bitcast(BF16)
        pbk = lambda b: "pb%d" % b
        bank = lambda b: PF[:, b * 512:(b + 1) * 512]

        x_sb = sb("x_sb", [128, 2, D])
        hb = sb("hb", [128, 2, D], BF16)
        hT = sb("hT", [128, KC, TP], BF16)
        wb = [sb("wb%d" % i, [128, KC, 512], BF16) for i in range(2)]
        pv = sb("pv", [128, PV_N]); cst = sb("cst", [128, C_N])
        omu = sb("omu", [128, 28]); omka = sb("omka", [128, 8])
        cstb = sb("cstb", [128, 384], BF16)
        fn_bc = sb("fn_bc", [128, D])
        esink = sb("esink", [128, 16])
        w2b = sb("w2b", [128, 1024], BF16); a2b = sb("a2b", [128, 1024], BF16)
        g2b = sb("g2b", [128, 2, 1024], BF16)
        ss = sb("ss", [128, 8]); sd = sb("sd", [128, 8]); rstd = sb("rstd", [128, 8])
        rwbig = sb("rwbig", [128, 6 * 8 * TP], BF16)
        def rwv(i):
            return rwbig[:, i * 8 * TP:(i + 1) * 8 * TP].rearrange("p (c t) -> p c t", c=8)
        Rt, Kt, At, Bt, vT, ybT = [rwv(i) for i in range(6)]
        hid = rwbig[:, 0:FC * TP].rearrange("p (c t) -> p c t", c=FC)
        ystg = rwbig.bitcast(F32)[:, 0:2 * D].rearrange("p (s d) -> p s d", s=2)
        gT = sb("gT", [128, 8, TP], BF16)
        dS = sb("dS", [128, 8, 16])
        tmV = sb("tmV", [128, 2, 1024], BF16); tmK = sb("tmK", [128, 2, 1024], BF16)
        tmB = sb("tmB", [128, 2, 1024], BF16)
        gC = sb("gC", [128, 8, 4])
        carry = sb("carry", [128, 28]); zlast = sb("zlast", [128, 28]); zs = sb("zs", [128, 28, 16])
        shT = sb("shT", [128, 28, 16]); msh = sb("msh", [128, 28, 16])
        txw = sb("txw", [128, TP], BF16); xaT = sb("xaT", [128, TP], BF16)
        sxg = sb("sxg", [128, 2, TP], BF16)
        tmr = [sb("tmr%d" % i, [128, TP + 1]) for i in range(3)]
        scr = sb("scr", [128, 13 * TP])
        def scv(i):
            return scr[:, i * TP:(i + 1) * TP]
        xs_r2 = sb("xs_r2", [128, TP]); xs_k2 = sb("xs_k2", [128, TP])
        xs_r, xs_k, xs_o, t_sg, t_L, t_eL, t_enL, t_eLm, t_a, t_kkr, t_sq, t_b, t_kh = [scv(i) for i in range(13)]
        t_rkr = sb("t_rkr", [128, TP], BF16)
        xs_rb = [xs_r, xs_r2]; xs_kb = [xs_k, xs_k2]
        def epv(i):
            return scr[:, i * 512:(i + 1) * 512].rearrange("p (c t) -> p c t", c=4)
        yT_sb, e_sq, e_m, e_v = [epv(i) for i in range(4)]
        chb = sb("chb", [128, 11 * 1024], BF16)
        def chv(i):
            return chb[:, i * 1024:(i + 1) * 1024]
        cY = [chv(0), chv(1)]; cYT = [chv(2), chv(3)]; cP = [chv(4), chv(5)]
        cNak, cMrb, cMrk, cW, cU = chv(6), chv(7), chv(8), chv(9), chv(10)
        chf = sb("chf", [128, 2048])
        cT1 = chf[:, 0:1024]; cYs = chf[:, 1024:2048]
        ST = sb("ST", [128, 8, 64]); STb = sb("STb", [128, 8, 128], BF16)
        qT = sb("qT", [128, 8, TP], BF16); kT = sb("kT", [128, 2, 128 + TP], BF16)
        Vall = sb("Vall", [128, 3, 4, 66], BF16)
        t_f1 = sb("t_f1", [128, 512]); t_f2 = sb("t_f2", [128, 512])
        a_sc = t_f1; ktm = t_f2
        a_PT = [sb("a_PT%d" % i, [128, 512], BF16) for i in range(2)]
        a_den = sb("a_den", [128, 4])
        wpp = sb("wpp", [128, 2, 512], BF16)
        pT_ = sb("pT_", [128, 2, TP], BF16); pe_sb = sb("pe_sb", [128, 2, 256]); pe_b = sb("pe_b", [128, 2, 256], BF16)
        x1 = x_sb[:, 1, :]
        sS = x1[:, 0:512].rearrange("p (c v) -> p c v", c=8)
        sSn = x1[:, 512:1024].rearrange("p (c v) -> p c v", c=8)
        sBlk = x1[:, 1024:2048].rearrange("p (c v) -> p c v", c=8)
        sAbd = chf[:, 0:1024].rearrange("p (c v) -> p c v", c=8)
        sRv = chf[:, 1024:1536].rearrange("p (c v) -> p c v", c=8)
        sT1 = chf[:, 1536:2048].rearrange("p (c v) -> p c v", c=8)
        chb32 = chb[:, 0:4096].bitcast(F32)
        sKV = chb32[:, 0:512]; sKn = sKV[:, 0:256]; sVn = sKV[:, 256:512]
        sPT = chb32[:, 512:768].rearrange("p (b h) -> p b h", b=16)
        sPTn = chb32[:, 768:1024].rearrange("p (b h) -> p b h", b=16)
        vf = chb32[:, 1024:1152].rearrange("p (c b) -> p c b", c=8)
        sKT = chb[:, 4096:4352].rearrange("p (k n) -> p k n", k=2)

        ident = cst[:, C_ID:C_ID + 128]; BDf = cst[:, C_BD:C_BD + 128]
        identb = cstb[:, 0:128]; BDb = cstb[:, 128:256]
        mS = cst[:, C_MS:C_MS + 64]; mI = cst[:, C_MI:C_MI + 64]; mST = cst[:, C_MST:C_MST + 64]
        rst = cst[:, C_RST:C_RST + 256]
        Dm = cst[:, C_DM:C_DM + 256]
        sbias = cst[:, C_SB:C_SB + 16]
        Ihalf = cst[:, C_IH:C_IH + 64]

        S.dma("sp", pv[:], pv_d[:, :], w=["pv"])
        S.dma("sp", cst[:], cst_d[:, :], w=["cst"])
        S.dma("sp", fn_bc[:], fng.partition_broadcast(128), w=["fn_bc"])
        S.dma("sp", esink[:], esk.partition_broadcast(128), w=["esink"])
        S.dma("pool", w2b[0:96, :], w2[:, :], w=["w2b"])
        S.dma("pool", a2b[0:96, :], a2[:, :], w=["a2b"])
        S.dma("pool", g2b[:], g2.rearrange("(k p) n -> p k n", p=128), w=["g2b"])
        S.op("dve", lambda: V.tensor_copy(cstb[:, 0:256], cst[:, 0:256]), r=["cst"], w=["cstb"])
        S.op("dve", lambda: V.tensor_scalar(omu[:], pv[:, PV_MU:PV_MU + 28], -1.0, 1.0, ALU.mult, ALU.add), r=["pv"], w=["omu"])
        S.op("dve", lambda: V.tensor_scalar(omka[:], pv[:, PV_KA:PV_KA + 8], -1.0, 1.0, ALU.mult, ALU.add), r=["pv"], w=["omka"])
        S.op("act", lambda: A.activation(esink[:], esink[:], AF.Exp), r=["esink"], w=["esink"])
        S.op("pool", lambda: P.memset(carry[:], 0.0), w=["carry%d" % i for i in range(28)])
        S.op("pool", lambda: P.memset(ST[:], 0.0), w=["ST"])
        S.op("pool", lambda: P.memset(STb[:], 0.0), w=["STb"])
        S.op("pool", lambda: P.memset(Vall[:], 1.0), w=["Vall0", "Vall1", "Vall2"])
        S.op("pool", lambda: P.memset(kT[:], 0.0), w=["kT"])
        for i_ in range(2):
            S.op("pool", lambda: P.memset(wb[i_][:], 0.0), w=["wb%d" % i_])

        def ckp(k):
            if stop == k:
                raise _Stop()
        wslot = [0]

        wblk = [0]

        def wload(W, specs, nk=KC, row0=0):
            sl = wslot[0]; wslot[0] ^= 1
            bi = wblk[0] % NBLK; first = wblk[0] < NBLK; wblk[0] += 1
            if first:
                for sp_ in specs:
                    c0, n, d0 = sp_[0:3]
                    Wm = sp_[3] if len(sp_) > 3 else W
                    src = Wm[row0:row0 + nk * 128, c0:c0 + n].rearrange("(k p) n -> p k n", p=128)
                    S.dma("pool", wb[sl][:, 0:nk, d0:d0 + n], src, w=["wb%d" % sl])
                S.dma("sp", wsc[bi], wb[sl][:, :, :].rearrange("p k n -> p (k n)"), r=["wb%d" % sl], w=["wsc%d" % bi])
            else:
                S.dma("pool", wb[sl][:, :, :].rearrange("p k n -> p (k n)"), wsc[bi], r=["wsc%d" % bi], w=["wb%d" % sl])
            return sl

        ev_rr = [0]

        def evac(out, in_, r, w, scale=None):
            ev_rr[0] ^= 1
            if ev_rr[0]:
                if scale is None:
                    S.op("act", lambda: A.activation(out, in_, AF.Copy), r=r, w=w)
                else:
                    S.op("act", lambda: A.activation(out, in_, AF.Copy, scale=scale), r=r, w=w)
            else:
                if scale is None:
                    S.op("dve", lambda: V.tensor_copy(out, in_), r=r, w=w)
                else:
                    S.op("dve", lambda: V.tensor_scalar(out, in_, scale, None, ALU.mult), r=r, w=w)

        pt_rr = [0]

        def norm_T(subs, gcol, ntok):
            for s, (t0, n) in enumerate(subs):
                S.op("act", lambda: A.activation(hb[0:n, s, :], x_sb[0:n, s, :], AF.Square, accum_out=ss[0:n, s:s + 1]),
                     r=["x%d" % s], w=["hb%d" % s, "ss"])
                S.op("act", lambda: A.activation(sd[0:n, s:s + 1], ss[0:n, s:s + 1], AF.Sqrt, bias=EPS, scale=1.0 / D),
                     r=["ss"], w=["sd"])
                S.op("dve", lambda: V.reciprocal(rstd[0:n, s:s + 1], sd[0:n, s:s + 1]), r=["sd"], w=["rstd%d" % s])
                S.op("dve", lambda: V.tensor_scalar(hb[0:n, s, :], x_sb[0:n, s, :], rstd[0:n, s:s + 1], None, ALU.mult),
                     r=["x%d" % s, "rstd%d" % s], w=["hb%d" % s])
            for kc in range(KC):
                pt_rr[0] ^= 1
                pt = PTb[:, pt_rr[0] * 1024:(pt_rr[0] + 1) * 1024]; pk = "pb%d" % (6 + pt_rr[0])
                for s, (t0, n) in enumerate(subs):
                    S.op("pe", lambda: PE.transpose(pt[:, t0:t0 + n], hb[0:n, s, kc * 128:(kc + 1) * 128], identb[0:n, 0:n]),
                         r=["hb%d" % s, "cstb"], w=[pk])
                evac(hT[:, kc, 0:ntok], pt[:, 0:ntok], r=[pk, "pv"], w=["hT%d" % kc], scale=pv[:, gcol + kc:gcol + kc + 1])

        def proj_fm(sl, d0, M, ps, pkeys, ntok):
            for kc in range(KC):
                S.op("pe", lambda: PE.matmul(ps[0:M, 0:ntok], wb[sl][:, kc, d0:d0 + M], hT[:, kc, 0:ntok],
                                             start=(kc == 0), stop=(kc == KC - 1)),
                     r=["wb%d" % sl, "hT%d" % kc], w=pkeys)

        def tshift(ci, ps, pkeys, M, ntok, sample, last, out_ap, okeys):
            mu = pv[0:M, PV_MU + ci:PV_MU + ci + 1]
            if sample:
                S.op("act", lambda: A.activation(zs[0:M, ci, :], ps[0:M, 0:16], AF.Copy), r=pkeys, w=["zs"])
                S.op("dve", lambda: V.scalar_tensor_tensor(out_ap, ps[0:M, 0:16], omu[0:M, ci:ci + 1], msh[0:M, ci, :],
                                                           ALU.mult, ALU.add), r=pkeys + ["omu", "msh"], w=okeys)
                return
            tm = tmr[ci % 3]; tk = "tmr%d" % (ci % 3)
            S.op("act", lambda: A.activation(tm[0:M, 0:1], carry[0:M, ci:ci + 1], AF.Copy), r=["carry%d" % ci], w=[tk])
            S.op("act", lambda: A.activation(tm[0:M, 1:1 + ntok], ps[0:M, 0:ntok], AF.Copy, scale=mu), r=pkeys + ["pv"], w=[tk])
            S.op("act", lambda: A.activation(carry[0:M, ci:ci + 1], tm[0:M, ntok:ntok + 1], AF.Copy), r=[tk], w=["carry%d" % ci])
            if last:
                S.op("act", lambda: A.activation(zlast[0:M, ci:ci + 1], ps[0:M, ntok - 1:ntok], AF.Copy), r=pkeys, w=["zlast"])
            S.op("dve", lambda: V.scalar_tensor_tensor(out_ap, ps[0:M, 0:ntok], omu[0:M, ci:ci + 1], tm[0:M, 0:ntok],
                                                       ALU.mult, ALU.add), r=pkeys + ["omu", tk], w=okeys)

        def prep_P(c, sl, ntok, sample, last):
            N = ntok
            b0 = 0 if c % 2 == 0 else 3
            pr, pk_, pvv = bank(b0), bank(b0 + 1), bank(b0 + 2)
            kr_, kk_, kv_ = ["pb%d" % (b0 + i) for i in range(3)]
            xs_r, xs_k = xs_rb[c % 2], xs_kb[c % 2]
            xr_k, xk_k = "xs_r%d" % (c % 2), "xs_k%d" % (c % 2)
            if sl is not None:
                proj_fm(sl, 0, 128, pr, [kr_], N)
                proj_fm(sl, 128, 128, pk_, [kk_], N)
                proj_fm(sl, 256, 128, pvv, [kv_], N)
                return
            tshift(c, pr, [kr_], 128, N, sample, last, xs_r[:, 0:N], [xr_k])
            tshift(8 + c, pk_, [kk_], 128, N, sample, last, xs_k[:, 0:N], [xk_k])
            tshift(16 + c, pvv, [kv_], 128, N, sample, last, vT[:, c, 0:N], ["vT%d" % c])

        def prep_H(c, ntok, sample, last):
            N = ntok
            xs_r, xs_k = xs_rb[c % 2], xs_kb[c % 2]
            xr_k, xk_k = "xs_r%d" % (c % 2), "xs_k%d" % (c % 2)
            cs = slice(c * 128, (c + 1) * 128)
            pc = lambda o: pv[:, o + c:o + c + 1]
            S.op("pe", lambda: PE.matmul(bank(6)[:, 0:N], w2b[0:96, cs], txw[0:96, 0:N], start=True, stop=True),
                 r=["w2b", "txw"], w=["pb6"])
            S.op("pe", lambda: PE.matmul(bank(7)[:, 0:N], a2b[0:96, cs], xaT[0:96, 0:N], start=True, stop=True),
                 r=["a2b", "xaT"], w=["pb7"])
            S.op("dve", lambda: V.tensor_scalar(t_kkr[:, 0:N], xs_k[:, 0:N], pc(PV_KK), None, ALU.mult), r=[xk_k, "pv"], w=["t_kkr"])
            S.op("act", lambda: A.activation(t_sg[:, 0:N], bank(6)[:, 0:N], AF.Sigmoid, bias=pc(PV_W0)), r=["pb6", "pv"], w=["t_sg"])
            S.op("act", lambda: A.activation(t_a[:, 0:N], bank(7)[:, 0:N], AF.Sigmoid, bias=pc(PV_A0)), r=["pb7", "pv"], w=["t_a"])
            S.op("act", lambda: A.activation(t_sq[:, 0:N], t_kkr[:, 0:N], AF.Square), r=["t_kkr"], w=["t_sq"])
            for k2 in range(2):
                S.op("pe", lambda: PE.matmul(bank(6)[:, 0:N], g2b[:, k2, cs], sxg[:, k2, 0:N], start=(k2 == 0), stop=(k2 == 1)),
                     r=["g2b", "sxg"], w=["pb6"])
            S.op("pe", lambda: PE.matmul(bank(7)[:, 0:N], BDf, t_sq[:, 0:N], start=True, stop=True), r=["cst", "t_sq"], w=["pb7"])
            S.op("dve", lambda: V.tensor_scalar(t_sg[:, 0:N], t_sg[:, 0:N], -DEC, None, ALU.mult), r=["t_sg"], w=["t_sg"])
            if sample:
                S.op("act", lambda: A.activation(dS[:, c, :], t_sg[:, 0:N], AF.Exp), r=["t_sg"], w=["dS"])
            else:
                S.op("dve", lambda: V.tensor_tensor_scan(t_L[:, 0:N], rst[:, 0:N], t_sg[:, 0:N], 0.0, ALU.mult, ALU.add),
                     r=["t_sg", "cst"], w=["t_L"])
                S.op("act", lambda: A.activation(t_eL[:, 0:N], t_L[:, 0:N], AF.Exp), r=["t_L"], w=["t_eL"])
                S.op("act", lambda: A.activation(t_enL[:, 0:N], t_L[:, 0:N], AF.Exp, scale=-1.0), r=["t_L"], w=["t_enL"])
                eLv = t_eL[:, 0:N].rearrange("p (j t) -> p j t", t=64)
                eLmv = t_eLm[:, 0:N].rearrange("p (j t) -> p j t", t=64)
                S.op("dve", lambda: V.memset(eLmv[:, :, 0:1], 1.0), w=["t_eLm"])
            S.op("dve", lambda: V.tensor_scalar(t_sq[:, 0:N], bank(7)[:, 0:N], 1e-24, None, ALU.max), r=["pb7"], w=["t_sq"])
            S.op("act", lambda: A.activation(t_sq[:, 0:N], t_sq[:, 0:N], AF.Sqrt), r=["t_sq"], w=["t_sq"])
            S.op("act", lambda: A.activation(gT[:, c, 0:N], bank(6)[:, 0:N], AF.Copy), r=["pb6"], w=["gT%d" % c])
            if not sample:
                S.op("dve", lambda: V.tensor_copy(eLmv[:, :, 1:64], eLv[:, :, 0:63]), r=["t_eL"], w=["t_eLm"])
                S.op("act", lambda: A.activation(gC[:, c, :], eLv[:, :, 63], AF.Copy), r=["t_eL"], w=["gC"])
                S.op("dve", lambda: V.tensor_tensor(Rt[:, c, 0:N], xs_r[:, 0:N], t_eL[:, 0:N], ALU.mult), r=[xr_k, "t_eL"], w=["Rt%d" % c])
            else:
                S.op("dve", lambda: V.tensor_copy(Rt[:, c, 0:N], xs_r[:, 0:N]), r=[xr_k], w=["Rt%d" % c])
            S.op("dve", lambda: V.reciprocal(t_sq[:, 0:N], t_sq[:, 0:N]), r=["t_sq"], w=["t_sq"])
            S.op("dve", lambda: V.tensor_tensor(t_kkr[:, 0:N], t_kkr[:, 0:N], t_sq[:, 0:N], ALU.mult), r=["t_kkr", "t_sq"], w=["t_kkr"])
            S.op("dve", lambda: V.tensor_tensor(t_b[:, 0:N], t_kkr[:, 0:N], t_a[:, 0:N], ALU.mult), r=["t_kkr", "t_a"], w=["t_b"])
            S.op("dve", lambda: V.tensor_scalar(t_a[:, 0:N], t_a[:, 0:N], pc(PV_KA), omka[:, c:c + 1], ALU.mult, ALU.add),
                 r=["t_a", "pv", "omka"], w=["t_a"])
            S.op("dve", lambda: V.tensor_tensor(t_kh[:, 0:N], xs_k[:, 0:N], t_a[:, 0:N], ALU.mult), r=[xk_k, "t_a"], w=["t_kh"])
            if sample:
                S.op("dve", lambda: V.tensor_scalar(At[:, c, 0:N], t_kkr[:, 0:N], -1.0, None, ALU.mult), r=["t_kkr"], w=["At%d" % c])
                S.op("dve", lambda: V.tensor_copy(Bt[:, c, 0:N], t_b[:, 0:N]), r=["t_b"], w=["Bt%d" % c])
                S.op("dve", lambda: V.tensor_copy(Kt[:, c, 0:N], t_kh[:, 0:N]), r=["t_kh"], w=["Kt%d" % c])
            else:
                S.op("dve", lambda: V.scalar_tensor_tensor(At[:, c, 0:N], t_kkr[:, 0:N], -1.0, t_eLm[:, 0:N], ALU.mult, ALU.mult),
                     r=["t_kkr", "t_eLm"], w=["At%d" % c])
                S.op("dve", lambda: V.tensor_tensor(Bt[:, c, 0:N], t_b[:, 0:N], t_enL[:, 0:N], ALU.mult), r=["t_b", "t_enL"], w=["Bt%d" % c])
                S.op("dve", lambda: V.tensor_tensor(Kt[:, c, 0:N], t_kh[:, 0:N], t_enL[:, 0:N], ALU.mult), r=["t_kh", "t_enL"], w=["Kt%d" % c])

        def prep_tail(c, ntok):
            N = ntok
            xs_r = xs_rb[c % 2]; xr_k = "xs_r%d" % (c % 2)
            pc = lambda o: pv[:, o + c:o + c + 1]
            S.op("dve", lambda: V.scalar_tensor_tensor(t_rkr[:, 0:N], xs_r[:, 0:N], pc(PV_RK), t_kh[:, 0:N], ALU.mult, ALU.mult),
                 r=[xr_k, "t_kh", "pv"], w=["t_rkr"])
            S.op("pe", lambda: PE.matmul(bank(6)[:, 0:N], BDb, t_rkr[:, 0:N], start=True, stop=True), r=["cstb", "t_rkr"], w=["pb6"])
            S.op("dve", lambda: V.tensor_tensor(ybT[:, c, 0:N], bank(6)[:, 0:N], vT[:, c, 0:N], ALU.mult), r=["pb6", "vT%d" % c], w=["ybT%d" % c])

        def epilogue(col0, n, srcs=None):
            for hf in range(2):
                if srcs is None:
                    for c4 in range(4):
                        c = hf * 4 + c4
                        S.op("pe", lambda: PE.transpose(bank(4)[:, c4 * 128:(c4 + 1) * 128], cYs[:, c * 128:(c + 1) * 128], ident),
                             r=["cYs_0", "cYs_1", "cst"], w=["pb4"])
                    src_ap, sk = bank(4)[:, 0:4 * n], "pb4"
                else:
                    src_ap, sk = srcs[hf]
                src = src_ap.rearrange("p (c t) -> p c t", c=4)
                yv = yT_sb[:, :, 0:n]; sq = e_sq[:, :, 0:n]; em = e_m[:, :, 0:n]; ev = e_v[:, :, 0:n]
                S.op("act", lambda: A.activation(yv, src, AF.Copy), r=[sk], w=["yT_sb"])
                S.op("act", lambda: A.activation(sq, src, AF.Square), r=[sk], w=["e_sq"])
                for c4 in range(4):
                    S.op("pe", lambda: PE.matmul(bank(5)[:, c4 * n:(c4 + 1) * n], BDf, yT_sb[:, c4, 0:n], start=True, stop=True),
                         r=["cst", "yT_sb"], w=["pb5"])
                m_ps = bank(5)[:, 0:4 * n].rearrange("p (c t) -> p c t", c=4)
                S.op("act", lambda: A.activation(em, m_ps, AF.Copy, scale=1.0 / 64), r=["pb5"], w=["e_m"])
                yield
                for c4 in range(4):
                    S.op("pe", lambda: PE.matmul(bank(5)[:, c4 * n:(c4 + 1) * n], BDf, e_sq[:, c4, 0:n], start=True, stop=True),
                         r=["cst", "e_sq"], w=["pb5"])
                S.op("dve", lambda: V.tensor_tensor(ev, em, em, ALU.mult), r=["e_m"], w=["e_v"])
                S.op("dve", lambda: V.scalar_tensor_tensor(ev, m_ps, 1.0 / 64, ev, ALU.mult, ALU.subtract), r=["pb5", "e_v"], w=["e_v"])
                S.op("act", lambda: A.activation(ev, ev, AF.Sqrt, bias=GN_EPS), r=["e_v"], w=["e_v"])
                S.op("dve", lambda: V.reciprocal(ev, ev), r=["e_v"], w=["e_v"])
                S.op("dve", lambda: V.tensor_tensor(yv, yv, em, ALU.subtract), r=["yT_sb", "e_m"], w=["yT_sb"])
                S.op("dve", lambda: V.tensor_tensor(yv, yv, ev, ALU.mult), r=["yT_sb", "e_v"], w=["yT_sb"])
                yield
                for c4 in range(4):
                    c = hf * 4 + c4
                    S.op("dve", lambda: V.tensor_scalar(yT_sb[:, c4, 0:n], yT_sb[:, c4, 0:n], pv[:, PV_LNW + c:PV_LNW + c + 1],
                                                        pv[:, PV_LNB + c:PV_LNB + c + 1], ALU.mult, ALU.add), r=["yT_sb", "pv"], w=["yT_sb"])
                    S.op("dve", lambda: V.tensor_tensor(yT_sb[:, c4, 0:n], yT_sb[:, c4, 0:n], ybT[:, c, col0:col0 + n], ALU.add),
                         r=["yT_sb", "ybT%d" % c], w=["yT_sb"])
                    S.op("dve", lambda: V.tensor_tensor(hT[:, c, col0:col0 + n], yT_sb[:, c4, 0:n], gT[:, c, col0:col0 + n], ALU.mult),
                         r=["yT_sb", "gT%d" % c], w=["hT%d" % c])
                yield

        def chunkA(j):
            hp = (j % 2) * 64; pq = "_%d" % (j % 2)
            tok = slice(j * 64, (j + 1) * 64)
            rows = slice(hp, hp + 64)
            A2 = PF[:, 0:1024]; B2 = PF[:, 1024:2048]
            ka, kb_ = ["pb0", "pb1"], ["pb2", "pb3"]
            def hmm(dst2, lh, rh, lh_n, rh_n, keys):
                for h in range(16):
                    c, hh = h // 2, h % 2
                    kr = slice(hh * 64, hh * 64 + 64)
                    S.op("pe", lambda: PE.matmul(dst2[rows, hcol(h):hcol(h) + 64], lh[kr, c, tok], rh[kr, c, tok], start=True, stop=True),
                         r=["%s%d" % (lh_n, c), "%s%d" % (rh_n, c)], w=keys)
            m3 = lambda m: m[rows, :].unsqueeze(1).broadcast_to([64, 16, 64])
            v3 = lambda t: t[rows, :].rearrange("p (h t) -> p h t", h=16)
            hmm(A2, Bt, At, "Bt", "At", ka)
            hmm(B2, At, Bt, "At", "Bt", kb_)
            S.op("dve", lambda: V.tensor_tensor(v3(cY[0]), v3(A2), m3(mS), ALU.mult), r=ka + ["cst"], w=["cY0" + pq])
            S.op("dve", lambda: V.tensor_tensor(v3(cYT[0]), v3(B2), m3(mST), ALU.mult), r=kb_ + ["cst"], w=["cYT0" + pq])
            yield
            hmm(A2, Kt, At, "Kt", "At", ka)
            S.op("dve", lambda: V.tensor_tensor(v3(cNak), v3(A2), m3(mS), ALU.mult), r=ka + ["cst"], w=["cNak" + pq])
            hmm(B2, Bt, Rt, "Bt", "Rt", kb_)
            S.op("dve", lambda: V.tensor_tensor(v3(cMrb), v3(B2), m3(mI), ALU.mult), r=kb_ + ["cst"], w=["cMrb" + pq])
            yield
            hmm(A2, Kt, Rt, "Kt", "Rt", ka)
            S.op("dve", lambda: V.tensor_tensor(v3(cMrk), v3(A2), m3(mI), ALU.mult), r=ka + ["cst"], w=["cMrk" + pq])
            idb3 = identb[rows, hp:hp + 64].unsqueeze(1).broadcast_to([64, 16, 64])
            S.op("dve", lambda: V.tensor_tensor(v3(cP[0]), v3(cY[0]), idb3, ALU.add), r=["cY0" + pq, "cstb"], w=["cP0" + pq])
            yield
            cur = 0
            for lvl in range(5):
                nxt = cur ^ 1
                Yc, YTc, Pc = cY[cur], cYT[cur], cP[cur]
                Yn, YTn, Pn = cY[nxt], cYT[nxt], cP[nxt]
                kY, kYT, kP = "cY%d" % cur + pq, "cYT%d" % cur + pq, "cP%d" % cur + pq
                for h in range(16):
                    hc = slice(hcol(h), hcol(h) + 64)
                    S.op("pe", lambda: PE.matmul(B2[rows, hc], Yc[rows, hc], YTc[rows, hc], start=True, stop=True), r=[kY, kYT], w=kb_)
                if lvl < 4:
                    for h in range(16):
                        hc = slice(hcol(h), hcol(h) + 64)
                        S.op("pe", lambda: PE.matmul(A2[rows, hc], YTc[rows, hc], Yc[rows, hc], start=True, stop=True), r=[kY, kYT], w=ka)
                S.op("act", lambda: A.activation(YTn[rows, :], B2[rows, :], AF.Copy), r=kb_, w=["cYT%d" % nxt + pq])
                if lvl < 4:
                    S.op("dve", lambda: V.tensor_copy(Yn[rows, :], A2[rows, :]), r=ka, w=["cY%d" % nxt + pq])
                yield
                for h in range(16):
                    hc = slice(hcol(h), hcol(h) + 64)
                    S.op("pe", lambda: PE.matmul(B2[rows, hc], YTn[rows, hc], Pc[rows, hc], start=True, stop=True), r=["cYT%d" % nxt + pq, kP], w=kb_)
                S.op("dve", lambda: V.tensor_tensor(Pn[rows, :], B2[rows, :], Pc[rows, :], ALU.add), r=kb_ + [kP], w=["cP%d" % nxt + pq])
                yield
                cur = nxt
            assert cur == 1

        def chunkB(j):
            hp = (j % 2) * 64; pq = "_%d" % (j % 2)
            s_, tok = j // 2, slice(j * 64, (j + 1) * 64)
            rows = slice(hp, hp + 64)
            TTt = cP[1]; kTT = "cP1" + pq
            C2 = PF[:, 2048:3072]; kc_ = ["pb4", "pb5"]
            for h in range(16):
                S.op("pe", lambda: PE.matmul(C2[rows, h * 64:(h + 1) * 64], cNak[rows, hcol(h):hcol(h) + 64], tmV[rows, s_, h * 64:(h + 1) * 64],
                                             start=True, stop=True), r=["cNak" + pq, "tmV"], w=kc_)
            S.op("act", lambda: A.activation(cT1[rows, :], C2[rows, :], AF.Copy), r=kc_, w=["cT1" + pq])
            yield
            for c in range(8):
                S.op("pe", lambda: PE.matmul(C2[rows, c * 128:(c + 1) * 128], At[:, c, tok], STb[:, c, :], start=True, stop=True),
                     r=["At%d" % c, "STb"], w=kc_)
            S.op("dve", lambda: V.tensor_tensor(cW[rows, :], C2[rows, :], cT1[rows, :], ALU.add), r=kc_ + ["cT1" + pq], w=["cW" + pq])
            yield
            for h in range(16):
                S.op("pe", lambda: PE.matmul(C2[rows, h * 64:(h + 1) * 64], TTt[rows, hcol(h):hcol(h) + 64], cW[rows, h * 64:(h + 1) * 64],
                                             start=True, stop=True), r=[kTT, "cW" + pq], w=kc_)
            S.op("act", lambda: A.activation(cU[rows, :], C2[rows, :], AF.Copy), r=kc_, w=["cU" + pq])
            yield
            for h in range(16):
                hs = slice(h * 64, (h + 1) * 64); hc = slice(hcol(h), hcol(h) + 64)
                S.op("pe", lambda: PE.matmul(C2[rows, hs], cMrb[rows, hc], cU[rows, hs], start=True, stop=False), r=["cMrb" + pq, "cU" + pq], w=kc_)
                S.op("pe", lambda: PE.matmul(C2[rows, hs], cMrk[rows, hc], tmV[rows, s_, hs], start=False, stop=True), r=["cMrk" + pq, "tmV"], w=kc_)
            S.op("act", lambda: A.activation(cT1[rows, :], C2[rows, :], AF.Copy), r=kc_, w=["cT1" + pq])
            yield
            for c in range(8):
                S.op("pe", lambda: PE.matmul(C2[rows, c * 128:(c + 1) * 128], Rt[:, c, tok], STb[:, c, :], start=True, stop=True),
                     r=["Rt%d" % c, "STb"], w=kc_)
            S.op("dve", lambda: V.tensor_tensor(cYs[rows, :], C2[rows, :], cT1[rows, :], ALU.add), r=kc_ + ["cT1" + pq], w=["cYs" + pq])
            yield
            SN = PF[:, 2048:2560]
            for h in range(16):
                c, hh = h // 2, h % 2
                hs = slice(h * 64, (h + 1) * 64)
                o = SN[hh * 64:hh * 64 + 64, c * 64:(c + 1) * 64]
                S.op("pe", lambda: PE.matmul(o, tmK[rows, s_, hs], tmV[rows, s_, hs], start=True, stop=False), r=["tmK", "tmV"], w=["pb4"])
                S.op("pe", lambda: PE.matmul(o, tmB[rows, s_, hs], cU[rows, hs], start=False, stop=True), r=["tmB", "cU" + pq], w=["pb4"])
            SN3 = SN.rearrange("p (c v) -> p c v", c=8)
            S.op("dve", lambda: V.tensor_tensor(ST[:], ST[:], SN3, ALU.add), r=["ST", "pb4"], w=["ST"])
            S.op("dve", lambda: V.tensor_tensor(ST[:], ST[:], gC[:, :, j:j + 1].broadcast_to([128, 8, 64]), ALU.mult), r=["ST", "gC"], w=["ST"])
            S.op("act", lambda: A.activation(STb[0:64, :, 0:64], ST[0:64, :, :], AF.Copy), r=["ST"], w=["STb"])
            S.op("act", lambda: A.activation(STb[64:128, :, 64:128], ST[64:128, :, :], AF.Copy), r=["ST"], w=["STb"])
            yield
            if j % 2 == 1:
                yield from epilogue((j // 2) * 128, 128)

        def att_block(p, n_):
            gblk = p * 2 + n_
            qs = slice(n_ * 128, (n_ + 1) * 128)
            sb_, sbk = bank(6), "pb6"
            ob, obk = bank(7), "pb7"
            for kv in range(4):
                half = (kv % 2) * 64; hr = slice(half, half + 64); kc2 = kv // 2
                kbs = [1] if gblk == 0 else [0, 1]
                for kb in kbs:
                    kcols = slice(n_ * 128 + kb * 128, n_ * 128 + kb * 128 + 128)
                    S.op("pe", lambda: PE.matmul(sb_[:, 0:512], kT[hr, kc2, kcols], qT[hr, (kv // 2) * 4:(kv // 2) * 4 + 4, qs],
                                                 start=True, stop=True), r=["kT", "qT"], w=[sbk])
                    for g in range(4):
                        h = 4 * kv + g
                        S.op("dve", lambda: V.scalar_tensor_tensor(a_sc[:, g * 128:(g + 1) * 128], Dm[:, kb * 128:(kb + 1) * 128],
                                                                   SLOPES[h] / SCALE, sb_[:, g * 128:(g + 1) * 128], ALU.mult, ALU.add),
                             r=[sbk, "cst"], w=["t_f1"])
                    S.op("act", lambda: A.activation(a_PT[kb][:, :], a_sc[:, :], AF.Exp, scale=SCALE), r=["t_f1"], w=["a_PT%d" % kb])
                    yield
                for g in range(4):
                    for ki, kb in enumerate(kbs):
                        S.op("pe", lambda: PE.matmul(ob[:, g * 66:(g + 1) * 66], a_PT[kb][:, g * 128:(g + 1) * 128], Vall[:, n_ + kb, kv, :],
                                                     start=(ki == 0), stop=(ki == len(kbs) - 1)),
                             r=["a_PT%d" % kb, "Vall%d" % (n_ + kb)], w=[obk])
                o3 = ob[:, 0:264].rearrange("p (g d) -> p g d", g=4)
                S.op("dve", lambda: V.tensor_tensor(a_den[:, :], o3[:, :, 64], esink[:, 4 * kv:4 * kv + 4], ALU.add), r=[obk, "esink"], w=["a_den"])
                S.op("dve", lambda: V.reciprocal(a_den[:, :], a_den[:, :]), r=["a_den"], w=["a_den"])
                S.op("dve", lambda: V.tensor_tensor(hb[:, n_, 1024 + kv * 256:1024 + (kv + 1) * 256].rearrange("p (g d) -> p g d", g=4),
                                                    o3[:, :, 0:64], a_den[:, :].unsqueeze(2).broadcast_to([128, 4, 64]), ALU.mult),
                     r=[obk, "a_den"], w=["hb%d" % n_])
                yield
            pt = PTb[:, 0:1024]; pk = "pb6"
            for c in range(8):
                S.op("pe", lambda: PE.transpose(pt[:, c * 128:(c + 1) * 128], hb[:, n_, 1024 + c * 128:1024 + (c + 1) * 128], identb),
                     r=["hb%d" % n_, "cstb"], w=[pk])
            for c in range(8):
                S.op("dve" if c % 2 else "act", (lambda: V.tensor_copy(hT[:, 8 + c, qs], pt[:, c * 128:(c + 1) * 128])) if c % 2 else
                     (lambda: A.activation(hT[:, 8 + c, qs], pt[:, c * 128:(c + 1) * 128], AF.Copy)), r=[pk], w=["hT%d" % (8 + c)])
            yield

        def interleave(gens):
            gens = list(gens)
            while gens:
                for g in list(gens):
                    try:
                        next(g)
                    except StopIteration:
                        gens.remove(g)

        def sample_mix():
            ONES = cst[:, C_ON:C_ON + 128]
            allk = lambda n: ["%s%d" % (n, c) for c in range(8)]
            wv = wkv_s.rearrange("b (c hh) k v -> b hh k c v", hh=2)
            sv = swkv.rearrange("b (c hh) k v -> b hh k c v", hh=2)
            bc8 = lambda ap, w_: ap.broadcast_to([128, 8, w_])
            c8 = lambda ap, w_: ap.rearrange("p (c v) -> p c v", c=8)
            W32 = chb.bitcast(F32)
            def mkset(b_sS, b_sSn, b_sBlk, b_sAbd, b_sRv, b_sT1):
                bfv = lambda base, n: base.bitcast(BF16)[:, 0:n]
                return dict(sS=c8(b_sS, 64), sSn=c8(b_sSn, 64), sT1=c8(b_sT1, 64),
                            sBlk=c8(bfv(b_sBlk, 1024), 128), sAbd=c8(bfv(b_sAbd, 1024), 128),
                            sSb=c8(b_sAbd[:, 512:768].bitcast(BF16), 64), sRv=c8(bfv(b_sRv, 512), 64))
            setA = mkset(x1[:, 0:512], x1[:, 512:1024], x1[:, 1024:2048], chf[:, 0:1024], chf[:, 1024:1536], chf[:, 1536:2048])
            setB = mkset(W32[:, 1792:2304], W32[:, 2304:2816], scr[:, 2048:3072], W32[:, 2816:3840], W32[:, 3840:4352], W32[:, 4352:4864])
            sets = [setA, setB]
            sKVb = [W32[:, 0:512], W32[:, 1152:1664]]
            sKTb = [W32[:, 4864:4992].bitcast(BF16).rearrange("p (k n) -> p k n", k=2), W32[:, 1664:1792].bitcast(BF16).rearrange("p (k n) -> p k n", k=2)]
            for i_ in range(2):
                S.op("dve", lambda: V.memset(sets[i_]["sBlk"], 0.0), w=["sBlk%d" % i_])
            for b in range(16):
                S.dma("sp", k_s[b, 0:127, :], ck[b, 1:128, :], w=["o_ks"])
                S.dma("sp", v_s[b, 0:127, :], cv[b, 1:128, :], w=["o_vs"])
            S.dma("sp", k_s[:, 127, :], ktm[0:16, 0:256], r=["ktm"], w=["o_ks"])
            S.dma("sp", v_s[:, 127, :], ktm[0:16, 256:512], r=["ktm"], w=["o_vs"])

            ckp(201)
            def rw_sample(b):
                i = b % 2; T_ = sets[i]; x = "%d" % i
                uB, vB = (4, 5) if i == 0 else (6, 7)
                for hh in range(2):
                    S.dma("sp", T_["sS"][hh * 64:(hh + 1) * 64, :, :], sv[b, hh], w=["sS" + x])
                S.op("dve", lambda: V.tensor_tensor(T_["sAbd"], BDf.unsqueeze(1).broadcast_to([128, 8, 128]), bc8(At[:, :, b:b + 1], 128), ALU.mult),
                     r=["cst"] + allk("At"), w=["sAbd" + x])
                S.op("dve", lambda: V.tensor_tensor(T_["sRv"], Ihalf.unsqueeze(1).broadcast_to([128, 8, 64]), bc8(vT[:, :, b:b + 1], 64), ALU.mult),
                     r=["cst"] + allk("vT"), w=["sRv" + x])
                yield
                S.op("act", lambda: A.activation(T_["sSb"], T_["sS"], AF.Copy), r=["sS" + x], w=["sSb" + x])
                for c in range(8):
                    S.op("pe", lambda: PE.matmul(bank(uB)[:, c * 64:(c + 1) * 64], T_["sAbd"][:, c, :], T_["sSb"][:, c, :], start=True, stop=True),
                         r=["sAbd" + x, "sSb" + x], w=["pb%d" % uB])
                for c in range(8):
                    S.op("pe", lambda: PE.matmul(bank(vB)[:, c * 64:(c + 1) * 64], BDb, T_["sRv"][:, c, :], start=True, stop=True),
                         r=["cstb", "sRv" + x], w=["pb%d" % vB])
                U3 = c8(bank(uB), 64); V3 = c8(bank(vB), 64)
                S.op("dve", lambda: V.tensor_tensor(T_["sSn"], T_["sS"], bc8(dS[:, :, b:b + 1], 64), ALU.mult), r=["sS" + x, "dS"], w=["sSn" + x])
                yield
                S.op("dve", lambda: V.tensor_tensor(T_["sT1"], U3, bc8(Bt[:, :, b:b + 1], 64), ALU.mult), r=["pb%d" % uB] + allk("Bt"), w=["sT1" + x])
                S.op("dve", lambda: V.tensor_tensor(T_["sSn"], T_["sSn"], T_["sT1"], ALU.add), r=["sSn" + x, "sT1" + x], w=["sSn" + x])
                S.op("dve", lambda: V.tensor_tensor(T_["sT1"], V3, bc8(Kt[:, :, b:b + 1], 64), ALU.mult), r=["pb%d" % vB] + allk("Kt"), w=["sT1" + x])
                S.op("dve", lambda: V.tensor_tensor(T_["sSn"], T_["sSn"], T_["sT1"], ALU.add), r=["sSn" + x, "sT1" + x], w=["sSn" + x])
                yield
                for hh in range(2):
                    S.dma("sp", wv[b, hh], T_["sSn"][hh * 64:(hh + 1) * 64, :, :], r=["sSn" + x], w=["o_wkvs"])
                S.op("act", lambda: A.activation(T_["sBlk"][0:64, :, 0:64], T_["sSn"][0:64, :, :], AF.Copy), r=["sSn" + x], w=["sBlk" + x])
                S.op("act", lambda: A.activation(T_["sBlk"][64:128, :, 64:128], T_["sSn"][64:128, :, :], AF.Copy), r=["sSn" + x], w=["sBlk" + x])
                for c in range(8):
                    S.op("pe", lambda: PE.matmul(bank(c // 4)[:, (c % 4) * 16 + b:(c % 4) * 16 + b + 1], T_["sBlk"][:, c, :], Rt[:, c, b:b + 1],
                                                 start=True, stop=True), r=["sBlk" + x, "Rt%d" % c], w=["pb%d" % (c // 4)])
                yield

            def at_sample(b):
                i = b % 2; x = "%d" % i
                kvb = sKVb[i]; Kn = kvb[:, 0:256]; Vn = kvb[:, 256:512]; KT = sKTb[i]
                tB, sB = (4, 5) if i == 0 else (6, 7)
                S.dma("sp", Kn[0:127, :], ck[b, 1:128, :], w=["sKa" + x])
                S.dma("sp", Vn[0:127, :], cv[b, 1:128, :], w=["sVa" + x])
                S.dma("sp", Kn[127:128, :], ktm[b:b + 1, 0:256], r=["ktm"], w=["sKb" + x])
                S.dma("sp", Vn[127:128, :], ktm[b:b + 1, 256:512], r=["ktm"], w=["sVb" + x])
                yield
                for k2 in range(2):
                    S.op("pe", lambda: PE.transpose(bank(tB)[:, k2 * 128:(k2 + 1) * 128], Kn[:, k2 * 128:(k2 + 1) * 128], ident),
                         r=["sKa" + x, "sKb" + x, "cst"], w=["pb%d" % tB])
                S.op("act", lambda: A.activation(KT[:, :, :], bank(tB)[:, 0:256].rearrange("p (k n) -> p k n", k=2), AF.Copy), r=["pb%d" % tB], w=["sKT" + x])
                yield
                for kv in range(4):
                    hr = slice((kv % 2) * 64, (kv % 2) * 64 + 64)
                    ob_ = sB if kv % 2 == 0 else tB
                    S.op("pe", lambda: PE.matmul(bank(ob_)[:, 256 + kv * 4:256 + kv * 4 + 4], KT[hr, kv // 2, :], qT[hr, (kv // 2) * 4:(kv // 2) * 4 + 4, b],
                                                 start=True, stop=True), r=["sKT" + x, "qT"], w=["pb%d" % ob_])
                ev_ = lambda ap, o: ap.rearrange("p (k two g) -> p k two g", k=2, two=2)[:, :, o, :]
                for o, ob_ in ((0, sB), (1, tB)):
                    S.op("dve", lambda: V.tensor_tensor(ev_(sPT[:, b, :], o), ev_(bank(ob_)[:, 256:272], o), ev_(sbias, o), ALU.add),
                         r=["pb%d" % ob_, "cst"], w=["sPT" + x])
                S.op("act", lambda: A.activation(sPT[:, b, :], sPT[:, b, :], AF.Exp, scale=SCALE), r=["sPT" + x], w=["sPT" + x])
                yield
                S.op("pe", lambda: PE.matmul(bank(sB)[:, 288:304], ONES, sPT[:, b, :], start=True, stop=True), r=["cst", "sPT" + x], w=["pb%d" % sB])
                S.op("dve", lambda: V.tensor_tensor(sPTn[:, b, :], bank(sB)[:, 288:304], esink[:, :], ALU.add), r=["pb%d" % sB, "esink"], w=["sPTn" + x])
                S.op("dve", lambda: V.reciprocal(sPTn[:, b, :], sPTn[:, b, :]), r=["sPTn" + x], w=["sPTn" + x])
                S.op("dve", lambda: V.tensor_tensor(sPTn[:, b, :], sPTn[:, b, :], sPT[:, b, :], ALU.mult), r=["sPTn" + x, "sPT" + x], w=["sPTn" + x])
                yield
                for h in range(16):
                    kv = h // 4
                    S.op("pe", lambda: PE.matmul(bank(3)[(h % 2) * 64:(h % 2) * 64 + 64, (h // 2) * 16 + b:(h // 2) * 16 + b + 1],
                                                 Vn[:, kv * 64:(kv + 1) * 64], sPTn[:, b, h:h + 1], start=True, stop=True),
                         r=["sVa" + x, "sVb" + x, "sPTn" + x], w=["pb3"])
                yield

            def pairs(fn):
                import os
                for b in range(0, 16, 2):
                    if os.environ.get("DBG_SEQ"):
                        interleave([fn(b)]); interleave([fn(b + 1)])
                    else:
                        interleave([fn(b), fn(b + 1)])
            pairs(rw_sample)
            ckp(202)
            for _ in epilogue(0, 16, [(bank(0)[:, 0:64], "pb0"), (bank(1)[:, 0:64], "pb1")]):
                pass
            for ci in range(28):
                M = RW_M[ci]
                S.dma("sp", sh_s[RW_COL[ci]:RW_COL[ci] + M, :], zs[0:M, ci, :], r=["zs"], w=["o_shs"])
            ckp(203)
            pairs(at_sample)
            ckp(204)
            for c in range(8):
                S.op("act", lambda: A.activation(hT[:, 8 + c, 0:16], bank(3)[:, c * 16:(c + 1) * 16], AF.Copy), r=["pb3"], w=["hT%d" % (8 + c)])

        def tm_proj(sl, subs, ncols, ps_of, handler, ntok):
            for s, (t0, n) in enumerate(subs):
                ps, pkeys = ps_of(s)
                for kc in range(KC):
                    S.op("pe", lambda: PE.matmul(ps[0:n, 0:ncols], hT[:, kc, t0:t0 + n], wb[sl][:, kc, 0:ncols],
                                                 start=(kc == 0), stop=(kc == KC - 1)), r=["hT%d" % kc, "wb%d" % sl], w=pkeys)
                handler(s, t0, n, ps, pkeys)

        def resid_add(W, subs, nk_list=None):
            for cb in range(4):
                sl = wload(W, [(cb * 512, 512, 0)])
                def h(s, t0, n, ps, pkeys):
                    S.op("dve", lambda: V.tensor_tensor(x_sb[0:n, s, cb * 512:(cb + 1) * 512], x_sb[0:n, s, cb * 512:(cb + 1) * 512],
                                                        ps[0:n, 0:512], ALU.add), r=pkeys + ["x%d" % s], w=["x%d" % s])
                tm_proj(sl, subs, 512, lambda s: (bank(4 + (s % 2)), ["pb%d" % (4 + (s % 2))]), h, None)

        try:
            for p in range(NPASS + 1):
                sample = (p == NPASS)
                last = (p == NPASS - 1)
                if sample:
                    subs = [(0, 16)]; ntok = 16
                else:
                    subs = [(0, 128), (128, 128)]; ntok = TP
                tok0 = p * TP
                for s, (t0, n) in enumerate(subs):
                    src = xs_d[0:16, :] if sample else xp[tok0 + t0:tok0 + t0 + n, :]
                    S.dma("sp", x_sb[0:n, s, :], src, w=["x%d" % s])
                if sample:
                    for ci in range(28):
                        M = RW_M[ci]
                        S.dma("sp", shT[0:M, ci, :], sshT[RW_COL[ci]:RW_COL[ci] + M, :], w=["shT"])
                    S.op("dve", lambda: V.memset(msh[:], 0.0), w=["msh"])
                    for ci in range(28):
                        M = RW_M[ci]
                        S.op("dve", lambda: V.tensor_scalar(msh[0:M, ci, :], shT[0:M, ci, :], pv[0:M, PV_MU + ci:PV_MU + ci + 1], None, ALU.mult),
                             r=["shT", "pv"], w=["msh"])
                ckp(1)
                norm_T(subs, PV_GMIX, ntok)
                ckp(2)
                sl = wload(w_in, [(RW_COL[24 + i], RW_M[24 + i], i * 128) for i in range(4)])
                for i in range(4):
                    ci = 24 + i; M = RW_M[ci]; b_ = 4 + (i % 2)
                    proj_fm(sl, i * 128, M, bank(b_), ["pb%d" % b_], ntok)
                    tshift(ci, bank(b_), ["pb%d" % b_], M, ntok, sample, last, xs_o[0:M, 0:ntok], ["xs_o"])
                    if i == 0:
                        S.op("act", lambda: A.activation(txw[0:96, 0:ntok], xs_o[0:96, 0:ntok], AF.Tanh), r=["xs_o"], w=["txw"])
                    elif i == 1:
                        S.op("act", lambda: A.activation(xaT[0:96, 0:ntok], xs_o[0:96, 0:ntok], AF.Copy), r=["xs_o"], w=["xaT"])
                    else:
                        S.op("act", lambda: A.activation(sxg[:, i - 2, 0:ntok], xs_o[:, 0:ntok], AF.Sigmoid), r=["xs_o"], w=["sxg"])
                ckp(3)
                S.op("dve", lambda: V.memset(ss[0:1, 7:8], 0.0), w=["ystg0", "ystg1"])
                def PP(c):
                    sl_ = wload(w_in, [(c * 128, 128, 0), (1024 + c * 128, 128, 128), (2048 + c * 128, 128, 256)])
                    prep_P(c, sl_, ntok, sample, last)
                def TT(c):
                    prep_P(c, None, ntok, sample, last)
                PP(0); TT(0); PP(1); TT(1)
                for c in range(8):
                    prep_H(c, ntok, sample, last)
                    if c + 2 < 8:
                        PP(c + 2)
                    prep_tail(c, ntok)
                    if c + 2 < 8:
                        TT(c + 2)
                ckp(5)
                for half in range(2):
                    sl = wload(w_in, [(RWP + QPERM[half * 8 + i] * 64, 64, i * 64) for i in range(8)])
                    for i in range(4):
                        b_ = 4 + (i % 2)
                        proj_fm(sl, i * 128, 128, bank(b_), ["pb%d" % b_], ntok)
                        evac(qT[:, half * 4 + i, 0:ntok], bank(b_)[:, 0:ntok], r=["pb%d" % b_], w=["qT"])
                sl = wload(w_in, [(RWP + 1024, 512, 0)])
                for i in range(2):
                    b_ = 4 + i
                    proj_fm(sl, i * 128, 128, bank(b_), ["pb%d" % b_], ntok)
                    evac(kT[:, i, 128:128 + ntok], bank(b_)[:, 0:ntok], r=["pb%d" % b_], w=["kT"])
                def kvh(s, t0, n, ps, pkeys):
                    if sample:
                        S.op("act", lambda: A.activation(ktm[0:n, :], ps[0:n, 0:512], AF.Copy), r=pkeys, w=["ktm"])
                        return
                    S.op("act", lambda: A.activation(Vall[0:n, 1 + s, :, 0:64], ps[0:n, 256:512].rearrange("p (k d) -> p k d", k=4), AF.Copy),
                         r=pkeys, w=["Vall%d" % (1 + s)])
                    if last and s == 1:
                        S.op("dve", lambda: V.tensor_copy(ktm[:, :], ps[:, 0:512]), r=pkeys, w=["ktm"])
                        S.dma("sp", k_p[:, :], ktm[:, 0:256], r=["ktm"], w=["o_kp"])
                        S.dma("sp", v_p[:, :], ktm[:, 256:512], r=["ktm"], w=["o_vp"])
                tm_proj(sl, subs, 512, lambda s: (bank(4 + (s % 2)), ["pb%d" % (4 + (s % 2))]), kvh, ntok)

                ckp(6)
                if not sample:
                    for arr, dst, an, dn in ((vT, tmV, "vT", "tmV"), (Kt, tmK, "Kt", "tmK"), (Bt, tmB, "Bt", "tmB")):
                        for s in range(2):
                            pt_rr[0] ^= 1
                            pt = PTb[:, pt_rr[0] * 1024:(pt_rr[0] + 1) * 1024]; pk = "pb%d" % (6 + pt_rr[0])
                            for c in range(8):
                                S.op("pe", lambda: PE.transpose(pt[:, c * 128:(c + 1) * 128], arr[:, c, s * 128:(s + 1) * 128], identb),
                                     r=["%s%d" % (an, c), "cstb"], w=[pk])
                            evac(dst[:, s, :], pt[:, :], r=[pk], w=[dn])
                    def att_all():
                        for n_ in range(2):
                            yield from att_block(p, n_)
                    att = att_all()
                    att_done = False
                    interleave([chunkA(0)])
                    for j in range(4):
                        live = [chunkB(j)] + ([chunkA(j + 1)] if j < 3 else [])
                        while live:
                            for g in list(live):
                                try:
                                    next(g)
                                except StopIteration:
                                    live.remove(g)
                            if not att_done:
                                try:
                                    next(att)
                                except StopIteration:
                                    att_done = True
                    if not att_done:
                        interleave([att])
                    if last:
                        S.dma("sp", wkv_p.rearrange("(c hh) k v -> hh k c v", hh=2)[0], ST[0:64, :, :], r=["ST"], w=["o_wkvp"])
                        S.dma("sp", wkv_p.rearrange("(c hh) k v -> hh k c v", hh=2)[1], ST[64:128, :, :], r=["ST"], w=["o_wkvp"])
                        for ci in range(28):
                            M = RW_M[ci]
                            S.dma("sp", sh_p[RW_COL[ci]:RW_COL[ci] + M, :], zlast[0:M, ci:ci + 1], r=["zlast"], w=["o_shp"])
                    S.op("act", lambda: A.activation(kT[:, :, 0:128], kT[:, :, TP:TP + 128], AF.Copy), r=["kT"], w=["kT"])
                    S.op("act", lambda: A.activation(Vall[:, 0, :, 0:64], Vall[:, 2, :, 0:64], AF.Copy), r=["Vall2"], w=["Vall0"])
                else:
                    sample_mix()
                ckp(10)
                resid_add(w_out, subs)
                ckp(11)
                norm_T(subs, PV_GFFN, ntok)
                for fb in range(FC // 2):
                    slg = wload(None, [(fb * 256, 256, 0, w_gate), (fb * 256, 256, 256, w_up)])
                    for i in range(2):
                        f = fb * 2 + i
                        proj_fm(slg, i * 128, 128, bank(0 + (f % 2)), ["pb%d" % (f % 2)], ntok)
                        proj_fm(slg, 256 + i * 128, 128, bank(2 + (f % 2)), ["pb%d" % (2 + (f % 2))], ntok)
                        S.op("act", lambda: A.activation(t_f1[:, 0:ntok], bank(f % 2)[:, 0:ntok], AF.Silu), r=["pb%d" % (f % 2)], w=["t_f1"])
                        S.op("dve", lambda: V.tensor_tensor(hid[:, f, 0:ntok], t_f1[:, 0:ntok], bank(2 + (f % 2))[:, 0:ntok], ALU.mult),
                             r=["t_f1", "pb%d" % (2 + (f % 2))], w=["hid"])
                for cb in range(4):
                    parts = [(0, 16), (16, 16), (32, 12)]
                    for pi, (k0, nk) in enumerate(parts):
                        sl = wload(w_down, [(cb * 512, 512, 0)], nk=nk, row0=k0 * 128)
                        for s, (t0, n) in enumerate(subs):
                            for kk_ in range(nk):
                                S.op("pe", lambda: PE.matmul(bank(4 + s)[0:n, 0:512], hid[:, k0 + kk_, t0:t0 + n], wb[sl][:, kk_, 0:512],
                                                             start=(pi == 0 and kk_ == 0), stop=(pi == 2 and kk_ == nk - 1)),
                                     r=["hid", "wb%d" % sl], w=["pb%d" % (4 + s)])
                    for s, (t0, n) in enumerate(subs):
                        S.op("dve", lambda: V.tensor_tensor(x_sb[0:n, s, cb * 512:(cb + 1) * 512], x_sb[0:n, s, cb * 512:(cb + 1) * 512],
                                                            bank(4 + s)[0:n, 0:512], ALU.add), r=["pb%d" % (4 + s), "x%d" % s], w=["x%d" % s])
                ckp(12)
                norm_T(subs, PV_GPLE, ntok)
                for s, (t0, n) in enumerate(subs):
                    src = ps_d[0:16, :] if sample else pp[tok0 + t0:tok0 + t0 + n, :]
                    S.dma("sp", pe_sb[0:n, s, :], src, w=["pe_sb"])
                    S.op("dve", lambda: V.tensor_copy(pe_b[0:n, s, :], pe_sb[0:n, s, :]), r=["pe_sb"], w=["pe_b"])
                    pt_rr[0] ^= 1
                    pt = PTb[:, pt_rr[0] * 1024:(pt_rr[0] + 1) * 1024]; pk = "pb%d" % (6 + pt_rr[0])
                    for k2 in range(2):
                        S.op("pe", lambda: PE.transpose(pt[:, k2 * 128:k2 * 128 + n], pe_b[0:n, s, k2 * 128:(k2 + 1) * 128], identb[0:n, 0:n]),
                             r=["pe_b", "cstb"], w=[pk])
                    for k2 in range(2):
                        evac(pT_[:, k2, t0:t0 + n], pt[:, k2 * 128:k2 * 128 + n], r=[pk], w=["pT_"])
                for cb in range(4):
                    sl = wload(ple_gate, [(cb * 512, 512, 0)])
                    S.dma("pool", wpp[:, :, :], ple_proj[:, cb * 512:(cb + 1) * 512].rearrange("(k p) n -> p k n", p=128), w=["wpp"])
                    for s, (t0, n) in enumerate(subs):
                        gb, pb_ = bank(4 + (s % 2)), bank(2 + (s % 2))
                        gk, pk2 = "pb%d" % (4 + (s % 2)), "pb%d" % (2 + (s % 2))
                        for kc in range(KC):
                            S.op("pe", lambda: PE.matmul(gb[0:n, 0:512], hT[:, kc, t0:t0 + n], wb[sl][:, kc, 0:512], start=(kc == 0), stop=(kc == KC - 1)),
                                 r=["hT%d" % kc, "wb%d" % sl], w=[gk])
                        for k2 in range(2):
                            S.op("pe", lambda: PE.matmul(pb_[0:n, 0:512], pT_[:, k2, t0:t0 + n], wpp[:, k2, :], start=(k2 == 0), stop=(k2 == 1)),
                                 r=["pT_", "wpp"], w=[pk2])
                        S.op("act", lambda: A.activation(t_f1[0:n, :], gb[0:n, 0:512], AF.Sigmoid), r=[gk], w=["t_f1"])
                        S.op("dve", lambda: V.tensor_tensor(t_f1[0:n, :], t_f1[0:n, :], pb_[0:n, 0:512], ALU.mult), r=["t_f1", pk2], w=["t_f1"])
                        S.op("dve", lambda: V.tensor_tensor(x_sb[0:n, s, cb * 512:(cb + 1) * 512], x_sb[0:n, s, cb * 512:(cb + 1) * 512], t_f1[0:n, :], ALU.add),
                             r=["t_f1", "x%d" % s], w=["x%d" % s])
                ckp(13)
                for s, (t0, n) in enumerate(subs):
                    S.op("act", lambda: A.activation(hb[0:n, s, :], x_sb[0:n, s, :], AF.Square, accum_out=ss[0:n, s:s + 1]), r=["x%d" % s], w=["hb%d" % s, "ss"])
                    S.op("act", lambda: A.activation(sd[0:n, s:s + 1], ss[0:n, s:s + 1], AF.Sqrt, bias=EPS, scale=1.0 / D), r=["ss"], w=["sd"])
                    S.op("dve", lambda: V.reciprocal(rstd[0:n, s:s + 1], sd[0:n, s:s + 1]), r=["sd"], w=["rstd%d" % s])
                    S.op("dve", lambda: V.scalar_tensor_tensor(ystg[0:n, s, :], x_sb[0:n, s, :], rstd[0:n, s:s + 1], fn_bc[0:n, :], ALU.mult, ALU.mult),
                         r=["x%d" % s, "rstd%d" % s, "fn_bc"], w=["ystg%d" % s, "hid"])
                    dst = y_s[0:16, :] if sample else y_p[tok0 + t0:tok0 + t0 + n, :]
                    S.dma("sp", dst, ystg[0:n, s, :], r=["ystg%d" % s], w=["o_y"])
                assert wblk[0] == NBLK * (p + 1), wblk[0]
                ckp(14 + p)
        except _Stop:
            pass
        S.finish("sp")
        print("build stats: waits", S.nwait, "dmas", S.ndma, "seq", dict(S.seq), flush=True)
    return nc


def _consts():
    c = np.zeros((128, C_N), np.float32)
    c[:, C_ID:C_ID + 128] = np.eye(128)
    p = np.arange(128)
    c[:, C_BD:C_BD + 128] = (p[:, None] // 64 == p[None, :] // 64)
    c[:, C_IH:C_IH + 64] = (p[:, None] % 64 == np.arange(64)[None, :])
    c[:, C_HSEL:C_HSEL + 2] = (p[:, None] // 64 == np.arange(2)[None, :])
    s_ = (p % 64)[:, None]; t_ = np.arange(64)[None, :]
    c[:, C_MS:C_MS + 64] = (s_ < t_)
    c[:, C_MI:C_MI + 64] = (s_ <= t_)
    c[:, C_MST:C_MST + 64] = (s_ > t_)
    c[:, C_RST:C_RST + 256] = (np.arange(256) % 64 != 0)[None, :]
    j = p[:, None].astype(np.float32); i = np.arange(128)[None, :].astype(np.float32)
    BIG = 1e5
    c[:, C_DM:C_DM + 128] = np.where(j > i, -(i + 128 - j), -BIG)
    c[:, C_DM + 128:C_DM + 256] = np.where(j <= i, -(i - j), -BIG)
    sl = np.array(SLOPES, np.float32)[None, :]
    c[:, C_SB:C_SB + 16] = -sl * (127 - j) / SCALE
    c[:, C_ON:C_ON + 128] = 1.0
    return c


def _pv(inp):
    pv = np.zeros((128, PV_N), np.float32)
    f = lambda v, n: np.ascontiguousarray(np.asarray(v, np.float32).reshape(n, 128).T)
    pv[:, PV_GMIX:PV_GMIX + 16] = f(inp["norm_mix"][0], 16)
    pv[:, PV_GFFN:PV_GFFN + 16] = f(inp["norm_ffn"][0], 16)
    pv[:, PV_GPLE:PV_GPLE + 16] = f(inp["norm_ple"][0], 16)
    mu = np.asarray(inp["mu_shift"][0], np.float32)
    for ci in range(28):
        pv[0:RW_M[ci], PV_MU + ci] = mu[RW_COL[ci]:RW_COL[ci] + RW_M[ci]]
    pv[:, PV_W0:PV_W0 + 8] = f(inp["rwkv_w0"][0], 8)
    pv[:, PV_A0:PV_A0 + 8] = f(inp["rwkv_a0"][0], 8)
    pv[:, PV_KK:PV_KK + 8] = f(inp["rwkv_k_k"][0], 8)
    pv[:, PV_KA:PV_KA + 8] = f(inp["rwkv_k_a"][0], 8)
    pv[:, PV_RK:PV_RK + 8] = f(np.asarray(inp["rwkv_r_k"][0]).reshape(-1), 8)
    pv[:, PV_LNW:PV_LNW + 8] = f(inp["rwkv_ln_w"][0], 8)
    pv[:, PV_LNB:PV_LNB + 8] = f(inp["rwkv_ln_b"][0], 8)
    return pv


_NC = [None]


def kernel(**inp):
    A_ = lambda k: np.ascontiguousarray(np.asarray(inp[k], np.float32))
    if _NC[0] is None:
        _NC[0] = build_program()
    nc = _NC[0]
    shared = {
        "w_in": A_("w_in")[0], "w_out": A_("w_out")[0], "w_gate": A_("w_gate")[0], "w_up": A_("w_up")[0],
        "w_down": A_("w_down")[0], "ple_gate": A_("ple_gate")[0], "ple_proj": A_("ple_proj")[0],
        "w2": A_("rwkv_w2")[0], "a2": A_("rwkv_a2")[0], "g2": A_("rwkv_g2")[0],
        "pv": _pv(inp), "cst": _consts(), "fng": A_("final_norm").reshape(1, D),
        "sinks": A_("attn_sinks").reshape(1, 16),
    }
    xp, xs = A_("x_prompt"), A_("x_sample")
    pp, ps = A_("p_prompt"), A_("p_sample")
    swkv, ssh = A_("state_wkv"), A_("state_shift")
    ck, cv = A_("cache_k"), A_("cache_v")
    in_maps = []
    for i in range(8):
        b = slice(16 * i, 16 * i + 16)
        m = dict(shared)
        m.update({
            "xp": xp[i], "xs": np.ascontiguousarray(xs[b, 0]), "pp": pp[0, i], "psm": np.ascontiguousarray(ps[0, b, 0]),
            "swkv": np.ascontiguousarray(np.swapaxes(swkv[0, b], -1, -2)),
            "sshT": np.ascontiguousarray(ssh[0, b].T),
            "ck": np.ascontiguousarray(ck[0, b].reshape(16, 128, 256)), "cv": np.ascontiguousarray(cv[0, b].reshape(16, 128, 256)),
        })
        in_maps.append(m)
    res = run_bass_kernel_spmd(nc, in_maps, core_ids=list(range(8)))
    R = res.results
    cat = lambda k: np.stack([np.asarray(r[k], np.float32) for r in R])
    y_p = cat("y_p")
    y_s = cat("y_s").reshape(128, 1, D)
    wkv_p = np.swapaxes(cat("wkv_p"), -1, -2)[None]
    sh_p = cat("sh_p").reshape(8, RWP)[None]
    k_p = cat("k_p").reshape(8, 128, 4, 64)[None]
    v_p = cat("v_p").reshape(8, 128, 4, 64)[None]
    wkv_s = np.swapaxes(cat("wkv_s").reshape(128, 16, 64, 64), -1, -2)[None]
    sh_s = np.swapaxes(cat("sh_s"), 1, 2).reshape(128, RWP)[None]
    k_s = cat("k_s").reshape(128, 128, 4, 64)[None]
    v_s = cat("v_s").reshape(128, 128, 4, 64)[None]
    c_ = np.ascontiguousarray
    return (c_(y_p), c_(y_s), c_(wkv_p), c_(sh_p), c_(k_p), c_(v_p), c_(wkv_s), c_(sh_s), c_(k_s), c_(v_s))
```

```python
import bisect
import concourse.bass as bass
import concourse.mybir as mybir

F32 = mybir.dt.float32
BF16 = mybir.dt.bfloat16
AF = mybir.ActivationFunctionType
ALU = mybir.AluOpType
AX = mybir.AxisListType

SEM_ROLL = 12000


class Sched:
    def __init__(self, nc, stack):
        self.nc = nc
        self.stack = stack
        self.engs = {"pe": nc.tensor, "dve": nc.vector, "act": nc.scalar,
                     "pool": nc.gpsimd, "sp": nc.sync}
        self.sem = {}
        self.cnt = {}
        self.seq = {}
        self.sigs = {}
        self.sigseq = {}
        self.last_ins = {}
        self.last_seq = {}
        self.nsem = 0
        for e in self.engs:
            self._new_sem(e)
            self.seq[e] = 0
            self.sigs[e] = []
            self.sigseq[e] = []
            self.last_ins[e] = None
        self.semobj = {}
        for e in self.engs:
            pass
        self.seen = {e: {} for e in self.engs}
        self.lastw = {}
        self.readers = {}
        self.dma_sems = {}
        self.dma_rr = {}
        self.ndma = 0
        self.nwait = 0

    def _new_sem(self, e):
        name = "s_%s_%d" % (e, self.nsem)
        self.nsem += 1
        s = self.stack.enter_context(self.nc.semaphore(name))
        self.sem[e] = (name, s)
        self.cnt[e] = 0
        if not hasattr(self, "semh"):
            self.semh = {}
        self.semh[name] = s

    def _resolve(self, tok):
        if tok[0] == "d":
            return tok[1], tok[2]
        _, e, q = tok
        i = bisect.bisect_left(self.sigseq[e], q)
        if i < len(self.sigseq[e]):
            _, sn, v = self.sigs[e][i]
            return sn, v
        ins = self.last_ins[e]
        assert ins is not None
        if self.cnt[e] >= SEM_ROLL:
            self._new_sem(e)
        sn, s = self.sem[e]
        self.cnt[e] += 1
        ins.then_inc(s, 1)
        self.sigs[e].append((self.last_seq[e], sn, self.cnt[e]))
        self.sigseq[e].append(self.last_seq[e])
        return sn, self.cnt[e]

    def _wait(self, e, toks):
        need = {}
        for t in toks:
            sn, v = self._resolve(t)
            if self.seen[e].get(sn, 0) >= v:
                continue
            if need.get(sn, 0) < v:
                need[sn] = v
        for sn, v in need.items():
            self.engs[e].wait_ge(self.semh[sn], v)
            self.seen[e][sn] = v
            self.nwait += 1

    def _deps(self, e, r, w):
        toks = []
        for k in r:
            t = self.lastw.get(k)
            if t is not None and not (e == "pe" and t[0] == "c" and t[1] == "pe"):
                toks.append(t)
            if k.startswith("pb") or k.startswith("pt"):
                for t in self.readers.get(k, ()):
                    if t[0] == "c" and t[1] != e:
                        toks.append(t)
        for k in w:
            t = self.lastw.get(k)
            if t is not None and not (e == "pe" and t[0] == "c" and t[1] == "pe"):
                toks.append(t)
            for t in self.readers.get(k, ()):
                toks.append(t)
        return toks

    def _record(self, tok, r, w):
        for k in r:
            self.readers.setdefault(k, []).append(tok)
        for k in w:
            self.lastw[k] = tok
            self.readers[k] = []

    def op(self, e, fn, r=(), w=()):
        self._wait(e, self._deps(e, r, w))
        ins = fn()
        self.seq[e] += 1
        self.last_ins[e] = ins
        self.last_seq[e] = self.seq[e]
        self._record(("c", e, self.seq[e]), r, w)
        return ins

    def dma(self, q, out, in_, r=(), w=(), **kw):
        ring = self.dma_sems.setdefault(q, [])
        if len(ring) < 8:
            name = "d_%s_%d" % (q, len(ring))
            s = self.stack.enter_context(self.nc.semaphore(name))
            self.semh[name] = s
            ring.append([name, 0])
            slot = ring[-1]
        else:
            i = self.dma_rr.get(q, 0)
            self.dma_rr[q] = (i + 1) % len(ring)
            slot = ring[i]
        toks = self._deps(q, r, w)
        if slot[1] > 0:
            toks.append(("d", slot[0], slot[1]))
        self._wait(q, toks)
        slot[1] += 16
        ins = self.engs[q].dma_start(out=out, in_=in_, **kw)
        ins.then_inc(self.semh[slot[0]], 16)
        self.seq[q] += 1
        self._record(("d", slot[0], slot[1]), r, w)
        self.ndma += 1
        return ins

    def finish(self, e="sp"):
        toks = []
        for k, t in self.lastw.items():
            toks.append(t)
        self._wait(e, toks)

import contextlib
import numpy as np
from concourse.bass_utils import run_bass_kernel_spmd

D = 2048
KC = 16
DFF = 5632
FC = 44
PROJ = 5056
RWP = 3520
TP = 256
NPASS = 2048 // TP
NBLK = 54
EPS = 1e-6
GN_EPS = 64e-5
SCALE = 0.125
DEC = 0.6065306597126334
SLOPES = [2.0 ** (-8.0 * (h + 1) / 16) for h in range(16)]
QPERM = [0, 4, 1, 5, 2, 6, 3, 7, 8, 12, 9, 13, 10, 14, 11, 15]
PV_GMIX, PV_GFFN, PV_GPLE, PV_MU = 0, 16, 32, 48
PV_W0, PV_A0, PV_KK, PV_KA, PV_RK, PV_LNW, PV_LNB = 76, 84, 92, 100, 108, 116, 124
PV_N = 132
C_ID, C_BD, C_IH, C_HSEL, C_MS, C_MI, C_MST, C_RST, C_DM, C_SB, C_ON, C_N = 0, 128, 256, 320, 322, 386, 450, 514, 770, 1026, 1042, 1170

RW_COL = [c * 128 for c in range(8)] + [1024 + c * 128 for c in range(8)] + \
         [2048 + c * 128 for c in range(8)] + [3072, 3168, 3264, 3392]
RW_M = [128] * 24 + [96, 96, 128, 128]


def hcol(h):
    return ((h % 2) * 8 + h // 2) * 64


class _Stop(Exception):
    pass


def build_program(debug=False, stop=None):
    nc = bass.Bass("TRN2", target_bir_lowering=False)
    din = lambda n, sh: nc.dram_tensor(n, sh, F32, kind="ExternalInput").ap()
    dout = lambda n, sh: nc.dram_tensor(n, sh, F32, kind="ExternalOutput").ap()
    xp = din("xp", [2048, D]); xs_d = din("xs", [16, D])
    pp = din("pp", [2048, 256]); ps_d = din("psm", [16, 256])
    swkv = din("swkv", [16, 16, 64, 64])
    sshT = din("sshT", [RWP, 16])
    ck = din("ck", [16, 128, 256]); cv = din("cv", [16, 128, 256])
    w_in = din("w_in", [D, PROJ]); w_out = din("w_out", [D, D])
    w_gate = din("w_gate", [D, DFF]); w_up = din("w_up", [D, DFF]); w_down = din("w_down", [DFF, D])
    ple_gate = din("ple_gate", [D, D]); ple_proj = din("ple_proj", [256, D])
    w2 = din("w2", [96, 1024]); a2 = din("a2", [96, 1024]); g2 = din("g2", [256, 1024])
    pv_d = din("pv", [128, PV_N]); cst_d = din("cst", [128, C_N])
    fng = din("fng", [1, D]); esk = din("sinks", [1, 16])

    wsc = nc.dram_tensor("wsc", [NBLK, 128, KC * 512], BF16, kind="Internal").ap()
    y_p = dout("y_p", [2048, D]); y_s = dout("y_s", [16, D])
    wkv_p = dout("wkv_p", [16, 64, 64])
    sh_p = dout("sh_p", [RWP, 1])
    k_p = dout("k_p", [128, 256]); v_p = dout("v_p", [128, 256])
    wkv_s = dout("wkv_s", [16, 16, 64, 64])
    sh_s = dout("sh_s", [RWP, 16])
    k_s = dout("k_s", [16, 128, 256]); v_s = dout("v_s", [16, 128, 256])

    with contextlib.ExitStack() as st:
        S = Sched(nc, st)
        sb = lambda n, sh, dt=F32: st.enter_context(nc.sbuf_tensor("sb_" + n, sh, dt))
        V, A, P, PE = nc.vector, nc.scalar, nc.gpsimd, nc.tensor
        PF = st.enter_context(nc.psum_tensor("PF", [128, 4096], F32))
        PTb = PF[:, 3072:4096].bitcast(BF16)
        pbk = lambda b: "pb%d" % b
        bank = lambda b: PF[:, b * 512:(b + 1) * 512]

        x_sb = sb("x_sb", [128, 2, D])
        hb = sb("hb", [128, 2, D], BF16)
        hT = sb("hT", [128, KC, TP], BF16)
        wb = [sb("wb%d" % i, [128, KC, 512], BF16) for i in range(2)]
        pv = sb("pv", [128, PV_N]); cst = sb("cst", [128, C_N])
        omu = sb("omu", [128, 28]); omka = sb("omka", [128, 8])
        cstb = sb("cstb", [128, 384], BF16)
        fn_bc = sb("fn_bc", [128, D])
        esink = sb("esink", [128, 16])
        w2b = sb("w2b", [128, 1024], BF16); a2b = sb("a2b", [128, 1024], BF16)
        g2b = sb("g2b", [128, 2, 1024], BF16)
        ss = sb("ss", [128, 8]); sd = sb("sd", [128, 8]); rstd = sb("rstd", [128, 8])
        rwbig = sb("rwbig", [128, 6 * 8 * TP], BF16)
        def rwv(i):
            return rwbig[:, i * 8 * TP:(i + 1) * 8 * TP].rearrange("p (c t) -> p c t", c=8)
        Rt, Kt, At, Bt, vT, ybT = [rwv(i) for i in range(6)]
        hid = rwbig[:, 0:FC * TP].rearrange("p (c t) -> p c t", c=FC)
        gT = sb("gT", [128, 8, TP], BF16)
        dS = sb("dS", [128, 8, 16])
        tmV = sb("tmV", [128, 2, 1024], BF16); tmK = sb("tmK", [128, 2, 1024], BF16)
        tmB = sb("tmB", [128, 2, 1024], BF16)
        gC = sb("gC", [128, 8, 4])
        carry = sb("carry", [128, 28]); zlast = sb("zlast", [128, 28]); zs = sb("zs", [128, 28, 16])
        shT = sb("shT", [128, 28, 16]); msh = sb("msh", [128, 28, 16])
        txw = sb("txw", [128, TP], BF16); xaT = sb("xaT", [128, TP], BF16)
        sxg = sb("sxg", [128, 2, TP], BF16)
        tmr = [sb("tmr%d" % i, [128, TP + 1]) for i in range(3)]
        scr = sb("scr", [128, 13 * TP])
        def scv(i):
            return scr[:, i * TP:(i + 1) * TP]
        xs_r2 = sb("xs_r2", [128, TP]); xs_k2 = sb("xs_k2", [128, TP])
        xs_r, xs_k, xs_o, t_sg, t_L, t_eL, t_enL, t_eLm, t_a, t_kkr, t_sq, t_b, t_kh = [scv(i) for i in range(13)]
        t_rkr = sb("t_rkr", [128, TP], BF16)
        xs_rb = [xs_r, xs_r2]; xs_kb = [xs_k, xs_k2]
        def epv(i):
            return scr[:, i * 512:(i + 1) * 512].rearrange("p (c t) -> p c t", c=4)
        yT_sb, e_sq, e_m, e_v = [epv(i) for i in range(4)]
        chb = sb("chb", [128, 11 * 1024], BF16)
        def chv(i):
            return chb[:, i * 1024:(i + 1) * 1024]
        cY = [chv(0), chv(1)]; cYT = [chv(2), chv(3)]; cP = [chv(4), chv(5)]
        cNak, cMrb, cMrk, cW, cU = chv(6), chv(7), chv(8), chv(9), chv(10)
        chf = sb("chf", [128, 2048])
        cT1 = chf[:, 0:1024]; cYs = chf[:, 1024:2048]
        ST = sb("ST", [128, 8, 64]); STb = sb("STb", [128, 8, 128], BF16)
        qT = sb("qT", [128, 8, TP], BF16); kT = sb("kT", [128, 2, 128 + TP], BF16)
        Vall = sb("Vall", [128, 3, 4, 66], BF16)
        t_f1 = sb("t_f1", [128, 512]); t_f2 = sb("t_f2", [128, 512])
        a_sc = t_f1; ktm = t_f2
        a_PT = [sb("a_PT%d" % i, [128, 512], BF16) for i in range(2)]
        a_den = sb("a_den", [128, 4])
        wpp = sb("wpp", [128, 2, 512], BF16)
        pT_ = sb("pT_", [128, 2, TP], BF16); pe_sb = sb("pe_sb", [128, 2, 256]); pe_b = sb("pe_b", [128, 2, 256], BF16)
        x1 = x_sb[:, 1, :]
        sS = x1[:, 0:512].rearrange("p (c v) -> p c v", c=8)
        sSn = x1[:, 512:1024].rearrange("p (c v) -> p c v", c=8)
        sBlk = x1[:, 1024:2048].rearrange("p (c v) -> p c v", c=8)
        sAbd = chf[:, 0:1024].rearrange("p (c v) -> p c v", c=8)
        sRv = chf[:, 1024:1536].rearrange("p (c v) -> p c v", c=8)
        sT1 = chf[:, 1536:2048].rearrange("p (c v) -> p c v", c=8)
        chb32 = chb[:, 0:4096].bitcast(F32)
        sKV = chb32[:, 0:512]; sKn = sKV[:, 0:256]; sVn = sKV[:, 256:512]
        sPT = chb32[:, 512:768].rearrange("p (b h) -> p b h", b=16)
        sPTn = chb32[:, 768:1024].rearrange("p (b h) -> p b h", b=16)
        vf = chb32[:, 1024:1152].rearrange("p (c b) -> p c b", c=8)
        sKT = chb[:, 4096:4352].rearrange("p (k n) -> p k n", k=2)

        ident = cst[:, C_ID:C_ID + 128]; BDf = cst[:, C_BD:C_BD + 128]
        identb = cstb[:, 0:128]; BDb = cstb[:, 128:256]
        mS = cst[:, C_MS:C_MS + 64]; mI = cst[:, C_MI:C_MI + 64]; mST = cst[:, C_MST:C_MST + 64]
        rst = cst[:, C_RST:C_RST + 256]
        Dm = cst[:, C_DM:C_DM + 256]
        sbias = cst[:, C_SB:C_SB + 16]
        Ihalf = cst[:, C_IH:C_IH + 64]

        S.dma("sp", pv[:], pv_d[:, :], w=["pv"])
        S.dma("sp", cst[:], cst_d[:, :], w=["cst"])
        S.dma("sp", fn_bc[:], fng.partition_broadcast(128), w=["fn_bc"])
        S.dma("sp", esink[:], esk.partition_broadcast(128), w=["esink"])
        S.dma("pool", w2b[0:96, :], w2[:, :], w=["w2b"])
        S.dma("pool", a2b[0:96, :], a2[:, :], w=["a2b"])
        S.dma("pool", g2b[:], g2.rearrange("(k p) n -> p k n", p=128), w=["g2b"])
        S.op("dve", lambda: V.tensor_copy(cstb[:, 0:256], cst[:, 0:256]), r=["cst"], w=["cstb"])
        S.op("dve", lambda: V.tensor_scalar(omu[:], pv[:, PV_MU:PV_MU + 28], -1.0, 1.0, ALU.mult, ALU.add), r=["pv"], w=["omu"])
        S.op("dve", lambda: V.tensor_scalar(omka[:], pv[:, PV_KA:PV_KA + 8], -1.0, 1.0, ALU.mult, ALU.add), r=["pv"], w=["omka"])
        S.op("act", lambda: A.activation(esink[:], esink[:], AF.Exp), r=["esink"], w=["esink"])
        S.op("pool", lambda: P.memset(carry[:], 0.0), w=["carry%d" % i for i in range(28)])
        S.op("pool", lambda: P.memset(ST[:], 0.0), w=["ST"])
        S.op("pool", lambda: P.memset(STb[:], 0.0), w=["STb"])
        S.op("pool", lambda: P.memset(Vall[:], 1.0), w=["Vall0", "Vall1", "Vall2"])
        S.op("pool", lambda: P.memset(kT[:], 0.0), w=["kT"])
        for i_ in range(2):
            S.op("pool", lambda: P.memset(wb[i_][:], 0.0), w=["wb%d" % i_])

        def ckp(k):
            if stop == k:
                raise _Stop()
        wslot = [0]

        wblk = [0]

        def wload(W, specs, nk=KC, row0=0):
            sl = wslot[0]; wslot[0] ^= 1
            bi = wblk[0] % NBLK; first = wblk[0] < NBLK; wblk[0] += 1
            if first:
                for sp_ in specs:
                    c0, n, d0 = sp_[0:3]
                    Wm = sp_[3] if len(sp_) > 3 else W
                    src = Wm[row0:row0 + nk * 128, c0:c0 + n].rearrange("(k p) n -> p k n", p=128)
                    S.dma("pool", wb[sl][:, 0:nk, d0:d0 + n], src, w=["wb%d" % sl])
                S.dma("sp", wsc[bi], wb[sl][:, :, :].rearrange("p k n -> p (k n)"), r=["wb%d" % sl], w=["wsc%d" % bi])
            else:
                S.dma("pool", wb[sl][:, :, :].rearrange("p k n -> p (k n)"), wsc[bi], r=["wsc%d" % bi], w=["wb%d" % sl])
            return sl

        ev_rr = [0]

        def evac(out, in_, r, w, scale=None):
            ev_rr[0] ^= 1
            if ev_rr[0]:
                if scale is None:
                    S.op("act", lambda: A.activation(out, in_, AF.Copy), r=r, w=w)
                else:
                    S.op("act", lambda: A.activation(out, in_, AF.Copy, scale=scale), r=r, w=w)
            else:
                if scale is None:
                    S.op("dve", lambda: V.tensor_copy(out, in_), r=r, w=w)
                else:
                    S.op("dve", lambda: V.tensor_scalar(out, in_, scale, None, ALU.mult), r=r, w=w)

        pt_rr = [0]

        def norm_T(subs, gcol, ntok):
            for s, (t0, n) in enumerate(subs):
                S.op("act", lambda: A.activation(hb[0:n, s, :], x_sb[0:n, s, :], AF.Square, accum_out=ss[0:n, s:s + 1]),
                     r=["x%d" % s], w=["hb%d" % s, "ss"])
                S.op("act", lambda: A.activation(sd[0:n, s:s + 1], ss[0:n, s:s + 1], AF.Sqrt, bias=EPS, scale=1.0 / D),
                     r=["ss"], w=["sd"])
                S.op("dve", lambda: V.reciprocal(rstd[0:n, s:s + 1], sd[0:n, s:s + 1]), r=["sd"], w=["rstd%d" % s])
                S.op("dve", lambda: V.tensor_scalar(hb[0:n, s, :], x_sb[0:n, s, :], rstd[0:n, s:s + 1], None, ALU.mult),
                     r=["x%d" % s, "rstd%d" % s], w=["hb%d" % s])
            for kc in range(KC):
                pt_rr[0] ^= 1
                pt = PTb[:, pt_rr[0] * 1024:(pt_rr[0] + 1) * 1024]; pk = "pb%d" % (6 + pt_rr[0])
                for s, (t0, n) in enumerate(subs):
                    S.op("pe", lambda: PE.transpose(pt[:, t0:t0 + n], hb[0:n, s, kc * 128:(kc + 1) * 128], identb[0:n, 0:n]),
                         r=["hb%d" % s, "cstb"], w=[pk])
                evac(hT[:, kc, 0:ntok], pt[:, 0:ntok], r=[pk, "pv"], w=["hT%d" % kc], scale=pv[:, gcol + kc:gcol + kc + 1])

        def proj_fm(sl, d0, M, ps, pkeys, ntok):
            for kc in range(KC):
                S.op("pe", lambda: PE.matmul(ps[0:M, 0:ntok], wb[sl][:, kc, d0:d0 + M], hT[:, kc, 0:ntok],
                                             start=(kc == 0), stop=(kc == KC - 1)),
                     r=["wb%d" % sl, "hT%d" % kc], w=pkeys)

        def tshift(ci, ps, pkeys, M, ntok, sample, last, out_ap, okeys):
            mu = pv[0:M, PV_MU + ci:PV_MU + ci + 1]
            if sample:
                S.op("act", lambda: A.activation(zs[0:M, ci, :], ps[0:M, 0:16], AF.Copy), r=pkeys, w=["zs"])
                S.op("dve", lambda: V.scalar_tensor_tensor(out_ap, ps[0:M, 0:16], omu[0:M, ci:ci + 1], msh[0:M, ci, :],
                                                           ALU.mult, ALU.add), r=pkeys + ["omu", "msh"], w=okeys)
                return
            tm = tmr[ci % 3]; tk = "tmr%d" % (ci % 3)
            S.op("act", lambda: A.activation(tm[0:M, 0:1], carry[0:M, ci:ci + 1], AF.Copy), r=["carry%d" % ci], w=[tk])
            S.op("act", lambda: A.activation(tm[0:M, 1:1 + ntok], ps[0:M, 0:ntok], AF.Copy, scale=mu), r=pkeys + ["pv"], w=[tk])
            S.op("act", lambda: A.activation(carry[0:M, ci:ci + 1], tm[0:M, ntok:ntok + 1], AF.Copy), r=[tk], w=["carry%d" % ci])
            if last:
                S.op("act", lambda: A.activation(zlast[0:M, ci:ci + 1], ps[0:M, ntok - 1:ntok], AF.Copy), r=pkeys, w=["zlast"])
            S.op("dve", lambda: V.scalar_tensor_tensor(out_ap, ps[0:M, 0:ntok], omu[0:M, ci:ci + 1], tm[0:M, 0:ntok],
                                                       ALU.mult, ALU.add), r=pkeys + ["omu", tk], w=okeys)

        def prep_P(c, sl, ntok, sample, last):
            N = ntok
            b0 = 0 if c % 2 == 0 else 3
            pr, pk_, pvv = bank(b0), bank(b0 + 1), bank(b0 + 2)
            kr_, kk_, kv_ = ["pb%d" % (b0 + i) for i in range(3)]
            xs_r, xs_k = xs_rb[c % 2], xs_kb[c % 2]
            xr_k, xk_k = "xs_r%d" % (c % 2), "xs_k%d" % (c % 2)
            if sl is not None:
                proj_fm(sl, 0, 128, pr, [kr_], N)
                proj_fm(sl, 128, 128, pk_, [kk_], N)
                proj_fm(sl, 256, 128, pvv, [kv_], N)
                return
            tshift(c, pr, [kr_], 128, N, sample, last, xs_r[:, 0:N], [xr_k])
            tshift(8 + c, pk_, [kk_], 128, N, sample, last, xs_k[:, 0:N], [xk_k])
            tshift(16 + c, pvv, [kv_], 128, N, sample, last, vT[:, c, 0:N], ["vT%d" % c])

        def prep_H(c, ntok, sample, last):
            N = ntok
            xs_r, xs_k = xs_rb[c % 2], xs_kb[c % 2]
            xr_k, xk_k = "xs_r%d" % (c % 2), "xs_k%d" % (c % 2)
            cs = slice(c * 128, (c + 1) * 128)
            pc = lambda o: pv[:, o + c:o + c + 1]
            S.op("pe", lambda: PE.matmul(bank(6)[:, 0:N], w2b[0:96, cs], txw[0:96, 0:N], start=True, stop=True),
                 r=["w2b", "txw"], w=["pb6"])
            S.op("pe", lambda: PE.matmul(bank(7)[:, 0:N], a2b[0:96, cs], xaT[0:96, 0:N], start=True, stop=True),
                 r=["a2b", "xaT"], w=["pb7"])
            S.op("dve", lambda: V.tensor_scalar(t_kkr[:, 0:N], xs_k[:, 0:N], pc(PV_KK), None, ALU.mult), r=[xk_k, "pv"], w=["t_kkr"])
            S.op("act", lambda: A.activation(t_sg[:, 0:N], bank(6)[:, 0:N], AF.Sigmoid, bias=pc(PV_W0)), r=["pb6", "pv"], w=["t_sg"])
            S.op("act", lambda: A.activation(t_a[:, 0:N], bank(7)[:, 0:N], AF.Sigmoid, bias=pc(PV_A0)), r=["pb7", "pv"], w=["t_a"])
            S.op("act", lambda: A.activation(t_sq[:, 0:N], t_kkr[:, 0:N], AF.Square), r=["t_kkr"], w=["t_sq"])
            for k2 in range(2):
                S.op("pe", lambda: PE.matmul(bank(6)[:, 0:N], g2b[:, k2, cs], sxg[:, k2, 0:N], start=(k2 == 0), stop=(k2 == 1)),
                     r=["g2b", "sxg"], w=["pb6"])
            S.op("pe", lambda: PE.matmul(bank(7)[:, 0:N], BDf, t_sq[:, 0:N], start=True, stop=True), r=["cst", "t_sq"], w=["pb7"])
            S.op("dve", lambda: V.tensor_scalar(t_sg[:, 0:N], t_sg[:, 0:N], -DEC, None, ALU.mult), r=["t_sg"], w=["t_sg"])
            if sample:
                S.op("act", lambda: A.activation(dS[:, c, :], t_sg[:, 0:N], AF.Exp), r=["t_sg"], w=["dS"])
            else:
                S.op("dve", lambda: V.tensor_tensor_scan(t_L[:, 0:N], rst[:, 0:N], t_sg[:, 0:N], 0.0, ALU.mult, ALU.add),
                     r=["t_sg", "cst"], w=["t_L"])
                S.op("act", lambda: A.activation(t_eL[:, 0:N], t_L[:, 0:N], AF.Exp), r=["t_L"], w=["t_eL"])
                S.op("act", lambda: A.activation(t_enL[:, 0:N], t_L[:, 0:N], AF.Exp, scale=-1.0), r=["t_L"], w=["t_enL"])
                eLv = t_eL[:, 0:N].rearrange("p (j t) -> p j t", t=64)
                eLmv = t_eLm[:, 0:N].rearrange("p (j t) -> p j t", t=64)
                S.op("dve", lambda: V.memset(eLmv[:, :, 0:1], 1.0), w=["t_eLm"])
            S.op("dve", lambda: V.tensor_scalar(t_sq[:, 0:N], bank(7)[:, 0:N], 1e-24, None, ALU.max), r=["pb7"], w=["t_sq"])
            S.op("act", lambda: A.activation(t_sq[:, 0:N], t_sq[:, 0:N], AF.Sqrt), r=["t_sq"], w=["t_sq"])
            S.op("act", lambda: A.activation(gT[:, c, 0:N], bank(6)[:, 0:N], AF.Copy), r=["pb6"], w=["gT%d" % c])
            if not sample:
                S.op("dve", lambda: V.tensor_copy(eLmv[:, :, 1:64], eLv[:, :, 0:63]), r=["t_eL"], w=["t_eLm"])
                S.op("act", lambda: A.activation(gC[:, c, :], eLv[:, :, 63], AF.Copy), r=["t_eL"], w=["gC"])
                S.op("dve", lambda: V.tensor_tensor(Rt[:, c, 0:N], xs_r[:, 0:N], t_eL[:, 0:N], ALU.mult), r=[xr_k, "t_eL"], w=["Rt%d" % c])
            else:
                S.op("dve", lambda: V.tensor_copy(Rt[:, c, 0:N], xs_r[:, 0:N]), r=[xr_k], w=["Rt%d" % c])
            S.op("dve", lambda: V.reciprocal(t_sq[:, 0:N], t_sq[:, 0:N]), r=["t_sq"], w=["t_sq"])
            S.op("dve", lambda: V.tensor_tensor(t_kkr[:, 0:N], t_kkr[:, 0:N], t_sq[:, 0:N], ALU.mult), r=["t_kkr", "t_sq"], w=["t_kkr"])
            S.op("dve", lambda: V.tensor_tensor(t_b[:, 0:N], t_kkr[:, 0:N], t_a[:, 0:N], ALU.mult), r=["t_kkr", "t_a"], w=["t_b"])
            S.op("dve", lambda: V.tensor_scalar(t_a[:, 0:N], t_a[:, 0:N], pc(PV_KA), omka[:, c:c + 1], ALU.mult, ALU.add),
                 r=["t_a", "pv", "omka"], w=["t_a"])
            S.op("dve", lambda: V.tensor_tensor(t_kh[:, 0:N], xs_k[:, 0:N], t_a[:, 0:N], ALU.mult), r=[xk_k, "t_a"], w=["t_kh"])
            if sample:
                S.op("dve", lambda: V.tensor_scalar(At[:, c, 0:N], t_kkr[:, 0:N], -1.0, None, ALU.mult), r=["t_kkr"], w=["At%d" % c])
                S.op("dve", lambda: V.tensor_copy(Bt[:, c, 0:N], t_b[:, 0:N]), r=["t_b"], w=["Bt%d" % c])
                S.op("dve", lambda: V.tensor_copy(Kt[:, c, 0:N], t_kh[:, 0:N]), r=["t_kh"], w=["Kt%d" % c])
            else:
                S.op("dve", lambda: V.scalar_tensor_tensor(At[:, c, 0:N], t_kkr[:, 0:N], -1.0, t_eLm[:, 0:N], ALU.mult, ALU.mult),
                     r=["t_kkr", "t_eLm"], w=["At%d" % c])
                S.op("dve", lambda: V.tensor_tensor(Bt[:, c, 0:N], t_b[:, 0:N], t_enL[:, 0:N], ALU.mult), r=["t_b", "t_enL"], w=["Bt%d" % c])
                S.op("dve", lambda: V.tensor_tensor(Kt[:, c, 0:N], t_kh[:, 0:N], t_enL[:, 0:N], ALU.mult), r=["t_kh", "t_enL"], w=["Kt%d" % c])

        def prep_tail(c, ntok):
            N = ntok
            xs_r = xs_rb[c % 2]; xr_k = "xs_r%d" % (c % 2)
            pc = lambda o: pv[:, o + c:o + c + 1]
            S.op("dve", lambda: V.scalar_tensor_tensor(t_rkr[:, 0:N], xs_r[:, 0:N], pc(PV_RK), t_kh[:, 0:N], ALU.mult, ALU.mult),
                 r=[xr_k, "t_kh", "pv"], w=["t_rkr"])
            S.op("pe", lambda: PE.matmul(bank(6)[:, 0:N], BDb, t_rkr[:, 0:N], start=True, stop=True), r=["cstb", "t_rkr"], w=["pb6"])
            S.op("dve", lambda: V.tensor_tensor(ybT[:, c, 0:N], bank(6)[:, 0:N], vT[:, c, 0:N], ALU.mult), r=["pb6", "vT%d" % c], w=["ybT%d" % c])

        def epilogue(col0, n, srcs=None):
            for hf in range(2):
                if srcs is None:
                    for c4 in range(4):
                        c = hf * 4 + c4
                        S.op("pe", lambda: PE.transpose(bank(4)[:, c4 * 128:(c4 + 1) * 128], cYs[:, c * 128:(c + 1) * 128], ident),
                             r=["cYs_0", "cYs_1", "cst"], w=["pb4"])
                    src_ap, sk = bank(4)[:, 0:4 * n], "pb4"
                else:
                    src_ap, sk = srcs[hf]
                src = src_ap.rearrange("p (c t) -> p c t", c=4)
                yv = yT_sb[:, :, 0:n]; sq = e_sq[:, :, 0:n]; em = e_m[:, :, 0:n]; ev = e_v[:, :, 0:n]
                S.op("act", lambda: A.activation(yv, src, AF.Copy), r=[sk], w=["yT_sb"])
                S.op("act", lambda: A.activation(sq, src, AF.Square), r=[sk], w=["e_sq"])
                for c4 in range(4):
                    S.op("pe", lambda: PE.matmul(bank(5)[:, c4 * n:(c4 + 1) * n], BDf, yT_sb[:, c4, 0:n], start=True, stop=True),
                         r=["cst", "yT_sb"], w=["pb5"])
                m_ps = bank(5)[:, 0:4 * n].rearrange("p (c t) -> p c t", c=4)
                S.op("act", lambda: A.activation(em, m_ps, AF.Copy, scale=1.0 / 64), r=["pb5"], w=["e_m"])
                yield
                for c4 in range(4):
                    S.op("pe", lambda: PE.matmul(bank(5)[:, c4 * n:(c4 + 1) * n], BDf, e_sq[:, c4, 0:n], start=True, stop=True),
                         r=["cst", "e_sq"], w=["pb5"])
                S.op("dve", lambda: V.tensor_tensor(ev, em, em, ALU.mult), r=["e_m"], w=["e_v"])
                S.op("dve", lambda: V.scalar_tensor_tensor(ev, m_ps, 1.0 / 64, ev, ALU.mult, ALU.subtract), r=["pb5", "e_v"], w=["e_v"])
                S.op("act", lambda: A.activation(ev, ev, AF.Sqrt, bias=GN_EPS), r=["e_v"], w=["e_v"])
                S.op("dve", lambda: V.reciprocal(ev, ev), r=["e_v"], w=["e_v"])
                S.op("dve", lambda: V.tensor_tensor(yv, yv, em, ALU.subtract), r=["yT_sb", "e_m"], w=["yT_sb"])
                S.op("dve", lambda: V.tensor_tensor(yv, yv, ev, ALU.mult), r=["yT_sb", "e_v"], w=["yT_sb"])
                yield
                for c4 in range(4):
                    c = hf * 4 + c4
                    S.op("dve", lambda: V.tensor_scalar(yT_sb[:, c4, 0:n], yT_sb[:, c4, 0:n], pv[:, PV_LNW + c:PV_LNW + c + 1],
                                                        pv[:, PV_LNB + c:PV_LNB + c + 1], ALU.mult, ALU.add), r=["yT_sb", "pv"], w=["yT_sb"])
                    S.op("dve", lambda: V.tensor_tensor(yT_sb[:, c4, 0:n], yT_sb[:, c4, 0:n], ybT[:, c, col0:col0 + n], ALU.add),
                         r=["yT_sb", "ybT%d" % c], w=["yT_sb"])
                    S.op("dve", lambda: V.tensor_tensor(hT[:, c, col0:col0 + n], yT_sb[:, c4, 0:n], gT[:, c, col0:col0 + n], ALU.mult),
                         r=["yT_sb", "gT%d" % c], w=["hT%d" % c])
                yield

        def chunkA(j):
            hp = (j % 2) * 64; pq = "_%d" % (j % 2)
            tok = slice(j * 64, (j + 1) * 64)
            rows = slice(hp, hp + 64)
            A2 = PF[:, 0:1024]; B2 = PF[:, 1024:2048]
            ka, kb_ = ["pb0", "pb1"], ["pb2", "pb3"]
            def hmm(dst2, lh, rh, lh_n, rh_n, keys):
                for h in range(16):
                    c, hh = h // 2, h % 2
                    kr = slice(hh * 64, hh * 64 + 64)
                    S.op("pe", lambda: PE.matmul(dst2[rows, hcol(h):hcol(h) + 64], lh[kr, c, tok], rh[kr, c, tok], start=True, stop=True),
                         r=["%s%d" % (lh_n, c), "%s%d" % (rh_n, c)], w=keys)
            m3 = lambda m: m[rows, :].unsqueeze(1).broadcast_to([64, 16, 64])
            v3 = lambda t: t[rows, :].rearrange("p (h t) -> p h t", h=16)
            hmm(A2, Bt, At, "Bt", "At", ka)
            hmm(B2, At, Bt, "At", "Bt", kb_)
            S.op("dve", lambda: V.tensor_tensor(v3(cY[0]), v3(A2), m3(mS), ALU.mult), r=ka + ["cst"], w=["cY0" + pq])
            S.op("dve", lambda: V.tensor_tensor(v3(cYT[0]), v3(B2), m3(mST), ALU.mult), r=kb_ + ["cst"], w=["cYT0" + pq])
            yield
            hmm(A2, Kt, At, "Kt", "At", ka)
            S.op("dve", lambda: V.tensor_tensor(v3(cNak), v3(A2), m3(mS), ALU.mult), r=ka + ["cst"], w=["cNak" + pq])
            hmm(B2, Bt, Rt, "Bt", "Rt", kb_)
            S.op("dve", lambda: V.tensor_tensor(v3(cMrb), v3(B2), m3(mI), ALU.mult), r=kb_ + ["cst"], w=["cMrb" + pq])
            yield
            hmm(A2, Kt, Rt, "Kt", "Rt", ka)
            S.op("dve", lambda: V.tensor_tensor(v3(cMrk), v3(A2), m3(mI), ALU.mult), r=ka + ["cst"], w=["cMrk" + pq])
            idb3 = identb[rows, hp:hp + 64].unsqueeze(1).broadcast_to([64, 16, 64])
            S.op("dve", lambda: V.tensor_tensor(v3(cP[0]), v3(cY[0]), idb3, ALU.add), r=["cY0" + pq, "cstb"], w=["cP0" + pq])
            yield
            cur = 0
            for lvl in range(5):
                nxt = cur ^ 1
                Yc, YTc, Pc = cY[cur], cYT[cur], cP[cur]
                Yn, YTn, Pn = cY[nxt], cYT[nxt], cP[nxt]
                kY, kYT, kP = "cY%d" % cur + pq, "cYT%d" % cur + pq, "cP%d" % cur + pq
                for h in range(16):
                    hc = slice(hcol(h), hcol(h) + 64)
                    S.op("pe", lambda: PE.matmul(B2[rows, hc], Yc[rows, hc], YTc[rows, hc], start=True, stop=True), r=[kY, kYT], w=kb_)
                if lvl < 4:
                    for h in range(16):
                        hc = slice(hcol(h), hcol(h) + 64)
                        S.op("pe", lambda: PE.matmul(A2[rows, hc], YTc[rows, hc], Yc[rows, hc], start=True, stop=True), r=[kY, kYT], w=ka)
                S.op("act", lambda: A.activation(YTn[rows, :], B2[rows, :], AF.Copy), r=kb_, w=["cYT%d" % nxt + pq])
                if lvl < 4:
                    S.op("dve", lambda: V.tensor_copy(Yn[rows, :], A2[rows, :]), r=ka, w=["cY%d" % nxt + pq])
                yield
                for h in range(16):
                    hc = slice(hcol(h), hcol(h) + 64)
                    S.op("pe", lambda: PE.matmul(B2[rows, hc], YTn[rows, hc], Pc[rows, hc], start=True, stop=True), r=["cYT%d" % nxt + pq, kP], w=kb_)
                S.op("dve", lambda: V.tensor_tensor(Pn[rows, :], B2[rows, :], Pc[rows, :], ALU.add), r=kb_ + [kP], w=["cP%d" % nxt + pq])
                yield
                cur = nxt
            assert cur == 1

        def chunkB(j):
            hp = (j % 2) * 64; pq = "_%d" % (j % 2)
            s_, tok = j // 2, slice(j * 64, (j + 1) * 64)
            rows = slice(hp, hp + 64)
            TTt = cP[1]; kTT = "cP1" + pq
            C2 = PF[:, 2048:3072]; kc_ = ["pb4", "pb5"]
            for h in range(16):
                S.op("pe", lambda: PE.matmul(C2[rows, h * 64:(h + 1) * 64], cNak[rows, hcol(h):hcol(h) + 64], tmV[rows, s_, h * 64:(h + 1) * 64],
                                             start=True, stop=True), r=["cNak" + pq, "tmV"], w=kc_)
            S.op("act", lambda: A.activation(cT1[rows, :], C2[rows, :], AF.Copy), r=kc_, w=["cT1" + pq])
            yield
            for c in range(8):
                S.op("pe", lambda: PE.matmul(C2[rows, c * 128:(c + 1) * 128], At[:, c, tok], STb[:, c, :], start=True, stop=True),
                     r=["At%d" % c, "STb"], w=kc_)
            S.op("dve", lambda: V.tensor_tensor(cW[rows, :], C2[rows, :], cT1[rows, :], ALU.add), r=kc_ + ["cT1" + pq], w=["cW" + pq])
            yield
            for h in range(16):
                S.op("pe", lambda: PE.matmul(C2[rows, h * 64:(h + 1) * 64], TTt[rows, hcol(h):hcol(h) + 64], cW[rows, h * 64:(h + 1) * 64],
                                             start=True, stop=True), r=[kTT, "cW" + pq], w=kc_)
            S.op("act", lambda: A.activation(cU[rows, :], C2[rows, :], AF.Copy), r=kc_, w=["cU" + pq])
            yield
            for h in range(16):
                hs = slice(h * 64, (h + 1) * 64); hc = slice(hcol(h), hcol(h) + 64)
                S.op("pe", lambda: PE.matmul(C2[rows, hs], cMrb[rows, hc], cU[rows, hs], start=True, stop=False), r=["cMrb" + pq, "cU" + pq], w=kc_)
                S.op("pe", lambda: PE.matmul(C2[rows, hs], cMrk[rows, hc], tmV[rows, s_, hs], start=False, stop=True), r=["cMrk" + pq, "tmV"], w=kc_)
            S.op("act", lambda: A.activation(cT1[rows, :], C2[rows, :], AF.Copy), r=kc_, w=["cT1" + pq])
            yield
            for c in range(8):
                S.op("pe", lambda: PE.matmul(C2[rows, c * 128:(c + 1) * 128], Rt[:, c, tok], STb[:, c, :], start=True, stop=True),
                     r=["Rt%d" % c, "STb"], w=kc_)
            S.op("dve", lambda: V.tensor_tensor(cYs[rows, :], C2[rows, :], cT1[rows, :], ALU.add), r=kc_ + ["cT1" + pq], w=["cYs" + pq])
            yield
            SN = PF[:, 2048:2560]
            for h in range(16):
                c, hh = h // 2, h % 2
                hs = slice(h * 64, (h + 1) * 64)
                o = SN[hh * 64:hh * 64 + 64, c * 64:(c + 1) * 64]
                S.op("pe", lambda: PE.matmul(o, tmK[rows, s_, hs], tmV[rows, s_, hs], start=True, stop=False), r=["tmK", "tmV"], w=["pb4"])
                S.op("pe", lambda: PE.matmul(o, tmB[rows, s_, hs], cU[rows, hs], start=False, stop=True), r=["tmB", "cU" + pq], w=["pb4"])
            SN3 = SN.rearrange("p (c v) -> p c v", c=8)
            S.op("dve", lambda: V.tensor_tensor(ST[:], ST[:], SN3, ALU.add), r=["ST", "pb4"], w=["ST"])
            S.op("dve", lambda: V.tensor_tensor(ST[:], ST[:], gC[:, :, j:j + 1].broadcast_to([128, 8, 64]), ALU.mult), r=["ST", "gC"], w=["ST"])
            S.op("act", lambda: A.activation(STb[0:64, :, 0:64], ST[0:64, :, :], AF.Copy), r=["ST"], w=["STb"])
            S.op("act", lambda: A.activation(STb[64:128, :, 64:128], ST[64:128, :, :], AF.Copy), r=["ST"], w=["STb"])
            yield
            if j % 2 == 1:
                yield from epilogue((j // 2) * 128, 128)

        def att_block(p, n_):
            gblk = p * 2 + n_
            qs = slice(n_ * 128, (n_ + 1) * 128)
            sb_, sbk = bank(6), "pb6"
            ob, obk = bank(7), "pb7"
            for kv in range(4):
                half = (kv % 2) * 64; hr = slice(half, half + 64); kc2 = kv // 2
                kbs = [1] if gblk == 0 else [0, 1]
                for kb in kbs:
                    kcols = slice(n_ * 128 + kb * 128, n_ * 128 + kb * 128 + 128)
                    S.op("pe", lambda: PE.matmul(sb_[:, 0:512], kT[hr, kc2, kcols], qT[hr, (kv // 2) * 4:(kv // 2) * 4 + 4, qs],
                                                 start=True, stop=True), r=["kT", "qT"], w=[sbk])
                    for g in range(4):
                        h = 4 * kv + g
                        S.op("dve", lambda: V.scalar_tensor_tensor(a_sc[:, g * 128:(g + 1) * 128], Dm[:, kb * 128:(kb + 1) * 128],
                                                                   SLOPES[h] / SCALE, sb_[:, g * 128:(g + 1) * 128], ALU.mult, ALU.add),
                             r=[sbk, "cst"], w=["t_f1"])
                    S.op("act", lambda: A.activation(a_PT[kb][:, :], a_sc[:, :], AF.Exp, scale=SCALE), r=["t_f1"], w=["a_PT%d" % kb])
                    yield
                for g in range(4):
                    for ki, kb in enumerate(kbs):
                        S.op("pe", lambda: PE.matmul(ob[:, g * 66:(g + 1) * 66], a_PT[kb][:, g * 128:(g + 1) * 128], Vall[:, n_ + kb, kv, :],
                                                     start=(ki == 0), stop=(ki == len(kbs) - 1)),
                             r=["a_PT%d" % kb, "Vall%d" % (n_ + kb)], w=[obk])
                o3 = ob[:, 0:264].rearrange("p (g d) -> p g d", g=4)
                S.op("dve", lambda: V.tensor_tensor(a_den[:, :], o3[:, :, 64], esink[:, 4 * kv:4 * kv + 4], ALU.add), r=[obk, "esink"], w=["a_den"])
                S.op("dve", lambda: V.reciprocal(a_den[:, :], a_den[:, :]), r=["a_den"], w=["a_den"])
                S.op("dve", lambda: V.tensor_tensor(hb[:, n_, 1024 + kv * 256:1024 + (kv + 1) * 256].rearrange("p (g d) -> p g d", g=4),
                                                    o3[:, :, 0:64], a_den[:, :].unsqueeze(2).broadcast_to([128, 4, 64]), ALU.mult),
                     r=[obk, "a_den"], w=["hb%d" % n_])
                yield
            pt = PTb[:, 0:1024]; pk = "pb6"
            for c in range(8):
                S.op("pe", lambda: PE.transpose(pt[:, c * 128:(c + 1) * 128], hb[:, n_, 1024 + c * 128:1024 + (c + 1) * 128], identb),
                     r=["hb%d" % n_, "cstb"], w=[pk])
            for c in range(8):
                S.op("dve" if c % 2 else "act", (lambda: V.tensor_copy(hT[:, 8 + c, qs], pt[:, c * 128:(c + 1) * 128])) if c % 2 else
                     (lambda: A.activation(hT[:, 8 + c, qs], pt[:, c * 128:(c + 1) * 128], AF.Copy)), r=[pk], w=["hT%d" % (8 + c)])
            yield

        def interleave(gens):
            gens = list(gens)
            while gens:
                for g in list(gens):
                    try:
                        next(g)
                    except StopIteration:
                        gens.remove(g)

        def sample_mix():
            ONES = cst[:, C_ON:C_ON + 128]
            allk = lambda n: ["%s%d" % (n, c) for c in range(8)]
            wv = wkv_s.rearrange("b (c hh) k v -> b hh k c v", hh=2)
            sv = swkv.rearrange("b (c hh) k v -> b hh k c v", hh=2)
            bc8 = lambda ap, w_: ap.broadcast_to([128, 8, w_])
            c8 = lambda ap, w_: ap.rearrange("p (c v) -> p c v", c=8)
            W32 = chb.bitcast(F32)
            def mkset(b_sS, b_sSn, b_sBlk, b_sAbd, b_sRv, b_sT1):
                bfv = lambda base, n: base.bitcast(BF16)[:, 0:n]
                return dict(sS=c8(b_sS, 64), sSn=c8(b_sSn, 64), sT1=c8(b_sT1, 64),
                            sBlk=c8(bfv(b_sBlk, 1024), 128), sAbd=c8(bfv(b_sAbd, 1024), 128),
                            sSb=c8(b_sAbd[:, 512:768].bitcast(BF16), 64), sRv=c8(bfv(b_sRv, 512), 64))
            setA = mkset(x1[:, 0:512], x1[:, 512:1024], x1[:, 1024:2048], chf[:, 0:1024], chf[:, 1024:1536], chf[:, 1536:2048])
            setB = mkset(W32[:, 1792:2304], W32[:, 2304:2816], scr[:, 2048:3072], W32[:, 2816:3840], W32[:, 3840:4352], W32[:, 4352:4864])
            sets = [setA, setB]
            sKVb = [W32[:, 0:512], W32[:, 1152:1664]]
            sKTb = [W32[:, 4864:4992].bitcast(BF16).rearrange("p (k n) -> p k n", k=2), W32[:, 1664:1792].bitcast(BF16).rearrange("p (k n) -> p k n", k=2)]
            for i_ in range(2):
                S.op("dve", lambda: V.memset(sets[i_]["sBlk"], 0.0), w=["sBlk%d" % i_])
            for b in range(16):
                S.dma("sp", k_s[b, 0:127, :], ck[b, 1:128, :], w=["o_ks"])
                S.dma("sp", v_s[b, 0:127, :], cv[b, 1:128, :], w=["o_vs"])
            S.dma("sp", k_s[:, 127, :], ktm[0:16, 0:256], r=["ktm"], w=["o_ks"])
            S.dma("sp", v_s[:, 127, :], ktm[0:16, 256:512], r=["ktm"], w=["o_vs"])

            ckp(201)
            def rw_sample(b):
                i = b % 2; T_ = sets[i]; x = "%d" % i
                uB, vB = (4, 5) if i == 0 else (6, 7)
                for hh in range(2):
                    S.dma("sp", T_["sS"][hh * 64:(hh + 1) * 64, :, :], sv[b, hh], w=["sS" + x])
                S.op("dve", lambda: V.tensor_tensor(T_["sAbd"], BDf.unsqueeze(1).broadcast_to([128, 8, 128]), bc8(At[:, :, b:b + 1], 128), ALU.mult),
                     r=["cst"] + allk("At"), w=["sAbd" + x])
                S.op("dve", lambda: V.tensor_tensor(T_["sRv"], Ihalf.unsqueeze(1).broadcast_to([128, 8, 64]), bc8(vT[:, :, b:b + 1], 64), ALU.mult),
                     r=["cst"] + allk("vT"), w=["sRv" + x])
                yield
                S.op("act", lambda: A.activation(T_["sSb"], T_["sS"], AF.Copy), r=["sS" + x], w=["sSb" + x])
                for c in range(8):
                    S.op("pe", lambda: PE.matmul(bank(uB)[:, c * 64:(c + 1) * 64], T_["sAbd"][:, c, :], T_["sSb"][:, c, :], start=True, stop=True),
                         r=["sAbd" + x, "sSb" + x], w=["pb%d" % uB])
                for c in range(8):
                    S.op("pe", lambda: PE.matmul(bank(vB)[:, c * 64:(c + 1) * 64], BDb, T_["sRv"][:, c, :], start=True, stop=True),
                         r=["cstb", "sRv" + x], w=["pb%d" % vB])
                U3 = c8(bank(uB), 64); V3 = c8(bank(vB), 64)
                S.op("dve", lambda: V.tensor_tensor(T_["sSn"], T_["sS"], bc8(dS[:, :, b:b + 1], 64), ALU.mult), r=["sS" + x, "dS"], w=["sSn" + x])
                yield
                S.op("dve", lambda: V.tensor_tensor(T_["sT1"], U3, bc8(Bt[:, :, b:b + 1], 64), ALU.mult), r=["pb%d" % uB] + allk("Bt"), w=["sT1" + x])
                S.op("dve", lambda: V.tensor_tensor(T_["sSn"], T_["sSn"], T_["sT1"], ALU.add), r=["sSn" + x, "sT1" + x], w=["sSn" + x])
                S.op("dve", lambda: V.tensor_tensor(T_["sT1"], V3, bc8(Kt[:, :, b:b + 1], 64), ALU.mult), r=["pb%d" % vB] + allk("Kt"), w=["sT1" + x])
                S.op("dve", lambda: V.tensor_tensor(T_["sSn"], T_["sSn"], T_["sT1"], ALU.add), r=["sSn" + x, "sT1" + x], w=["sSn" + x])
                yield
                for hh in range(2):
                    S.dma("sp", wv[b, hh], T_["sSn"][hh * 64:(hh + 1) * 64, :, :], r=["sSn" + x], w=["o_wkvs"])
                S.op("act", lambda: A.activation(T_["sBlk"][0:64, :, 0:64], T_["sSn"][0:64, :, :], AF.Copy), r=["sSn" + x], w=["sBlk" + x])
                S.op("act", lambda: A.activation(T_["sBlk"][64:128, :, 64:128], T_["sSn"][64:128, :, :], AF.Copy), r=["sSn" + x], w=["sBlk" + x])
                for c in range(8):
                    S.op("pe", lambda: PE.matmul(bank(c // 4)[:, (c % 4) * 16 + b:(c % 4) * 16 + b + 1], T_["sBlk"][:, c, :], Rt[:, c, b:b + 1],
                                                 start=True, stop=True), r=["sBlk" + x, "Rt%d" % c], w=["pb%d" % (c // 4)])
                yield

            def at_sample(b):
                i = b % 2; x = "%d" % i
                kvb = sKVb[i]; Kn = kvb[:, 0:256]; Vn = kvb[:, 256:512]; KT = sKTb[i]
                tB, sB = (4, 5) if i == 0 else (6, 7)
                S.dma("sp", Kn[0:127, :], ck[b, 1:128, :], w=["sKa" + x])
                S.dma("sp", Vn[0:127, :], cv[b, 1:128, :], w=["sVa" + x])
                S.dma("sp", Kn[127:128, :], ktm[b:b + 1, 0:256], r=["ktm"], w=["sKb" + x])
                S.dma("sp", Vn[127:128, :], ktm[b:b + 1, 256:512], r=["ktm"], w=["sVb" + x])
                yield
                for k2 in range(2):
                    S.op("pe", lambda: PE.transpose(bank(tB)[:, k2 * 128:(k2 + 1) * 128], Kn[:, k2 * 128:(k2 + 1) * 128], ident),
                         r=["sKa" + x, "sKb" + x, "cst"], w=["pb%d" % tB])
                S.op("act", lambda: A.activation(KT[:, :, :], bank(tB)[:, 0:256].rearrange("p (k n) -> p k n", k=2), AF.Copy), r=["pb%d" % tB], w=["sKT" + x])
                yield
                for kv in range(4):
                    hr = slice((kv % 2) * 64, (kv % 2) * 64 + 64)
                    ob_ = sB if kv % 2 == 0 else tB
                    S.op("pe", lambda: PE.matmul(bank(ob_)[:, 256 + kv * 4:256 + kv * 4 + 4], KT[hr, kv // 2, :], qT[hr, (kv // 2) * 4:(kv // 2) * 4 + 4, b],
                                                 start=True, stop=True), r=["sKT" + x, "qT"], w=["pb%d" % ob_])
                ev_ = lambda ap, o: ap.rearrange("p (k two g) -> p k two g", k=2, two=2)[:, :, o, :]
                for o, ob_ in ((0, sB), (1, tB)):
                    S.op("dve", lambda: V.tensor_tensor(ev_(sPT[:, b, :], o), ev_(bank(ob_)[:, 256:272], o), ev_(sbias, o), ALU.add),
                         r=["pb%d" % ob_, "cst"], w=["sPT" + x])
                S.op("act", lambda: A.activation(sPT[:, b, :], sPT[:, b, :], AF.Exp, scale=SCALE), r=["sPT" + x], w=["sPT" + x])
                yield
                S.op("pe", lambda: PE.matmul(bank(sB)[:, 288:304], ONES, sPT[:, b, :], start=True, stop=True), r=["cst", "sPT" + x], w=["pb%d" % sB])
                S.op("dve", lambda: V.tensor_tensor(sPTn[:, b, :], bank(sB)[:, 288:304], esink[:, :], ALU.add), r=["pb%d" % sB, "esink"], w=["sPTn" + x])
                S.op("dve", lambda: V.reciprocal(sPTn[:, b, :], sPTn[:, b, :]), r=["sPTn" + x], w=["sPTn" + x])
                S.op("dve", lambda: V.tensor_tensor(sPTn[:, b, :], sPTn[:, b, :], sPT[:, b, :], ALU.mult), r=["sPTn" + x, "sPT" + x], w=["sPTn" + x])
                yield
                for h in range(16):
                    kv = h // 4
                    S.op("pe", lambda: PE.matmul(bank(3)[(h % 2) * 64:(h % 2) * 64 + 64, (h // 2) * 16 + b:(h // 2) * 16 + b + 1],
                                                 Vn[:, kv * 64:(kv + 1) * 64], sPTn[:, b, h:h + 1], start=True, stop=True),
                         r=["sVa" + x, "sVb" + x, "sPTn" + x], w=["pb3"])
                yield

            def pairs(fn):
                import os
                for b in range(0, 16, 2):
                    if os.environ.get("DBG_SEQ"):
                        interleave([fn(b)]); interleave([fn(b + 1)])
                    else:
                        interleave([fn(b), fn(b + 1)])
            pairs(rw_sample)
            ckp(202)
            for _ in epilogue(0, 16, [(bank(0)[:, 0:64], "pb0"), (bank(1)[:, 0:64], "pb1")]):
                pass
            for ci in range(28):
                M = RW_M[ci]
                S.dma("sp", sh_s[RW_COL[ci]:RW_COL[ci] + M, :], zs[0:M, ci, :], r=["zs"], w=["o_shs"])
            ckp(203)
            pairs(at_sample)
            ckp(204)
            for c in range(8):
                S.op("act", lambda: A.activation(hT[:, 8 + c, 0:16], bank(3)[:, c * 16:(c + 1) * 16], AF.Copy), r=["pb3"], w=["hT%d" % (8 + c)])

        def tm_proj(sl, subs, ncols, ps_of, handler, ntok):
            for s, (t0, n) in enumerate(subs):
                ps, pkeys = ps_of(s)
                for kc in range(KC):
                    S.op("pe", lambda: PE.matmul(ps[0:n, 0:ncols], hT[:, kc, t0:t0 + n], wb[sl][:, kc, 0:ncols],
                                                 start=(kc == 0), stop=(kc == KC - 1)), r=["hT%d" % kc, "wb%d" % sl], w=pkeys)
                handler(s, t0, n, ps, pkeys)

        def resid_add(W, subs, nk_list=None):
            for cb in range(4):
                sl = wload(W, [(cb * 512, 512, 0)])
                def h(s, t0, n, ps, pkeys):
                    S.op("dve", lambda: V.tensor_tensor(x_sb[0:n, s, cb * 512:(cb + 1) * 512], x_sb[0:n, s, cb * 512:(cb + 1) * 512],
                                                        ps[0:n, 0:512], ALU.add), r=pkeys + ["x%d" % s], w=["x%d" % s])
                tm_proj(sl, subs, 512, lambda s: (bank(4 + (s % 2)), ["pb%d" % (4 + (s % 2))]), h, None)

        try:
            for p in range(NPASS + 1):
                sample = (p == NPASS)
                last = (p == NPASS - 1)
                if sample:
                    subs = [(0, 16)]; ntok = 16
                else:
                    subs = [(0, 128), (128, 128)]; ntok = TP
                tok0 = p * TP
                for s, (t0, n) in enumerate(subs):
                    src = xs_d[0:16, :] if sample else xp[tok0 + t0:tok0 + t0 + n, :]
                    S.dma("sp", x_sb[0:n, s, :], src, w=["x%d" % s])
                if sample:
                    for ci in range(28):
                        M = RW_M[ci]
                        S.dma("sp", shT[0:M, ci, :], sshT[RW_COL[ci]:RW_COL[ci] + M, :], w=["shT"])
                    S.op("dve", lambda: V.memset(msh[:], 0.0), w=["msh"])
                    for ci in range(28):
                        M = RW_M[ci]
                        S.op("dve", lambda: V.tensor_scalar(msh[0:M, ci, :], shT[0:M, ci, :], pv[0:M, PV_MU + ci:PV_MU + ci + 1], None, ALU.mult),
                             r=["shT", "pv"], w=["msh"])
                ckp(1)
                norm_T(subs, PV_GMIX, ntok)
                ckp(2)
                sl = wload(w_in, [(RW_COL[24 + i], RW_M[24 + i], i * 128) for i in range(4)])
                for i in range(4):
                    ci = 24 + i; M = RW_M[ci]; b_ = 4 + (i % 2)
                    proj_fm(sl, i * 128, M, bank(b_), ["pb%d" % b_], ntok)
                    tshift(ci, bank(b_), ["pb%d" % b_], M, ntok, sample, last, xs_o[0:M, 0:ntok], ["xs_o"])
                    if i == 0:
                        S.op("act", lambda: A.activation(txw[0:96, 0:ntok], xs_o[0:96, 0:ntok], AF.Tanh), r=["xs_o"], w=["txw"])
                    elif i == 1:
                        S.op("act", lambda: A.activation(xaT[0:96, 0:ntok], xs_o[0:96, 0:ntok], AF.Copy), r=["xs_o"], w=["xaT"])
                    else:
                        S.op("act", lambda: A.activation(sxg[:, i - 2, 0:ntok], xs_o[:, 0:ntok], AF.Sigmoid), r=["xs_o"], w=["sxg"])
                ckp(3)
                def PP(c):
                    sl_ = wload(w_in, [(c * 128, 128, 0), (1024 + c * 128, 128, 128), (2048 + c * 128, 128, 256)])
                    prep_P(c, sl_, ntok, sample, last)
                def TT(c):
                    prep_P(c, None, ntok, sample, last)
                PP(0); TT(0); PP(1); TT(1)
                for c in range(8):
                    prep_H(c, ntok, sample, last)
                    if c + 2 < 8:
                        PP(c + 2)
                    prep_tail(c, ntok)
                    if c + 2 < 8:
                        TT(c + 2)
                ckp(5)
                for half in range(2):
                    sl = wload(w_in, [(RWP + QPERM[half * 8 + i] * 64, 64, i * 64) for i in range(8)])
                    for i in range(4):
                        b_ = 4 + (i % 2)
                        proj_fm(sl, i * 128, 128, bank(b_), ["pb%d" % b_], ntok)
                        evac(qT[:, half * 4 + i, 0:ntok], bank(b_)[:, 0:ntok], r=["pb%d" % b_], w=["qT"])
                sl = wload(w_in, [(RWP + 1024, 512, 0)])
                for i in range(2):
                    b_ = 4 + i
                    proj_fm(sl, i * 128, 128, bank(b_), ["pb%d" % b_], ntok)
                    evac(kT[:, i, 128:128 + ntok], bank(b_)[:, 0:ntok], r=["pb%d" % b_], w=["kT"])
                def kvh(s, t0, n, ps, pkeys):
                    if sample:
                        S.op("act", lambda: A.activation(ktm[0:n, :], ps[0:n, 0:512], AF.Copy), r=pkeys, w=["ktm"])
                        return
                    S.op("act", lambda: A.activation(Vall[0:n, 1 + s, :, 0:64], ps[0:n, 256:512].rearrange("p (k d) -> p k d", k=4), AF.Copy),
                         r=pkeys, w=["Vall%d" % (1 + s)])
                    if last and s == 1:
                        S.op("dve", lambda: V.tensor_copy(ktm[:, :], ps[:, 0:512]), r=pkeys, w=["ktm"])
                        S.dma("sp", k_p[:, :], ktm[:, 0:256], r=["ktm"], w=["o_kp"])
                        S.dma("sp", v_p[:, :], ktm[:, 256:512], r=["ktm"], w=["o_vp"])
                tm_proj(sl, subs, 512, lambda s: (bank(4 + (s % 2)), ["pb%d" % (4 + (s % 2))]), kvh, ntok)

                ckp(6)
                if not sample:
                    def side_all():
                        for arr, dst, an, dn in ((vT, tmV, "vT", "tmV"), (Kt, tmK, "Kt", "tmK"), (Bt, tmB, "Bt", "tmB")):
                            for s in range(2):
                                pt_rr[0] ^= 1
                                pt = PTb[:, pt_rr[0] * 1024:(pt_rr[0] + 1) * 1024]; pk = "pb%d" % (6 + pt_rr[0])
                                for c in range(8):
                                    S.op("pe", lambda: PE.transpose(pt[:, c * 128:(c + 1) * 128], arr[:, c, s * 128:(s + 1) * 128], identb),
                                         r=["%s%d" % (an, c), "cstb"], w=[pk])
                                evac(dst[:, s, :], pt[:, :], r=[pk], w=[dn])
                                yield
                        for n_ in range(2):
                            yield from att_block(p, n_)
                    att = side_all()
                    att_done = False
                    for j in range(-1, 4):
                        live = ([chunkB(j)] if j >= 0 else []) + ([chunkA(j + 1)] if j < 3 else [])
                        while live:
                            for g in list(live):
                                try:
                                    next(g)
                                except StopIteration:
                                    live.remove(g)
                            if not att_done:
                                try:
                                    next(att)
                                except StopIteration:
                                    att_done = True
                    if not att_done:
                        interleave([att])
                    if last:
                        S.dma("sp", wkv_p.rearrange("(c hh) k v -> hh k c v", hh=2)[0], ST[0:64, :, :], r=["ST"], w=["o_wkvp"])
                        S.dma("sp", wkv_p.rearrange("(c hh) k v -> hh k c v", hh=2)[1], ST[64:128, :, :], r=["ST"], w=["o_wkvp"])
                        for ci in range(28):
                            M = RW_M[ci]
                            S.dma("sp", sh_p[RW_COL[ci]:RW_COL[ci] + M, :], zlast[0:M, ci:ci + 1], r=["zlast"], w=["o_shp"])
                    S.op("act", lambda: A.activation(kT[:, :, 0:128], kT[:, :, TP:TP + 128], AF.Copy), r=["kT"], w=["kT"])
                    S.op("act", lambda: A.activation(Vall[:, 0, :, 0:64], Vall[:, 2, :, 0:64], AF.Copy), r=["Vall2"], w=["Vall0"])
                else:
                    sample_mix()
                ckp(10)
                resid_add(w_out, subs)
                ckp(11)
                norm_T(subs, PV_GFFN, ntok)
                for fb in range(FC // 2):
                    slg = wload(None, [(fb * 256, 256, 0, w_gate), (fb * 256, 256, 256, w_up)])
                    for i in range(2):
                        f = fb * 2 + i
                        proj_fm(slg, i * 128, 128, bank(0 + (f % 2)), ["pb%d" % (f % 2)], ntok)
                        proj_fm(slg, 256 + i * 128, 128, bank(2 + (f % 2)), ["pb%d" % (2 + (f % 2))], ntok)
                        S.op("act", lambda: A.activation(t_f1[:, 0:ntok], bank(f % 2)[:, 0:ntok], AF.Silu), r=["pb%d" % (f % 2)], w=["t_f1"])
                        S.op("dve", lambda: V.tensor_tensor(hid[:, f, 0:ntok], t_f1[:, 0:ntok], bank(2 + (f % 2))[:, 0:ntok], ALU.mult),
                             r=["t_f1", "pb%d" % (2 + (f % 2))], w=["hid"])
                for cb in range(4):
                    parts = [(0, 16), (16, 16), (32, 12)]
                    for pi, (k0, nk) in enumerate(parts):
                        sl = wload(w_down, [(cb * 512, 512, 0)], nk=nk, row0=k0 * 128)
                        for s, (t0, n) in enumerate(subs):
                            for kk_ in range(nk):
                                S.op("pe", lambda: PE.matmul(bank(4 + s)[0:n, 0:512], hid[:, k0 + kk_, t0:t0 + n], wb[sl][:, kk_, 0:512],
                                                             start=(pi == 0 and kk_ == 0), stop=(pi == 2 and kk_ == nk - 1)),
                                     r=["hid", "wb%d" % sl], w=["pb%d" % (4 + s)])
                    for s, (t0, n) in enumerate(subs):
                        S.op("dve", lambda: V.tensor_tensor(x_sb[0:n, s, cb * 512:(cb + 1) * 512], x_sb[0:n, s, cb * 512:(cb + 1) * 512],
                                                            bank(4 + s)[0:n, 0:512], ALU.add), r=["pb%d" % (4 + s), "x%d" % s], w=["x%d" % s])
                ckp(12)
                norm_T(subs, PV_GPLE, ntok)
                for s, (t0, n) in enumerate(subs):
                    src = ps_d[0:16, :] if sample else pp[tok0 + t0:tok0 + t0 + n, :]
                    S.dma("sp", pe_sb[0:n, s, :], src, w=["pe_sb"])
                    S.op("dve", lambda: V.tensor_copy(pe_b[0:n, s, :], pe_sb[0:n, s, :]), r=["pe_sb"], w=["pe_b"])
                    pt_rr[0] ^= 1
                    pt = PTb[:, pt_rr[0] * 1024:(pt_rr[0] + 1) * 1024]; pk = "pb%d" % (6 + pt_rr[0])
                    for k2 in range(2):
                        S.op("pe", lambda: PE.transpose(pt[:, k2 * 128:k2 * 128 + n], pe_b[0:n, s, k2 * 128:(k2 + 1) * 128], identb[0:n, 0:n]),
                             r=["pe_b", "cstb"], w=[pk])
                    for k2 in range(2):
                        evac(pT_[:, k2, t0:t0 + n], pt[:, k2 * 128:k2 * 128 + n], r=[pk], w=["pT_"])
                for cb in range(4):
                    sl = wload(ple_gate, [(cb * 512, 512, 0)])
                    S.dma("pool", wpp[:, :, :], ple_proj[:, cb * 512:(cb + 1) * 512].rearrange("(k p) n -> p k n", p=128), w=["wpp"])
                    for s, (t0, n) in enumerate(subs):
                        gb, pb_ = bank(4 + (s % 2)), bank(2 + (s % 2))
                        gk, pk2 = "pb%d" % (4 + (s % 2)), "pb%d" % (2 + (s % 2))
                        for kc in range(KC):
                            S.op("pe", lambda: PE.matmul(gb[0:n, 0:512], hT[:, kc, t0:t0 + n], wb[sl][:, kc, 0:512], start=(kc == 0), stop=(kc == KC - 1)),
                                 r=["hT%d" % kc, "wb%d" % sl], w=[gk])
                        for k2 in range(2):
                            S.op("pe", lambda: PE.matmul(pb_[0:n, 0:512], pT_[:, k2, t0:t0 + n], wpp[:, k2, :], start=(k2 == 0), stop=(k2 == 1)),
                                 r=["pT_", "wpp"], w=[pk2])
                        S.op("act", lambda: A.activation(t_f1[0:n, :], gb[0:n, 0:512], AF.Sigmoid), r=[gk], w=["t_f1"])
                        S.op("dve", lambda: V.tensor_tensor(t_f1[0:n, :], t_f1[0:n, :], pb_[0:n, 0:512], ALU.mult), r=["t_f1", pk2], w=["t_f1"])
                        S.op("dve", lambda: V.tensor_tensor(x_sb[0:n, s, cb * 512:(cb + 1) * 512], x_sb[0:n, s, cb * 512:(cb + 1) * 512], t_f1[0:n, :], ALU.add),
                             r=["t_f1", "x%d" % s], w=["x%d" % s])
                ckp(13)
                for s, (t0, n) in enumerate(subs):
                    S.op("act", lambda: A.activation(hb[0:n, s, :], x_sb[0:n, s, :], AF.Square, accum_out=ss[0:n, s:s + 1]), r=["x%d" % s], w=["hb%d" % s, "ss"])
                    S.op("act", lambda: A.activation(sd[0:n, s:s + 1], ss[0:n, s:s + 1], AF.Sqrt, bias=EPS, scale=1.0 / D), r=["ss"], w=["sd"])
                    S.op("dve", lambda: V.reciprocal(rstd[0:n, s:s + 1], sd[0:n, s:s + 1]), r=["sd"], w=["rstd%d" % s])
                    S.op("dve", lambda: V.scalar_tensor_tensor(x_sb[0:n, s, :], x_sb[0:n, s, :], rstd[0:n, s:s + 1], fn_bc[0:n, :], ALU.mult, ALU.mult),
                         r=["x%d" % s, "rstd%d" % s, "fn_bc"], w=["x%d" % s])
                    dst = y_s[0:16, :] if sample else y_p[tok0 + t0:tok0 + t0 + n, :]
                    S.dma("sp", dst, x_sb[0:n, s, :], r=["x%d" % s], w=["o_y"])
                assert wblk[0] == NBLK * (p + 1), wblk[0]
                ckp(14 + p)
        except _Stop:
            pass
        S.finish("sp")
        print("build stats: waits", S.nwait, "dmas", S.ndma, "seq", dict(S.seq), flush=True)
    return nc


def _consts():
    c = np.zeros((128, C_N), np.float32)
    c[:, C_ID:C_ID + 128] = np.eye(128)
    p = np.arange(128)
    c[:, C_BD:C_BD + 128] = (p[:, None] // 64 == p[None, :] // 64)
    c[:, C_IH:C_IH + 64] = (p[:, None] % 64 == np.arange(64)[None, :])
    c[:, C_HSEL:C_HSEL + 2] = (p[:, None] // 64 == np.arange(2)[None, :])
    s_ = (p % 64)[:, None]; t_ = np.arange(64)[None, :]
    c[:, C_MS:C_MS + 64] = (s_ < t_)
    c[:, C_MI:C_MI + 64] = (s_ <= t_)
    c[:, C_MST:C_MST + 64] = (s_ > t_)
    c[:, C_RST:C_RST + 256] = (np.arange(256) % 64 != 0)[None, :]
    j = p[:, None].astype(np.float32); i = np.arange(128)[None, :].astype(np.float32)
    BIG = 1e5
    c[:, C_DM:C_DM + 128] = np.where(j > i, -(i + 128 - j), -BIG)
    c[:, C_DM + 128:C_DM + 256] = np.where(j <= i, -(i - j), -BIG)
    sl = np.array(SLOPES, np.float32)[None, :]
    c[:, C_SB:C_SB + 16] = -sl * (127 - j) / SCALE
    c[:, C_ON:C_ON + 128] = 1.0
    return c


def _pv(inp):
    pv = np.zeros((128, PV_N), np.float32)
    f = lambda v, n: np.ascontiguousarray(np.asarray(v, np.float32).reshape(n, 128).T)
    pv[:, PV_GMIX:PV_GMIX + 16] = f(inp["norm_mix"][0], 16)
    pv[:, PV_GFFN:PV_GFFN + 16] = f(inp["norm_ffn"][0], 16)
    pv[:, PV_GPLE:PV_GPLE + 16] = f(inp["norm_ple"][0], 16)
    mu = np.asarray(inp["mu_shift"][0], np.float32)
    for ci in range(28):
        pv[0:RW_M[ci], PV_MU + ci] = mu[RW_COL[ci]:RW_COL[ci] + RW_M[ci]]
    pv[:, PV_W0:PV_W0 + 8] = f(inp["rwkv_w0"][0], 8)
    pv[:, PV_A0:PV_A0 + 8] = f(inp["rwkv_a0"][0], 8)
    pv[:, PV_KK:PV_KK + 8] = f(inp["rwkv_k_k"][0], 8)
    pv[:, PV_KA:PV_KA + 8] = f(inp["rwkv_k_a"][0], 8)
    pv[:, PV_RK:PV_RK + 8] = f(np.asarray(inp["rwkv_r_k"][0]).reshape(-1), 8)
    pv[:, PV_LNW:PV_LNW + 8] = f(inp["rwkv_ln_w"][0], 8)
    pv[:, PV_LNB:PV_LNB + 8] = f(inp["rwkv_ln_b"][0], 8)
    return pv


_NC = [None]


def kernel(**inp):
    A_ = lambda k: np.ascontiguousarray(np.asarray(inp[k], np.float32))
    if _NC[0] is None:
        _NC[0] = build_program()
    nc = _NC[0]
    shared = {
        "w_in": A_("w_in")[0], "w_out": A_("w_out")[0], "w_gate": A_("w_gate")[0], "w_up": A_("w_up")[0],
        "w_down": A_("w_down")[0], "ple_gate": A_("ple_gate")[0], "ple_proj": A_("ple_proj")[0],
        "w2": A_("rwkv_w2")[0], "a2": A_("rwkv_a2")[0], "g2": A_("rwkv_g2")[0],
        "pv": _pv(inp), "cst": _consts(), "fng": A_("final_norm").reshape(1, D),
        "sinks": A_("attn_sinks").reshape(1, 16),
    }
    xp, xs = A_("x_prompt"), A_("x_sample")
    pp, ps = A_("p_prompt"), A_("p_sample")
    swkv, ssh = A_("state_wkv"), A_("state_shift")
    ck, cv = A_("cache_k"), A_("cache_v")
    in_maps = []
    for i in range(8):
        b = slice(16 * i, 16 * i + 16)
        m = dict(shared)
        m.update({
            "xp": xp[i], "xs": np.ascontiguousarray(xs[b, 0]), "pp": pp[0, i], "psm": np.ascontiguousarray(ps[0, b, 0]),
            "swkv": np.ascontiguousarray(np.swapaxes(swkv[0, b], -1, -2)),
            "sshT": np.ascontiguousarray(ssh[0, b].T),
            "ck": np.ascontiguousarray(ck[0, b].reshape(16, 128, 256)), "cv": np.ascontiguousarray(cv[0, b].reshape(16, 128, 256)),
        })
        in_maps.append(m)
    res = run_bass_kernel_spmd(nc, in_maps, core_ids=list(range(8)))
    R = res.results
    cat = lambda k: np.stack([np.asarray(r[k], np.float32) for r in R])
    y_p = cat("y_p")
    y_s = cat("y_s").reshape(128, 1, D)
    wkv_p = np.swapaxes(cat("wkv_p"), -1, -2)[None]
    sh_p = cat("sh_p").reshape(8, RWP)[None]
    k_p = cat("k_p").reshape(8, 128, 4, 64)[None]
    v_p = cat("v_p").reshape(8, 128, 4, 64)[None]
    wkv_s = np.swapaxes(cat("wkv_s").reshape(128, 16, 64, 64), -1, -2)[None]
    sh_s = np.swapaxes(cat("sh_s"), 1, 2).reshape(128, RWP)[None]
    k_s = cat("k_s").reshape(128, 128, 4, 64)[None]
    v_s = cat("v_s").reshape(128, 128, 4, 64)[None]
    c_ = np.ascontiguousarray
    return (c_(y_p), c_(y_s), c_(wkv_p), c_(sh_p), c_(k_p), c_(v_p), c_(wkv_s), c_(sh_s), c_(k_s), c_(v_s))
```

```python
import bisect
import concourse.bass as bass
import concourse.mybir as mybir

F32 = mybir.dt.float32
BF16 = mybir.dt.bfloat16
AF = mybir.ActivationFunctionType
ALU = mybir.AluOpType
AX = mybir.AxisListType

SEM_ROLL = 12000


class Sched:
    def __init__(self, nc, stack):
        self.nc = nc
        self.stack = stack
        self.engs = {"pe": nc.tensor, "dve": nc.vector, "act": nc.scalar,
                     "pool": nc.gpsimd, "sp": nc.sync}
        self.sem = {}
        self.cnt = {}
        self.seq = {}
        self.sigs = {}
        self.sigseq = {}
        self.last_ins = {}
        self.last_seq = {}
        self.nsem = 0
        for e in self.engs:
            self._new_sem(e)
            self.seq[e] = 0
            self.sigs[e] = []
            self.sigseq[e] = []
            self.last_ins[e] = None
        self.semobj = {}
        for e in self.engs:
            pass
        self.seen = {e: {} for e in self.engs}
        self.lastw = {}
        self.readers = {}
        self.dma_sems = {}
        self.dma_rr = {}
        self.ndma = 0
        self.nwait = 0

    def _new_sem(self, e):
        name = "s_%s_%d" % (e, self.nsem)
        self.nsem += 1
        s = self.stack.enter_context(self.nc.semaphore(name))
        self.sem[e] = (name, s)
        self.cnt[e] = 0
        if not hasattr(self, "semh"):
            self.semh = {}
        self.semh[name] = s

    def _resolve(self, tok):
        if tok[0] == "d":
            return tok[1], tok[2]
        _, e, q = tok
        i = bisect.bisect_left(self.sigseq[e], q)
        if i < len(self.sigseq[e]):
            _, sn, v = self.sigs[e][i]
            return sn, v
        ins = self.last_ins[e]
        assert ins is not None
        if self.cnt[e] >= SEM_ROLL:
            self._new_sem(e)
        sn, s = self.sem[e]
        self.cnt[e] += 1
        ins.then_inc(s, 1)
        self.sigs[e].append((self.last_seq[e], sn, self.cnt[e]))
        self.sigseq[e].append(self.last_seq[e])
        return sn, self.cnt[e]

    def _wait(self, e, toks):
        need = {}
        for t in toks:
            sn, v = self._resolve(t)
            if self.seen[e].get(sn, 0) >= v:
                continue
            if need.get(sn, 0) < v:
                need[sn] = v
        for sn, v in need.items():
            self.engs[e].wait_ge(self.semh[sn], v)
            self.seen[e][sn] = v
            self.nwait += 1

    def _deps(self, e, r, w):
        toks = []
        for k in r:
            t = self.lastw.get(k)
            if t is not None and not (e == "pe" and t[0] == "c" and t[1] == "pe"):
                toks.append(t)
            if k.startswith("pb") or k.startswith("pt"):
                for t in self.readers.get(k, ()):
                    if t[0] == "c" and t[1] != e:
                        toks.append(t)
        for k in w:
            t = self.lastw.get(k)
            if t is not None and not (e == "pe" and t[0] == "c" and t[1] == "pe"):
                toks.append(t)
            for t in self.readers.get(k, ()):
                toks.append(t)
        return toks

    def _record(self, tok, r, w):
        for k in r:
            self.readers.setdefault(k, []).append(tok)
        for k in w:
            self.lastw[k] = tok
            self.readers[k] = []

    def op(self, e, fn, r=(), w=()):
        self._wait(e, self._deps(e, r, w))
        ins = fn()
        self.seq[e] += 1
        self.last_ins[e] = ins
        self.last_seq[e] = self.seq[e]
        self._record(("c", e, self.seq[e]), r, w)
        return ins

    def dma(self, q, out, in_, r=(), w=(), **kw):
        ring = self.dma_sems.setdefault(q, [])
        if len(ring) < 8:
            name = "d_%s_%d" % (q, len(ring))
            s = self.stack.enter_context(self.nc.semaphore(name))
            self.semh[name] = s
            ring.append([name, 0])
            slot = ring[-1]
        else:
            i = self.dma_rr.get(q, 0)
            self.dma_rr[q] = (i + 1) % len(ring)
            slot = ring[i]
        toks = self._deps(q, r, w)
        if slot[1] > 0:
            toks.append(("d", slot[0], slot[1]))
        self._wait(q, toks)
        slot[1] += 16
        ins = self.engs[q].dma_start(out=out, in_=in_, **kw)
        ins.then_inc(self.semh[slot[0]], 16)
        self.seq[q] += 1
        self._record(("d", slot[0], slot[1]), r, w)
        self.ndma += 1
        return ins

    def finish(self, e="sp"):
        toks = []
        for k, t in self.lastw.items():
            toks.append(t)
        self._wait(e, toks)

import contextlib
import numpy as np
from concourse.bass_utils import run_bass_kernel_spmd

D = 2048
KC = 16
DFF = 5632
FC = 44
PROJ = 5056
RWP = 3520
TP = 256
NPASS = 2048 // TP
NBLK = 54
EPS = 1e-6
GN_EPS = 64e-5
SCALE = 0.125
DEC = 0.6065306597126334
SLOPES = [2.0 ** (-8.0 * (h + 1) / 16) for h in range(16)]
QPERM = [0, 4, 1, 5, 2, 6, 3, 7, 8, 12, 9, 13, 10, 14, 11, 15]
PV_GMIX, PV_GFFN, PV_GPLE, PV_MU = 0, 16, 32, 48
PV_W0, PV_A0, PV_KK, PV_KA, PV_RK, PV_LNW, PV_LNB = 76, 84, 92, 100, 108, 116, 124
PV_N = 132
C_ID, C_BD, C_IH, C_HSEL, C_MS, C_MI, C_MST, C_RST, C_DM, C_SB, C_ON, C_N = 0, 128, 256, 320, 322, 386, 450, 514, 770, 1026, 1042, 1170

RW_COL = [c * 128 for c in range(8)] + [1024 + c * 128 for c in range(8)] + \
         [2048 + c * 128 for c in range(8)] + [3072, 3168, 3264, 3392]
RW_M = [128] * 24 + [96, 96, 128, 128]


def hcol(h):
    return ((h % 2) * 8 + h // 2) * 64


class _Stop(Exception):
    pass


def build_program(debug=False, stop=None):
    nc = bass.Bass("TRN2", target_bir_lowering=False)
    din = lambda n, sh: nc.dram_tensor(n, sh, F32, kind="ExternalInput").ap()
    dout = lambda n, sh: nc.dram_tensor(n, sh, F32, kind="ExternalOutput").ap()
    xp = din("xp", [2048, D]); xs_d = din("xs", [16, D])
    pp = din("pp", [2048, 256]); ps_d = din("psm", [16, 256])
    swkv = din("swkv", [16, 16, 64, 64])
    sshT = din("sshT", [RWP, 16])
    ck = din("ck", [16, 128, 256]); cv = din("cv", [16, 128, 256])
    w_in = din("w_in", [D, PROJ]); w_out = din("w_out", [D, D])
    w_gate = din("w_gate", [D, DFF]); w_up = din("w_up", [D, DFF]); w_down = din("w_down", [DFF, D])
    ple_gate = din("ple_gate", [D, D]); ple_proj = din("ple_proj", [256, D])
    w2 = din("w2", [96, 1024]); a2 = din("a2", [96, 1024]); g2 = din("g2", [256, 1024])
    pv_d = din("pv", [128, PV_N]); cst_d = din("cst", [128, C_N])
    fng = din("fng", [1, D]); esk = din("sinks", [1, 16])

    wsc = nc.dram_tensor("wsc", [NBLK, 128, KC * 512], BF16, kind="Internal").ap()
    y_p = dout("y_p", [2048, D]); y_s = dout("y_s", [16, D])
    wkv_p = dout("wkv_p", [16, 64, 64])
    sh_p = dout("sh_p", [RWP, 1])
    k_p = dout("k_p", [128, 256]); v_p = dout("v_p", [128, 256])
    wkv_s = dout("wkv_s", [16, 16, 64, 64])
    sh_s = dout("sh_s", [RWP, 16])
    k_s = dout("k_s", [16, 128, 256]); v_s = dout("v_s", [16, 128, 256])

    with contextlib.ExitStack() as st:
        S = Sched(nc, st)
        sb = lambda n, sh, dt=F32: st.enter_context(nc.sbuf_tensor("sb_" + n, sh, dt))
        V, A, P, PE = nc.vector, nc.scalar, nc.gpsimd, nc.tensor
        PF = st.enter_context(nc.psum_tensor("PF", [128, 4096], F32))
        PTb = PF[:, 3072:4096].bitcast(BF16)
        pbk = lambda b: "pb%d" % b
        bank = lambda b: PF[:, b * 512:(b + 1) * 512]

        x_sb = sb("x_sb", [128, 2, D])
        hb = sb("hb", [128, 2, D], BF16)
        hT = sb("hT", [128, KC, TP], BF16)
        wb = [sb("wb%d" % i, [128, KC, 512], BF16) for i in range(2)]
        pv = sb("pv", [128, PV_N]); cst = sb("cst", [128, C_N])
        omu = sb("omu", [128, 28]); omka = sb("omka", [128, 8])
        cstb = sb("cstb", [128, 384], BF16)
        fn_bc = sb("fn_bc", [128, D])
        esink = sb("esink", [128, 16])
        w2b = sb("w2b", [128, 1024], BF16); a2b = sb("a2b", [128, 1024], BF16)
        g2b = sb("g2b", [128, 2, 1024], BF16)
        ss = sb("ss", [128, 8]); sd = sb("sd", [128, 8]); rstd = sb("rstd", [128, 8])
        rwbig = sb("rwbig", [128, 6 * 8 * TP], BF16)
        def rwv(i):
            return rwbig[:, i * 8 * TP:(i + 1) * 8 * TP].rearrange("p (c t) -> p c t", c=8)
        Rt, Kt, At, Bt, vT, ybT = [rwv(i) for i in range(6)]
        hid = rwbig[:, 0:FC * TP].rearrange("p (c t) -> p c t", c=FC)
        ystg = rwbig.bitcast(F32)[:, 0:2 * D].rearrange("p (s d) -> p s d", s=2)
        gT = sb("gT", [128, 8, TP], BF16)
        dS = sb("dS", [128, 8, 16])
        tmV = sb("tmV", [128, 2, 1024], BF16); tmK = sb("tmK", [128, 2, 1024], BF16)
        tmB = sb("tmB", [128, 2, 1024], BF16)
        gC = sb("gC", [128, 8, 4])
        carry = sb("carry", [128, 28]); zlast = sb("zlast", [128, 28]); zs = sb("zs", [128, 28, 16])
        shT = sb("shT", [128, 28, 16]); msh = sb("msh", [128, 28, 16])
        txw = sb("txw", [128, TP], BF16); xaT = sb("xaT", [128, TP], BF16)
        sxg = sb("sxg", [128, 2, TP], BF16)
        tmr = [sb("tmr%d" % i, [128, TP + 1]) for i in range(3)]
        scr = sb("scr", [128, 13 * TP])
        def scv(i):
            return scr[:, i * TP:(i + 1) * TP]
        xs_r2 = sb("xs_r2", [128, TP]); xs_k2 = sb("xs_k2", [128, TP])
        xs_r, xs_k, xs_o, t_sg, t_L, t_eL, t_enL, t_eLm, t_a, t_kkr, t_sq, t_b, t_kh = [scv(i) for i in range(13)]
        t_rkr = sb("t_rkr", [128, TP], BF16)
        xs_rb = [xs_r, xs_r2]; xs_kb = [xs_k, xs_k2]
        def epv(i):
            return scr[:, i * 512:(i + 1) * 512].rearrange("p (c t) -> p c t", c=4)
        yT_sb, e_sq, e_m, e_v = [epv(i) for i in range(4)]
        chb = sb("chb", [128, 11 * 1024], BF16)
        def chv(i):
            return chb[:, i * 1024:(i + 1) * 1024]
        cY = [chv(0), chv(1)]; cYT = [chv(2), chv(3)]; cP = [chv(4), chv(5)]
        cNak, cMrb, cMrk, cW, cU = chv(6), chv(7), chv(8), chv(9), chv(10)
        chf = sb("chf", [128, 2048])
        cT1 = chf[:, 0:1024]; cYs = chf[:, 1024:2048]
        ST = sb("ST", [128, 8, 64]); STb = sb("STb", [128, 8, 128], BF16)
        qT = sb("qT", [128, 8, TP], BF16); kT = sb("kT", [128, 2, 128 + TP], BF16)
        Vall = sb("Vall", [128, 3, 4, 66], BF16)
        t_f1 = sb("t_f1", [128, 512]); t_f2 = sb("t_f2", [128, 512])
        a_sc = t_f1; ktm = t_f2
        a_PT = [sb("a_PT%d" % i, [128, 512], BF16) for i in range(2)]
        a_den = sb("a_den", [128, 4])
        wpp = sb("wpp", [128, 2, 512], BF16)
        pT_ = sb("pT_", [128, 2, TP], BF16); pe_sb = sb("pe_sb", [128, 2, 256]); pe_b = sb("pe_b", [128, 2, 256], BF16)
        x1 = x_sb[:, 1, :]
        sS = x1[:, 0:512].rearrange("p (c v) -> p c v", c=8)
        sSn = x1[:, 512:1024].rearrange("p (c v) -> p c v", c=8)
        sBlk = x1[:, 1024:2048].rearrange("p (c v) -> p c v", c=8)
        sAbd = chf[:, 0:1024].rearrange("p (c v) -> p c v", c=8)
        sRv = chf[:, 1024:1536].rearrange("p (c v) -> p c v", c=8)
        sT1 = chf[:, 1536:2048].rearrange("p (c v) -> p c v", c=8)
        chb32 = chb[:, 0:4096].bitcast(F32)
        sKV = chb32[:, 0:512]; sKn = sKV[:, 0:256]; sVn = sKV[:, 256:512]
        sPT = chb32[:, 512:768].rearrange("p (b h) -> p b h", b=16)
        sPTn = chb32[:, 768:1024].rearrange("p (b h) -> p b h", b=16)
        vf = chb32[:, 1024:1152].rearrange("p (c b) -> p c b", c=8)
        sKT = chb[:, 4096:4352].rearrange("p (k n) -> p k n", k=2)

        ident = cst[:, C_ID:C_ID + 128]; BDf = cst[:, C_BD:C_BD + 128]
        identb = cstb[:, 0:128]; BDb = cstb[:, 128:256]
        mS = cst[:, C_MS:C_MS + 64]; mI = cst[:, C_MI:C_MI + 64]; mST = cst[:, C_MST:C_MST + 64]
        rst = cst[:, C_RST:C_RST + 256]
        Dm = cst[:, C_DM:C_DM + 256]
        sbias = cst[:, C_SB:C_SB + 16]
        Ihalf = cst[:, C_IH:C_IH + 64]

        S.dma("sp", pv[:], pv_d[:, :], w=["pv"])
        S.dma("sp", cst[:], cst_d[:, :], w=["cst"])
        S.dma("sp", fn_bc[:], fng.partition_broadcast(128), w=["fn_bc"])
        S.dma("sp", esink[:], esk.partition_broadcast(128), w=["esink"])
        S.dma("pool", w2b[0:96, :], w2[:, :], w=["w2b"])
        S.dma("pool", a2b[0:96, :], a2[:, :], w=["a2b"])
        S.dma("pool", g2b[:], g2.rearrange("(k p) n -> p k n", p=128), w=["g2b"])
        S.op("dve", lambda: V.tensor_copy(cstb[:, 0:256], cst[:, 0:256]), r=["cst"], w=["cstb"])
        S.op("dve", lambda: V.tensor_scalar(omu[:], pv[:, PV_MU:PV_MU + 28], -1.0, 1.0, ALU.mult, ALU.add), r=["pv"], w=["omu"])
        S.op("dve", lambda: V.tensor_scalar(omka[:], pv[:, PV_KA:PV_KA + 8], -1.0, 1.0, ALU.mult, ALU.add), r=["pv"], w=["omka"])
        S.op("act", lambda: A.activation(esink[:], esink[:], AF.Exp), r=["esink"], w=["esink"])
        S.op("pool", lambda: P.memset(carry[:], 0.0), w=["carry%d" % i for i in range(28)])
        S.op("pool", lambda: P.memset(ST[:], 0.0), w=["ST"])
        S.op("pool", lambda: P.memset(STb[:], 0.0), w=["STb"])
        S.op("pool", lambda: P.memset(Vall[:], 1.0), w=["Vall0", "Vall1", "Vall2"])
        S.op("pool", lambda: P.memset(kT[:], 0.0), w=["kT"])
        for i_ in range(2):
            S.op("pool", lambda: P.memset(wb[i_][:], 0.0), w=["wb%d" % i_])

        def ckp(k):
            if stop == k:
                raise _Stop()
        wslot = [0]

        wblk = [0]

        def wload(W, specs, nk=KC, row0=0):
            sl = wslot[0]; wslot[0] ^= 1
            bi = wblk[0] % NBLK; first = wblk[0] < NBLK; wblk[0] += 1
            if first:
                for sp_ in specs:
                    c0, n, d0 = sp_[0:3]
                    Wm = sp_[3] if len(sp_) > 3 else W
                    src = Wm[row0:row0 + nk * 128, c0:c0 + n].rearrange("(k p) n -> p k n", p=128)
                    S.dma("pool", wb[sl][:, 0:nk, d0:d0 + n], src, w=["wb%d" % sl])
                S.dma("sp", wsc[bi], wb[sl][:, :, :].rearrange("p k n -> p (k n)"), r=["wb%d" % sl], w=["wsc%d" % bi])
            else:
                S.dma("pool", wb[sl][:, :, :].rearrange("p k n -> p (k n)"), wsc[bi], r=["wsc%d" % bi], w=["wb%d" % sl])
            return sl

        ev_rr = [0]

        def evac(out, in_, r, w, scale=None):
            ev_rr[0] ^= 1
            if ev_rr[0]:
                if scale is None:
                    S.op("act", lambda: A.activation(out, in_, AF.Copy), r=r, w=w)
                else:
                    S.op("act", lambda: A.activation(out, in_, AF.Copy, scale=scale), r=r, w=w)
            else:
                if scale is None:
                    S.op("dve", lambda: V.tensor_copy(out, in_), r=r, w=w)
                else:
                    S.op("dve", lambda: V.tensor_scalar(out, in_, scale, None, ALU.mult), r=r, w=w)

        pt_rr = [0]

        def norm_T(subs, gcol, ntok):
            for s, (t0, n) in enumerate(subs):
                S.op("act", lambda: A.activation(hb[0:n, s, :], x_sb[0:n, s, :], AF.Square, accum_out=ss[0:n, s:s + 1]),
                     r=["x%d" % s], w=["hb%d" % s, "ss"])
                S.op("act", lambda: A.activation(sd[0:n, s:s + 1], ss[0:n, s:s + 1], AF.Sqrt, bias=EPS, scale=1.0 / D),
                     r=["ss"], w=["sd"])
                S.op("dve", lambda: V.reciprocal(rstd[0:n, s:s + 1], sd[0:n, s:s + 1]), r=["sd"], w=["rstd%d" % s])
                S.op("dve", lambda: V.tensor_scalar(hb[0:n, s, :], x_sb[0:n, s, :], rstd[0:n, s:s + 1], None, ALU.mult),
                     r=["x%d" % s, "rstd%d" % s], w=["hb%d" % s])
            for kc in range(KC):
                pt_rr[0] ^= 1
                pt = PTb[:, pt_rr[0] * 1024:(pt_rr[0] + 1) * 1024]; pk = "pb%d" % (6 + pt_rr[0])
                for s, (t0, n) in enumerate(subs):
                    S.op("pe", lambda: PE.transpose(pt[:, t0:t0 + n], hb[0:n, s, kc * 128:(kc + 1) * 128], identb[0:n, 0:n]),
                         r=["hb%d" % s, "cstb"], w=[pk])
                evac(hT[:, kc, 0:ntok], pt[:, 0:ntok], r=[pk, "pv"], w=["hT%d" % kc], scale=pv[:, gcol + kc:gcol + kc + 1])

        def proj_fm(sl, d0, M, ps, pkeys, ntok):
            for kc in range(KC):
                S.op("pe", lambda: PE.matmul(ps[0:M, 0:ntok], wb[sl][:, kc, d0:d0 + M], hT[:, kc, 0:ntok],
                                             start=(kc == 0), stop=(kc == KC - 1)),
                     r=["wb%d" % sl, "hT%d" % kc], w=pkeys)

        def tshift(ci, ps, pkeys, M, ntok, sample, last, out_ap, okeys):
            mu = pv[0:M, PV_MU + ci:PV_MU + ci + 1]
            if sample:
                S.op("act", lambda: A.activation(zs[0:M, ci, :], ps[0:M, 0:16], AF.Copy), r=pkeys, w=["zs"])
                S.op("dve", lambda: V.scalar_tensor_tensor(out_ap, ps[0:M, 0:16], omu[0:M, ci:ci + 1], msh[0:M, ci, :],
                                                           ALU.mult, ALU.add), r=pkeys + ["omu", "msh"], w=okeys)
                return
            tm = tmr[ci % 3]; tk = "tmr%d" % (ci % 3)
            S.op("act", lambda: A.activation(tm[0:M, 0:1], carry[0:M, ci:ci + 1], AF.Copy), r=["carry%d" % ci], w=[tk])
            S.op("act", lambda: A.activation(tm[0:M, 1:1 + ntok], ps[0:M, 0:ntok], AF.Copy, scale=mu), r=pkeys + ["pv"], w=[tk])
            S.op("act", lambda: A.activation(carry[0:M, ci:ci + 1], tm[0:M, ntok:ntok + 1], AF.Copy), r=[tk], w=["carry%d" % ci])
            if last:
                S.op("act", lambda: A.activation(zlast[0:M, ci:ci + 1], ps[0:M, ntok - 1:ntok], AF.Copy), r=pkeys, w=["zlast"])
            S.op("dve", lambda: V.scalar_tensor_tensor(out_ap, ps[0:M, 0:ntok], omu[0:M, ci:ci + 1], tm[0:M, 0:ntok],
                                                       ALU.mult, ALU.add), r=pkeys + ["omu", tk], w=okeys)

        def prep_P(c, sl, ntok, sample, last):
            N = ntok
            b0 = 0 if c % 2 == 0 else 3
            pr, pk_, pvv = bank(b0), bank(b0 + 1), bank(b0 + 2)
            kr_, kk_, kv_ = ["pb%d" % (b0 + i) for i in range(3)]
            xs_r, xs_k = xs_rb[c % 2], xs_kb[c % 2]
            xr_k, xk_k = "xs_r%d" % (c % 2), "xs_k%d" % (c % 2)
            if sl is not None:
                proj_fm(sl, 0, 128, pr, [kr_], N)
                proj_fm(sl, 128, 128, pk_, [kk_], N)
                proj_fm(sl, 256, 128, pvv, [kv_], N)
                return
            tshift(c, pr, [kr_], 128, N, sample, last, xs_r[:, 0:N], [xr_k])
            tshift(8 + c, pk_, [kk_], 128, N, sample, last, xs_k[:, 0:N], [xk_k])
            tshift(16 + c, pvv, [kv_], 128, N, sample, last, vT[:, c, 0:N], ["vT%d" % c])

        def prep_H(c, ntok, sample, last):
            N = ntok
            xs_r, xs_k = xs_rb[c % 2], xs_kb[c % 2]
            xr_k, xk_k = "xs_r%d" % (c % 2), "xs_k%d" % (c % 2)
            cs = slice(c * 128, (c + 1) * 128)
            pc = lambda o: pv[:, o + c:o + c + 1]
            S.op("pe", lambda: PE.matmul(bank(6)[:, 0:N], w2b[0:96, cs], txw[0:96, 0:N], start=True, stop=True),
                 r=["w2b", "txw"], w=["pb6"])
            S.op("pe", lambda: PE.matmul(bank(7)[:, 0:N], a2b[0:96, cs], xaT[0:96, 0:N], start=True, stop=True),
                 r=["a2b", "xaT"], w=["pb7"])
            S.op("dve", lambda: V.tensor_scalar(t_kkr[:, 0:N], xs_k[:, 0:N], pc(PV_KK), None, ALU.mult), r=[xk_k, "pv"], w=["t_kkr"])
            S.op("act", lambda: A.activation(t_sg[:, 0:N], bank(6)[:, 0:N], AF.Sigmoid, bias=pc(PV_W0)), r=["pb6", "pv"], w=["t_sg"])
            S.op("act", lambda: A.activation(t_a[:, 0:N], bank(7)[:, 0:N], AF.Sigmoid, bias=pc(PV_A0)), r=["pb7", "pv"], w=["t_a"])
            S.op("act", lambda: A.activation(t_sq[:, 0:N], t_kkr[:, 0:N], AF.Square), r=["t_kkr"], w=["t_sq"])
            for k2 in range(2):
                S.op("pe", lambda: PE.matmul(bank(6)[:, 0:N], g2b[:, k2, cs], sxg[:, k2, 0:N], start=(k2 == 0), stop=(k2 == 1)),
                     r=["g2b", "sxg"], w=["pb6"])
            S.op("pe", lambda: PE.matmul(bank(7)[:, 0:N], BDf, t_sq[:, 0:N], start=True, stop=True), r=["cst", "t_sq"], w=["pb7"])
            S.op("dve", lambda: V.tensor_scalar(t_sg[:, 0:N], t_sg[:, 0:N], -DEC, None, ALU.mult), r=["t_sg"], w=["t_sg"])
            if sample:
                S.op("act", lambda: A.activation(dS[:, c, :], t_sg[:, 0:N], AF.Exp), r=["t_sg"], w=["dS"])
            else:
                S.op("dve", lambda: V.tensor_tensor_scan(t_L[:, 0:N], rst[:, 0:N], t_sg[:, 0:N], 0.0, ALU.mult, ALU.add),
                     r=["t_sg", "cst"], w=["t_L"])
                S.op("act", lambda: A.activation(t_eL[:, 0:N], t_L[:, 0:N], AF.Exp), r=["t_L"], w=["t_eL"])
                S.op("act", lambda: A.activation(t_enL[:, 0:N], t_L[:, 0:N], AF.Exp, scale=-1.0), r=["t_L"], w=["t_enL"])
                eLv = t_eL[:, 0:N].rearrange("p (j t) -> p j t", t=64)
                eLmv = t_eLm[:, 0:N].rearrange("p (j t) -> p j t", t=64)
                S.op("dve", lambda: V.memset(eLmv[:, :, 0:1], 1.0), w=["t_eLm"])
            S.op("dve", lambda: V.tensor_scalar(t_sq[:, 0:N], bank(7)[:, 0:N], 1e-24, None, ALU.max), r=["pb7"], w=["t_sq"])
            S.op("act", lambda: A.activation(t_sq[:, 0:N], t_sq[:, 0:N], AF.Sqrt), r=["t_sq"], w=["t_sq"])
            S.op("act", lambda: A.activation(gT[:, c, 0:N], bank(6)[:, 0:N], AF.Copy), r=["pb6"], w=["gT%d" % c])
            if not sample:
                S.op("dve", lambda: V.tensor_copy(eLmv[:, :, 1:64], eLv[:, :, 0:63]), r=["t_eL"], w=["t_eLm"])
                S.op("act", lambda: A.activation(gC[:, c, :], eLv[:, :, 63], AF.Copy), r=["t_eL"], w=["gC"])
                S.op("dve", lambda: V.tensor_tensor(Rt[:, c, 0:N], xs_r[:, 0:N], t_eL[:, 0:N], ALU.mult), r=[xr_k, "t_eL"], w=["Rt%d" % c])
            else:
                S.op("dve", lambda: V.tensor_copy(Rt[:, c, 0:N], xs_r[:, 0:N]), r=[xr_k], w=["Rt%d" % c])
            S.op("dve", lambda: V.reciprocal(t_sq[:, 0:N], t_sq[:, 0:N]), r=["t_sq"], w=["t_sq"])
            S.op("dve", lambda: V.tensor_tensor(t_kkr[:, 0:N], t_kkr[:, 0:N], t_sq[:, 0:N], ALU.mult), r=["t_kkr", "t_sq"], w=["t_kkr"])
            S.op("dve", lambda: V.tensor_tensor(t_b[:, 0:N], t_kkr[:, 0:N], t_a[:, 0:N], ALU.mult), r=["t_kkr", "t_a"], w=["t_b"])
            S.op("dve", lambda: V.tensor_scalar(t_a[:, 0:N], t_a[:, 0:N], pc(PV_KA), omka[:, c:c + 1], ALU.mult, ALU.add),
                 r=["t_a", "pv", "omka"], w=["t_a"])
            S.op("dve", lambda: V.tensor_tensor(t_kh[:, 0:N], xs_k[:, 0:N], t_a[:, 0:N], ALU.mult), r=[xk_k, "t_a"], w=["t_kh"])
            if sample:
                S.op("dve", lambda: V.tensor_scalar(At[:, c, 0:N], t_kkr[:, 0:N], -1.0, None, ALU.mult), r=["t_kkr"], w=["At%d" % c])
                S.op("dve", lambda: V.tensor_copy(Bt[:, c, 0:N], t_b[:, 0:N]), r=["t_b"], w=["Bt%d" % c])
                S.op("dve", lambda: V.tensor_copy(Kt[:, c, 0:N], t_kh[:, 0:N]), r=["t_kh"], w=["Kt%d" % c])
            else:
                S.op("dve", lambda: V.scalar_tensor_tensor(At[:, c, 0:N], t_kkr[:, 0:N], -1.0, t_eLm[:, 0:N], ALU.mult, ALU.mult),
                     r=["t_kkr", "t_eLm"], w=["At%d" % c])
                S.op("dve", lambda: V.tensor_tensor(Bt[:, c, 0:N], t_b[:, 0:N], t_enL[:, 0:N], ALU.mult), r=["t_b", "t_enL"], w=["Bt%d" % c])
                S.op("dve", lambda: V.tensor_tensor(Kt[:, c, 0:N], t_kh[:, 0:N], t_enL[:, 0:N], ALU.mult), r=["t_kh", "t_enL"], w=["Kt%d" % c])

        def prep_tail(c, ntok):
            N = ntok
            xs_r = xs_rb[c % 2]; xr_k = "xs_r%d" % (c % 2)
            pc = lambda o: pv[:, o + c:o + c + 1]
            S.op("dve", lambda: V.scalar_tensor_tensor(t_rkr[:, 0:N], xs_r[:, 0:N], pc(PV_RK), t_kh[:, 0:N], ALU.mult, ALU.mult),
                 r=[xr_k, "t_kh", "pv"], w=["t_rkr"])
            S.op("pe", lambda: PE.matmul(bank(6)[:, 0:N], BDb, t_rkr[:, 0:N], start=True, stop=True), r=["cstb", "t_rkr"], w=["pb6"])
            S.op("dve", lambda: V.tensor_tensor(ybT[:, c, 0:N], bank(6)[:, 0:N], vT[:, c, 0:N], ALU.mult), r=["pb6", "vT%d" % c], w=["ybT%d" % c])

        def epilogue(col0, n, srcs=None):
            for hf in range(2):
                if srcs is None:
                    for c4 in range(4):
                        c = hf * 4 + c4
                        S.op("pe", lambda: PE.transpose(bank(4)[:, c4 * 128:(c4 + 1) * 128], cYs[:, c * 128:(c + 1) * 128], ident),
                             r=["cYs_0", "cYs_1", "cst"], w=["pb4"])
                    src_ap, sk = bank(4)[:, 0:4 * n], "pb4"
                else:
                    src_ap, sk = srcs[hf]
                src = src_ap.rearrange("p (c t) -> p c t", c=4)
                yv = yT_sb[:, :, 0:n]; sq = e_sq[:, :, 0:n]; em = e_m[:, :, 0:n]; ev = e_v[:, :, 0:n]
                S.op("act", lambda: A.activation(yv, src, AF.Copy), r=[sk], w=["yT_sb"])
                S.op("act", lambda: A.activation(sq, src, AF.Square), r=[sk], w=["e_sq"])
                for c4 in range(4):
                    S.op("pe", lambda: PE.matmul(bank(5)[:, c4 * n:(c4 + 1) * n], BDf, yT_sb[:, c4, 0:n], start=True, stop=True),
                         r=["cst", "yT_sb"], w=["pb5"])
                m_ps = bank(5)[:, 0:4 * n].rearrange("p (c t) -> p c t", c=4)
                S.op("act", lambda: A.activation(em, m_ps, AF.Copy, scale=1.0 / 64), r=["pb5"], w=["e_m"])
                yield
                for c4 in range(4):
                    S.op("pe", lambda: PE.matmul(bank(5)[:, c4 * n:(c4 + 1) * n], BDf, e_sq[:, c4, 0:n], start=True, stop=True),
                         r=["cst", "e_sq"], w=["pb5"])
                S.op("dve", lambda: V.tensor_tensor(ev, em, em, ALU.mult), r=["e_m"], w=["e_v"])
                S.op("dve", lambda: V.scalar_tensor_tensor(ev, m_ps, 1.0 / 64, ev, ALU.mult, ALU.subtract), r=["pb5", "e_v"], w=["e_v"])
                S.op("act", lambda: A.activation(ev, ev, AF.Sqrt, bias=GN_EPS), r=["e_v"], w=["e_v"])
                S.op("dve", lambda: V.reciprocal(ev, ev), r=["e_v"], w=["e_v"])
                S.op("dve", lambda: V.tensor_tensor(yv, yv, em, ALU.subtract), r=["yT_sb", "e_m"], w=["yT_sb"])
                S.op("dve", lambda: V.tensor_tensor(yv, yv, ev, ALU.mult), r=["yT_sb", "e_v"], w=["yT_sb"])
                yield
                for c4 in range(4):
                    c = hf * 4 + c4
                    S.op("dve", lambda: V.tensor_scalar(yT_sb[:, c4, 0:n], yT_sb[:, c4, 0:n], pv[:, PV_LNW + c:PV_LNW + c + 1],
                                                        pv[:, PV_LNB + c:PV_LNB + c + 1], ALU.mult, ALU.add), r=["yT_sb", "pv"], w=["yT_sb"])
                    S.op("dve", lambda: V.tensor_tensor(yT_sb[:, c4, 0:n], yT_sb[:, c4, 0:n], ybT[:, c, col0:col0 + n], ALU.add),
                         r=["yT_sb", "ybT%d" % c], w=["yT_sb"])
                    S.op("dve", lambda: V.tensor_tensor(hT[:, c, col0:col0 + n], yT_sb[:, c4, 0:n], gT[:, c, col0:col0 + n], ALU.mult),
                         r=["yT_sb", "gT%d" % c], w=["hT%d" % c])
                yield

        def chunkA(j):
            hp = (j % 2) * 64; pq = "_%d" % (j % 2)
            tok = slice(j * 64, (j + 1) * 64)
            rows = slice(hp, hp + 64)
            A2 = PF[:, 0:1024]; B2 = PF[:, 1024:2048]
            ka, kb_ = ["pb0", "pb1"], ["pb2", "pb3"]
            def hmm(dst2, lh, rh, lh_n, rh_n, keys):
                for h in range(16):
                    c, hh = h // 2, h % 2
                    kr = slice(hh * 64, hh * 64 + 64)
                    S.op("pe", lambda: PE.matmul(dst2[rows, hcol(h):hcol(h) + 64], lh[kr, c, tok], rh[kr, c, tok], start=True, stop=True),
                         r=["%s%d" % (lh_n, c), "%s%d" % (rh_n, c)], w=keys)
            m3 = lambda m: m[rows, :].unsqueeze(1).broadcast_to([64, 16, 64])
            v3 = lambda t: t[rows, :].rearrange("p (h t) -> p h t", h=16)
            hmm(A2, Bt, At, "Bt", "At", ka)
            hmm(B2, At, Bt, "At", "Bt", kb_)
            S.op("dve", lambda: V.tensor_tensor(v3(cY[0]), v3(A2), m3(mS), ALU.mult), r=ka + ["cst"], w=["cY0" + pq])
            S.op("dve", lambda: V.tensor_tensor(v3(cYT[0]), v3(B2), m3(mST), ALU.mult), r=kb_ + ["cst"], w=["cYT0" + pq])
            yield
            hmm(A2, Kt, At, "Kt", "At", ka)
            S.op("dve", lambda: V.tensor_tensor(v3(cNak), v3(A2), m3(mS), ALU.mult), r=ka + ["cst"], w=["cNak" + pq])
            hmm(B2, Bt, Rt, "Bt", "Rt", kb_)
            S.op("dve", lambda: V.tensor_tensor(v3(cMrb), v3(B2), m3(mI), ALU.mult), r=kb_ + ["cst"], w=["cMrb" + pq])
            yield
            hmm(A2, Kt, Rt, "Kt", "Rt", ka)
            S.op("dve", lambda: V.tensor_tensor(v3(cMrk), v3(A2), m3(mI), ALU.mult), r=ka + ["cst"], w=["cMrk" + pq])
            idb3 = identb[rows, hp:hp + 64].unsqueeze(1).broadcast_to([64, 16, 64])
            S.op("dve", lambda: V.tensor_tensor(v3(cP[0]), v3(cY[0]), idb3, ALU.add), r=["cY0" + pq, "cstb"], w=["cP0" + pq])
            yield
            cur = 0
            for lvl in range(5):
                nxt = cur ^ 1
                Yc, YTc, Pc = cY[cur], cYT[cur], cP[cur]
                Yn, YTn, Pn = cY[nxt], cYT[nxt], cP[nxt]
                kY, kYT, kP = "cY%d" % cur + pq, "cYT%d" % cur + pq, "cP%d" % cur + pq
                for h in range(16):
                    hc = slice(hcol(h), hcol(h) + 64)
                    S.op("pe", lambda: PE.matmul(B2[rows, hc], Yc[rows, hc], YTc[rows, hc], start=True, stop=True), r=[kY, kYT], w=kb_)
                if lvl < 4:
                    for h in range(16):
                        hc = slice(hcol(h), hcol(h) + 64)
                        S.op("pe", lambda: PE.matmul(A2[rows, hc], YTc[rows, hc], Yc[rows, hc], start=True, stop=True), r=[kY, kYT], w=ka)
                S.op("act", lambda: A.activation(YTn[rows, :], B2[rows, :], AF.Copy), r=kb_, w=["cYT%d" % nxt + pq])
                if lvl < 4:
                    S.op("dve", lambda: V.tensor_copy(Yn[rows, :], A2[rows, :]), r=ka, w=["cY%d" % nxt + pq])
                yield
                for h in range(16):
                    hc = slice(hcol(h), hcol(h) + 64)
                    S.op("pe", lambda: PE.matmul(B2[rows, hc], YTn[rows, hc], Pc[rows, hc], start=True, stop=True), r=["cYT%d" % nxt + pq, kP], w=kb_)
                S.op("dve", lambda: V.tensor_tensor(Pn[rows, :], B2[rows, :], Pc[rows, :], ALU.add), r=kb_ + [kP], w=["cP%d" % nxt + pq])
                yield
                cur = nxt
            assert cur == 1

        def chunkB(j):
            hp = (j % 2) * 64; pq = "_%d" % (j % 2)
            s_, tok = j // 2, slice(j * 64, (j + 1) * 64)
            rows = slice(hp, hp + 64)
            TTt = cP[1]; kTT = "cP1" + pq
            C2 = PF[:, 2048:3072]; kc_ = ["pb4", "pb5"]
            for h in range(16):
                S.op("pe", lambda: PE.matmul(C2[rows, h * 64:(h + 1) * 64], cNak[rows, hcol(h):hcol(h) + 64], tmV[rows, s_, h * 64:(h + 1) * 64],
                                             start=True, stop=True), r=["cNak" + pq, "tmV"], w=kc_)
            S.op("act", lambda: A.activation(cT1[rows, :], C2[rows, :], AF.Copy), r=kc_, w=["cT1" + pq])
            yield
            for c in range(8):
                S.op("pe", lambda: PE.matmul(C2[rows, c * 128:(c + 1) * 128], At[:, c, tok], STb[:, c, :], start=True, stop=True),
                     r=["At%d" % c, "STb"], w=kc_)
            S.op("dve", lambda: V.tensor_tensor(cW[rows, :], C2[rows, :], cT1[rows, :], ALU.add), r=kc_ + ["cT1" + pq], w=["cW" + pq])
            yield
            for h in range(16):
                S.op("pe", lambda: PE.matmul(C2[rows, h * 64:(h + 1) * 64], TTt[rows, hcol(h):hcol(h) + 64], cW[rows, h * 64:(h + 1) * 64],
                                             start=True, stop=True), r=[kTT, "cW" + pq], w=kc_)
            S.op("act", lambda: A.activation(cU[rows, :], C2[rows, :], AF.Copy), r=kc_, w=["cU" + pq])
            yield
            for h in range(16):
                hs = slice(h * 64, (h + 1) * 64); hc = slice(hcol(h), hcol(h) + 64)
                S.op("pe", lambda: PE.matmul(C2[rows, hs], cMrb[rows, hc], cU[rows, hs], start=True, stop=False), r=["cMrb" + pq, "cU" + pq], w=kc_)
                S.op("pe", lambda: PE.matmul(C2[rows, hs], cMrk[rows, hc], tmV[rows, s_, hs], start=False, stop=True), r=["cMrk" + pq, "tmV"], w=kc_)
            S.op("act", lambda: A.activation(cT1[rows, :], C2[rows, :], AF.Copy), r=kc_, w=["cT1" + pq])
            yield
            for c in range(8):
                S.op("pe", lambda: PE.matmul(C2[rows, c * 128:(c + 1) * 128], Rt[:, c, tok], STb[:, c, :], start=True, stop=True),
                     r=["Rt%d" % c, "STb"], w=kc_)
            S.op("dve", lambda: V.tensor_tensor(cYs[rows, :], C2[rows, :], cT1[rows, :], ALU.add), r=kc_ + ["cT1" + pq], w=["cYs" + pq])
            yield
            SN = PF[:, 2048:2560]
            for h in range(16):
                c, hh = h // 2, h % 2
                hs = slice(h * 64, (h + 1) * 64)
                o = SN[hh * 64:hh * 64 + 64, c * 64:(c + 1) * 64]
                S.op("pe", lambda: PE.matmul(o, tmK[rows, s_, hs], tmV[rows, s_, hs], start=True, stop=False), r=["tmK", "tmV"], w=["pb4"])
                S.op("pe", lambda: PE.matmul(o, tmB[rows, s_, hs], cU[rows, hs], start=False, stop=True), r=["tmB", "cU" + pq], w=["pb4"])
            SN3 = SN.rearrange("p (c v) -> p c v", c=8)
            S.op("dve", lambda: V.tensor_tensor(ST[:], ST[:], SN3, ALU.add), r=["ST", "pb4"], w=["ST"])
            S.op("dve", lambda: V.tensor_tensor(ST[:], ST[:], gC[:, :, j:j + 1].broadcast_to([128, 8, 64]), ALU.mult), r=["ST", "gC"], w=["ST"])
            S.op("act", lambda: A.activation(STb[0:64, :, 0:64], ST[0:64, :, :], AF.Copy), r=["ST"], w=["STb"])
            S.op("act", lambda: A.activation(STb[64:128, :, 64:128], ST[64:128, :, :], AF.Copy), r=["ST"], w=["STb"])
            yield
            if j % 2 == 1:
                yield from epilogue((j // 2) * 128, 128)

        def att_block(p, n_):
            gblk = p * 2 + n_
            qs = slice(n_ * 128, (n_ + 1) * 128)
            sb_, sbk = bank(6), "pb6"
            ob, obk = bank(7), "pb7"
            for kv in range(4):
                half = (kv % 2) * 64; hr = slice(half, half + 64); kc2 = kv // 2
                kbs = [1] if gblk == 0 else [0, 1]
                for kb in kbs:
                    kcols = slice(n_ * 128 + kb * 128, n_ * 128 + kb * 128 + 128)
                    S.op("pe", lambda: PE.matmul(sb_[:, 0:512], kT[hr, kc2, kcols], qT[hr, (kv // 2) * 4:(kv // 2) * 4 + 4, qs],
                                                 start=True, stop=True), r=["kT", "qT"], w=[sbk])
                    for g in range(4):
                        h = 4 * kv + g
                        S.op("dve", lambda: V.scalar_tensor_tensor(a_sc[:, g * 128:(g + 1) * 128], Dm[:, kb * 128:(kb + 1) * 128],
                                                                   SLOPES[h] / SCALE, sb_[:, g * 128:(g + 1) * 128], ALU.mult, ALU.add),
                             r=[sbk, "cst"], w=["t_f1"])
                    S.op("act", lambda: A.activation(a_PT[kb][:, :], a_sc[:, :], AF.Exp, scale=SCALE), r=["t_f1"], w=["a_PT%d" % kb])
                    yield
                for g in range(4):
                    for ki, kb in enumerate(kbs):
                        S.op("pe", lambda: PE.matmul(ob[:, g * 66:(g + 1) * 66], a_PT[kb][:, g * 128:(g + 1) * 128], Vall[:, n_ + kb, kv, :],
                                                     start=(ki == 0), stop=(ki == len(kbs) - 1)),
                             r=["a_PT%d" % kb, "Vall%d" % (n_ + kb)], w=[obk])
                o3 = ob[:, 0:264].rearrange("p (g d) -> p g d", g=4)
                S.op("dve", lambda: V.tensor_tensor(a_den[:, :], o3[:, :, 64], esink[:, 4 * kv:4 * kv + 4], ALU.add), r=[obk, "esink"], w=["a_den"])
                S.op("dve", lambda: V.reciprocal(a_den[:, :], a_den[:, :]), r=["a_den"], w=["a_den"])
                S.op("dve", lambda: V.tensor_tensor(hb[:, n_, 1024 + kv * 256:1024 + (kv + 1) * 256].rearrange("p (g d) -> p g d", g=4),
                                                    o3[:, :, 0:64], a_den[:, :].unsqueeze(2).broadcast_to([128, 4, 64]), ALU.mult),
                     r=[obk, "a_den"], w=["hb%d" % n_])
                yield
            pt = PTb[:, 0:1024]; pk = "pb6"
            for c in range(8):
                S.op("pe", lambda: PE.transpose(pt[:, c * 128:(c + 1) * 128], hb[:, n_, 1024 + c * 128:1024 + (c + 1) * 128], identb),
                     r=["hb%d" % n_, "cstb"], w=[pk])
            for c in range(8):
                S.op("dve" if c % 2 else "act", (lambda: V.tensor_copy(hT[:, 8 + c, qs], pt[:, c * 128:(c + 1) * 128])) if c % 2 else
                     (lambda: A.activation(hT[:, 8 + c, qs], pt[:, c * 128:(c + 1) * 128], AF.Copy)), r=[pk], w=["hT%d" % (8 + c)])
            yield

        def interleave(gens):
            gens = list(gens)
            while gens:
                for g in list(gens):
                    try:
                        next(g)
                    except StopIteration:
                        gens.remove(g)

        def sample_mix():
            ONES = cst[:, C_ON:C_ON + 128]
            allk = lambda n: ["%s%d" % (n, c) for c in range(8)]
            wv = wkv_s.rearrange("b (c hh) k v -> b hh k c v", hh=2)
            sv = swkv.rearrange("b (c hh) k v -> b hh k c v", hh=2)
            bc8 = lambda ap, w_: ap.broadcast_to([128, 8, w_])
            c8 = lambda ap, w_: ap.rearrange("p (c v) -> p c v", c=8)
            W32 = chb.bitcast(F32)
            def mkset(b_sS, b_sSn, b_sBlk, b_sAbd, b_sRv, b_sT1):
                bfv = lambda base, n: base.bitcast(BF16)[:, 0:n]
                return dict(sS=c8(b_sS, 64), sSn=c8(b_sSn, 64), sT1=c8(b_sT1, 64),
                            sBlk=c8(bfv(b_sBlk, 1024), 128), sAbd=c8(bfv(b_sAbd, 1024), 128),
                            sSb=c8(b_sAbd[:, 512:768].bitcast(BF16), 64), sRv=c8(bfv(b_sRv, 512), 64))
            setA = mkset(x1[:, 0:512], x1[:, 512:1024], x1[:, 1024:2048], chf[:, 0:1024], chf[:, 1024:1536], chf[:, 1536:2048])
            setB = mkset(W32[:, 1792:2304], W32[:, 2304:2816], scr[:, 2048:3072], W32[:, 2816:3840], W32[:, 3840:4352], W32[:, 4352:4864])
            sets = [setA, setB]
            sKVb = [W32[:, 0:512], W32[:, 1152:1664]]
            sKTb = [W32[:, 4864:4992].bitcast(BF16).rearrange("p (k n) -> p k n", k=2), W32[:, 1664:1792].bitcast(BF16).rearrange("p (k n) -> p k n", k=2)]
            for i_ in range(2):
                S.op("dve", lambda: V.memset(sets[i_]["sBlk"], 0.0), w=["sBlk%d" % i_])
            for b in range(16):
                S.dma("sp", k_s[b, 0:127, :], ck[b, 1:128, :], w=["o_ks"])
                S.dma("sp", v_s[b, 0:127, :], cv[b, 1:128, :], w=["o_vs"])
            S.dma("sp", k_s[:, 127, :], ktm[0:16, 0:256], r=["ktm"], w=["o_ks"])
            S.dma("sp", v_s[:, 127, :], ktm[0:16, 256:512], r=["ktm"], w=["o_vs"])

            ckp(201)
            def rw_sample(b):
                i = b % 2; T_ = sets[i]; x = "%d" % i
                uB, vB = (4, 5) if i == 0 else (6, 7)
                for hh in range(2):
                    S.dma("sp", T_["sS"][hh * 64:(hh + 1) * 64, :, :], sv[b, hh], w=["sS" + x])
                S.op("dve", lambda: V.tensor_tensor(T_["sAbd"], BDf.unsqueeze(1).broadcast_to([128, 8, 128]), bc8(At[:, :, b:b + 1], 128), ALU.mult),
                     r=["cst"] + allk("At"), w=["sAbd" + x])
                S.op("dve", lambda: V.tensor_tensor(T_["sRv"], Ihalf.unsqueeze(1).broadcast_to([128, 8, 64]), bc8(vT[:, :, b:b + 1], 64), ALU.mult),
                     r=["cst"] + allk("vT"), w=["sRv" + x])
                yield
                S.op("act", lambda: A.activation(T_["sSb"], T_["sS"], AF.Copy), r=["sS" + x], w=["sSb" + x])
                for c in range(8):
                    S.op("pe", lambda: PE.matmul(bank(uB)[:, c * 64:(c + 1) * 64], T_["sAbd"][:, c, :], T_["sSb"][:, c, :], start=True, stop=True),
                         r=["sAbd" + x, "sSb" + x], w=["pb%d" % uB])
                for c in range(8):
                    S.op("pe", lambda: PE.matmul(bank(vB)[:, c * 64:(c + 1) * 64], BDb, T_["sRv"][:, c, :], start=True, stop=True),
                         r=["cstb", "sRv" + x], w=["pb%d" % vB])
                U3 = c8(bank(uB), 64); V3 = c8(bank(vB), 64)
                S.op("dve", lambda: V.tensor_tensor(T_["sSn"], T_["sS"], bc8(dS[:, :, b:b + 1], 64), ALU.mult), r=["sS" + x, "dS"], w=["sSn" + x])
                yield
                S.op("dve", lambda: V.tensor_tensor(T_["sT1"], U3, bc8(Bt[:, :, b:b + 1], 64), ALU.mult), r=["pb%d" % uB] + allk("Bt"), w=["sT1" + x])
                S.op("dve", lambda: V.tensor_tensor(T_["sSn"], T_["sSn"], T_["sT1"], ALU.add), r=["sSn" + x, "sT1" + x], w=["sSn" + x])
                S.op("dve", lambda: V.tensor_tensor(T_["sT1"], V3, bc8(Kt[:, :, b:b + 1], 64), ALU.mult), r=["pb%d" % vB] + allk("Kt"), w=["sT1" + x])
                S.op("dve", lambda: V.tensor_tensor(T_["sSn"], T_["sSn"], T_["sT1"], ALU.add), r=["sSn" + x, "sT1" + x], w=["sSn" + x])
                yield
                for hh in range(2):
                    S.dma("sp", wv[b, hh], T_["sSn"][hh * 64:(hh + 1) * 64, :, :], r=["sSn" + x], w=["o_wkvs"])
                S.op("act", lambda: A.activation(T_["sBlk"][0:64, :, 0:64], T_["sSn"][0:64, :, :], AF.Copy), r=["sSn" + x], w=["sBlk" + x])
                S.op("act", lambda: A.activation(T_["sBlk"][64:128, :, 64:128], T_["sSn"][64:128, :, :], AF.Copy), r=["sSn" + x], w=["sBlk" + x])
                for c in range(8):
                    S.op("pe", lambda: PE.matmul(bank(c // 4)[:, (c % 4) * 16 + b:(c % 4) * 16 + b + 1], T_["sBlk"][:, c, :], Rt[:, c, b:b + 1],
                                                 start=True, stop=True), r=["sBlk" + x, "Rt%d" % c], w=["pb%d" % (c // 4)])
                yield

            def at_sample(b):
                i = b % 2; x = "%d" % i
                kvb = sKVb[i]; Kn = kvb[:, 0:256]; Vn = kvb[:, 256:512]; KT = sKTb[i]
                tB, sB = (4, 5) if i == 0 else (6, 7)
                S.dma("sp", Kn[0:127, :], ck[b, 1:128, :], w=["sKa" + x])
                S.dma("sp", Vn[0:127, :], cv[b, 1:128, :], w=["sVa" + x])
                S.dma("sp", Kn[127:128, :], ktm[b:b + 1, 0:256], r=["ktm"], w=["sKb" + x])
                S.dma("sp", Vn[127:128, :], ktm[b:b + 1, 256:512], r=["ktm"], w=["sVb" + x])
                yield
                for k2 in range(2):
                    S.op("pe", lambda: PE.transpose(bank(tB)[:, k2 * 128:(k2 + 1) * 128], Kn[:, k2 * 128:(k2 + 1) * 128], ident),
                         r=["sKa" + x, "sKb" + x, "cst"], w=["pb%d" % tB])
                S.op("act", lambda: A.activation(KT[:, :, :], bank(tB)[:, 0:256].rearrange("p (k n) -> p k n", k=2), AF.Copy), r=["pb%d" % tB], w=["sKT" + x])
                yield
                for kv in range(4):
                    hr = slice((kv % 2) * 64, (kv % 2) * 64 + 64)
                    ob_ = sB if kv % 2 == 0 else tB
                    S.op("pe", lambda: PE.matmul(bank(ob_)[:, 256 + kv * 4:256 + kv * 4 + 4], KT[hr, kv // 2, :], qT[hr, (kv // 2) * 4:(kv // 2) * 4 + 4, b],
                                                 start=True, stop=True), r=["sKT" + x, "qT"], w=["pb%d" % ob_])
                ev_ = lambda ap, o: ap.rearrange("p (k two g) -> p k two g", k=2, two=2)[:, :, o, :]
                for o, ob_ in ((0, sB), (1, tB)):
                    S.op("dve", lambda: V.tensor_tensor(ev_(sPT[:, b, :], o), ev_(bank(ob_)[:, 256:272], o), ev_(sbias, o), ALU.add),
                         r=["pb%d" % ob_, "cst"], w=["sPT" + x])
                S.op("act", lambda: A.activation(sPT[:, b, :], sPT[:, b, :], AF.Exp, scale=SCALE), r=["sPT" + x], w=["sPT" + x])
                yield
                S.op("pe", lambda: PE.matmul(bank(sB)[:, 288:304], ONES, sPT[:, b, :], start=True, stop=True), r=["cst", "sPT" + x], w=["pb%d" % sB])
                S.op("dve", lambda: V.tensor_tensor(sPTn[:, b, :], bank(sB)[:, 288:304], esink[:, :], ALU.add), r=["pb%d" % sB, "esink"], w=["sPTn" + x])
                S.op("dve", lambda: V.reciprocal(sPTn[:, b, :], sPTn[:, b, :]), r=["sPTn" + x], w=["sPTn" + x])
                S.op("dve", lambda: V.tensor_tensor(sPTn[:, b, :], sPTn[:, b, :], sPT[:, b, :], ALU.mult), r=["sPTn" + x, "sPT" + x], w=["sPTn" + x])
                yield
                for h in range(16):
                    kv = h // 4
                    S.op("pe", lambda: PE.matmul(bank(3)[(h % 2) * 64:(h % 2) * 64 + 64, (h // 2) * 16 + b:(h // 2) * 16 + b + 1],
                                                 Vn[:, kv * 64:(kv + 1) * 64], sPTn[:, b, h:h + 1], start=True, stop=True),
                         r=["sVa" + x, "sVb" + x, "sPTn" + x], w=["pb3"])
                yield

            def pairs(fn):
                for b in range(0, 16, 2):
                    interleave([fn(b), fn(b + 1)])
            pairs(rw_sample)
            ckp(202)
            for _ in epilogue(0, 16, [(bank(0)[:, 0:64], "pb0"), (bank(1)[:, 0:64], "pb1")]):
                pass
            for ci in range(28):
                M = RW_M[ci]
                S.dma("sp", sh_s[RW_COL[ci]:RW_COL[ci] + M, :], zs[0:M, ci, :], r=["zs"], w=["o_shs"])
            ckp(203)
            pairs(at_sample)
            ckp(204)
            for c in range(8):
                S.op("act", lambda: A.activation(hT[:, 8 + c, 0:16], bank(3)[:, c * 16:(c + 1) * 16], AF.Copy), r=["pb3"], w=["hT%d" % (8 + c)])

        def tm_proj(sl, subs, ncols, ps_of, handler, ntok):
            for s, (t0, n) in enumerate(subs):
                ps, pkeys = ps_of(s)
                for kc in range(KC):
                    S.op("pe", lambda: PE.matmul(ps[0:n, 0:ncols], hT[:, kc, t0:t0 + n], wb[sl][:, kc, 0:ncols],
                                                 start=(kc == 0), stop=(kc == KC - 1)), r=["hT%d" % kc, "wb%d" % sl], w=pkeys)
                handler(s, t0, n, ps, pkeys)

        def resid_add(W, subs, nk_list=None):
            for cb in range(4):
                sl = wload(W, [(cb * 512, 512, 0)])
                def h(s, t0, n, ps, pkeys):
                    S.op("dve", lambda: V.tensor_tensor(x_sb[0:n, s, cb * 512:(cb + 1) * 512], x_sb[0:n, s, cb * 512:(cb + 1) * 512],
                                                        ps[0:n, 0:512], ALU.add), r=pkeys + ["x%d" % s], w=["x%d" % s])
                tm_proj(sl, subs, 512, lambda s: (bank(4 + (s % 2)), ["pb%d" % (4 + (s % 2))]), h, None)

        try:
            for p in range(NPASS + 1):
                sample = (p == NPASS)
                last = (p == NPASS - 1)
                if sample:
                    subs = [(0, 16)]; ntok = 16
                else:
                    subs = [(0, 128), (128, 128)]; ntok = TP
                tok0 = p * TP
                for s, (t0, n) in enumerate(subs):
                    src = xs_d[0:16, :] if sample else xp[tok0 + t0:tok0 + t0 + n, :]
                    S.dma("sp", x_sb[0:n, s, :], src, w=["x%d" % s])
                if sample:
                    for ci in range(28):
                        M = RW_M[ci]
                        S.dma("sp", shT[0:M, ci, :], sshT[RW_COL[ci]:RW_COL[ci] + M, :], w=["shT"])
                    S.op("dve", lambda: V.memset(msh[:], 0.0), w=["msh"])
                    for ci in range(28):
                        M = RW_M[ci]
                        S.op("dve", lambda: V.tensor_scalar(msh[0:M, ci, :], shT[0:M, ci, :], pv[0:M, PV_MU + ci:PV_MU + ci + 1], None, ALU.mult),
                             r=["shT", "pv"], w=["msh"])
                ckp(1)
                norm_T(subs, PV_GMIX, ntok)
                ckp(2)
                sl = wload(w_in, [(RW_COL[24 + i], RW_M[24 + i], i * 128) for i in range(4)])
                for i in range(4):
                    ci = 24 + i; M = RW_M[ci]; b_ = 4 + (i % 2)
                    proj_fm(sl, i * 128, M, bank(b_), ["pb%d" % b_], ntok)
                    tshift(ci, bank(b_), ["pb%d" % b_], M, ntok, sample, last, xs_o[0:M, 0:ntok], ["xs_o"])
                    if i == 0:
                        S.op("act", lambda: A.activation(txw[0:96, 0:ntok], xs_o[0:96, 0:ntok], AF.Tanh), r=["xs_o"], w=["txw"])
                    elif i == 1:
                        S.op("act", lambda: A.activation(xaT[0:96, 0:ntok], xs_o[0:96, 0:ntok], AF.Copy), r=["xs_o"], w=["xaT"])
                    else:
                        S.op("act", lambda: A.activation(sxg[:, i - 2, 0:ntok], xs_o[:, 0:ntok], AF.Sigmoid), r=["xs_o"], w=["sxg"])
                ckp(3)
                S.op("dve", lambda: V.memset(ss[0:1, 7:8], 0.0), w=["ystg0", "ystg1"])
                def PP(c):
                    sl_ = wload(w_in, [(c * 128, 128, 0), (1024 + c * 128, 128, 128), (2048 + c * 128, 128, 256)])
                    prep_P(c, sl_, ntok, sample, last)
                def TT(c):
                    prep_P(c, None, ntok, sample, last)
                PP(0); TT(0); PP(1); TT(1)
                for c in range(8):
                    prep_H(c, ntok, sample, last)
                    if c + 2 < 8:
                        PP(c + 2)
                    prep_tail(c, ntok)
                    if c + 2 < 8:
                        TT(c + 2)
                ckp(5)
                for half in range(2):
                    sl = wload(w_in, [(RWP + QPERM[half * 8 + i] * 64, 64, i * 64) for i in range(8)])
                    for i in range(4):
                        b_ = 4 + (i % 2)
                        proj_fm(sl, i * 128, 128, bank(b_), ["pb%d" % b_], ntok)
                        evac(qT[:, half * 4 + i, 0:ntok], bank(b_)[:, 0:ntok], r=["pb%d" % b_], w=["qT"])
                sl = wload(w_in, [(RWP + 1024, 512, 0)])
                for i in range(2):
                    b_ = 4 + i
                    proj_fm(sl, i * 128, 128, bank(b_), ["pb%d" % b_], ntok)
                    evac(kT[:, i, 128:128 + ntok], bank(b_)[:, 0:ntok], r=["pb%d" % b_], w=["kT"])
                def kvh(s, t0, n, ps, pkeys):
                    if sample:
                        S.op("act", lambda: A.activation(ktm[0:n, :], ps[0:n, 0:512], AF.Copy), r=pkeys, w=["ktm"])
                        return
                    S.op("act", lambda: A.activation(Vall[0:n, 1 + s, :, 0:64], ps[0:n, 256:512].rearrange("p (k d) -> p k d", k=4), AF.Copy),
                         r=pkeys, w=["Vall%d" % (1 + s)])
                    if last and s == 1:
                        S.op("dve", lambda: V.tensor_copy(ktm[:, :], ps[:, 0:512]), r=pkeys, w=["ktm"])
                        S.dma("sp", k_p[:, :], ktm[:, 0:256], r=["ktm"], w=["o_kp"])
                        S.dma("sp", v_p[:, :], ktm[:, 256:512], r=["ktm"], w=["o_vp"])
                tm_proj(sl, subs, 512, lambda s: (bank(4 + (s % 2)), ["pb%d" % (4 + (s % 2))]), kvh, ntok)

                for s, (t0, n) in enumerate(subs):
                    src = ps_d[0:16, :] if sample else pp[tok0 + t0:tok0 + t0 + n, :]
                    S.dma("sp", pe_sb[0:n, s, :], src, w=["pe_sb"])
                    S.op("dve", lambda: V.tensor_copy(pe_b[0:n, s, :], pe_sb[0:n, s, :]), r=["pe_sb"], w=["pe_b"])
                    pt_rr[0] ^= 1
                    pt = PTb[:, pt_rr[0] * 1024:(pt_rr[0] + 1) * 1024]; pk = "pb%d" % (6 + pt_rr[0])
                    for k2 in range(2):
                        S.op("pe", lambda: PE.transpose(pt[:, k2 * 128:k2 * 128 + n], pe_b[0:n, s, k2 * 128:(k2 + 1) * 128], identb[0:n, 0:n]),
                             r=["pe_b", "cstb"], w=[pk])
                    for k2 in range(2):
                        evac(pT_[:, k2, t0:t0 + n], pt[:, k2 * 128:k2 * 128 + n], r=[pk], w=["pT_"])
                ckp(6)
                if not sample:
                    def side_all():
                        for arr, dst, an, dn in ((vT, tmV, "vT", "tmV"), (Kt, tmK, "Kt", "tmK"), (Bt, tmB, "Bt", "tmB")):
                            for s in range(2):
                                pt_rr[0] ^= 1
                                pt = PTb[:, pt_rr[0] * 1024:(pt_rr[0] + 1) * 1024]; pk = "pb%d" % (6 + pt_rr[0])
                                for c in range(8):
                                    S.op("pe", lambda: PE.transpose(pt[:, c * 128:(c + 1) * 128], arr[:, c, s * 128:(s + 1) * 128], identb),
                                         r=["%s%d" % (an, c), "cstb"], w=[pk])
                                evac(dst[:, s, :], pt[:, :], r=[pk], w=[dn])
                                yield
                        for n_ in range(2):
                            yield from att_block(p, n_)
                    att = side_all()
                    att_done = False
                    for j in range(-1, 4):
                        live = ([chunkB(j)] if j >= 0 else []) + ([chunkA(j + 1)] if j < 3 else [])
                        while live:
                            for g in list(live):
                                try:
                                    next(g)
                                except StopIteration:
                                    live.remove(g)
                            if not att_done:
                                try:
                                    next(att)
                                except StopIteration:
                                    att_done = True
                    if not att_done:
                        interleave([att])
                    if last:
                        S.dma("sp", wkv_p.rearrange("(c hh) k v -> hh k c v", hh=2)[0], ST[0:64, :, :], r=["ST"], w=["o_wkvp"])
                        S.dma("sp", wkv_p.rearrange("(c hh) k v -> hh k c v", hh=2)[1], ST[64:128, :, :], r=["ST"], w=["o_wkvp"])
                        for ci in range(28):
                            M = RW_M[ci]
                            S.dma("sp", sh_p[RW_COL[ci]:RW_COL[ci] + M, :], zlast[0:M, ci:ci + 1], r=["zlast"], w=["o_shp"])
                    S.op("act", lambda: A.activation(kT[:, :, 0:128], kT[:, :, TP:TP + 128], AF.Copy), r=["kT"], w=["kT"])
                    S.op("act", lambda: A.activation(Vall[:, 0, :, 0:64], Vall[:, 2, :, 0:64], AF.Copy), r=["Vall2"], w=["Vall0"])
                else:
                    sample_mix()
                ckp(10)
                resid_add(w_out, subs)
                ckp(11)
                norm_T(subs, PV_GFFN, ntok)
                for fb in range(FC // 2):
                    slg = wload(None, [(fb * 256, 256, 0, w_gate), (fb * 256, 256, 256, w_up)])
                    for i in range(2):
                        f = fb * 2 + i
                        proj_fm(slg, i * 128, 128, bank(0 + (f % 2)), ["pb%d" % (f % 2)], ntok)
                        proj_fm(slg, 256 + i * 128, 128, bank(2 + (f % 2)), ["pb%d" % (2 + (f % 2))], ntok)
                        S.op("act", lambda: A.activation(t_f1[:, 0:ntok], bank(f % 2)[:, 0:ntok], AF.Silu), r=["pb%d" % (f % 2)], w=["t_f1"])
                        S.op("dve", lambda: V.tensor_tensor(hid[:, f, 0:ntok], t_f1[:, 0:ntok], bank(2 + (f % 2))[:, 0:ntok], ALU.mult),
                             r=["t_f1", "pb%d" % (2 + (f % 2))], w=["hid"])
                for cb in range(4):
                    parts = [(0, 16), (16, 16), (32, 12)]
                    for pi, (k0, nk) in enumerate(parts):
                        sl = wload(w_down, [(cb * 512, 512, 0)], nk=nk, row0=k0 * 128)
                        for s, (t0, n) in enumerate(subs):
                            for kk_ in range(nk):
                                S.op("pe", lambda: PE.matmul(bank(4 + s)[0:n, 0:512], hid[:, k0 + kk_, t0:t0 + n], wb[sl][:, kk_, 0:512],
                                                             start=(pi == 0 and kk_ == 0), stop=(pi == 2 and kk_ == nk - 1)),
                                     r=["hid", "wb%d" % sl], w=["pb%d" % (4 + s)])
                    for s, (t0, n) in enumerate(subs):
                        S.op("dve", lambda: V.tensor_tensor(x_sb[0:n, s, cb * 512:(cb + 1) * 512], x_sb[0:n, s, cb * 512:(cb + 1) * 512],
                                                            bank(4 + s)[0:n, 0:512], ALU.add), r=["pb%d" % (4 + s), "x%d" % s], w=["x%d" % s])
                ckp(12)
                norm_T(subs, PV_GPLE, ntok)
                for cb in range(4):
                    sl = wload(ple_gate, [(cb * 512, 512, 0)])
                    S.dma("pool", wpp[:, :, :], ple_proj[:, cb * 512:(cb + 1) * 512].rearrange("(k p) n -> p k n", p=128), w=["wpp"])
                    for s, (t0, n) in enumerate(subs):
                        gb, pb_ = bank(4 + (s % 2)), bank(2 + (s % 2))
                        gk, pk2 = "pb%d" % (4 + (s % 2)), "pb%d" % (2 + (s % 2))
                        for kc in range(KC):
                            S.op("pe", lambda: PE.matmul(gb[0:n, 0:512], hT[:, kc, t0:t0 + n], wb[sl][:, kc, 0:512], start=(kc == 0), stop=(kc == KC - 1)),
                                 r=["hT%d" % kc, "wb%d" % sl], w=[gk])
                        for k2 in range(2):
                            S.op("pe", lambda: PE.matmul(pb_[0:n, 0:512], pT_[:, k2, t0:t0 + n], wpp[:, k2, :], start=(k2 == 0), stop=(k2 == 1)),
                                 r=["pT_", "wpp"], w=[pk2])
                        S.op("act", lambda: A.activation(t_f1[0:n, :], gb[0:n, 0:512], AF.Sigmoid), r=[gk], w=["t_f1"])
                        S.op("dve", lambda: V.tensor_tensor(t_f1[0:n, :], t_f1[0:n, :], pb_[0:n, 0:512], ALU.mult), r=["t_f1", pk2], w=["t_f1"])
                        S.op("dve", lambda: V.tensor_tensor(x_sb[0:n, s, cb * 512:(cb + 1) * 512], x_sb[0:n, s, cb * 512:(cb + 1) * 512], t_f1[0:n, :], ALU.add),
                             r=["t_f1", "x%d" % s], w=["x%d" % s])
                ckp(13)
                for s, (t0, n) in enumerate(subs):
                    S.op("act", lambda: A.activation(hb[0:n, s, :], x_sb[0:n, s, :], AF.Square, accum_out=ss[0:n, s:s + 1]), r=["x%d" % s], w=["hb%d" % s, "ss"])
                    S.op("act", lambda: A.activation(sd[0:n, s:s + 1], ss[0:n, s:s + 1], AF.Sqrt, bias=EPS, scale=1.0 / D), r=["ss"], w=["sd"])
                    S.op("dve", lambda: V.reciprocal(rstd[0:n, s:s + 1], sd[0:n, s:s + 1]), r=["sd"], w=["rstd%d" % s])
                    S.op("dve", lambda: V.scalar_tensor_tensor(ystg[0:n, s, :], x_sb[0:n, s, :], rstd[0:n, s:s + 1], fn_bc[0:n, :], ALU.mult, ALU.mult),
                         r=["x%d" % s, "rstd%d" % s, "fn_bc"], w=["ystg%d" % s, "hid"])
                    dst = y_s[0:16, :] if sample else y_p[tok0 + t0:tok0 + t0 + n, :]
                    S.dma("sp", dst, ystg[0:n, s, :], r=["ystg%d" % s], w=["o_y"])
                assert wblk[0] == NBLK * (p + 1), wblk[0]
                ckp(14 + p)
        except _Stop:
            pass
        S.finish("sp")
        print("build stats: waits", S.nwait, "dmas", S.ndma, "seq", dict(S.seq), flush=True)
    return nc


def _consts():
    c = np.zeros((128, C_N), np.float32)
    c[:, C_ID:C_ID + 128] = np.eye(128)
    p = np.arange(128)
    c[:, C_BD:C_BD + 128] = (p[:, None] // 64 == p[None, :] // 64)
    c[:, C_IH:C_IH + 64] = (p[:, None] % 64 == np.arange(64)[None, :])
    c[:, C_HSEL:C_HSEL + 2] = (p[:, None] // 64 == np.arange(2)[None, :])
    s_ = (p % 64)[:, None]; t_ = np.arange(64)[None, :]
    c[:, C_MS:C_MS + 64] = (s_ < t_)
    c[:, C_MI:C_MI + 64] = (s_ <= t_)
    c[:, C_MST:C_MST + 64] = (s_ > t_)
    c[:, C_RST:C_RST + 256] = (np.arange(256) % 64 != 0)[None, :]
    j = p[:, None].astype(np.float32); i = np.arange(128)[None, :].astype(np.float32)
    BIG = 1e5
    c[:, C_DM:C_DM + 128] = np.where(j > i, -(i + 128 - j), -BIG)
    c[:, C_DM + 128:C_DM + 256] = np.where(j <= i, -(i - j), -BIG)
    sl = np.array(SLOPES, np.float32)[None, :]
    c[:, C_SB:C_SB + 16] = -sl * (127 - j) / SCALE
    c[:, C_ON:C_ON + 128] = 1.0
    return c


def _pv(inp):
    pv = np.zeros((128, PV_N), np.float32)
    f = lambda v, n: np.ascontiguousarray(np.asarray(v, np.float32).reshape(n, 128).T)
    pv[:, PV_GMIX:PV_GMIX + 16] = f(inp["norm_mix"][0], 16)
    pv[:, PV_GFFN:PV_GFFN + 16] = f(inp["norm_ffn"][0], 16)
    pv[:, PV_GPLE:PV_GPLE + 16] = f(inp["norm_ple"][0], 16)
    mu = np.asarray(inp["mu_shift"][0], np.float32)
    for ci in range(28):
        pv[0:RW_M[ci], PV_MU + ci] = mu[RW_COL[ci]:RW_COL[ci] + RW_M[ci]]
    pv[:, PV_W0:PV_W0 + 8] = f(inp["rwkv_w0"][0], 8)
    pv[:, PV_A0:PV_A0 + 8] = f(inp["rwkv_a0"][0], 8)
    pv[:, PV_KK:PV_KK + 8] = f(inp["rwkv_k_k"][0], 8)
    pv[:, PV_KA:PV_KA + 8] = f(inp["rwkv_k_a"][0], 8)
    pv[:, PV_RK:PV_RK + 8] = f(np.asarray(inp["rwkv_r_k"][0]).reshape(-1), 8)
    pv[:, PV_LNW:PV_LNW + 8] = f(inp["rwkv_ln_w"][0], 8)
    pv[:, PV_LNB:PV_LNB + 8] = f(inp["rwkv_ln_b"][0], 8)
    return pv


_NC = [None]


def kernel(**inp):
    A_ = lambda k: np.ascontiguousarray(np.asarray(inp[k], np.float32))
    if _NC[0] is None:
        _NC[0] = build_program()
    nc = _NC[0]
    shared = {
        "w_in": A_("w_in")[0], "w_out": A_("w_out")[0], "w_gate": A_("w_gate")[0], "w_up": A_("w_up")[0],
        "w_down": A_("w_down")[0], "ple_gate": A_("ple_gate")[0], "ple_proj": A_("ple_proj")[0],
        "w2": A_("rwkv_w2")[0], "a2": A_("rwkv_a2")[0], "g2": A_("rwkv_g2")[0],
        "pv": _pv(inp), "cst": _consts(), "fng": A_("final_norm").reshape(1, D),
        "sinks": A_("attn_sinks").reshape(1, 16),
    }
    xp, xs = A_("x_prompt"), A_("x_sample")
    pp, ps = A_("p_prompt"), A_("p_sample")
    swkv, ssh = A_("state_wkv"), A_("state_shift")
    ck, cv = A_("cache_k"), A_("cache_v")
    in_maps = []
    for i in range(8):
        b = slice(16 * i, 16 * i + 16)
        m = dict(shared)
        m.update({
            "xp": xp[i], "xs": np.ascontiguousarray(xs[b, 0]), "pp": pp[0, i], "psm": np.ascontiguousarray(ps[0, b, 0]),
            "swkv": np.ascontiguousarray(np.swapaxes(swkv[0, b], -1, -2)),
            "sshT": np.ascontiguousarray(ssh[0, b].T),
            "ck": np.ascontiguousarray(ck[0, b].reshape(16, 128, 256)), "cv": np.ascontiguousarray(cv[0, b].reshape(16, 128, 256)),
        })
        in_maps.append(m)
    res = run_bass_kernel_spmd(nc, in_maps, core_ids=list(range(8)))
    R = res.results
    cat = lambda k: np.stack([np.asarray(r[k], np.float32) for r in R])
    y_p = cat("y_p")
    y_s = cat("y_s").reshape(128, 1, D)
    wkv_p = np.swapaxes(cat("wkv_p"), -1, -2)[None]
    sh_p = cat("sh_p").reshape(8, RWP)[None]
    k_p = cat("k_p").reshape(8, 128, 4, 64)[None]
    v_p = cat("v_p").reshape(8, 128, 4, 64)[None]
    wkv_s = np.swapaxes(cat("wkv_s").reshape(128, 16, 64, 64), -1, -2)[None]
    sh_s = np.swapaxes(cat("sh_s"), 1, 2).reshape(128, RWP)[None]
    k_s = cat("k_s").reshape(128, 128, 4, 64)[None]
    v_s = cat("v_s").reshape(128, 128, 4, 64)[None]
    c_ = np.ascontiguousarray
    return (c_(y_p), c_(y_s), c_(wkv_p), c_(sh_p), c_(k_p), c_(v_p), c_(wkv_s), c_(sh_s), c_(k_s), c_(v_s))
```

```python
import bisect
import concourse.bass as bass
import concourse.mybir as mybir

F32 = mybir.dt.float32
BF16 = mybir.dt.bfloat16
AF = mybir.ActivationFunctionType
ALU = mybir.AluOpType
AX = mybir.AxisListType

SEM_ROLL = 12000


class Sched:
    def __init__(self, nc, stack):
        self.nc = nc
        self.stack = stack
        self.engs = {"pe": nc.tensor, "dve": nc.vector, "act": nc.scalar,
                     "pool": nc.gpsimd, "sp": nc.sync}
        self.sem = {}
        self.cnt = {}
        self.seq = {}
        self.sigs = {}
        self.sigseq = {}
        self.last_ins = {}
        self.last_seq = {}
        self.nsem = 0
        for e in self.engs:
            self._new_sem(e)
            self.seq[e] = 0
            self.sigs[e] = []
            self.sigseq[e] = []
            self.last_ins[e] = None
        self.semobj = {}
        for e in self.engs:
            pass
        self.seen = {e: {} for e in self.engs}
        self.lastw = {}
        self.readers = {}
        self.dma_sems = {}
        self.dma_rr = {}
        self.ndma = 0
        self.nwait = 0

    def _new_sem(self, e):
        name = "s_%s_%d" % (e, self.nsem)
        self.nsem += 1
        s = self.stack.enter_context(self.nc.semaphore(name))
        self.sem[e] = (name, s)
        self.cnt[e] = 0
        if not hasattr(self, "semh"):
            self.semh = {}
        self.semh[name] = s

    def _resolve(self, tok):
        if tok[0] == "d":
            return tok[1], tok[2]
        _, e, q = tok
        i = bisect.bisect_left(self.sigseq[e], q)
        if i < len(self.sigseq[e]):
            _, sn, v = self.sigs[e][i]
            return sn, v
        ins = self.last_ins[e]
        assert ins is not None
        if self.cnt[e] >= SEM_ROLL:
            self._new_sem(e)
        sn, s = self.sem[e]
        self.cnt[e] += 1
        ins.then_inc(s, 1)
        self.sigs[e].append((self.last_seq[e], sn, self.cnt[e]))
        self.sigseq[e].append(self.last_seq[e])
        return sn, self.cnt[e]

    def _wait(self, e, toks):
        need = {}
        for t in toks:
            sn, v = self._resolve(t)
            if self.seen[e].get(sn, 0) >= v:
                continue
            if need.get(sn, 0) < v:
                need[sn] = v
        for sn, v in need.items():
            self.engs[e].wait_ge(self.semh[sn], v)
            self.seen[e][sn] = v
            self.nwait += 1

    def _deps(self, e, r, w):
        toks = []
        for k in r:
            t = self.lastw.get(k)
            if t is not None and not (e == "pe" and t[0] == "c" and t[1] == "pe"):
                toks.append(t)
            if k.startswith("pb") or k.startswith("pt"):
                for t in self.readers.get(k, ()):
                    if t[0] == "c" and t[1] != e:
                        toks.append(t)
        for k in w:
            t = self.lastw.get(k)
            if t is not None and not (e == "pe" and t[0] == "c" and t[1] == "pe"):
                toks.append(t)
            for t in self.readers.get(k, ()):
                toks.append(t)
        return toks

    def _record(self, tok, r, w):
        for k in r:
            self.readers.setdefault(k, []).append(tok)
        for k in w:
            self.lastw[k] = tok
            self.readers[k] = []

    def op(self, e, fn, r=(), w=()):
        self._wait(e, self._deps(e, r, w))
        ins = fn()
        self.seq[e] += 1
        self.last_ins[e] = ins
        self.last_seq[e] = self.seq[e]
        self._record(("c", e, self.seq[e]), r, w)
        return ins

    def dma(self, q, out, in_, r=(), w=(), **kw):
        ring = self.dma_sems.setdefault(q, [])
        if len(ring) < 8:
            name = "d_%s_%d" % (q, len(ring))
            s = self.stack.enter_context(self.nc.semaphore(name))
            self.semh[name] = s
            ring.append([name, 0])
            slot = ring[-1]
        else:
            i = self.dma_rr.get(q, 0)
            self.dma_rr[q] = (i + 1) % len(ring)
            slot = ring[i]
        toks = self._deps(q, r, w)
        if slot[1] > 0:
            toks.append(("d", slot[0], slot[1]))
        self._wait(q, toks)
        slot[1] += 16
        ins = self.engs[q].dma_start(out=out, in_=in_, **kw)
        ins.then_inc(self.semh[slot[0]], 16)
        self.seq[q] += 1
        self._record(("d", slot[0], slot[1]), r, w)
        self.ndma += 1
        return ins

    def finish(self, e="sp"):
        toks = []
        for k, t in self.lastw.items():
            toks.append(t)
        self._wait(e, toks)

import contextlib
import numpy as np
from concourse.bass_utils import run_bass_kernel_spmd

D = 2048
KC = 16
DFF = 5632
FC = 44
PROJ = 5056
RWP = 3520
TP = 256
NPASS = 2048 // TP
NBLK = 54
EPS = 1e-6
GN_EPS = 64e-5
SCALE = 0.125
DEC = 0.6065306597126334
SLOPES = [2.0 ** (-8.0 * (h + 1) / 16) for h in range(16)]
QPERM = [0, 4, 1, 5, 2, 6, 3, 7, 8, 12, 9, 13, 10, 14, 11, 15]
PV_GMIX, PV_GFFN, PV_GPLE, PV_MU = 0, 16, 32, 48
PV_W0, PV_A0, PV_KK, PV_KA, PV_RK, PV_LNW, PV_LNB = 76, 84, 92, 100, 108, 116, 124
PV_N = 132
C_ID, C_BD, C_IH, C_HSEL, C_MS, C_MI, C_MST, C_RST, C_DM, C_SB, C_ON, C_N = 0, 128, 256, 320, 322, 386, 450, 514, 770, 1026, 1042, 1170

RW_COL = [c * 128 for c in range(8)] + [1024 + c * 128 for c in range(8)] + \
         [2048 + c * 128 for c in range(8)] + [3072, 3168, 3264, 3392]
RW_M = [128] * 24 + [96, 96, 128, 128]


def hcol(h):
    return ((h % 2) * 8 + h // 2) * 64


class _Stop(Exception):
    pass


def build_program(debug=False, stop=None):
    nc = bass.Bass("TRN2", target_bir_lowering=False)
    din = lambda n, sh: nc.dram_tensor(n, sh, F32, kind="ExternalInput").ap()
    dout = lambda n, sh: nc.dram_tensor(n, sh, F32, kind="ExternalOutput").ap()
    xp = din("xp", [2048, D]); xs_d = din("xs", [16, D])
    pp = din("pp", [2048, 256]); ps_d = din("psm", [16, 256])
    swkv = din("swkv", [16, 16, 64, 64])
    sshT = din("sshT", [RWP, 16])
    ck = din("ck", [16, 128, 256]); cv = din("cv", [16, 128, 256])
    w_in = din("w_in", [D, PROJ]); w_out = din("w_out", [D, D])
    w_gate = din("w_gate", [D, DFF]); w_up = din("w_up", [D, DFF]); w_down = din("w_down", [DFF, D])
    ple_gate = din("ple_gate", [D, D]); ple_proj = din("ple_proj", [256, D])
    w2 = din("w2", [96, 1024]); a2 = din("a2", [96, 1024]); g2 = din("g2", [256, 1024])
    pv_d = din("pv", [128, PV_N]); cst_d = din("cst", [128, C_N])
    fng = din("fng", [1, D]); esk = din("sinks", [1, 16])

    wsc = nc.dram_tensor("wsc", [NBLK, 128, KC * 512], BF16, kind="Internal").ap()
    y_p = dout("y_p", [2048, D]); y_s = dout("y_s", [16, D])
    wkv_p = dout("wkv_p", [16, 64, 64])
    sh_p = dout("sh_p", [RWP, 1])
    k_p = dout("k_p", [128, 256]); v_p = dout("v_p", [128, 256])
    wkv_s = dout("wkv_s", [16, 16, 64, 64])
    sh_s = dout("sh_s", [RWP, 16])
    k_s = dout("k_s", [16, 128, 256]); v_s = dout("v_s", [16, 128, 256])

    with contextlib.ExitStack() as st:
        S = Sched(nc, st)
        sb = lambda n, sh, dt=F32: st.enter_context(nc.sbuf_tensor("sb_" + n, sh, dt))
        V, A, P, PE = nc.vector, nc.scalar, nc.gpsimd, nc.tensor
        PF = st.enter_context(nc.psum_tensor("PF", [128, 4096], F32))
        PTb = PF[:, 3072:4096].bitcast(BF16)
        pbk = lambda b: "pb%d" % b
        bank = lambda b: PF[:, b * 512:(b + 1) * 512]

        x_sb = sb("x_sb", [128, 2, D])
        hb = sb("hb", [128, 2, D], BF16)
        hT = sb("hT", [128, KC, TP], BF16)
        wb = [sb("wb%d" % i, [128, KC, 512], BF16) for i in range(2)]
        pv = sb("pv", [128, PV_N]); cst = sb("cst", [128, C_N])
        omu = sb("omu", [128, 28]); omka = sb("omka", [128, 8])
        cstb = sb("cstb", [128, 384], BF16)
        fn_bc = sb("fn_bc", [128, D])
        esink = sb("esink", [128, 16])
        w2b = sb("w2b", [128, 1024], BF16); a2b = sb("a2b", [128, 1024], BF16)
        g2b = sb("g2b", [128, 2, 1024], BF16)
        ss = sb("ss", [128, 8]); sd = sb("sd", [128, 8]); rstd = sb("rstd", [128, 8])
        rwbig = sb("rwbig", [128, 6 * 8 * TP], BF16)
        def rwv(i):
            return rwbig[:, i * 8 * TP:(i + 1) * 8 * TP].rearrange("p (c t) -> p c t", c=8)
        Rt, Kt, At, Bt, vT, ybT = [rwv(i) for i in range(6)]
        hid = rwbig[:, 0:FC * TP].rearrange("p (c t) -> p c t", c=FC)
        ystg = rwbig.bitcast(F32)[:, 0:2 * D].rearrange("p (s d) -> p s d", s=2)
        gT = sb("gT", [128, 8, TP], BF16)
        dS = sb("dS", [128, 8, 16])
        tmV = sb("tmV", [128, 2, 1024], BF16); tmK = sb("tmK", [128, 2, 1024], BF16)
        tmB = sb("tmB", [128, 2, 1024], BF16)
        gC = sb("gC", [128, 8, 4])
        carry = sb("carry", [128, 28]); zlast = sb("zlast", [128, 28]); zs = sb("zs", [128, 28, 16])
        shT = sb("shT", [128, 28, 16]); msh = sb("msh", [128, 28, 16])
        txw = sb("txw", [128, TP], BF16); xaT = sb("xaT", [128, TP], BF16)
        sxg = sb("sxg", [128, 2, TP], BF16)
        tmr = [sb("tmr%d" % i, [128, TP + 1]) for i in range(3)]
        scr = sb("scr", [128, 13 * TP])
        def scv(i):
            return scr[:, i * TP:(i + 1) * TP]
        xs_r2 = sb("xs_r2", [128, TP]); xs_k2 = sb("xs_k2", [128, TP])
        xs_r, xs_k, xs_o, t_sg, t_L, t_eL, t_enL, t_eLm, t_a, t_kkr, t_sq, t_b, t_kh = [scv(i) for i in range(13)]
        t_rkr = sb("t_rkr", [128, TP], BF16)
        xs_rb = [xs_r, xs_r2]; xs_kb = [xs_k, xs_k2]
        def epv(i):
            return scr[:, i * 512:(i + 1) * 512].rearrange("p (c t) -> p c t", c=4)
        yT_sb, e_sq, e_m, e_v = [epv(i) for i in range(4)]
        chb = sb("chb", [128, 11 * 1024], BF16)
        def chv(i):
            return chb[:, i * 1024:(i + 1) * 1024]
        cY = [chv(0), chv(1)]; cYT = [chv(2), chv(3)]; cP = [chv(4), chv(5)]
        cNak, cMrb, cMrk, cW, cU = chv(6), chv(7), chv(8), chv(9), chv(10)
        chf = sb("chf", [128, 2048])
        cT1 = chf[:, 0:1024]; cYs = chf[:, 1024:2048]
        ST = sb("ST", [128, 8, 64]); STb = sb("STb", [128, 8, 128], BF16)
        qT = sb("qT", [128, 8, TP], BF16); kT = sb("kT", [128, 2, 128 + TP], BF16)
        Vall = sb("Vall", [128, 3, 4, 66], BF16)
        t_f1 = sb("t_f1", [128, 512]); t_f2 = sb("t_f2", [128, 512])
        a_sc = t_f1; ktm = t_f2
        a_PT = [sb("a_PT%d" % i, [128, 512], BF16) for i in range(2)]
        a_den = sb("a_den", [128, 4])
        wpp = sb("wpp", [128, 2, 512], BF16)
        pT_ = sb("pT_", [128, 2, TP], BF16); pe_sb = sb("pe_sb", [128, 2, 256]); pe_b = sb("pe_b", [128, 2, 256], BF16)
        x1 = x_sb[:, 1, :]
        sS = x1[:, 0:512].rearrange("p (c v) -> p c v", c=8)
        sSn = x1[:, 512:1024].rearrange("p (c v) -> p c v", c=8)
        sBlk = x1[:, 1024:2048].rearrange("p (c v) -> p c v", c=8)
        sAbd = chf[:, 0:1024].rearrange("p (c v) -> p c v", c=8)
        sRv = chf[:, 1024:1536].rearrange("p (c v) -> p c v", c=8)
        sT1 = chf[:, 1536:2048].rearrange("p (c v) -> p c v", c=8)
        chb32 = chb[:, 0:4096].bitcast(F32)
        sKV = chb32[:, 0:512]; sKn = sKV[:, 0:256]; sVn = sKV[:, 256:512]
        sPT = chb32[:, 512:768].rearrange("p (b h) -> p b h", b=16)
        sPTn = chb32[:, 768:1024].rearrange("p (b h) -> p b h", b=16)
        vf = chb32[:, 1024:1152].rearrange("p (c b) -> p c b", c=8)
        sKT = chb[:, 4096:4352].rearrange("p (k n) -> p k n", k=2)

        ident = cst[:, C_ID:C_ID + 128]; BDf = cst[:, C_BD:C_BD + 128]
        identb = cstb[:, 0:128]; BDb = cstb[:, 128:256]
        mS = cst[:, C_MS:C_MS + 64]; mI = cst[:, C_MI:C_MI + 64]; mST = cst[:, C_MST:C_MST + 64]
        rst = cst[:, C_RST:C_RST + 256]
        Dm = cst[:, C_DM:C_DM + 256]
        sbias = cst[:, C_SB:C_SB + 16]
        Ihalf = cst[:, C_IH:C_IH + 64]

        S.dma("sp", pv[:], pv_d[:, :], w=["pv"])
        S.dma("sp", cst[:], cst_d[:, :], w=["cst"])
        S.dma("sp", fn_bc[:], fng.partition_broadcast(128), w=["fn_bc"])
        S.dma("sp", esink[:], esk.partition_broadcast(128), w=["esink"])
        S.dma("pool", w2b[0:96, :], w2[:, :], w=["w2b"])
        S.dma("pool", a2b[0:96, :], a2[:, :], w=["a2b"])
        S.dma("pool", g2b[:], g2.rearrange("(k p) n -> p k n", p=128), w=["g2b"])
        S.op("dve", lambda: V.tensor_copy(cstb[:, 0:256], cst[:, 0:256]), r=["cst"], w=["cstb"])
        S.op("dve", lambda: V.tensor_scalar(omu[:], pv[:, PV_MU:PV_MU + 28], -1.0, 1.0, ALU.mult, ALU.add), r=["pv"], w=["omu"])
        S.op("dve", lambda: V.tensor_scalar(omka[:], pv[:, PV_KA:PV_KA + 8], -1.0, 1.0, ALU.mult, ALU.add), r=["pv"], w=["omka"])
        S.op("act", lambda: A.activation(esink[:], esink[:], AF.Exp), r=["esink"], w=["esink"])
        S.op("pool", lambda: P.memset(carry[:], 0.0), w=["carry%d" % i for i in range(28)])
        S.op("pool", lambda: P.memset(ST[:], 0.0), w=["ST"])
        S.op("pool", lambda: P.memset(STb[:], 0.0), w=["STb"])
        S.op("pool", lambda: P.memset(Vall[:], 1.0), w=["Vall0", "Vall1", "Vall2"])
        S.op("pool", lambda: P.memset(kT[:], 0.0), w=["kT"])
        for i_ in range(2):
            S.op("pool", lambda: P.memset(wb[i_][:], 0.0), w=["wb%d" % i_])

        def ckp(k):
            if stop == k:
                raise _Stop()
        wslot = [0]

        wblk = [0]

        def wload(W, specs, nk=KC, row0=0):
            sl = wslot[0]; wslot[0] ^= 1
            bi = wblk[0] % NBLK; first = wblk[0] < NBLK; wblk[0] += 1
            if first:
                for sp_ in specs:
                    c0, n, d0 = sp_[0:3]
                    Wm = sp_[3] if len(sp_) > 3 else W
                    src = Wm[row0:row0 + nk * 128, c0:c0 + n].rearrange("(k p) n -> p k n", p=128)
                    S.dma("pool", wb[sl][:, 0:nk, d0:d0 + n], src, w=["wb%d" % sl])
                S.dma("sp", wsc[bi], wb[sl][:, :, :].rearrange("p k n -> p (k n)"), r=["wb%d" % sl], w=["wsc%d" % bi])
            else:
                S.dma("pool", wb[sl][:, :, :].rearrange("p k n -> p (k n)"), wsc[bi], r=["wsc%d" % bi], w=["wb%d" % sl])
            return sl

        ev_rr = [0]

        def evac(out, in_, r, w, scale=None):
            ev_rr[0] ^= 1
            if ev_rr[0]:
                if scale is None:
                    S.op("act", lambda: A.activation(out, in_, AF.Copy), r=r, w=w)
                else:
                    S.op("act", lambda: A.activation(out, in_, AF.Copy, scale=scale), r=r, w=w)
            else:
                if scale is None:
                    S.op("dve", lambda: V.tensor_copy(out, in_), r=r, w=w)
                else:
                    S.op("dve", lambda: V.tensor_scalar(out, in_, scale, None, ALU.mult), r=r, w=w)

        pt_rr = [0]

        def norm_T(subs, gcol, ntok):
            for s, (t0, n) in enumerate(subs):
                S.op("act", lambda: A.activation(hb[0:n, s, :], x_sb[0:n, s, :], AF.Square, accum_out=ss[0:n, s:s + 1]),
                     r=["x%d" % s], w=["hb%d" % s, "ss"])
                S.op("act", lambda: A.activation(sd[0:n, s:s + 1], ss[0:n, s:s + 1], AF.Sqrt, bias=EPS, scale=1.0 / D),
                     r=["ss"], w=["sd"])
                S.op("dve", lambda: V.reciprocal(rstd[0:n, s:s + 1], sd[0:n, s:s + 1]), r=["sd"], w=["rstd%d" % s])
                S.op("dve", lambda: V.tensor_scalar(hb[0:n, s, :], x_sb[0:n, s, :], rstd[0:n, s:s + 1], None, ALU.mult),
                     r=["x%d" % s, "rstd%d" % s], w=["hb%d" % s])
            for kc in range(KC):
                pt_rr[0] ^= 1
                pt = PTb[:, pt_rr[0] * 1024:(pt_rr[0] + 1) * 1024]; pk = "pb%d" % (6 + pt_rr[0])
                for s, (t0, n) in enumerate(subs):
                    S.op("pe", lambda: PE.transpose(pt[:, t0:t0 + n], hb[0:n, s, kc * 128:(kc + 1) * 128], identb[0:n, 0:n]),
                         r=["hb%d" % s, "cstb"], w=[pk])
                evac(hT[:, kc, 0:ntok], pt[:, 0:ntok], r=[pk, "pv"], w=["hT%d" % kc], scale=pv[:, gcol + kc:gcol + kc + 1])

        def proj_fm(sl, d0, M, ps, pkeys, ntok):
            for kc in range(KC):
                S.op("pe", lambda: PE.matmul(ps[0:M, 0:ntok], wb[sl][:, kc, d0:d0 + M], hT[:, kc, 0:ntok],
                                             start=(kc == 0), stop=(kc == KC - 1)),
                     r=["wb%d" % sl, "hT%d" % kc], w=pkeys)

        def tshift(ci, ps, pkeys, M, ntok, sample, last, out_ap, okeys):
            mu = pv[0:M, PV_MU + ci:PV_MU + ci + 1]
            if sample:
                S.op("act", lambda: A.activation(zs[0:M, ci, :], ps[0:M, 0:16], AF.Copy), r=pkeys, w=["zs"])
                S.op("dve", lambda: V.scalar_tensor_tensor(out_ap, ps[0:M, 0:16], omu[0:M, ci:ci + 1], msh[0:M, ci, :],
                                                           ALU.mult, ALU.add), r=pkeys + ["omu", "msh"], w=okeys)
                return
            tm = tmr[ci % 3]; tk = "tmr%d" % (ci % 3)
            S.op("act", lambda: A.activation(tm[0:M, 0:1], carry[0:M, ci:ci + 1], AF.Copy), r=["carry%d" % ci], w=[tk])
            S.op("act", lambda: A.activation(tm[0:M, 1:1 + ntok], ps[0:M, 0:ntok], AF.Copy, scale=mu), r=pkeys + ["pv"], w=[tk])
            S.op("act", lambda: A.activation(carry[0:M, ci:ci + 1], tm[0:M, ntok:ntok + 1], AF.Copy), r=[tk], w=["carry%d" % ci])
            if last:
                S.op("act", lambda: A.activation(zlast[0:M, ci:ci + 1], ps[0:M, ntok - 1:ntok], AF.Copy), r=pkeys, w=["zlast"])
            S.op("dve", lambda: V.scalar_tensor_tensor(out_ap, ps[0:M, 0:ntok], omu[0:M, ci:ci + 1], tm[0:M, 0:ntok],
                                                       ALU.mult, ALU.add), r=pkeys + ["omu", tk], w=okeys)

        def prep_P(c, sl, ntok, sample, last):
            N = ntok
            b0 = 0 if c % 2 == 0 else 3
            pr, pk_, pvv = bank(b0), bank(b0 + 1), bank(b0 + 2)
            kr_, kk_, kv_ = ["pb%d" % (b0 + i) for i in range(3)]
            xs_r, xs_k = xs_rb[c % 2], xs_kb[c % 2]
            xr_k, xk_k = "xs_r%d" % (c % 2), "xs_k%d" % (c % 2)
            if sl is not None:
                proj_fm(sl, 0, 128, pr, [kr_], N)
                proj_fm(sl, 128, 128, pk_, [kk_], N)
                proj_fm(sl, 256, 128, pvv, [kv_], N)
                return
            tshift(c, pr, [kr_], 128, N, sample, last, xs_r[:, 0:N], [xr_k])
            tshift(8 + c, pk_, [kk_], 128, N, sample, last, xs_k[:, 0:N], [xk_k])
            tshift(16 + c, pvv, [kv_], 128, N, sample, last, vT[:, c, 0:N], ["vT%d" % c])

        def prep_H(c, ntok, sample, last):
            N = ntok
            xs_r, xs_k = xs_rb[c % 2], xs_kb[c % 2]
            xr_k, xk_k = "xs_r%d" % (c % 2), "xs_k%d" % (c % 2)
            cs = slice(c * 128, (c + 1) * 128)
            pc = lambda o: pv[:, o + c:o + c + 1]
            S.op("pe", lambda: PE.matmul(bank(6)[:, 0:N], w2b[0:96, cs], txw[0:96, 0:N], start=True, stop=True),
                 r=["w2b", "txw"], w=["pb6"])
            S.op("pe", lambda: PE.matmul(bank(7)[:, 0:N], a2b[0:96, cs], xaT[0:96, 0:N], start=True, stop=True),
                 r=["a2b", "xaT"], w=["pb7"])
            S.op("dve", lambda: V.tensor_scalar(t_kkr[:, 0:N], xs_k[:, 0:N], pc(PV_KK), None, ALU.mult), r=[xk_k, "pv"], w=["t_kkr"])
            S.op("act", lambda: A.activation(t_sg[:, 0:N], bank(6)[:, 0:N], AF.Sigmoid, bias=pc(PV_W0)), r=["pb6", "pv"], w=["t_sg"])
            S.op("act", lambda: A.activation(t_a[:, 0:N], bank(7)[:, 0:N], AF.Sigmoid, bias=pc(PV_A0)), r=["pb7", "pv"], w=["t_a"])
            S.op("act", lambda: A.activation(t_sq[:, 0:N], t_kkr[:, 0:N], AF.Square), r=["t_kkr"], w=["t_sq"])
            for k2 in range(2):
                S.op("pe", lambda: PE.matmul(bank(6)[:, 0:N], g2b[:, k2, cs], sxg[:, k2, 0:N], start=(k2 == 0), stop=(k2 == 1)),
                     r=["g2b", "sxg"], w=["pb6"])
            S.op("pe", lambda: PE.matmul(bank(7)[:, 0:N], BDf, t_sq[:, 0:N], start=True, stop=True), r=["cst", "t_sq"], w=["pb7"])
            S.op("dve", lambda: V.tensor_scalar(t_sg[:, 0:N], t_sg[:, 0:N], -DEC, None, ALU.mult), r=["t_sg"], w=["t_sg"])
            if sample:
                S.op("act", lambda: A.activation(dS[:, c, :], t_sg[:, 0:N], AF.Exp), r=["t_sg"], w=["dS"])
            else:
                S.op("dve", lambda: V.tensor_tensor_scan(t_L[:, 0:N], rst[:, 0:N], t_sg[:, 0:N], 0.0, ALU.mult, ALU.add),
                     r=["t_sg", "cst"], w=["t_L"])
                S.op("act", lambda: A.activation(t_eL[:, 0:N], t_L[:, 0:N], AF.Exp), r=["t_L"], w=["t_eL"])
                S.op("act", lambda: A.activation(t_enL[:, 0:N], t_L[:, 0:N], AF.Exp, scale=-1.0), r=["t_L"], w=["t_enL"])
                eLv = t_eL[:, 0:N].rearrange("p (j t) -> p j t", t=64)
                eLmv = t_eLm[:, 0:N].rearrange("p (j t) -> p j t", t=64)
                S.op("dve", lambda: V.memset(eLmv[:, :, 0:1], 1.0), w=["t_eLm"])
            S.op("dve", lambda: V.tensor_scalar(t_sq[:, 0:N], bank(7)[:, 0:N], 1e-24, None, ALU.max), r=["pb7"], w=["t_sq"])
            S.op("act", lambda: A.activation(t_sq[:, 0:N], t_sq[:, 0:N], AF.Ln), r=["t_sq"], w=["t_sq"])
            S.op("act", lambda: A.activation(t_sq[:, 0:N], t_sq[:, 0:N], AF.Exp, scale=-0.5), r=["t_sq"], w=["t_sq"])
            S.op("act", lambda: A.activation(gT[:, c, 0:N], bank(6)[:, 0:N], AF.Copy), r=["pb6"], w=["gT%d" % c])
            if not sample:
                S.op("dve", lambda: V.tensor_copy(eLmv[:, :, 1:64], eLv[:, :, 0:63]), r=["t_eL"], w=["t_eLm"])
                S.op("act", lambda: A.activation(gC[:, c, :], eLv[:, :, 63], AF.Copy), r=["t_eL"], w=["gC"])
                S.op("dve", lambda: V.tensor_tensor(Rt[:, c, 0:N], xs_r[:, 0:N], t_eL[:, 0:N], ALU.mult), r=[xr_k, "t_eL"], w=["Rt%d" % c])
            else:
                S.op("dve", lambda: V.tensor_copy(Rt[:, c, 0:N], xs_r[:, 0:N]), r=[xr_k], w=["Rt%d" % c])
            S.op("dve", lambda: V.tensor_tensor(t_kkr[:, 0:N], t_kkr[:, 0:N], t_sq[:, 0:N], ALU.mult), r=["t_kkr", "t_sq"], w=["t_kkr"])
            S.op("dve", lambda: V.tensor_tensor(t_b[:, 0:N], t_kkr[:, 0:N], t_a[:, 0:N], ALU.mult), r=["t_kkr", "t_a"], w=["t_b"])
            S.op("dve", lambda: V.tensor_scalar(t_a[:, 0:N], t_a[:, 0:N], pc(PV_KA), omka[:, c:c + 1], ALU.mult, ALU.add),
                 r=["t_a", "pv", "omka"], w=["t_a"])
            S.op("dve", lambda: V.tensor_tensor(t_kh[:, 0:N], xs_k[:, 0:N], t_a[:, 0:N], ALU.mult), r=[xk_k, "t_a"], w=["t_kh"])
            if sample:
                S.op("dve", lambda: V.tensor_scalar(At[:, c, 0:N], t_kkr[:, 0:N], -1.0, None, ALU.mult), r=["t_kkr"], w=["At%d" % c])
                S.op("dve", lambda: V.tensor_copy(Bt[:, c, 0:N], t_b[:, 0:N]), r=["t_b"], w=["Bt%d" % c])
                S.op("dve", lambda: V.tensor_copy(Kt[:, c, 0:N], t_kh[:, 0:N]), r=["t_kh"], w=["Kt%d" % c])
            else:
                S.op("dve", lambda: V.scalar_tensor_tensor(At[:, c, 0:N], t_kkr[:, 0:N], -1.0, t_eLm[:, 0:N], ALU.mult, ALU.mult),
                     r=["t_kkr", "t_eLm"], w=["At%d" % c])
                S.op("dve", lambda: V.tensor_tensor(Bt[:, c, 0:N], t_b[:, 0:N], t_enL[:, 0:N], ALU.mult), r=["t_b", "t_enL"], w=["Bt%d" % c])
                S.op("dve", lambda: V.tensor_tensor(Kt[:, c, 0:N], t_kh[:, 0:N], t_enL[:, 0:N], ALU.mult), r=["t_kh", "t_enL"], w=["Kt%d" % c])

        def prep_tail(c, ntok):
            N = ntok
            xs_r = xs_rb[c % 2]; xr_k = "xs_r%d" % (c % 2)
            pc = lambda o: pv[:, o + c:o + c + 1]
            S.op("dve", lambda: V.scalar_tensor_tensor(t_rkr[:, 0:N], xs_r[:, 0:N], pc(PV_RK), t_kh[:, 0:N], ALU.mult, ALU.mult),
                 r=[xr_k, "t_kh", "pv"], w=["t_rkr"])
            S.op("pe", lambda: PE.matmul(bank(6)[:, 0:N], BDb, t_rkr[:, 0:N], start=True, stop=True), r=["cstb", "t_rkr"], w=["pb6"])
            S.op("dve", lambda: V.tensor_tensor(ybT[:, c, 0:N], bank(6)[:, 0:N], vT[:, c, 0:N], ALU.mult), r=["pb6", "vT%d" % c], w=["ybT%d" % c])

        def epilogue(col0, n, srcs=None):
            for hf in range(2):
                if srcs is None:
                    for c4 in range(4):
                        c = hf * 4 + c4
                        S.op("pe", lambda: PE.transpose(bank(4)[:, c4 * 128:(c4 + 1) * 128], cYs[:, c * 128:(c + 1) * 128], ident),
                             r=["cYs_0", "cYs_1", "cst"], w=["pb4"])
                    src_ap, sk = bank(4)[:, 0:4 * n], "pb4"
                else:
                    src_ap, sk = srcs[hf]
                src = src_ap.rearrange("p (c t) -> p c t", c=4)
                yv = yT_sb[:, :, 0:n]; sq = e_sq[:, :, 0:n]; em = e_m[:, :, 0:n]; ev = e_v[:, :, 0:n]
                S.op("act", lambda: A.activation(yv, src, AF.Copy), r=[sk], w=["yT_sb"])
                S.op("act", lambda: A.activation(sq, src, AF.Square), r=[sk], w=["e_sq"])
                for c4 in range(4):
                    S.op("pe", lambda: PE.matmul(bank(5)[:, c4 * n:(c4 + 1) * n], BDf, yT_sb[:, c4, 0:n], start=True, stop=True),
                         r=["cst", "yT_sb"], w=["pb5"])
                m_ps = bank(5)[:, 0:4 * n].rearrange("p (c t) -> p c t", c=4)
                S.op("act", lambda: A.activation(em, m_ps, AF.Copy, scale=1.0 / 64), r=["pb5"], w=["e_m"])
                yield
                for c4 in range(4):
                    S.op("pe", lambda: PE.matmul(bank(5)[:, c4 * n:(c4 + 1) * n], BDf, e_sq[:, c4, 0:n], start=True, stop=True),
                         r=["cst", "e_sq"], w=["pb5"])
                S.op("dve", lambda: V.tensor_tensor(ev, em, em, ALU.mult), r=["e_m"], w=["e_v"])
                S.op("dve", lambda: V.scalar_tensor_tensor(ev, m_ps, 1.0 / 64, ev, ALU.mult, ALU.subtract), r=["pb5", "e_v"], w=["e_v"])
                S.op("act", lambda: A.activation(ev, ev, AF.Sqrt, bias=GN_EPS), r=["e_v"], w=["e_v"])
                S.op("dve", lambda: V.reciprocal(ev, ev), r=["e_v"], w=["e_v"])
                S.op("dve", lambda: V.tensor_tensor(yv, yv, em, ALU.subtract), r=["yT_sb", "e_m"], w=["yT_sb"])
                S.op("dve", lambda: V.tensor_tensor(yv, yv, ev, ALU.mult), r=["yT_sb", "e_v"], w=["yT_sb"])
                yield
                for c4 in range(4):
                    c = hf * 4 + c4
                    S.op("dve", lambda: V.tensor_scalar(yT_sb[:, c4, 0:n], yT_sb[:, c4, 0:n], pv[:, PV_LNW + c:PV_LNW + c + 1],
                                                        pv[:, PV_LNB + c:PV_LNB + c + 1], ALU.mult, ALU.add), r=["yT_sb", "pv"], w=["yT_sb"])
                    S.op("dve", lambda: V.tensor_tensor(yT_sb[:, c4, 0:n], yT_sb[:, c4, 0:n], ybT[:, c, col0:col0 + n], ALU.add),
                         r=["yT_sb", "ybT%d" % c], w=["yT_sb"])
                    S.op("dve", lambda: V.tensor_tensor(hT[:, c, col0:col0 + n], yT_sb[:, c4, 0:n], gT[:, c, col0:col0 + n], ALU.mult),
                         r=["yT_sb", "gT%d" % c], w=["hT%d" % c])
                yield

        def chunkA(j):
            hp = (j % 2) * 64; pq = "_%d" % (j % 2)
            tok = slice(j * 64, (j + 1) * 64)
            rows = slice(hp, hp + 64)
            A2 = PF[:, 0:1024]; B2 = PF[:, 1024:2048]
            ka, kb_ = ["pb0", "pb1"], ["pb2", "pb3"]
            def hmm(dst2, lh, rh, lh_n, rh_n, keys):
                for h in range(16):
                    c, hh = h // 2, h % 2
                    kr = slice(hh * 64, hh * 64 + 64)
                    S.op("pe", lambda: PE.matmul(dst2[rows, hcol(h):hcol(h) + 64], lh[kr, c, tok], rh[kr, c, tok], start=True, stop=True),
                         r=["%s%d" % (lh_n, c), "%s%d" % (rh_n, c)], w=keys)
            m3 = lambda m: m[rows, :].unsqueeze(1).broadcast_to([64, 16, 64])
            v3 = lambda t: t[rows, :].rearrange("p (h t) -> p h t", h=16)
            hmm(A2, Bt, At, "Bt", "At", ka)
            hmm(B2, At, Bt, "At", "Bt", kb_)
            S.op("dve", lambda: V.tensor_tensor(v3(cY[0]), v3(A2), m3(mS), ALU.mult), r=ka + ["cst"], w=["cY0" + pq])
            S.op("dve", lambda: V.tensor_tensor(v3(cYT[0]), v3(B2), m3(mST), ALU.mult), r=kb_ + ["cst"], w=["cYT0" + pq])
            yield
            hmm(A2, Kt, At, "Kt", "At", ka)
            S.op("dve", lambda: V.tensor_tensor(v3(cNak), v3(A2), m3(mS), ALU.mult), r=ka + ["cst"], w=["cNak" + pq])
            hmm(B2, Bt, Rt, "Bt", "Rt", kb_)
            S.op("dve", lambda: V.tensor_tensor(v3(cMrb), v3(B2), m3(mI), ALU.mult), r=kb_ + ["cst"], w=["cMrb" + pq])
            yield
            hmm(A2, Kt, Rt, "Kt", "Rt", ka)
            S.op("dve", lambda: V.tensor_tensor(v3(cMrk), v3(A2), m3(mI), ALU.mult), r=ka + ["cst"], w=["cMrk" + pq])
            idb3 = identb[rows, hp:hp + 64].unsqueeze(1).broadcast_to([64, 16, 64])
            S.op("dve", lambda: V.tensor_tensor(v3(cP[0]), v3(cY[0]), idb3, ALU.add), r=["cY0" + pq, "cstb"], w=["cP0" + pq])
            yield
            cur = 0
            for lvl in range(5):
                nxt = cur ^ 1
                Yc, YTc, Pc = cY[cur], cYT[cur], cP[cur]
                Yn, YTn, Pn = cY[nxt], cYT[nxt], cP[nxt]
                kY, kYT, kP = "cY%d" % cur + pq, "cYT%d" % cur + pq, "cP%d" % cur + pq
                for h in range(16):
                    hc = slice(hcol(h), hcol(h) + 64)
                    S.op("pe", lambda: PE.matmul(B2[rows, hc], Yc[rows, hc], YTc[rows, hc], start=True, stop=True), r=[kY, kYT], w=kb_)
                if lvl < 4:
                    for h in range(16):
                        hc = slice(hcol(h), hcol(h) + 64)
                        S.op("pe", lambda: PE.matmul(A2[rows, hc], YTc[rows, hc], Yc[rows, hc], start=True, stop=True), r=[kY, kYT], w=ka)
                S.op("act", lambda: A.activation(YTn[rows, :], B2[rows, :], AF.Copy), r=kb_, w=["cYT%d" % nxt + pq])
                if lvl < 4:
                    S.op("dve", lambda: V.tensor_copy(Yn[rows, :], A2[rows, :]), r=ka, w=["cY%d" % nxt + pq])
                yield
                for h in range(16):
                    hc = slice(hcol(h), hcol(h) + 64)
                    S.op("pe", lambda: PE.matmul(B2[rows, hc], YTn[rows, hc], Pc[rows, hc], start=True, stop=True), r=["cYT%d" % nxt + pq, kP], w=kb_)
                S.op("dve", lambda: V.tensor_tensor(Pn[rows, :], B2[rows, :], Pc[rows, :], ALU.add), r=kb_ + [kP], w=["cP%d" % nxt + pq])
                yield
                cur = nxt
            assert cur == 1

        def chunkB(j):
            hp = (j % 2) * 64; pq = "_%d" % (j % 2)
            s_, tok = j // 2, slice(j * 64, (j + 1) * 64)
            rows = slice(hp, hp + 64)
            TTt = cP[1]; kTT = "cP1" + pq
            C2 = PF[:, 2048:3072]; kc_ = ["pb4", "pb5"]
            for h in range(16):
                S.op("pe", lambda: PE.matmul(C2[rows, h * 64:(h + 1) * 64], cNak[rows, hcol(h):hcol(h) + 64], tmV[rows, s_, h * 64:(h + 1) * 64],
                                             start=True, stop=True), r=["cNak" + pq, "tmV"], w=kc_)
            S.op("act", lambda: A.activation(cT1[rows, :], C2[rows, :], AF.Copy), r=kc_, w=["cT1" + pq])
            yield
            for c in range(8):
                S.op("pe", lambda: PE.matmul(C2[rows, c * 128:(c + 1) * 128], At[:, c, tok], STb[:, c, :], start=True, stop=True),
                     r=["At%d" % c, "STb"], w=kc_)
            S.op("dve", lambda: V.tensor_tensor(cW[rows, :], C2[rows, :], cT1[rows, :], ALU.add), r=kc_ + ["cT1" + pq], w=["cW" + pq])
            yield
            for h in range(16):
                S.op("pe", lambda: PE.matmul(C2[rows, h * 64:(h + 1) * 64], TTt[rows, hcol(h):hcol(h) + 64], cW[rows, h * 64:(h + 1) * 64],
                                             start=True, stop=True), r=[kTT, "cW" + pq], w=kc_)
            S.op("act", lambda: A.activation(cU[rows, :], C2[rows, :], AF.Copy), r=kc_, w=["cU" + pq])
            yield
            for h in range(16):
                hs = slice(h * 64, (h + 1) * 64); hc = slice(hcol(h), hcol(h) + 64)
                S.op("pe", lambda: PE.matmul(C2[rows, hs], cMrb[rows, hc], cU[rows, hs], start=True, stop=False), r=["cMrb" + pq, "cU" + pq], w=kc_)
                S.op("pe", lambda: PE.matmul(C2[rows, hs], cMrk[rows, hc], tmV[rows, s_, hs], start=False, stop=True), r=["cMrk" + pq, "tmV"], w=kc_)
            S.op("act", lambda: A.activation(cT1[rows, :], C2[rows, :], AF.Copy), r=kc_, w=["cT1" + pq])
            yield
            for c in range(8):
                S.op("pe", lambda: PE.matmul(C2[rows, c * 128:(c + 1) * 128], Rt[:, c, tok], STb[:, c, :], start=True, stop=True),
                     r=["Rt%d" % c, "STb"], w=kc_)
            S.op("dve", lambda: V.tensor_tensor(cYs[rows, :], C2[rows, :], cT1[rows, :], ALU.add), r=kc_ + ["cT1" + pq], w=["cYs" + pq])
            yield
            SN = PF[:, 2048:2560]
            for h in range(16):
                c, hh = h // 2, h % 2
                hs = slice(h * 64, (h + 1) * 64)
                o = SN[hh * 64:hh * 64 + 64, c * 64:(c + 1) * 64]
                S.op("pe", lambda: PE.matmul(o, tmK[rows, s_, hs], tmV[rows, s_, hs], start=True, stop=False), r=["tmK", "tmV"], w=["pb4"])
                S.op("pe", lambda: PE.matmul(o, tmB[rows, s_, hs], cU[rows, hs], start=False, stop=True), r=["tmB", "cU" + pq], w=["pb4"])
            SN3 = SN.rearrange("p (c v) -> p c v", c=8)
            S.op("dve", lambda: V.tensor_tensor(ST[:], ST[:], SN3, ALU.add), r=["ST", "pb4"], w=["ST"])
            S.op("dve", lambda: V.tensor_tensor(ST[:], ST[:], gC[:, :, j:j + 1].broadcast_to([128, 8, 64]), ALU.mult), r=["ST", "gC"], w=["ST"])
            S.op("act", lambda: A.activation(STb[0:64, :, 0:64], ST[0:64, :, :], AF.Copy), r=["ST"], w=["STb"])
            S.op("act", lambda: A.activation(STb[64:128, :, 64:128], ST[64:128, :, :], AF.Copy), r=["ST"], w=["STb"])
            yield
            if j % 2 == 1:
                yield from epilogue((j // 2) * 128, 128)

        def att_block(p, n_):
            gblk = p * 2 + n_
            qs = slice(n_ * 128, (n_ + 1) * 128)
            sb_, sbk = bank(6), "pb6"
            ob, obk = bank(7), "pb7"
            for kv in range(4):
                half = (kv % 2) * 64; hr = slice(half, half + 64); kc2 = kv // 2
                kbs = [1] if gblk == 0 else [0, 1]
                for kb in kbs:
                    kcols = slice(n_ * 128 + kb * 128, n_ * 128 + kb * 128 + 128)
                    S.op("pe", lambda: PE.matmul(sb_[:, 0:512], kT[hr, kc2, kcols], qT[hr, (kv // 2) * 4:(kv // 2) * 4 + 4, qs],
                                                 start=True, stop=True), r=["kT", "qT"], w=[sbk])
                    for g in range(4):
                        h = 4 * kv + g
                        S.op("dve", lambda: V.scalar_tensor_tensor(a_sc[:, g * 128:(g + 1) * 128], Dm[:, kb * 128:(kb + 1) * 128],
                                                                   SLOPES[h] / SCALE, sb_[:, g * 128:(g + 1) * 128], ALU.mult, ALU.add),
                             r=[sbk, "cst"], w=["t_f1"])
                    S.op("act", lambda: A.activation(a_PT[kb][:, :], a_sc[:, :], AF.Exp, scale=SCALE), r=["t_f1"], w=["a_PT%d" % kb])
                    yield
                for g in range(4):
                    for ki, kb in enumerate(kbs):
                        S.op("pe", lambda: PE.matmul(ob[:, g * 66:(g + 1) * 66], a_PT[kb][:, g * 128:(g + 1) * 128], Vall[:, n_ + kb, kv, :],
                                                     start=(ki == 0), stop=(ki == len(kbs) - 1)),
                             r=["a_PT%d" % kb, "Vall%d" % (n_ + kb)], w=[obk])
                o3 = ob[:, 0:264].rearrange("p (g d) -> p g d", g=4)
                S.op("dve", lambda: V.tensor_tensor(a_den[:, :], o3[:, :, 64], esink[:, 4 * kv:4 * kv + 4], ALU.add), r=[obk, "esink"], w=["a_den"])
                S.op("dve", lambda: V.reciprocal(a_den[:, :], a_den[:, :]), r=["a_den"], w=["a_den"])
                S.op("dve", lambda: V.tensor_tensor(hb[:, n_, 1024 + kv * 256:1024 + (kv + 1) * 256].rearrange("p (g d) -> p g d", g=4),
                                                    o3[:, :, 0:64], a_den[:, :].unsqueeze(2).broadcast_to([128, 4, 64]), ALU.mult),
                     r=[obk, "a_den"], w=["hb%d" % n_])
                yield
            pt = PTb[:, 0:1024]; pk = "pb6"
            for c in range(8):
                S.op("pe", lambda: PE.transpose(pt[:, c * 128:(c + 1) * 128], hb[:, n_, 1024 + c * 128:1024 + (c + 1) * 128], identb),
                     r=["hb%d" % n_, "cstb"], w=[pk])
            for c in range(8):
                S.op("dve" if c % 2 else "act", (lambda: V.tensor_copy(hT[:, 8 + c, qs], pt[:, c * 128:(c + 1) * 128])) if c % 2 else
                     (lambda: A.activation(hT[:, 8 + c, qs], pt[:, c * 128:(c + 1) * 128], AF.Copy)), r=[pk], w=["hT%d" % (8 + c)])
            yield

        def interleave(gens):
            gens = list(gens)
            while gens:
                for g in list(gens):
                    try:
                        next(g)
                    except StopIteration:
                        gens.remove(g)

        def sample_mix():
            ONES = cst[:, C_ON:C_ON + 128]
            allk = lambda n: ["%s%d" % (n, c) for c in range(8)]
            wv = wkv_s.rearrange("b (c hh) k v -> b hh k c v", hh=2)
            sv = swkv.rearrange("b (c hh) k v -> b hh k c v", hh=2)
            bc8 = lambda ap, w_: ap.broadcast_to([128, 8, w_])
            c8 = lambda ap, w_: ap.rearrange("p (c v) -> p c v", c=8)
            W32 = chb.bitcast(F32)
            def mkset(b_sS, b_sSn, b_sBlk, b_sAbd, b_sRv, b_sT1):
                bfv = lambda base, n: base.bitcast(BF16)[:, 0:n]
                return dict(sS=c8(b_sS, 64), sSn=c8(b_sSn, 64), sT1=c8(b_sT1, 64),
                            sBlk=c8(bfv(b_sBlk, 1024), 128), sAbd=c8(bfv(b_sAbd, 1024), 128),
                            sSb=c8(b_sAbd[:, 512:768].bitcast(BF16), 64), sRv=c8(bfv(b_sRv, 512), 64))
            setA = mkset(x1[:, 0:512], x1[:, 512:1024], x1[:, 1024:2048], chf[:, 0:1024], chf[:, 1024:1536], chf[:, 1536:2048])
            setB = mkset(W32[:, 1792:2304], W32[:, 2304:2816], scr[:, 2048:3072], W32[:, 2816:3840], W32[:, 3840:4352], W32[:, 4352:4864])
            sets = [setA, setB]
            sKVb = [W32[:, 0:512], W32[:, 1152:1664]]
            sKTb = [W32[:, 4864:4992].bitcast(BF16).rearrange("p (k n) -> p k n", k=2), W32[:, 1664:1792].bitcast(BF16).rearrange("p (k n) -> p k n", k=2)]
            for i_ in range(2):
                S.op("dve", lambda: V.memset(sets[i_]["sBlk"], 0.0), w=["sBlk%d" % i_])
            for b in range(16):
                S.dma("sp", k_s[b, 0:127, :], ck[b, 1:128, :], w=["o_ks"])
                S.dma("sp", v_s[b, 0:127, :], cv[b, 1:128, :], w=["o_vs"])
            S.dma("sp", k_s[:, 127, :], ktm[0:16, 0:256], r=["ktm"], w=["o_ks"])
            S.dma("sp", v_s[:, 127, :], ktm[0:16, 256:512], r=["ktm"], w=["o_vs"])

            ckp(201)
            def rw_sample(b):
                i = b % 2; T_ = sets[i]; x = "%d" % i
                uB, vB = (4, 5) if i == 0 else (6, 7)
                for hh in range(2):
                    S.dma("sp", T_["sS"][hh * 64:(hh + 1) * 64, :, :], sv[b, hh], w=["sS" + x])
                S.op("dve", lambda: V.tensor_tensor(T_["sAbd"], BDf.unsqueeze(1).broadcast_to([128, 8, 128]), bc8(At[:, :, b:b + 1], 128), ALU.mult),
                     r=["cst"] + allk("At"), w=["sAbd" + x])
                S.op("dve", lambda: V.tensor_tensor(T_["sRv"], Ihalf.unsqueeze(1).broadcast_to([128, 8, 64]), bc8(vT[:, :, b:b + 1], 64), ALU.mult),
                     r=["cst"] + allk("vT"), w=["sRv" + x])
                yield
                S.op("act", lambda: A.activation(T_["sSb"], T_["sS"], AF.Copy), r=["sS" + x], w=["sSb" + x])
                for c in range(8):
                    S.op("pe", lambda: PE.matmul(bank(uB)[:, c * 64:(c + 1) * 64], T_["sAbd"][:, c, :], T_["sSb"][:, c, :], start=True, stop=True),
                         r=["sAbd" + x, "sSb" + x], w=["pb%d" % uB])
                for c in range(8):
                    S.op("pe", lambda: PE.matmul(bank(vB)[:, c * 64:(c + 1) * 64], BDb, T_["sRv"][:, c, :], start=True, stop=True),
                         r=["cstb", "sRv" + x], w=["pb%d" % vB])
                U3 = c8(bank(uB), 64); V3 = c8(bank(vB), 64)
                S.op("dve", lambda: V.tensor_tensor(T_["sSn"], T_["sS"], bc8(dS[:, :, b:b + 1], 64), ALU.mult), r=["sS" + x, "dS"], w=["sSn" + x])
                yield
                S.op("dve", lambda: V.tensor_tensor(T_["sT1"], U3, bc8(Bt[:, :, b:b + 1], 64), ALU.mult), r=["pb%d" % uB] + allk("Bt"), w=["sT1" + x])
                S.op("dve", lambda: V.tensor_tensor(T_["sSn"], T_["sSn"], T_["sT1"], ALU.add), r=["sSn" + x, "sT1" + x], w=["sSn" + x])
                S.op("dve", lambda: V.tensor_tensor(T_["sT1"], V3, bc8(Kt[:, :, b:b + 1], 64), ALU.mult), r=["pb%d" % vB] + allk("Kt"), w=["sT1" + x])
                S.op("dve", lambda: V.tensor_tensor(T_["sSn"], T_["sSn"], T_["sT1"], ALU.add), r=["sSn" + x, "sT1" + x], w=["sSn" + x])
                yield
                for hh in range(2):
                    S.dma("sp", wv[b, hh], T_["sSn"][hh * 64:(hh + 1) * 64, :, :], r=["sSn" + x], w=["o_wkvs"])
                S.op("act", lambda: A.activation(T_["sBlk"][0:64, :, 0:64], T_["sSn"][0:64, :, :], AF.Copy), r=["sSn" + x], w=["sBlk" + x])
                S.op("act", lambda: A.activation(T_["sBlk"][64:128, :, 64:128], T_["sSn"][64:128, :, :], AF.Copy), r=["sSn" + x], w=["sBlk" + x])
                for c in range(8):
                    S.op("pe", lambda: PE.matmul(bank(c // 4)[:, (c % 4) * 16 + b:(c % 4) * 16 + b + 1], T_["sBlk"][:, c, :], Rt[:, c, b:b + 1],
                                                 start=True, stop=True), r=["sBlk" + x, "Rt%d" % c], w=["pb%d" % (c // 4)])
                yield

            def at_sample(b):
                i = b % 2; x = "%d" % i
                kvb = sKVb[i]; Kn = kvb[:, 0:256]; Vn = kvb[:, 256:512]; KT = sKTb[i]
                tB, sB = (4, 5) if i == 0 else (6, 7)
                S.dma("sp", Kn[0:127, :], ck[b, 1:128, :], w=["sKa" + x])
                S.dma("sp", Vn[0:127, :], cv[b, 1:128, :], w=["sVa" + x])
                S.dma("sp", Kn[127:128, :], ktm[b:b + 1, 0:256], r=["ktm"], w=["sKb" + x])
                S.dma("sp", Vn[127:128, :], ktm[b:b + 1, 256:512], r=["ktm"], w=["sVb" + x])
                yield
                for k2 in range(2):
                    S.op("pe", lambda: PE.transpose(bank(tB)[:, k2 * 128:(k2 + 1) * 128], Kn[:, k2 * 128:(k2 + 1) * 128], ident),
                         r=["sKa" + x, "sKb" + x, "cst"], w=["pb%d" % tB])
                S.op("act", lambda: A.activation(KT[:, :, :], bank(tB)[:, 0:256].rearrange("p (k n) -> p k n", k=2), AF.Copy), r=["pb%d" % tB], w=["sKT" + x])
                yield
                for kv in range(4):
                    hr = slice((kv % 2) * 64, (kv % 2) * 64 + 64)
                    ob_ = sB if kv % 2 == 0 else tB
                    S.op("pe", lambda: PE.matmul(bank(ob_)[:, 256 + kv * 4:256 + kv * 4 + 4], KT[hr, kv // 2, :], qT[hr, (kv // 2) * 4:(kv // 2) * 4 + 4, b],
                                                 start=True, stop=True), r=["sKT" + x, "qT"], w=["pb%d" % ob_])
                ev_ = lambda ap, o: ap.rearrange("p (k two g) -> p k two g", k=2, two=2)[:, :, o, :]
                for o, ob_ in ((0, sB), (1, tB)):
                    S.op("dve", lambda: V.tensor_tensor(ev_(sPT[:, b, :], o), ev_(bank(ob_)[:, 256:272], o), ev_(sbias, o), ALU.add),
                         r=["pb%d" % ob_, "cst"], w=["sPT" + x])
                S.op("act", lambda: A.activation(sPT[:, b, :], sPT[:, b, :], AF.Exp, scale=SCALE), r=["sPT" + x], w=["sPT" + x])
                yield
                S.op("pe", lambda: PE.matmul(bank(sB)[:, 288:304], ONES, sPT[:, b, :], start=True, stop=True), r=["cst", "sPT" + x], w=["pb%d" % sB])
                S.op("dve", lambda: V.tensor_tensor(sPTn[:, b, :], bank(sB)[:, 288:304], esink[:, :], ALU.add), r=["pb%d" % sB, "esink"], w=["sPTn" + x])
                S.op("dve", lambda: V.reciprocal(sPTn[:, b, :], sPTn[:, b, :]), r=["sPTn" + x], w=["sPTn" + x])
                S.op("dve", lambda: V.tensor_tensor(sPTn[:, b, :], sPTn[:, b, :], sPT[:, b, :], ALU.mult), r=["sPTn" + x, "sPT" + x], w=["sPTn" + x])
                yield
                for h in range(16):
                    kv = h // 4
                    S.op("pe", lambda: PE.matmul(bank(3)[(h % 2) * 64:(h % 2) * 64 + 64, (h // 2) * 16 + b:(h // 2) * 16 + b + 1],
                                                 Vn[:, kv * 64:(kv + 1) * 64], sPTn[:, b, h:h + 1], start=True, stop=True),
                         r=["sVa" + x, "sVb" + x, "sPTn" + x], w=["pb3"])
                yield

            def pairs(fn):
                for b in range(0, 16, 2):
                    interleave([fn(b), fn(b + 1)])
            pairs(rw_sample)
            ckp(202)
            for _ in epilogue(0, 16, [(bank(0)[:, 0:64], "pb0"), (bank(1)[:, 0:64], "pb1")]):
                pass
            for ci in range(28):
                M = RW_M[ci]
                S.dma("sp", sh_s[RW_COL[ci]:RW_COL[ci] + M, :], zs[0:M, ci, :], r=["zs"], w=["o_shs"])
            ckp(203)
            pairs(at_sample)
            ckp(204)
            for c in range(8):
                S.op("act", lambda: A.activation(hT[:, 8 + c, 0:16], bank(3)[:, c * 16:(c + 1) * 16], AF.Copy), r=["pb3"], w=["hT%d" % (8 + c)])

        def tm_proj(sl, subs, ncols, ps_of, handler, ntok):
            for s, (t0, n) in enumerate(subs):
                ps, pkeys = ps_of(s)
                for kc in range(KC):
                    S.op("pe", lambda: PE.matmul(ps[0:n, 0:ncols], hT[:, kc, t0:t0 + n], wb[sl][:, kc, 0:ncols],
                                                 start=(kc == 0), stop=(kc == KC - 1)), r=["hT%d" % kc, "wb%d" % sl], w=pkeys)
                handler(s, t0, n, ps, pkeys)

        def resid_add(W, subs, nk_list=None):
            for cb in range(4):
                sl = wload(W, [(cb * 512, 512, 0)])
                def h(s, t0, n, ps, pkeys):
                    S.op("dve", lambda: V.tensor_tensor(x_sb[0:n, s, cb * 512:(cb + 1) * 512], x_sb[0:n, s, cb * 512:(cb + 1) * 512],
                                                        ps[0:n, 0:512], ALU.add), r=pkeys + ["x%d" % s], w=["x%d" % s])
                tm_proj(sl, subs, 512, lambda s: (bank(4 + (s % 2)), ["pb%d" % (4 + (s % 2))]), h, None)

        try:
            for p in range(NPASS + 1):
                sample = (p == NPASS)
                last = (p == NPASS - 1)
                if sample:
                    subs = [(0, 16)]; ntok = 16
                else:
                    subs = [(0, 128), (128, 128)]; ntok = TP
                tok0 = p * TP
                for s, (t0, n) in enumerate(subs):
                    src = xs_d[0:16, :] if sample else xp[tok0 + t0:tok0 + t0 + n, :]
                    S.dma("sp", x_sb[0:n, s, :], src, w=["x%d" % s])
                if sample:
                    for ci in range(28):
                        M = RW_M[ci]
                        S.dma("sp", shT[0:M, ci, :], sshT[RW_COL[ci]:RW_COL[ci] + M, :], w=["shT"])
                    S.op("dve", lambda: V.memset(msh[:], 0.0), w=["msh"])
                    for ci in range(28):
                        M = RW_M[ci]
                        S.op("dve", lambda: V.tensor_scalar(msh[0:M, ci, :], shT[0:M, ci, :], pv[0:M, PV_MU + ci:PV_MU + ci + 1], None, ALU.mult),
                             r=["shT", "pv"], w=["msh"])
                ckp(1)
                norm_T(subs, PV_GMIX, ntok)
                ckp(2)
                sl = wload(w_in, [(RW_COL[24 + i], RW_M[24 + i], i * 128) for i in range(4)])
                for i in range(4):
                    ci = 24 + i; M = RW_M[ci]; b_ = 4 + (i % 2)
                    proj_fm(sl, i * 128, M, bank(b_), ["pb%d" % b_], ntok)
                    tshift(ci, bank(b_), ["pb%d" % b_], M, ntok, sample, last, xs_o[0:M, 0:ntok], ["xs_o"])
                    if i == 0:
                        S.op("act", lambda: A.activation(txw[0:96, 0:ntok], xs_o[0:96, 0:ntok], AF.Tanh), r=["xs_o"], w=["txw"])
                    elif i == 1:
                        S.op("act", lambda: A.activation(xaT[0:96, 0:ntok], xs_o[0:96, 0:ntok], AF.Copy), r=["xs_o"], w=["xaT"])
                    else:
                        S.op("act", lambda: A.activation(sxg[:, i - 2, 0:ntok], xs_o[:, 0:ntok], AF.Sigmoid), r=["xs_o"], w=["sxg"])
                ckp(3)
                S.op("dve", lambda: V.memset(ss[0:1, 7:8], 0.0), w=["ystg0", "ystg1"])
                def PP(c):
                    sl_ = wload(w_in, [(c * 128, 128, 0), (1024 + c * 128, 128, 128), (2048 + c * 128, 128, 256)])
                    prep_P(c, sl_, ntok, sample, last)
                def TT(c):
                    prep_P(c, None, ntok, sample, last)
                PP(0); TT(0); PP(1); TT(1)
                for c in range(8):
                    prep_H(c, ntok, sample, last)
                    if c + 2 < 8:
                        PP(c + 2)
                    prep_tail(c, ntok)
                    if c + 2 < 8:
                        TT(c + 2)
                ckp(5)
                for half in range(2):
                    sl = wload(w_in, [(RWP + QPERM[half * 8 + i] * 64, 64, i * 64) for i in range(8)])
                    for i in range(4):
                        b_ = 4 + (i % 2)
                        proj_fm(sl, i * 128, 128, bank(b_), ["pb%d" % b_], ntok)
                        evac(qT[:, half * 4 + i, 0:ntok], bank(b_)[:, 0:ntok], r=["pb%d" % b_], w=["qT"])
                sl = wload(w_in, [(RWP + 1024, 512, 0)])
                for i in range(2):
                    b_ = 4 + i
                    proj_fm(sl, i * 128, 128, bank(b_), ["pb%d" % b_], ntok)
                    evac(kT[:, i, 128:128 + ntok], bank(b_)[:, 0:ntok], r=["pb%d" % b_], w=["kT"])
                def kvh(s, t0, n, ps, pkeys):
                    if sample:
                        S.op("act", lambda: A.activation(ktm[0:n, :], ps[0:n, 0:512], AF.Copy), r=pkeys, w=["ktm"])
                        return
                    S.op("act", lambda: A.activation(Vall[0:n, 1 + s, :, 0:64], ps[0:n, 256:512].rearrange("p (k d) -> p k d", k=4), AF.Copy),
                         r=pkeys, w=["Vall%d" % (1 + s)])
                    if last and s == 1:
                        S.op("dve", lambda: V.tensor_copy(ktm[:, :], ps[:, 0:512]), r=pkeys, w=["ktm"])
                        S.dma("sp", k_p[:, :], ktm[:, 0:256], r=["ktm"], w=["o_kp"])
                        S.dma("sp", v_p[:, :], ktm[:, 256:512], r=["ktm"], w=["o_vp"])
                tm_proj(sl, subs, 512, lambda s: (bank(4 + (s % 2)), ["pb%d" % (4 + (s % 2))]), kvh, ntok)

                for s, (t0, n) in enumerate(subs):
                    src = ps_d[0:16, :] if sample else pp[tok0 + t0:tok0 + t0 + n, :]
                    S.dma("sp", pe_sb[0:n, s, :], src, w=["pe_sb"])
                    S.op("dve", lambda: V.tensor_copy(pe_b[0:n, s, :], pe_sb[0:n, s, :]), r=["pe_sb"], w=["pe_b"])
                    pt_rr[0] ^= 1
                    pt = PTb[:, pt_rr[0] * 1024:(pt_rr[0] + 1) * 1024]; pk = "pb%d" % (6 + pt_rr[0])
                    for k2 in range(2):
                        S.op("pe", lambda: PE.transpose(pt[:, k2 * 128:k2 * 128 + n], pe_b[0:n, s, k2 * 128:(k2 + 1) * 128], identb[0:n, 0:n]),
                             r=["pe_b", "cstb"], w=[pk])
                    for k2 in range(2):
                        evac(pT_[:, k2, t0:t0 + n], pt[:, k2 * 128:k2 * 128 + n], r=[pk], w=["pT_"])
                ckp(6)
                if not sample:
                    def side_all():
                        for arr, dst, an, dn in ((vT, tmV, "vT", "tmV"), (Kt, tmK, "Kt", "tmK"), (Bt, tmB, "Bt", "tmB")):
                            for s in range(2):
                                pt_rr[0] ^= 1
                                pt = PTb[:, pt_rr[0] * 1024:(pt_rr[0] + 1) * 1024]; pk = "pb%d" % (6 + pt_rr[0])
                                for c in range(8):
                                    S.op("pe", lambda: PE.transpose(pt[:, c * 128:(c + 1) * 128], arr[:, c, s * 128:(s + 1) * 128], identb),
                                         r=["%s%d" % (an, c), "cstb"], w=[pk])
                                evac(dst[:, s, :], pt[:, :], r=[pk], w=[dn])
                                yield
                        for n_ in range(2):
                            yield from att_block(p, n_)
                    att = side_all()
                    att_done = False
                    for j in range(-1, 4):
                        live = ([chunkB(j)] if j >= 0 else []) + ([chunkA(j + 1)] if j < 3 else [])
                        while live:
                            for g in list(live):
                                try:
                                    next(g)
                                except StopIteration:
                                    live.remove(g)
                            if not att_done:
                                try:
                                    next(att)
                                except StopIteration:
                                    att_done = True
                    if not att_done:
                        interleave([att])
                    if last:
                        S.dma("sp", wkv_p.rearrange("(c hh) k v -> hh k c v", hh=2)[0], ST[0:64, :, :], r=["ST"], w=["o_wkvp"])
                        S.dma("sp", wkv_p.rearrange("(c hh) k v -> hh k c v", hh=2)[1], ST[64:128, :, :], r=["ST"], w=["o_wkvp"])
                        for ci in range(28):
                            M = RW_M[ci]
                            S.dma("sp", sh_p[RW_COL[ci]:RW_COL[ci] + M, :], zlast[0:M, ci:ci + 1], r=["zlast"], w=["o_shp"])
                    S.op("act", lambda: A.activation(kT[:, :, 0:128], kT[:, :, TP:TP + 128], AF.Copy), r=["kT"], w=["kT"])
                    S.op("act", lambda: A.activation(Vall[:, 0, :, 0:64], Vall[:, 2, :, 0:64], AF.Copy), r=["Vall2"], w=["Vall0"])
                else:
                    sample_mix()
                ckp(10)
                resid_add(w_out, subs)
                ckp(11)
                norm_T(subs, PV_GFFN, ntok)
                for fb in range(FC // 2):
                    slg = wload(None, [(fb * 256, 256, 0, w_gate), (fb * 256, 256, 256, w_up)])
                    for i in range(2):
                        f = fb * 2 + i
                        proj_fm(slg, i * 128, 128, bank(0 + (f % 2)), ["pb%d" % (f % 2)], ntok)
                        proj_fm(slg, 256 + i * 128, 128, bank(2 + (f % 2)), ["pb%d" % (2 + (f % 2))], ntok)
                        S.op("act", lambda: A.activation(t_f1[:, 0:ntok], bank(f % 2)[:, 0:ntok], AF.Silu), r=["pb%d" % (f % 2)], w=["t_f1"])
                        S.op("dve", lambda: V.tensor_tensor(hid[:, f, 0:ntok], t_f1[:, 0:ntok], bank(2 + (f % 2))[:, 0:ntok], ALU.mult),
                             r=["t_f1", "pb%d" % (2 + (f % 2))], w=["hid"])
                for cb in range(4):
                    parts = [(0, 16), (16, 16), (32, 12)]
                    for pi, (k0, nk) in enumerate(parts):
                        sl = wload(w_down, [(cb * 512, 512, 0)], nk=nk, row0=k0 * 128)
                        for s, (t0, n) in enumerate(subs):
                            for kk_ in range(nk):
                                S.op("pe", lambda: PE.matmul(bank(4 + s)[0:n, 0:512], hid[:, k0 + kk_, t0:t0 + n], wb[sl][:, kk_, 0:512],
                                                             start=(pi == 0 and kk_ == 0), stop=(pi == 2 and kk_ == nk - 1)),
                                     r=["hid", "wb%d" % sl], w=["pb%d" % (4 + s)])
                    for s, (t0, n) in enumerate(subs):
                        S.op("dve", lambda: V.tensor_tensor(x_sb[0:n, s, cb * 512:(cb + 1) * 512], x_sb[0:n, s, cb * 512:(cb + 1) * 512],
                                                            bank(4 + s)[0:n, 0:512], ALU.add), r=["pb%d" % (4 + s), "x%d" % s], w=["x%d" % s])
                ckp(12)
                norm_T(subs, PV_GPLE, ntok)
                for cb in range(4):
                    sl = wload(ple_gate, [(cb * 512, 512, 0)])
                    S.dma("pool", wpp[:, :, :], ple_proj[:, cb * 512:(cb + 1) * 512].rearrange("(k p) n -> p k n", p=128), w=["wpp"])
                    for s, (t0, n) in enumerate(subs):
                        gb, pb_ = bank(4 + (s % 2)), bank(2 + (s % 2))
                        gk, pk2 = "pb%d" % (4 + (s % 2)), "pb%d" % (2 + (s % 2))
                        for kc in range(KC):
                            S.op("pe", lambda: PE.matmul(gb[0:n, 0:512], hT[:, kc, t0:t0 + n], wb[sl][:, kc, 0:512], start=(kc == 0), stop=(kc == KC - 1)),
                                 r=["hT%d" % kc, "wb%d" % sl], w=[gk])
                        for k2 in range(2):
                            S.op("pe", lambda: PE.matmul(pb_[0:n, 0:512], pT_[:, k2, t0:t0 + n], wpp[:, k2, :], start=(k2 == 0), stop=(k2 == 1)),
                                 r=["pT_", "wpp"], w=[pk2])
                        S.op("act", lambda: A.activation(t_f1[0:n, :], gb[0:n, 0:512], AF.Sigmoid), r=[gk], w=["t_f1"])
                        S.op("dve", lambda: V.tensor_tensor(t_f1[0:n, :], t_f1[0:n, :], pb_[0:n, 0:512], ALU.mult), r=["t_f1", pk2], w=["t_f1"])
                        S.op("dve", lambda: V.tensor_tensor(x_sb[0:n, s, cb * 512:(cb + 1) * 512], x_sb[0:n, s, cb * 512:(cb + 1) * 512], t_f1[0:n, :], ALU.add),
                             r=["t_f1", "x%d" % s], w=["x%d" % s])
                ckp(13)
                for s, (t0, n) in enumerate(subs):
                    S.op("act", lambda: A.activation(hb[0:n, s, :], x_sb[0:n, s, :], AF.Square, accum_out=ss[0:n, s:s + 1]), r=["x%d" % s], w=["hb%d" % s, "ss"])
                    S.op("act", lambda: A.activation(sd[0:n, s:s + 1], ss[0:n, s:s + 1], AF.Sqrt, bias=EPS, scale=1.0 / D), r=["ss"], w=["sd"])
                    S.op("dve", lambda: V.reciprocal(rstd[0:n, s:s + 1], sd[0:n, s:s + 1]), r=["sd"], w=["rstd%d" % s])
                    S.op("dve", lambda: V.scalar_tensor_tensor(ystg[0:n, s, :], x_sb[0:n, s, :], rstd[0:n, s:s + 1], fn_bc[0:n, :], ALU.mult, ALU.mult),
                         r=["x%d" % s, "rstd%d" % s, "fn_bc"], w=["ystg%d" % s, "hid"])
                    dst = y_s[0:16, :] if sample else y_p[tok0 + t0:tok0 + t0 + n, :]
                    S.dma("sp", dst, ystg[0:n, s, :], r=["ystg%d" % s], w=["o_y"])
                assert wblk[0] == NBLK * (p + 1), wblk[0]
                ckp(14 + p)
        except _Stop:
            pass
        S.finish("sp")
        print("build stats: waits", S.nwait, "dmas", S.ndma, "seq", dict(S.seq), flush=True)
    return nc


def _consts():
    c = np.zeros((128, C_N), np.float32)
    c[:, C_ID:C_ID + 128] = np.eye(128)
    p = np.arange(128)
    c[:, C_BD:C_BD + 128] = (p[:, None] // 64 == p[None, :] // 64)
    c[:, C_IH:C_IH + 64] = (p[:, None] % 64 == np.arange(64)[None, :])
    c[:, C_HSEL:C_HSEL + 2] = (p[:, None] // 64 == np.arange(2)[None, :])
    s_ = (p % 64)[:, None]; t_ = np.arange(64)[None, :]
    c[:, C_MS:C_MS + 64] = (s_ < t_)
    c[:, C_MI:C_MI + 64] = (s_ <= t_)
    c[:, C_MST:C_MST + 64] = (s_ > t_)
    c[:, C_RST:C_RST + 256] = (np.arange(256) % 64 != 0)[None, :]
    j = p[:, None].astype(np.float32); i = np.arange(128)[None, :].astype(np.float32)
    BIG = 1e5
    c[:, C_DM:C_DM + 128] = np.where(j > i, -(i + 128 - j), -BIG)
    c[:, C_DM + 128:C_DM + 256] = np.where(j <= i, -(i - j), -BIG)
    sl = np.array(SLOPES, np.float32)[None, :]
    c[:, C_SB:C_SB + 16] = -sl * (127 - j) / SCALE
    c[:, C_ON:C_ON + 128] = 1.0
    return c


def _pv(inp):
    pv = np.zeros((128, PV_N), np.float32)
    f = lambda v, n: np.ascontiguousarray(np.asarray(v, np.float32).reshape(n, 128).T)
    pv[:, PV_GMIX:PV_GMIX + 16] = f(inp["norm_mix"][0], 16)
    pv[:, PV_GFFN:PV_GFFN + 16] = f(inp["norm_ffn"][0], 16)
    pv[:, PV_GPLE:PV_GPLE + 16] = f(inp["norm_ple"][0], 16)
    mu = np.asarray(inp["mu_shift"][0], np.float32)
    for ci in range(28):
        pv[0:RW_M[ci], PV_MU + ci] = mu[RW_COL[ci]:RW_COL[ci] + RW_M[ci]]
    pv[:, PV_W0:PV_W0 + 8] = f(inp["rwkv_w0"][0], 8)
    pv[:, PV_A0:PV_A0 + 8] = f(inp["rwkv_a0"][0], 8)
    pv[:, PV_KK:PV_KK + 8] = f(inp["rwkv_k_k"][0], 8)
    pv[:, PV_KA:PV_KA + 8] = f(inp["rwkv_k_a"][0], 8)
    pv[:, PV_RK:PV_RK + 8] = f(np.asarray(inp["rwkv_r_k"][0]).reshape(-1), 8)
    pv[:, PV_LNW:PV_LNW + 8] = f(inp["rwkv_ln_w"][0], 8)
    pv[:, PV_LNB:PV_LNB + 8] = f(inp["rwkv_ln_b"][0], 8)
    return pv


_NC = [None]


def kernel(**inp):
    A_ = lambda k: np.ascontiguousarray(np.asarray(inp[k], np.float32))
    if _NC[0] is None:
        _NC[0] = build_program()
    nc = _NC[0]
    shared = {
        "w_in": A_("w_in")[0], "w_out": A_("w_out")[0], "w_gate": A_("w_gate")[0], "w_up": A_("w_up")[0],
        "w_down": A_("w_down")[0], "ple_gate": A_("ple_gate")[0], "ple_proj": A_("ple_proj")[0],
        "w2": A_("rwkv_w2")[0], "a2": A_("rwkv_a2")[0], "g2": A_("rwkv_g2")[0],
        "pv": _pv(inp), "cst": _consts(), "fng": A_("final_norm").reshape(1, D),
        "sinks": A_("attn_sinks").reshape(1, 16),
    }
    xp, xs = A_("x_prompt"), A_("x_sample")
    pp, ps = A_("p_prompt"), A_("p_sample")
    swkv, ssh = A_("state_wkv"), A_("state_shift")
    ck, cv = A_("cache_k"), A_("cache_v")
    in_maps = []
    for i in range(8):
        b = slice(16 * i, 16 * i + 16)
        m = dict(shared)
        m.update({
            "xp": xp[i], "xs": np.ascontiguousarray(xs[b, 0]), "pp": pp[0, i], "psm": np.ascontiguousarray(ps[0, b, 0]),
            "swkv": np.ascontiguousarray(np.swapaxes(swkv[0, b], -1, -2)),
            "sshT": np.ascontiguousarray(ssh[0, b].T),
            "ck": np.ascontiguousarray(ck[0, b].reshape(16, 128, 256)), "cv": np.ascontiguousarray(cv[0, b].reshape(16, 128, 256)),
        })
        in_maps.append(m)
    res = run_bass_kernel_spmd(nc, in_maps, core_ids=list(range(8)))
    R = res.results
    cat = lambda k: np.stack([np.asarray(r[k], np.float32) for r in R])
    y_p = cat("y_p")
    y_s = cat("y_s").reshape(128, 1, D)
    wkv_p = np.swapaxes(cat("wkv_p"), -1, -2)[None]
    sh_p = cat("sh_p").reshape(8, RWP)[None]
    k_p = cat("k_p").reshape(8, 128, 4, 64)[None]
    v_p = cat("v_p").reshape(8, 128, 4, 64)[None]
    wkv_s = np.swapaxes(cat("wkv_s").reshape(128, 16, 64, 64), -1, -2)[None]
    sh_s = np.swapaxes(cat("sh_s"), 1, 2).reshape(128, RWP)[None]
    k_s = cat("k_s").reshape(128, 128, 4, 64)[None]
    v_s = cat("v_s").reshape(128, 128, 4, 64)[None]
    c_ = np.ascontiguousarray
    return (c_(y_p), c_(y_s), c_(wkv_p), c_(sh_p), c_(k_p), c_(v_p), c_(wkv_s), c_(sh_s), c_(k_s), c_(v_s))
```

```python
import bisect
import concourse.bass as bass
import concourse.mybir as mybir

F32 = mybir.dt.float32
BF16 = mybir.dt.bfloat16
AF = mybir.ActivationFunctionType
ALU = mybir.AluOpType
AX = mybir.AxisListType

SEM_ROLL = 12000


class Sched:
    def __init__(self, nc, stack):
        self.nc = nc
        self.stack = stack
        self.engs = {"pe": nc.tensor, "dve": nc.vector, "act": nc.scalar,
                     "pool": nc.gpsimd, "sp": nc.sync}
        self.sem = {}
        self.cnt = {}
        self.seq = {}
        self.sigs = {}
        self.sigseq = {}
        self.last_ins = {}
        self.last_seq = {}
        self.nsem = 0
        for e in self.engs:
            self._new_sem(e)
            self.seq[e] = 0
            self.sigs[e] = []
            self.sigseq[e] = []
            self.last_ins[e] = None
        self.semobj = {}
        for e in self.engs:
            pass
        self.seen = {e: {} for e in self.engs}
        self.lastw = {}
        self.readers = {}
        self.dma_sems = {}
        self.dma_rr = {}
        self.ndma = 0
        self.nwait = 0

    def _new_sem(self, e):
        name = "s_%s_%d" % (e, self.nsem)
        self.nsem += 1
        s = self.stack.enter_context(self.nc.semaphore(name))
        self.sem[e] = (name, s)
        self.cnt[e] = 0
        if not hasattr(self, "semh"):
            self.semh = {}
        self.semh[name] = s

    def _resolve(self, tok):
        if tok[0] == "d":
            return tok[1], tok[2]
        _, e, q = tok
        i = bisect.bisect_left(self.sigseq[e], q)
        if i < len(self.sigseq[e]):
            _, sn, v = self.sigs[e][i]
            return sn, v
        ins = self.last_ins[e]
        assert ins is not None
        if self.cnt[e] >= SEM_ROLL:
            self._new_sem(e)
        sn, s = self.sem[e]
        self.cnt[e] += 1
        ins.then_inc(s, 1)
        self.sigs[e].append((self.last_seq[e], sn, self.cnt[e]))
        self.sigseq[e].append(self.last_seq[e])
        return sn, self.cnt[e]

    def _wait(self, e, toks):
        need = {}
        for t in toks:
            sn, v = self._resolve(t)
            if self.seen[e].get(sn, 0) >= v:
                continue
            if need.get(sn, 0) < v:
                need[sn] = v
        for sn, v in need.items():
            self.engs[e].wait_ge(self.semh[sn], v)
            self.seen[e][sn] = v
            self.nwait += 1

    def _deps(self, e, r, w):
        toks = []
        for k in r:
            t = self.lastw.get(k)
            if t is not None and not (e == "pe" and t[0] == "c" and t[1] == "pe"):
                toks.append(t)
            if k.startswith("pb") or k.startswith("pt"):
                for t in self.readers.get(k, ()):
                    if t[0] == "c" and t[1] != e:
                        toks.append(t)
        for k in w:
            t = self.lastw.get(k)
            if t is not None and not (e == "pe" and t[0] == "c" and t[1] == "pe"):
                toks.append(t)
            for t in self.readers.get(k, ()):
                toks.append(t)
        return toks

    def _record(self, tok, r, w):
        for k in r:
            self.readers.setdefault(k, []).append(tok)
        for k in w:
            self.lastw[k] = tok
            self.readers[k] = []

    def op(self, e, fn, r=(), w=()):
        self._wait(e, self._deps(e, r, w))
        ins = fn()
        self.seq[e] += 1
        self.last_ins[e] = ins
        self.last_seq[e] = self.seq[e]
        self._record(("c", e, self.seq[e]), r, w)
        return ins

    def dma(self, q, out, in_, r=(), w=(), **kw):
        ring = self.dma_sems.setdefault(q, [])
        if len(ring) < 8:
            name = "d_%s_%d" % (q, len(ring))
            s = self.stack.enter_context(self.nc.semaphore(name))
            self.semh[name] = s
            ring.append([name, 0])
            slot = ring[-1]
        else:
            i = self.dma_rr.get(q, 0)
            self.dma_rr[q] = (i + 1) % len(ring)
            slot = ring[i]
        toks = self._deps(q, r, w)
        if slot[1] > 0:
            toks.append(("d", slot[0], slot[1]))
        self._wait(q, toks)
        slot[1] += 16
        ins = self.engs[q].dma_start(out=out, in_=in_, **kw)
        ins.then_inc(self.semh[slot[0]], 16)
        self.seq[q] += 1
        self._record(("d", slot[0], slot[1]), r, w)
        self.ndma += 1
        return ins

    def finish(self, e="sp"):
        toks = []
        for k, t in self.lastw.items():
            toks.append(t)
        self._wait(e, toks)

import contextlib
import numpy as np
from concourse.bass_utils import run_bass_kernel_spmd

D = 2048
KC = 16
DFF = 5632
FC = 44
PROJ = 5056
RWP = 3520
TP = 256
NPASS = 2048 // TP
NBLK = 54
EPS = 1e-6
GN_EPS = 64e-5
SCALE = 0.125
DEC = 0.6065306597126334
LN2_15 = 15.0 * 0.6931471805599453
SLOPES = [2.0 ** (-8.0 * (h + 1) / 16) for h in range(16)]
QPERM = [0, 4, 1, 5, 2, 6, 3, 7, 8, 12, 9, 13, 10, 14, 11, 15]
PV_GMIX, PV_GFFN, PV_GPLE, PV_MU = 0, 16, 32, 48
PV_W0, PV_A0, PV_KK, PV_KA, PV_RK, PV_LNW, PV_LNB = 76, 84, 92, 100, 108, 116, 124
PV_N = 132
C_ID, C_BD, C_IH, C_HSEL, C_MS, C_MI, C_MST, C_RST, C_DM, C_SB, C_ON, C_N = 0, 128, 256, 320, 322, 386, 450, 514, 770, 1026, 1042, 1170

RW_COL = [c * 128 for c in range(8)] + [1024 + c * 128 for c in range(8)] + \
         [2048 + c * 128 for c in range(8)] + [3072, 3168, 3264, 3392]
RW_M = [128] * 24 + [96, 96, 128, 128]


def hcol(h):
    return ((h % 2) * 8 + h // 2) * 64


class _Stop(Exception):
    pass


def build_program(debug=False, stop=None):
    nc = bass.Bass("TRN2", target_bir_lowering=False)
    din = lambda n, sh: nc.dram_tensor(n, sh, F32, kind="ExternalInput").ap()
    dout = lambda n, sh: nc.dram_tensor(n, sh, F32, kind="ExternalOutput").ap()
    xp = din("xp", [2048, D]); xs_d = din("xs", [16, D])
    pp = din("pp", [2048, 256]); ps_d = din("psm", [16, 256])
    swkv = din("swkv", [16, 16, 64, 64])
    sshT = din("sshT", [RWP, 16])
    ck = din("ck", [16, 128, 256]); cv = din("cv", [16, 128, 256])
    w_in = din("w_in", [D, PROJ]); w_out = din("w_out", [D, D])
    w_gate = din("w_gate", [D, DFF]); w_up = din("w_up", [D, DFF]); w_down = din("w_down", [DFF, D])
    ple_gate = din("ple_gate", [D, D]); ple_proj = din("ple_proj", [256, D])
    w2 = din("w2", [96, 1024]); a2 = din("a2", [96, 1024]); g2 = din("g2", [256, 1024])
    pv_d = din("pv", [128, PV_N]); cst_d = din("cst", [128, C_N])
    fng = din("fng", [1, D]); esk = din("sinks", [1, 16])

    wsc = nc.dram_tensor("wsc", [NBLK, 128, KC * 512], BF16, kind="Internal").ap()
    y_p = dout("y_p", [2048, D]); y_s = dout("y_s", [16, D])
    wkv_p = dout("wkv_p", [16, 64, 64])
    sh_p = dout("sh_p", [RWP, 1])
    k_p = dout("k_p", [128, 256]); v_p = dout("v_p", [128, 256])
    wkv_s = dout("wkv_s", [16, 16, 64, 64])
    sh_s = dout("sh_s", [RWP, 16])
    k_s = dout("k_s", [16, 128, 256]); v_s = dout("v_s", [16, 128, 256])

    with contextlib.ExitStack() as st:
        S = Sched(nc, st)
        sb = lambda n, sh, dt=F32: st.enter_context(nc.sbuf_tensor("sb_" + n, sh, dt))
        V, A, P, PE = nc.vector, nc.scalar, nc.gpsimd, nc.tensor
        PF = st.enter_context(nc.psum_tensor("PF", [128, 4096], F32))
        PTb = PF[:, 3072:4096].bitcast(BF16)
        pbk = lambda b: "pb%d" % b
        bank = lambda b: PF[:, b * 512:(b + 1) * 512]

        x_sb = sb("x_sb", [128, 2, D])
        hb = sb("hb", [128, 2, D], BF16)
        hT = sb("hT", [128, KC, TP], BF16)
        wb = [sb("wb%d" % i, [128, KC, 512], BF16) for i in range(2)]
        pv = sb("pv", [128, PV_N]); cst = sb("cst", [128, C_N])
        omu = sb("omu", [128, 28]); omka = sb("omka", [128, 8])
        cstb = sb("cstb", [128, 384], BF16)
        fn_bc = sb("fn_bc", [128, D])
        esink = sb("esink", [128, 16])
        w2b = sb("w2b", [128, 1024], BF16); a2b = sb("a2b", [128, 1024], BF16)
        g2b = sb("g2b", [128, 2, 1024], BF16)
        ss = sb("ss", [128, 8]); sd = sb("sd", [128, 8]); rstd = sb("rstd", [128, 8])
        rwbig = sb("rwbig", [128, 6 * 8 * TP], BF16)
        def rwv(i):
            return rwbig[:, i * 8 * TP:(i + 1) * 8 * TP].rearrange("p (c t) -> p c t", c=8)
        Rt, Kt, At, Bt, vT, ybT = [rwv(i) for i in range(6)]
        hid = rwbig[:, 0:FC * TP].rearrange("p (c t) -> p c t", c=FC)
        ystg = rwbig.bitcast(F32)[:, 0:2 * D].rearrange("p (s d) -> p s d", s=2)
        gT = sb("gT", [128, 8, TP], BF16)
        dS = sb("dS", [128, 8, 16])
        tmV = sb("tmV", [128, 2, 1024], BF16); tmK = sb("tmK", [128, 2, 1024], BF16)
        tmB = sb("tmB", [128, 2, 1024], BF16)
        gC = sb("gC", [128, 8, 4])
        carry = sb("carry", [128, 28]); zlast = sb("zlast", [128, 28]); zs = sb("zs", [128, 28, 16])
        shT = sb("shT", [128, 28, 16]); msh = sb("msh", [128, 28, 16])
        txw = sb("txw", [128, TP], BF16); xaT = sb("xaT", [128, TP], BF16)
        sxg = sb("sxg", [128, 2, TP], BF16)
        tmr = [sb("tmr%d" % i, [128, TP + 1]) for i in range(3)]
        scr = sb("scr", [128, 13 * TP])
        def scv(i):
            return scr[:, i * TP:(i + 1) * TP]
        xs_r2 = sb("xs_r2", [128, TP]); xs_k2 = sb("xs_k2", [128, TP])
        xs_r, xs_k, xs_o, t_sg, t_L, t_eL, t_enL, t_eLm, t_a, t_kkr, t_sq, t_b, t_kh = [scv(i) for i in range(13)]
        t_rkr = sb("t_rkr", [128, TP], BF16)
        xs_rb = [xs_r, xs_r2]; xs_kb = [xs_k, xs_k2]
        def epv(i):
            return scr[:, i * 512:(i + 1) * 512].rearrange("p (c t) -> p c t", c=4)
        yT_sb, e_sq, e_m, e_v = [epv(i) for i in range(4)]
        chb = sb("chb", [128, 11 * 1024], BF16)
        def chv(i):
            return chb[:, i * 1024:(i + 1) * 1024]
        cY = [chv(0), chv(1)]; cYT = [chv(2), chv(3)]; cP = [chv(4), chv(5)]
        cNak, cMrb, cMrk, cW, cU = chv(6), chv(7), chv(8), chv(9), chv(10)
        chf = sb("chf", [128, 2048])
        cT1 = chf[:, 0:1024]; cYs = chf[:, 1024:2048]
        ST = sb("ST", [128, 8, 64]); STb = sb("STb", [128, 8, 128], BF16)
        qT = sb("qT", [128, 8, TP], BF16); kT = sb("kT", [128, 2, 128 + TP], BF16)
        Vall = sb("Vall", [128, 3, 4, 66], BF16)
        t_f1 = sb("t_f1", [128, 512]); t_f2 = sb("t_f2", [128, 512])
        a_sc = t_f1; ktm = t_f2
        a_PT = [sb("a_PT%d" % i, [128, 512], BF16) for i in range(2)]
        a_den = sb("a_den", [128, 4])
        wpp = sb("wpp", [128, 2, 512], BF16)
        pT_ = sb("pT_", [128, 2, TP], BF16); pe_sb = sb("pe_sb", [128, 2, 256]); pe_b = sb("pe_b", [128, 2, 256], BF16)
        x1 = x_sb[:, 1, :]
        sS = x1[:, 0:512].rearrange("p (c v) -> p c v", c=8)
        sSn = x1[:, 512:1024].rearrange("p (c v) -> p c v", c=8)
        sBlk = x1[:, 1024:2048].rearrange("p (c v) -> p c v", c=8)
        sAbd = chf[:, 0:1024].rearrange("p (c v) -> p c v", c=8)
        sRv = chf[:, 1024:1536].rearrange("p (c v) -> p c v", c=8)
        sT1 = chf[:, 1536:2048].rearrange("p (c v) -> p c v", c=8)
        chb32 = chb[:, 0:4096].bitcast(F32)
        sKV = chb32[:, 0:512]; sKn = sKV[:, 0:256]; sVn = sKV[:, 256:512]
        sPT = chb32[:, 512:768].rearrange("p (b h) -> p b h", b=16)
        sPTn = chb32[:, 768:1024].rearrange("p (b h) -> p b h", b=16)
        vf = chb32[:, 1024:1152].rearrange("p (c b) -> p c b", c=8)
        sKT = chb[:, 4096:4352].rearrange("p (k n) -> p k n", k=2)

        ident = cst[:, C_ID:C_ID + 128]; BDf = cst[:, C_BD:C_BD + 128]
        identb = cstb[:, 0:128]; BDb = cstb[:, 128:256]
        mS = cst[:, C_MS:C_MS + 64]; mI = cst[:, C_MI:C_MI + 64]; mST = cst[:, C_MST:C_MST + 64]
        rst = cst[:, C_RST:C_RST + 256]
        Dm = cst[:, C_DM:C_DM + 256]
        sbias = cst[:, C_SB:C_SB + 16]
        Ihalf = cst[:, C_IH:C_IH + 64]

        S.dma("sp", pv[:], pv_d[:, :], w=["pv"])
        S.dma("sp", cst[:], cst_d[:, :], w=["cst"])
        S.dma("sp", fn_bc[:], fng.partition_broadcast(128), w=["fn_bc"])
        S.dma("sp", esink[:], esk.partition_broadcast(128), w=["esink"])
        S.dma("pool", w2b[0:96, :], w2[:, :], w=["w2b"])
        S.dma("pool", a2b[0:96, :], a2[:, :], w=["a2b"])
        S.dma("pool", g2b[:], g2.rearrange("(k p) n -> p k n", p=128), w=["g2b"])
        S.op("dve", lambda: V.tensor_copy(cstb[:, 0:256], cst[:, 0:256]), r=["cst"], w=["cstb"])
        S.op("dve", lambda: V.tensor_scalar(omu[:], pv[:, PV_MU:PV_MU + 28], -1.0, 1.0, ALU.mult, ALU.add), r=["pv"], w=["omu"])
        S.op("dve", lambda: V.tensor_scalar(omka[:], pv[:, PV_KA:PV_KA + 8], -1.0, 1.0, ALU.mult, ALU.add), r=["pv"], w=["omka"])
        S.op("act", lambda: A.activation(esink[:], esink[:], AF.Exp), r=["esink"], w=["esink"])
        S.op("pool", lambda: P.memset(carry[:], 0.0), w=["carry%d" % i for i in range(28)])
        S.op("pool", lambda: P.memset(ST[:], 0.0), w=["ST"])
        S.op("pool", lambda: P.memset(STb[:], 0.0), w=["STb"])
        S.op("pool", lambda: P.memset(Vall[:], 1.0), w=["Vall0", "Vall1", "Vall2"])
        S.op("pool", lambda: P.memset(kT[:], 0.0), w=["kT"])
        for i_ in range(2):
            S.op("pool", lambda: P.memset(wb[i_][:], 0.0), w=["wb%d" % i_])

        def ckp(k):
            if stop == k:
                raise _Stop()
        wslot = [0]

        wblk = [0]

        def wload(W, specs, nk=KC, row0=0):
            sl = wslot[0]; wslot[0] ^= 1
            bi = wblk[0] % NBLK; first = wblk[0] < NBLK; wblk[0] += 1
            if first:
                for sp_ in specs:
                    c0, n, d0 = sp_[0:3]
                    Wm = sp_[3] if len(sp_) > 3 else W
                    src = Wm[row0:row0 + nk * 128, c0:c0 + n].rearrange("(k p) n -> p k n", p=128)
                    S.dma("pool", wb[sl][:, 0:nk, d0:d0 + n], src, w=["wb%d" % sl])
                S.dma("sp", wsc[bi], wb[sl][:, :, :].rearrange("p k n -> p (k n)"), r=["wb%d" % sl], w=["wsc%d" % bi])
            else:
                S.dma("pool", wb[sl][:, :, :].rearrange("p k n -> p (k n)"), wsc[bi], r=["wsc%d" % bi], w=["wb%d" % sl])
            return sl

        ev_rr = [0]

        def evac(out, in_, r, w, scale=None):
            ev_rr[0] ^= 1
            if ev_rr[0]:
                if scale is None:
                    S.op("act", lambda: A.activation(out, in_, AF.Copy), r=r, w=w)
                else:
                    S.op("act", lambda: A.activation(out, in_, AF.Copy, scale=scale), r=r, w=w)
            else:
                if scale is None:
                    S.op("dve", lambda: V.tensor_copy(out, in_), r=r, w=w)
                else:
                    S.op("dve", lambda: V.tensor_scalar(out, in_, scale, None, ALU.mult), r=r, w=w)

        pt_rr = [0]

        def norm_T(subs, gcol, ntok):
            for s, (t0, n) in enumerate(subs):
                S.op("act", lambda: A.activation(hb[0:n, s, :], x_sb[0:n, s, :], AF.Square, accum_out=ss[0:n, s:s + 1]),
                     r=["x%d" % s], w=["hb%d" % s, "ss"])
                S.op("act", lambda: A.activation(sd[0:n, s:s + 1], ss[0:n, s:s + 1], AF.Ln, bias=EPS, scale=1.0 / D),
                     r=["ss"], w=["sd"])
                S.op("act", lambda: A.activation(rstd[0:n, s:s + 1], sd[0:n, s:s + 1], AF.Exp, scale=-0.5), r=["sd"], w=["rstd%d" % s])
                S.op("dve", lambda: V.tensor_scalar(hb[0:n, s, :], x_sb[0:n, s, :], rstd[0:n, s:s + 1], None, ALU.mult),
                     r=["x%d" % s, "rstd%d" % s], w=["hb%d" % s])
            for kc in range(KC):
                pt_rr[0] ^= 1
                pt = PTb[:, pt_rr[0] * 1024:(pt_rr[0] + 1) * 1024]; pk = "pb%d" % (6 + pt_rr[0])
                for s, (t0, n) in enumerate(subs):
                    S.op("pe", lambda: PE.transpose(pt[:, t0:t0 + n], hb[0:n, s, kc * 128:(kc + 1) * 128], identb[0:n, 0:n]),
                         r=["hb%d" % s, "cstb"], w=[pk])
                evac(hT[:, kc, 0:ntok], pt[:, 0:ntok], r=[pk, "pv"], w=["hT%d" % kc], scale=pv[:, gcol + kc:gcol + kc + 1])

        def proj_fm(sl, d0, M, ps, pkeys, ntok):
            for kc in range(KC):
                S.op("pe", lambda: PE.matmul(ps[0:M, 0:ntok], wb[sl][:, kc, d0:d0 + M], hT[:, kc, 0:ntok],
                                             start=(kc == 0), stop=(kc == KC - 1)),
                     r=["wb%d" % sl, "hT%d" % kc], w=pkeys)

        def tshift(ci, ps, pkeys, M, ntok, sample, last, out_ap, okeys):
            mu = pv[0:M, PV_MU + ci:PV_MU + ci + 1]
            if sample:
                S.op("act", lambda: A.activation(zs[0:M, ci, :], ps[0:M, 0:16], AF.Copy), r=pkeys, w=["zs"])
                S.op("dve", lambda: V.scalar_tensor_tensor(out_ap, ps[0:M, 0:16], omu[0:M, ci:ci + 1], msh[0:M, ci, :],
                                                           ALU.mult, ALU.add), r=pkeys + ["omu", "msh"], w=okeys)
                return
            tm = tmr[ci % 3]; tk = "tmr%d" % (ci % 3)
            S.op("act", lambda: A.activation(tm[0:M, 0:1], carry[0:M, ci:ci + 1], AF.Copy), r=["carry%d" % ci], w=[tk])
            S.op("act", lambda: A.activation(tm[0:M, 1:1 + ntok], ps[0:M, 0:ntok], AF.Copy, scale=mu), r=pkeys + ["pv"], w=[tk])
            S.op("act", lambda: A.activation(carry[0:M, ci:ci + 1], tm[0:M, ntok:ntok + 1], AF.Copy), r=[tk], w=["carry%d" % ci])
            if last:
                S.op("act", lambda: A.activation(zlast[0:M, ci:ci + 1], ps[0:M, ntok - 1:ntok], AF.Copy), r=pkeys, w=["zlast"])
            S.op("dve", lambda: V.scalar_tensor_tensor(out_ap, ps[0:M, 0:ntok], omu[0:M, ci:ci + 1], tm[0:M, 0:ntok],
                                                       ALU.mult, ALU.add), r=pkeys + ["omu", tk], w=okeys)

        def prep_P(c, sl, ntok, sample, last):
            N = ntok
            b0 = 0 if c % 2 == 0 else 3
            pr, pk_, pvv = bank(b0), bank(b0 + 1), bank(b0 + 2)
            kr_, kk_, kv_ = ["pb%d" % (b0 + i) for i in range(3)]
            xs_r, xs_k = xs_rb[c % 2], xs_kb[c % 2]
            xr_k, xk_k = "xs_r%d" % (c % 2), "xs_k%d" % (c % 2)
            if sl is not None:
                proj_fm(sl, 0, 128, pr, [kr_], N)
                proj_fm(sl, 128, 128, pk_, [kk_], N)
                proj_fm(sl, 256, 128, pvv, [kv_], N)
                return
            tshift(c, pr, [kr_], 128, N, sample, last, xs_r[:, 0:N], [xr_k])
            tshift(8 + c, pk_, [kk_], 128, N, sample, last, xs_k[:, 0:N], [xk_k])
            tshift(16 + c, pvv, [kv_], 128, N, sample, last, vT[:, c, 0:N], ["vT%d" % c])

        def prep_H(c, ntok, sample, last):
            N = ntok
            xs_r, xs_k = xs_rb[c % 2], xs_kb[c % 2]
            xr_k, xk_k = "xs_r%d" % (c % 2), "xs_k%d" % (c % 2)
            cs = slice(c * 128, (c + 1) * 128)
            pc = lambda o: pv[:, o + c:o + c + 1]
            S.op("pe", lambda: PE.matmul(bank(6)[:, 0:N], w2b[0:96, cs], txw[0:96, 0:N], start=True, stop=True),
                 r=["w2b", "txw"], w=["pb6"])
            S.op("pe", lambda: PE.matmul(bank(7)[:, 0:N], a2b[0:96, cs], xaT[0:96, 0:N], start=True, stop=True),
                 r=["a2b", "xaT"], w=["pb7"])
            S.op("dve", lambda: V.tensor_scalar(t_kkr[:, 0:N], xs_k[:, 0:N], pc(PV_KK), None, ALU.mult), r=[xk_k, "pv"], w=["t_kkr"])
            S.op("act", lambda: A.activation(t_sg[:, 0:N], bank(6)[:, 0:N], AF.Sigmoid, bias=pc(PV_W0)), r=["pb6", "pv"], w=["t_sg"])
            S.op("act", lambda: A.activation(t_a[:, 0:N], bank(7)[:, 0:N], AF.Sigmoid, bias=pc(PV_A0)), r=["pb7", "pv"], w=["t_a"])
            S.op("act", lambda: A.activation(t_sq[:, 0:N], t_kkr[:, 0:N], AF.Square), r=["t_kkr"], w=["t_sq"])
            for k2 in range(2):
                S.op("pe", lambda: PE.matmul(bank(6)[:, 0:N], g2b[:, k2, cs], sxg[:, k2, 0:N], start=(k2 == 0), stop=(k2 == 1)),
                     r=["g2b", "sxg"], w=["pb6"])
            S.op("pe", lambda: PE.matmul(bank(7)[:, 0:N], BDf, t_sq[:, 0:N], start=True, stop=True), r=["cst", "t_sq"], w=["pb7"])
            S.op("dve", lambda: V.tensor_scalar(t_sg[:, 0:N], t_sg[:, 0:N], -DEC, None, ALU.mult), r=["t_sg"], w=["t_sg"])
            if sample:
                S.op("act", lambda: A.activation(dS[:, c, :], t_sg[:, 0:N], AF.Exp), r=["t_sg"], w=["dS"])
            else:
                S.op("dve", lambda: V.tensor_tensor_scan(t_L[:, 0:N], rst[:, 0:N], t_sg[:, 0:N], 0.0, ALU.mult, ALU.add),
                     r=["t_sg", "cst"], w=["t_L"])
                S.op("act", lambda: A.activation(t_eL[:, 0:N], t_L[:, 0:N], AF.Exp), r=["t_L"], w=["t_eL"])
                S.op("act", lambda: A.activation(t_enL[:, 0:N], t_L[:, 0:N], AF.Exp, scale=-1.0), r=["t_L"], w=["t_enL"])
                eLv = t_eL[:, 0:N].rearrange("p (j t) -> p j t", t=64)
                eLmv = t_eLm[:, 0:N].rearrange("p (j t) -> p j t", t=64)
                S.op("dve", lambda: V.memset(eLmv[:, :, 0:1], 1.0), w=["t_eLm"])
            S.op("dve", lambda: V.tensor_scalar(t_sq[:, 0:N], bank(7)[:, 0:N], 1e-24, None, ALU.max), r=["pb7"], w=["t_sq"])
            S.op("act", lambda: A.activation(t_sq[:, 0:N], t_sq[:, 0:N], AF.Ln, scale=float(2 ** 30)), r=["t_sq"], w=["t_sq"])
            S.op("act", lambda: A.activation(t_sq[:, 0:N], t_sq[:, 0:N], AF.Exp, scale=-0.5, bias=LN2_15), r=["t_sq"], w=["t_sq"])
            S.op("act", lambda: A.activation(gT[:, c, 0:N], bank(6)[:, 0:N], AF.Copy), r=["pb6"], w=["gT%d" % c])
            if not sample:
                S.op("dve", lambda: V.tensor_copy(eLmv[:, :, 1:64], eLv[:, :, 0:63]), r=["t_eL"], w=["t_eLm"])
                S.op("act", lambda: A.activation(gC[:, c, :], eLv[:, :, 63], AF.Copy), r=["t_eL"], w=["gC"])
                S.op("dve", lambda: V.tensor_tensor(Rt[:, c, 0:N], xs_r[:, 0:N], t_eL[:, 0:N], ALU.mult), r=[xr_k, "t_eL"], w=["Rt%d" % c])
            else:
                S.op("dve", lambda: V.tensor_copy(Rt[:, c, 0:N], xs_r[:, 0:N]), r=[xr_k], w=["Rt%d" % c])
            S.op("dve", lambda: V.tensor_tensor(t_kkr[:, 0:N], t_kkr[:, 0:N], t_sq[:, 0:N], ALU.mult), r=["t_kkr", "t_sq"], w=["t_kkr"])
            S.op("dve", lambda: V.tensor_tensor(t_b[:, 0:N], t_kkr[:, 0:N], t_a[:, 0:N], ALU.mult), r=["t_kkr", "t_a"], w=["t_b"])
            S.op("dve", lambda: V.tensor_scalar(t_a[:, 0:N], t_a[:, 0:N], pc(PV_KA), omka[:, c:c + 1], ALU.mult, ALU.add),
                 r=["t_a", "pv", "omka"], w=["t_a"])
            S.op("dve", lambda: V.tensor_tensor(t_kh[:, 0:N], xs_k[:, 0:N], t_a[:, 0:N], ALU.mult), r=[xk_k, "t_a"], w=["t_kh"])
            if sample:
                S.op("dve", lambda: V.tensor_scalar(At[:, c, 0:N], t_kkr[:, 0:N], -1.0, None, ALU.mult), r=["t_kkr"], w=["At%d" % c])
                S.op("dve", lambda: V.tensor_copy(Bt[:, c, 0:N], t_b[:, 0:N]), r=["t_b"], w=["Bt%d" % c])
                S.op("dve", lambda: V.tensor_copy(Kt[:, c, 0:N], t_kh[:, 0:N]), r=["t_kh"], w=["Kt%d" % c])
            else:
                S.op("dve", lambda: V.scalar_tensor_tensor(At[:, c, 0:N], t_kkr[:, 0:N], -1.0, t_eLm[:, 0:N], ALU.mult, ALU.mult),
                     r=["t_kkr", "t_eLm"], w=["At%d" % c])
                S.op("dve", lambda: V.tensor_tensor(Bt[:, c, 0:N], t_b[:, 0:N], t_enL[:, 0:N], ALU.mult), r=["t_b", "t_enL"], w=["Bt%d" % c])
                S.op("dve", lambda: V.tensor_tensor(Kt[:, c, 0:N], t_kh[:, 0:N], t_enL[:, 0:N], ALU.mult), r=["t_kh", "t_enL"], w=["Kt%d" % c])

        def prep_tail(c, ntok):
            N = ntok
            xs_r = xs_rb[c % 2]; xr_k = "xs_r%d" % (c % 2)
            pc = lambda o: pv[:, o + c:o + c + 1]
            S.op("dve", lambda: V.scalar_tensor_tensor(t_rkr[:, 0:N], xs_r[:, 0:N], pc(PV_RK), t_kh[:, 0:N], ALU.mult, ALU.mult),
                 r=[xr_k, "t_kh", "pv"], w=["t_rkr"])
            S.op("pe", lambda: PE.matmul(bank(6)[:, 0:N], BDb, t_rkr[:, 0:N], start=True, stop=True), r=["cstb", "t_rkr"], w=["pb6"])
            S.op("dve", lambda: V.tensor_tensor(ybT[:, c, 0:N], bank(6)[:, 0:N], vT[:, c, 0:N], ALU.mult), r=["pb6", "vT%d" % c], w=["ybT%d" % c])

        def epilogue(col0, n, srcs=None):
            for hf in range(2):
                if srcs is None:
                    for c4 in range(4):
                        c = hf * 4 + c4
                        S.op("pe", lambda: PE.transpose(bank(4)[:, c4 * 128:(c4 + 1) * 128], cYs[:, c * 128:(c + 1) * 128], ident),
                             r=["cYs_0", "cYs_1", "cst"], w=["pb4"])
                    src_ap, sk = bank(4)[:, 0:4 * n], "pb4"
                else:
                    src_ap, sk = srcs[hf]
                src = src_ap.rearrange("p (c t) -> p c t", c=4)
                yv = yT_sb[:, :, 0:n]; sq = e_sq[:, :, 0:n]; em = e_m[:, :, 0:n]; ev = e_v[:, :, 0:n]
                S.op("act", lambda: A.activation(yv, src, AF.Copy), r=[sk], w=["yT_sb"])
                S.op("act", lambda: A.activation(sq, src, AF.Square), r=[sk], w=["e_sq"])
                for c4 in range(4):
                    S.op("pe", lambda: PE.matmul(bank(5)[:, c4 * n:(c4 + 1) * n], BDf, yT_sb[:, c4, 0:n], start=True, stop=True),
                         r=["cst", "yT_sb"], w=["pb5"])
                m_ps = bank(5)[:, 0:4 * n].rearrange("p (c t) -> p c t", c=4)
                S.op("act", lambda: A.activation(em, m_ps, AF.Copy, scale=1.0 / 64), r=["pb5"], w=["e_m"])
                yield
                for c4 in range(4):
                    S.op("pe", lambda: PE.matmul(bank(5)[:, c4 * n:(c4 + 1) * n], BDf, e_sq[:, c4, 0:n], start=True, stop=True),
                         r=["cst", "e_sq"], w=["pb5"])
                S.op("dve", lambda: V.tensor_tensor(ev, em, em, ALU.mult), r=["e_m"], w=["e_v"])
                S.op("dve", lambda: V.scalar_tensor_tensor(ev, m_ps, 1.0 / 64, ev, ALU.mult, ALU.subtract), r=["pb5", "e_v"], w=["e_v"])
                S.op("act", lambda: A.activation(ev, ev, AF.Ln, bias=GN_EPS), r=["e_v"], w=["e_v"])
                S.op("act", lambda: A.activation(ev, ev, AF.Exp, scale=-0.5), r=["e_v"], w=["e_v"])
                S.op("dve", lambda: V.tensor_tensor(yv, yv, em, ALU.subtract), r=["yT_sb", "e_m"], w=["yT_sb"])
                S.op("dve", lambda: V.tensor_tensor(yv, yv, ev, ALU.mult), r=["yT_sb", "e_v"], w=["yT_sb"])
                yield
                for c4 in range(4):
                    c = hf * 4 + c4
                    S.op("dve", lambda: V.tensor_scalar(yT_sb[:, c4, 0:n], yT_sb[:, c4, 0:n], pv[:, PV_LNW + c:PV_LNW + c + 1],
                                                        pv[:, PV_LNB + c:PV_LNB + c + 1], ALU.mult, ALU.add), r=["yT_sb", "pv"], w=["yT_sb"])
                    S.op("dve", lambda: V.tensor_tensor(yT_sb[:, c4, 0:n], yT_sb[:, c4, 0:n], ybT[:, c, col0:col0 + n], ALU.add),
                         r=["yT_sb", "ybT%d" % c], w=["yT_sb"])
                    S.op("dve", lambda: V.tensor_tensor(hT[:, c, col0:col0 + n], yT_sb[:, c4, 0:n], gT[:, c, col0:col0 + n], ALU.mult),
                         r=["yT_sb", "gT%d" % c], w=["hT%d" % c])
                yield

        def chunkA(j):
            hp = (j % 2) * 64; pq = "_%d" % (j % 2)
            tok = slice(j * 64, (j + 1) * 64)
            rows = slice(hp, hp + 64)
            A2 = PF[:, 0:1024]; B2 = PF[:, 1024:2048]
            ka, kb_ = ["pb0", "pb1"], ["pb2", "pb3"]
            def hmm(dst2, lh, rh, lh_n, rh_n, keys):
                for h in range(16):
                    c, hh = h // 2, h % 2
                    kr = slice(hh * 64, hh * 64 + 64)
                    S.op("pe", lambda: PE.matmul(dst2[rows, hcol(h):hcol(h) + 64], lh[kr, c, tok], rh[kr, c, tok], start=True, stop=True),
                         r=["%s%d" % (lh_n, c), "%s%d" % (rh_n, c)], w=keys)
            m3 = lambda m: m[rows, :].unsqueeze(1).broadcast_to([64, 16, 64])
            v3 = lambda t: t[rows, :].rearrange("p (h t) -> p h t", h=16)
            hmm(A2, Bt, At, "Bt", "At", ka)
            hmm(B2, At, Bt, "At", "Bt", kb_)
            S.op("dve", lambda: V.tensor_tensor(v3(cY[0]), v3(A2), m3(mS), ALU.mult), r=ka + ["cst"], w=["cY0" + pq])
            S.op("dve", lambda: V.tensor_tensor(v3(cYT[0]), v3(B2), m3(mST), ALU.mult), r=kb_ + ["cst"], w=["cYT0" + pq])
            yield
            hmm(A2, Kt, At, "Kt", "At", ka)
            S.op("dve", lambda: V.tensor_tensor(v3(cNak), v3(A2), m3(mS), ALU.mult), r=ka + ["cst"], w=["cNak" + pq])
            hmm(B2, Bt, Rt, "Bt", "Rt", kb_)
            S.op("dve", lambda: V.tensor_tensor(v3(cMrb), v3(B2), m3(mI), ALU.mult), r=kb_ + ["cst"], w=["cMrb" + pq])
            yield
            hmm(A2, Kt, Rt, "Kt", "Rt", ka)
            S.op("dve", lambda: V.tensor_tensor(v3(cMrk), v3(A2), m3(mI), ALU.mult), r=ka + ["cst"], w=["cMrk" + pq])
            idb3 = identb[rows, hp:hp + 64].unsqueeze(1).broadcast_to([64, 16, 64])
            S.op("dve", lambda: V.tensor_tensor(v3(cP[0]), v3(cY[0]), idb3, ALU.add), r=["cY0" + pq, "cstb"], w=["cP0" + pq])
            yield
            cur = 0
            for lvl in range(5):
                nxt = cur ^ 1
                Yc, YTc, Pc = cY[cur], cYT[cur], cP[cur]
                Yn, YTn, Pn = cY[nxt], cYT[nxt], cP[nxt]
                kY, kYT, kP = "cY%d" % cur + pq, "cYT%d" % cur + pq, "cP%d" % cur + pq
                for h in range(16):
                    hc = slice(hcol(h), hcol(h) + 64)
                    S.op("pe", lambda: PE.matmul(B2[rows, hc], Yc[rows, hc], YTc[rows, hc], start=True, stop=True), r=[kY, kYT], w=kb_)
                if lvl < 4:
                    for h in range(16):
                        hc = slice(hcol(h), hcol(h) + 64)
                        S.op("pe", lambda: PE.matmul(A2[rows, hc], YTc[rows, hc], Yc[rows, hc], start=True, stop=True), r=[kY, kYT], w=ka)
                S.op("act", lambda: A.activation(YTn[rows, :], B2[rows, :], AF.Copy), r=kb_, w=["cYT%d" % nxt + pq])
                if lvl < 4:
                    S.op("dve", lambda: V.tensor_copy(Yn[rows, :], A2[rows, :]), r=ka, w=["cY%d" % nxt + pq])
                yield
                for h in range(16):
                    hc = slice(hcol(h), hcol(h) + 64)
                    S.op("pe", lambda: PE.matmul(B2[rows, hc], YTn[rows, hc], Pc[rows, hc], start=True, stop=True), r=["cYT%d" % nxt + pq, kP], w=kb_)
                S.op("dve", lambda: V.tensor_tensor(Pn[rows, :], B2[rows, :], Pc[rows, :], ALU.add), r=kb_ + [kP], w=["cP%d" % nxt + pq])
                yield
                cur = nxt
            assert cur == 1

        def chunkB(j):
            hp = (j % 2) * 64; pq = "_%d" % (j % 2)
            s_, tok = j // 2, slice(j * 64, (j + 1) * 64)
            rows = slice(hp, hp + 64)
            TTt = cP[1]; kTT = "cP1" + pq
            C2 = PF[:, 2048:3072]; kc_ = ["pb4", "pb5"]
            for h in range(16):
                S.op("pe", lambda: PE.matmul(C2[rows, h * 64:(h + 1) * 64], cNak[rows, hcol(h):hcol(h) + 64], tmV[rows, s_, h * 64:(h + 1) * 64],
                                             start=True, stop=True), r=["cNak" + pq, "tmV"], w=kc_)
            S.op("act", lambda: A.activation(cT1[rows, :], C2[rows, :], AF.Copy), r=kc_, w=["cT1" + pq])
            yield
            for c in range(8):
                S.op("pe", lambda: PE.matmul(C2[rows, c * 128:(c + 1) * 128], At[:, c, tok], STb[:, c, :], start=True, stop=True),
                     r=["At%d" % c, "STb"], w=kc_)
            S.op("dve", lambda: V.tensor_tensor(cW[rows, :], C2[rows, :], cT1[rows, :], ALU.add), r=kc_ + ["cT1" + pq], w=["cW" + pq])
            yield
            for h in range(16):
                S.op("pe", lambda: PE.matmul(C2[rows, h * 64:(h + 1) * 64], TTt[rows, hcol(h):hcol(h) + 64], cW[rows, h * 64:(h + 1) * 64],
                                             start=True, stop=True), r=[kTT, "cW" + pq], w=kc_)
            S.op("act", lambda: A.activation(cU[rows, :], C2[rows, :], AF.Copy), r=kc_, w=["cU" + pq])
            yield
            for h in range(16):
                hs = slice(h * 64, (h + 1) * 64); hc = slice(hcol(h), hcol(h) + 64)
                S.op("pe", lambda: PE.matmul(C2[rows, hs], cMrb[rows, hc], cU[rows, hs], start=True, stop=False), r=["cMrb" + pq, "cU" + pq], w=kc_)
                S.op("pe", lambda: PE.matmul(C2[rows, hs], cMrk[rows, hc], tmV[rows, s_, hs], start=False, stop=True), r=["cMrk" + pq, "tmV"], w=kc_)
            S.op("act", lambda: A.activation(cT1[rows, :], C2[rows, :], AF.Copy), r=kc_, w=["cT1" + pq])
            yield
            for c in range(8):
                S.op("pe", lambda: PE.matmul(C2[rows, c * 128:(c + 1) * 128], Rt[:, c, tok], STb[:, c, :], start=True, stop=True),
                     r=["Rt%d" % c, "STb"], w=kc_)
            S.op("dve", lambda: V.tensor_tensor(cYs[rows, :], C2[rows, :], cT1[rows, :], ALU.add), r=kc_ + ["cT1" + pq], w=["cYs" + pq])
            yield
            SN = PF[:, 2048:2560]
            for h in range(16):
                c, hh = h // 2, h % 2
                hs = slice(h * 64, (h + 1) * 64)
                o = SN[hh * 64:hh * 64 + 64, c * 64:(c + 1) * 64]
                S.op("pe", lambda: PE.matmul(o, tmK[rows, s_, hs], tmV[rows, s_, hs], start=True, stop=False), r=["tmK", "tmV"], w=["pb4"])
                S.op("pe", lambda: PE.matmul(o, tmB[rows, s_, hs], cU[rows, hs], start=False, stop=True), r=["tmB", "cU" + pq], w=["pb4"])
            SN3 = SN.rearrange("p (c v) -> p c v", c=8)
            S.op("dve", lambda: V.tensor_tensor(ST[:], ST[:], SN3, ALU.add), r=["ST", "pb4"], w=["ST"])
            S.op("dve", lambda: V.tensor_tensor(ST[:], ST[:], gC[:, :, j:j + 1].broadcast_to([128, 8, 64]), ALU.mult), r=["ST", "gC"], w=["ST"])
            S.op("act", lambda: A.activation(STb[0:64, :, 0:64], ST[0:64, :, :], AF.Copy), r=["ST"], w=["STb"])
            S.op("act", lambda: A.activation(STb[64:128, :, 64:128], ST[64:128, :, :], AF.Copy), r=["ST"], w=["STb"])
            yield
            if j % 2 == 1:
                yield from epilogue((j // 2) * 128, 128)

        def att_block(p, n_):
            gblk = p * 2 + n_
            qs = slice(n_ * 128, (n_ + 1) * 128)
            sb_, sbk = bank(6), "pb6"
            ob, obk = bank(7), "pb7"
            for kv in range(4):
                half = (kv % 2) * 64; hr = slice(half, half + 64); kc2 = kv // 2
                kbs = [1] if gblk == 0 else [0, 1]
                for kb in kbs:
                    kcols = slice(n_ * 128 + kb * 128, n_ * 128 + kb * 128 + 128)
                    S.op("pe", lambda: PE.matmul(sb_[:, 0:512], kT[hr, kc2, kcols], qT[hr, (kv // 2) * 4:(kv // 2) * 4 + 4, qs],
                                                 start=True, stop=True), r=["kT", "qT"], w=[sbk])
                    for g in range(4):
                        h = 4 * kv + g
                        S.op("dve", lambda: V.scalar_tensor_tensor(a_sc[:, g * 128:(g + 1) * 128], Dm[:, kb * 128:(kb + 1) * 128],
                                                                   SLOPES[h] / SCALE, sb_[:, g * 128:(g + 1) * 128], ALU.mult, ALU.add),
                             r=[sbk, "cst"], w=["t_f1"])
                    S.op("act", lambda: A.activation(a_PT[kb][:, :], a_sc[:, :], AF.Exp, scale=SCALE), r=["t_f1"], w=["a_PT%d" % kb])
                    yield
                for g in range(4):
                    for ki, kb in enumerate(kbs):
                        S.op("pe", lambda: PE.matmul(ob[:, g * 66:(g + 1) * 66], a_PT[kb][:, g * 128:(g + 1) * 128], Vall[:, n_ + kb, kv, :],
                                                     start=(ki == 0), stop=(ki == len(kbs) - 1)),
                             r=["a_PT%d" % kb, "Vall%d" % (n_ + kb)], w=[obk])
                o3 = ob[:, 0:264].rearrange("p (g d) -> p g d", g=4)
                S.op("dve", lambda: V.tensor_tensor(a_den[:, :], o3[:, :, 64], esink[:, 4 * kv:4 * kv + 4], ALU.add), r=[obk, "esink"], w=["a_den"])
                S.op("dve", lambda: V.reciprocal(a_den[:, :], a_den[:, :]), r=["a_den"], w=["a_den"])
                S.op("dve", lambda: V.tensor_tensor(hb[:, n_, 1024 + kv * 256:1024 + (kv + 1) * 256].rearrange("p (g d) -> p g d", g=4),
                                                    o3[:, :, 0:64], a_den[:, :].unsqueeze(2).broadcast_to([128, 4, 64]), ALU.mult),
                     r=[obk, "a_den"], w=["hb%d" % n_])
                yield
            pt = PTb[:, 0:1024]; pk = "pb6"
            for c in range(8):
                S.op("pe", lambda: PE.transpose(pt[:, c * 128:(c + 1) * 128], hb[:, n_, 1024 + c * 128:1024 + (c + 1) * 128], identb),
                     r=["hb%d" % n_, "cstb"], w=[pk])
            for c in range(8):
                S.op("dve" if c % 2 else "act", (lambda: V.tensor_copy(hT[:, 8 + c, qs], pt[:, c * 128:(c + 1) * 128])) if c % 2 else
                     (lambda: A.activation(hT[:, 8 + c, qs], pt[:, c * 128:(c + 1) * 128], AF.Copy)), r=[pk], w=["hT%d" % (8 + c)])
            yield

        def interleave(gens):
            gens = list(gens)
            while gens:
                for g in list(gens):
                    try:
                        next(g)
                    except StopIteration:
                        gens.remove(g)

        def sample_mix():
            ONES = cst[:, C_ON:C_ON + 128]
            allk = lambda n: ["%s%d" % (n, c) for c in range(8)]
            wv = wkv_s.rearrange("b (c hh) k v -> b hh k c v", hh=2)
            sv = swkv.rearrange("b (c hh) k v -> b hh k c v", hh=2)
            bc8 = lambda ap, w_: ap.broadcast_to([128, 8, w_])
            c8 = lambda ap, w_: ap.rearrange("p (c v) -> p c v", c=8)
            W32 = chb.bitcast(F32)
            def mkset(b_sS, b_sSn, b_sBlk, b_sAbd, b_sRv, b_sT1):
                bfv = lambda base, n: base.bitcast(BF16)[:, 0:n]
                return dict(sS=c8(b_sS, 64), sSn=c8(b_sSn, 64), sT1=c8(b_sT1, 64),
                            sBlk=c8(bfv(b_sBlk, 1024), 128), sAbd=c8(bfv(b_sAbd, 1024), 128),
                            sSb=c8(b_sAbd[:, 512:768].bitcast(BF16), 64), sRv=c8(bfv(b_sRv, 512), 64))
            setA = mkset(x1[:, 0:512], x1[:, 512:1024], x1[:, 1024:2048], chf[:, 0:1024], chf[:, 1024:1536], chf[:, 1536:2048])
            setB = mkset(W32[:, 1792:2304], W32[:, 2304:2816], scr[:, 2048:3072], W32[:, 2816:3840], W32[:, 3840:4352], W32[:, 4352:4864])
            sets = [setA, setB]
            sKVb = [W32[:, 0:512], W32[:, 1152:1664]]
            sKTb = [W32[:, 4864:4992].bitcast(BF16).rearrange("p (k n) -> p k n", k=2), W32[:, 1664:1792].bitcast(BF16).rearrange("p (k n) -> p k n", k=2)]
            for i_ in range(2):
                S.op("dve", lambda: V.memset(sets[i_]["sBlk"], 0.0), w=["sBlk%d" % i_])
            for b in range(16):
                S.dma("sp", k_s[b, 0:127, :], ck[b, 1:128, :], w=["o_ks"])
                S.dma("sp", v_s[b, 0:127, :], cv[b, 1:128, :], w=["o_vs"])
            S.dma("sp", k_s[:, 127, :], ktm[0:16, 0:256], r=["ktm"], w=["o_ks"])
            S.dma("sp", v_s[:, 127, :], ktm[0:16, 256:512], r=["ktm"], w=["o_vs"])

            ckp(201)
            def rw_sample(b):
                i = b % 2; T_ = sets[i]; x = "%d" % i
                uB, vB = (4, 5) if i == 0 else (6, 7)
                for hh in range(2):
                    S.dma("sp", T_["sS"][hh * 64:(hh + 1) * 64, :, :], sv[b, hh], w=["sS" + x])
                S.op("dve", lambda: V.tensor_tensor(T_["sAbd"], BDf.unsqueeze(1).broadcast_to([128, 8, 128]), bc8(At[:, :, b:b + 1], 128), ALU.mult),
                     r=["cst"] + allk("At"), w=["sAbd" + x])
                S.op("dve", lambda: V.tensor_tensor(T_["sRv"], Ihalf.unsqueeze(1).broadcast_to([128, 8, 64]), bc8(vT[:, :, b:b + 1], 64), ALU.mult),
                     r=["cst"] + allk("vT"), w=["sRv" + x])
                yield
                S.op("act", lambda: A.activation(T_["sSb"], T_["sS"], AF.Copy), r=["sS" + x], w=["sSb" + x])
                for c in range(8):
                    S.op("pe", lambda: PE.matmul(bank(uB)[:, c * 64:(c + 1) * 64], T_["sAbd"][:, c, :], T_["sSb"][:, c, :], start=True, stop=True),
                         r=["sAbd" + x, "sSb" + x], w=["pb%d" % uB])
                for c in range(8):
                    S.op("pe", lambda: PE.matmul(bank(vB)[:, c * 64:(c + 1) * 64], BDb, T_["sRv"][:, c, :], start=True, stop=True),
                         r=["cstb", "sRv" + x], w=["pb%d" % vB])
                U3 = c8(bank(uB), 64); V3 = c8(bank(vB), 64)
                S.op("dve", lambda: V.tensor_tensor(T_["sSn"], T_["sS"], bc8(dS[:, :, b:b + 1], 64), ALU.mult), r=["sS" + x, "dS"], w=["sSn" + x])
                yield
                S.op("dve", lambda: V.tensor_tensor(T_["sT1"], U3, bc8(Bt[:, :, b:b + 1], 64), ALU.mult), r=["pb%d" % uB] + allk("Bt"), w=["sT1" + x])
                S.op("dve", lambda: V.tensor_tensor(T_["sSn"], T_["sSn"], T_["sT1"], ALU.add), r=["sSn" + x, "sT1" + x], w=["sSn" + x])
                S.op("dve", lambda: V.tensor_tensor(T_["sT1"], V3, bc8(Kt[:, :, b:b + 1], 64), ALU.mult), r=["pb%d" % vB] + allk("Kt"), w=["sT1" + x])
                S.op("dve", lambda: V.tensor_tensor(T_["sSn"], T_["sSn"], T_["sT1"], ALU.add), r=["sSn" + x, "sT1" + x], w=["sSn" + x])
                yield
                for hh in range(2):
                    S.dma("sp", wv[b, hh], T_["sSn"][hh * 64:(hh + 1) * 64, :, :], r=["sSn" + x], w=["o_wkvs"])
                S.op("act", lambda: A.activation(T_["sBlk"][0:64, :, 0:64], T_["sSn"][0:64, :, :], AF.Copy), r=["sSn" + x], w=["sBlk" + x])
                S.op("act", lambda: A.activation(T_["sBlk"][64:128, :, 64:128], T_["sSn"][64:128, :, :], AF.Copy), r=["sSn" + x], w=["sBlk" + x])
                for c in range(8):
                    S.op("pe", lambda: PE.matmul(bank(c // 4)[:, (c % 4) * 16 + b:(c % 4) * 16 + b + 1], T_["sBlk"][:, c, :], Rt[:, c, b:b + 1],
                                                 start=True, stop=True), r=["sBlk" + x, "Rt%d" % c], w=["pb%d" % (c // 4)])
                yield

            def at_sample(b):
                i = b % 2; x = "%d" % i
                kvb = sKVb[i]; Kn = kvb[:, 0:256]; Vn = kvb[:, 256:512]; KT = sKTb[i]
                tB, sB = (4, 5) if i == 0 else (6, 7)
                S.dma("sp", Kn[0:127, :], ck[b, 1:128, :], w=["sKa" + x])
                S.dma("sp", Vn[0:127, :], cv[b, 1:128, :], w=["sVa" + x])
                S.dma("sp", Kn[127:128, :], ktm[b:b + 1, 0:256], r=["ktm"], w=["sKb" + x])
                S.dma("sp", Vn[127:128, :], ktm[b:b + 1, 256:512], r=["ktm"], w=["sVb" + x])
                yield
                for k2 in range(2):
                    S.op("pe", lambda: PE.transpose(bank(tB)[:, k2 * 128:(k2 + 1) * 128], Kn[:, k2 * 128:(k2 + 1) * 128], ident),
                         r=["sKa" + x, "sKb" + x, "cst"], w=["pb%d" % tB])
                S.op("act", lambda: A.activation(KT[:, :, :], bank(tB)[:, 0:256].rearrange("p (k n) -> p k n", k=2), AF.Copy), r=["pb%d" % tB], w=["sKT" + x])
                yield
                for kv in range(4):
                    hr = slice((kv % 2) * 64, (kv % 2) * 64 + 64)
                    ob_ = sB if kv % 2 == 0 else tB
                    S.op("pe", lambda: PE.matmul(bank(ob_)[:, 256 + kv * 4:256 + kv * 4 + 4], KT[hr, kv // 2, :], qT[hr, (kv // 2) * 4:(kv // 2) * 4 + 4, b],
                                                 start=True, stop=True), r=["sKT" + x, "qT"], w=["pb%d" % ob_])
                ev_ = lambda ap, o: ap.rearrange("p (k two g) -> p k two g", k=2, two=2)[:, :, o, :]
                for o, ob_ in ((0, sB), (1, tB)):
                    S.op("dve", lambda: V.tensor_tensor(ev_(sPT[:, b, :], o), ev_(bank(ob_)[:, 256:272], o), ev_(sbias, o), ALU.add),
                         r=["pb%d" % ob_, "cst"], w=["sPT" + x])
                S.op("act", lambda: A.activation(sPT[:, b, :], sPT[:, b, :], AF.Exp, scale=SCALE), r=["sPT" + x], w=["sPT" + x])
                yield
                S.op("pe", lambda: PE.matmul(bank(sB)[:, 288:304], ONES, sPT[:, b, :], start=True, stop=True), r=["cst", "sPT" + x], w=["pb%d" % sB])
                S.op("dve", lambda: V.tensor_tensor(sPTn[:, b, :], bank(sB)[:, 288:304], esink[:, :], ALU.add), r=["pb%d" % sB, "esink"], w=["sPTn" + x])
                S.op("dve", lambda: V.reciprocal(sPTn[:, b, :], sPTn[:, b, :]), r=["sPTn" + x], w=["sPTn" + x])
                S.op("dve", lambda: V.tensor_tensor(sPTn[:, b, :], sPTn[:, b, :], sPT[:, b, :], ALU.mult), r=["sPTn" + x, "sPT" + x], w=["sPTn" + x])
                yield
                for h in range(16):
                    kv = h // 4
                    S.op("pe", lambda: PE.matmul(bank(3)[(h % 2) * 64:(h % 2) * 64 + 64, (h // 2) * 16 + b:(h // 2) * 16 + b + 1],
                                                 Vn[:, kv * 64:(kv + 1) * 64], sPTn[:, b, h:h + 1], start=True, stop=True),
                         r=["sVa" + x, "sVb" + x, "sPTn" + x], w=["pb3"])
                yield

            def pairs(fn):
                for b in range(0, 16, 2):
                    interleave([fn(b), fn(b + 1)])
            pairs(rw_sample)
            ckp(202)
            for _ in epilogue(0, 16, [(bank(0)[:, 0:64], "pb0"), (bank(1)[:, 0:64], "pb1")]):
                pass
            for ci in range(28):
                M = RW_M[ci]
                S.dma("sp", sh_s[RW_COL[ci]:RW_COL[ci] + M, :], zs[0:M, ci, :], r=["zs"], w=["o_shs"])
            ckp(203)
            pairs(at_sample)
            ckp(204)
            for c in range(8):
                S.op("act", lambda: A.activation(hT[:, 8 + c, 0:16], bank(3)[:, c * 16:(c + 1) * 16], AF.Copy), r=["pb3"], w=["hT%d" % (8 + c)])

        def tm_proj(sl, subs, ncols, ps_of, handler, ntok):
            for s, (t0, n) in enumerate(subs):
                ps, pkeys = ps_of(s)
                for kc in range(KC):
                    S.op("pe", lambda: PE.matmul(ps[0:n, 0:ncols], hT[:, kc, t0:t0 + n], wb[sl][:, kc, 0:ncols],
                                                 start=(kc == 0), stop=(kc == KC - 1)), r=["hT%d" % kc, "wb%d" % sl], w=pkeys)
                handler(s, t0, n, ps, pkeys)

        def resid_add(W, subs, nk_list=None):
            for cb in range(4):
                sl = wload(W, [(cb * 512, 512, 0)])
                def h(s, t0, n, ps, pkeys):
                    S.op("dve", lambda: V.tensor_tensor(x_sb[0:n, s, cb * 512:(cb + 1) * 512], x_sb[0:n, s, cb * 512:(cb + 1) * 512],
                                                        ps[0:n, 0:512], ALU.add), r=pkeys + ["x%d" % s], w=["x%d" % s])
                tm_proj(sl, subs, 512, lambda s: (bank(4 + (s % 2)), ["pb%d" % (4 + (s % 2))]), h, None)

        try:
            for p in range(NPASS + 1):
                sample = (p == NPASS)
                last = (p == NPASS - 1)
                if sample:
                    subs = [(0, 16)]; ntok = 16
                else:
                    subs = [(0, 128), (128, 128)]; ntok = TP
                tok0 = p * TP
                for s, (t0, n) in enumerate(subs):
                    src = xs_d[0:16, :] if sample else xp[tok0 + t0:tok0 + t0 + n, :]
                    S.dma("sp", x_sb[0:n, s, :], src, w=["x%d" % s])
                if sample:
                    for ci in range(28):
                        M = RW_M[ci]
                        S.dma("sp", shT[0:M, ci, :], sshT[RW_COL[ci]:RW_COL[ci] + M, :], w=["shT"])
                    S.op("dve", lambda: V.memset(msh[:], 0.0), w=["msh"])
                    for ci in range(28):
                        M = RW_M[ci]
                        S.op("dve", lambda: V.tensor_scalar(msh[0:M, ci, :], shT[0:M, ci, :], pv[0:M, PV_MU + ci:PV_MU + ci + 1], None, ALU.mult),
                             r=["shT", "pv"], w=["msh"])
                ckp(1)
                norm_T(subs, PV_GMIX, ntok)
                ckp(2)
                sl = wload(w_in, [(RW_COL[24 + i], RW_M[24 + i], i * 128) for i in range(4)])
                for i in range(4):
                    ci = 24 + i; M = RW_M[ci]; b_ = 4 + (i % 2)
                    proj_fm(sl, i * 128, M, bank(b_), ["pb%d" % b_], ntok)
                    tshift(ci, bank(b_), ["pb%d" % b_], M, ntok, sample, last, xs_o[0:M, 0:ntok], ["xs_o"])
                    if i == 0:
                        S.op("act", lambda: A.activation(txw[0:96, 0:ntok], xs_o[0:96, 0:ntok], AF.Tanh), r=["xs_o"], w=["txw"])
                    elif i == 1:
                        S.op("act", lambda: A.activation(xaT[0:96, 0:ntok], xs_o[0:96, 0:ntok], AF.Copy), r=["xs_o"], w=["xaT"])
                    else:
                        S.op("act", lambda: A.activation(sxg[:, i - 2, 0:ntok], xs_o[:, 0:ntok], AF.Sigmoid), r=["xs_o"], w=["sxg"])
                ckp(3)
                S.op("dve", lambda: V.memset(ss[0:1, 7:8], 0.0), w=["ystg0", "ystg1"])
                def PP(c):
                    sl_ = wload(w_in, [(c * 128, 128, 0), (1024 + c * 128, 128, 128), (2048 + c * 128, 128, 256)])
                    prep_P(c, sl_, ntok, sample, last)
                def TT(c):
                    prep_P(c, None, ntok, sample, last)
                PP(0); TT(0); PP(1); TT(1)
                for c in range(8):
                    prep_H(c, ntok, sample, last)
                    if c + 2 < 8:
                        PP(c + 2)
                    prep_tail(c, ntok)
                    if c + 2 < 8:
                        TT(c + 2)
                ckp(5)
                for half in range(2):
                    sl = wload(w_in, [(RWP + QPERM[half * 8 + i] * 64, 64, i * 64) for i in range(8)])
                    for i in range(4):
                        b_ = 4 + (i % 2)
                        proj_fm(sl, i * 128, 128, bank(b_), ["pb%d" % b_], ntok)
                        evac(qT[:, half * 4 + i, 0:ntok], bank(b_)[:, 0:ntok], r=["pb%d" % b_], w=["qT"])
                sl = wload(w_in, [(RWP + 1024, 512, 0)])
                for i in range(2):
                    b_ = 4 + i
                    proj_fm(sl, i * 128, 128, bank(b_), ["pb%d" % b_], ntok)
                    evac(kT[:, i, 128:128 + ntok], bank(b_)[:, 0:ntok], r=["pb%d" % b_], w=["kT"])
                def kvh(s, t0, n, ps, pkeys):
                    if sample:
                        S.op("act", lambda: A.activation(ktm[0:n, :], ps[0:n, 0:512], AF.Copy), r=pkeys, w=["ktm"])
                        return
                    S.op("act", lambda: A.activation(Vall[0:n, 1 + s, :, 0:64], ps[0:n, 256:512].rearrange("p (k d) -> p k d", k=4), AF.Copy),
                         r=pkeys, w=["Vall%d" % (1 + s)])
                    if last and s == 1:
                        S.op("dve", lambda: V.tensor_copy(ktm[:, :], ps[:, 0:512]), r=pkeys, w=["ktm"])
                        S.dma("sp", k_p[:, :], ktm[:, 0:256], r=["ktm"], w=["o_kp"])
                        S.dma("sp", v_p[:, :], ktm[:, 256:512], r=["ktm"], w=["o_vp"])
                tm_proj(sl, subs, 512, lambda s: (bank(4 + (s % 2)), ["pb%d" % (4 + (s % 2))]), kvh, ntok)

                for s, (t0, n) in enumerate(subs):
                    src = ps_d[0:16, :] if sample else pp[tok0 + t0:tok0 + t0 + n, :]
                    S.dma("sp", pe_sb[0:n, s, :], src, w=["pe_sb"])
                    S.op("dve", lambda: V.tensor_copy(pe_b[0:n, s, :], pe_sb[0:n, s, :]), r=["pe_sb"], w=["pe_b"])
                    pt_rr[0] ^= 1
                    pt = PTb[:, pt_rr[0] * 1024:(pt_rr[0] + 1) * 1024]; pk = "pb%d" % (6 + pt_rr[0])
                    for k2 in range(2):
                        S.op("pe", lambda: PE.transpose(pt[:, k2 * 128:k2 * 128 + n], pe_b[0:n, s, k2 * 128:(k2 + 1) * 128], identb[0:n, 0:n]),
                             r=["pe_b", "cstb"], w=[pk])
                    for k2 in range(2):
                        evac(pT_[:, k2, t0:t0 + n], pt[:, k2 * 128:k2 * 128 + n], r=[pk], w=["pT_"])
                ckp(6)
                if not sample:
                    def side_all():
                        for arr, dst, an, dn in ((vT, tmV, "vT", "tmV"), (Kt, tmK, "Kt", "tmK"), (Bt, tmB, "Bt", "tmB")):
                            for s in range(2):
                                pt_rr[0] ^= 1
                                pt = PTb[:, pt_rr[0] * 1024:(pt_rr[0] + 1) * 1024]; pk = "pb%d" % (6 + pt_rr[0])
                                for c in range(8):
                                    S.op("pe", lambda: PE.transpose(pt[:, c * 128:(c + 1) * 128], arr[:, c, s * 128:(s + 1) * 128], identb),
                                         r=["%s%d" % (an, c), "cstb"], w=[pk])
                                evac(dst[:, s, :], pt[:, :], r=[pk], w=[dn])
                                yield
                        for n_ in range(2):
                            yield from att_block(p, n_)
                    att = side_all()
                    att_done = False
                    for j in range(-1, 4):
                        live = ([chunkB(j)] if j >= 0 else []) + ([chunkA(j + 1)] if j < 3 else [])
                        while live:
                            for g in list(live):
                                try:
                                    next(g)
                                except StopIteration:
                                    live.remove(g)
                            if not att_done:
                                try:
                                    next(att)
                                except StopIteration:
                                    att_done = True
                    if not att_done:
                        interleave([att])
                    if last:
                        S.dma("sp", wkv_p.rearrange("(c hh) k v -> hh k c v", hh=2)[0], ST[0:64, :, :], r=["ST"], w=["o_wkvp"])
                        S.dma("sp", wkv_p.rearrange("(c hh) k v -> hh k c v", hh=2)[1], ST[64:128, :, :], r=["ST"], w=["o_wkvp"])
                        for ci in range(28):
                            M = RW_M[ci]
                            S.dma("sp", sh_p[RW_COL[ci]:RW_COL[ci] + M, :], zlast[0:M, ci:ci + 1], r=["zlast"], w=["o_shp"])
                    S.op("act", lambda: A.activation(kT[:, :, 0:128], kT[:, :, TP:TP + 128], AF.Copy), r=["kT"], w=["kT"])
                    S.op("act", lambda: A.activation(Vall[:, 0, :, 0:64], Vall[:, 2, :, 0:64], AF.Copy), r=["Vall2"], w=["Vall0"])
                else:
                    sample_mix()
                ckp(10)
                resid_add(w_out, subs)
                ckp(11)
                norm_T(subs, PV_GFFN, ntok)
                for fb in range(FC // 2):
                    slg = wload(None, [(fb * 256, 256, 0, w_gate), (fb * 256, 256, 256, w_up)])
                    for i in range(2):
                        f = fb * 2 + i
                        proj_fm(slg, i * 128, 128, bank(0 + (f % 2)), ["pb%d" % (f % 2)], ntok)
                        proj_fm(slg, 256 + i * 128, 128, bank(2 + (f % 2)), ["pb%d" % (2 + (f % 2))], ntok)
                        S.op("act", lambda: A.activation(t_f1[:, 0:ntok], bank(f % 2)[:, 0:ntok], AF.Silu), r=["pb%d" % (f % 2)], w=["t_f1"])
                        S.op("dve", lambda: V.tensor_tensor(hid[:, f, 0:ntok], t_f1[:, 0:ntok], bank(2 + (f % 2))[:, 0:ntok], ALU.mult),
                             r=["t_f1", "pb%d" % (2 + (f % 2))], w=["hid"])
                for cb in range(4):
                    parts = [(0, 16), (16, 16), (32, 12)]
                    for pi, (k0, nk) in enumerate(parts):
                        sl = wload(w_down, [(cb * 512, 512, 0)], nk=nk, row0=k0 * 128)
                        for s, (t0, n) in enumerate(subs):
                            for kk_ in range(nk):
                                S.op("pe", lambda: PE.matmul(bank(4 + s)[0:n, 0:512], hid[:, k0 + kk_, t0:t0 + n], wb[sl][:, kk_, 0:512],
                                                             start=(pi == 0 and kk_ == 0), stop=(pi == 2 and kk_ == nk - 1)),
                                     r=["hid", "wb%d" % sl], w=["pb%d" % (4 + s)])
                    for s, (t0, n) in enumerate(subs):
                        S.op("dve", lambda: V.tensor_tensor(x_sb[0:n, s, cb * 512:(cb + 1) * 512], x_sb[0:n, s, cb * 512:(cb + 1) * 512],
                                                            bank(4 + s)[0:n, 0:512], ALU.add), r=["pb%d" % (4 + s), "x%d" % s], w=["x%d" % s])
                ckp(12)
                norm_T(subs, PV_GPLE, ntok)
                for cb in range(4):
                    sl = wload(ple_gate, [(cb * 512, 512, 0)])
                    S.dma("pool", wpp[:, :, :], ple_proj[:, cb * 512:(cb + 1) * 512].rearrange("(k p) n -> p k n", p=128), w=["wpp"])
                    for s, (t0, n) in enumerate(subs):
                        gb, pb_ = bank(4 + (s % 2)), bank(2 + (s % 2))
                        gk, pk2 = "pb%d" % (4 + (s % 2)), "pb%d" % (2 + (s % 2))
                        for kc in range(KC):
                            S.op("pe", lambda: PE.matmul(gb[0:n, 0:512], hT[:, kc, t0:t0 + n], wb[sl][:, kc, 0:512], start=(kc == 0), stop=(kc == KC - 1)),
                                 r=["hT%d" % kc, "wb%d" % sl], w=[gk])
                        for k2 in range(2):
                            S.op("pe", lambda: PE.matmul(pb_[0:n, 0:512], pT_[:, k2, t0:t0 + n], wpp[:, k2, :], start=(k2 == 0), stop=(k2 == 1)),
                                 r=["pT_", "wpp"], w=[pk2])
                        S.op("act", lambda: A.activation(t_f1[0:n, :], gb[0:n, 0:512], AF.Sigmoid), r=[gk], w=["t_f1"])
                        S.op("dve", lambda: V.tensor_tensor(t_f1[0:n, :], t_f1[0:n, :], pb_[0:n, 0:512], ALU.mult), r=["t_f1", pk2], w=["t_f1"])
                        S.op("dve", lambda: V.tensor_tensor(x_sb[0:n, s, cb * 512:(cb + 1) * 512], x_sb[0:n, s, cb * 512:(cb + 1) * 512], t_f1[0:n, :], ALU.add),
                             r=["t_f1", "x%d" % s], w=["x%d" % s])
                ckp(13)
                for s, (t0, n) in enumerate(subs):
                    S.op("act", lambda: A.activation(hb[0:n, s, :], x_sb[0:n, s, :], AF.Square, accum_out=ss[0:n, s:s + 1]), r=["x%d" % s], w=["hb%d" % s, "ss"])
                    S.op("act", lambda: A.activation(sd[0:n, s:s + 1], ss[0:n, s:s + 1], AF.Ln, bias=EPS, scale=1.0 / D), r=["ss"], w=["sd"])
                    S.op("act", lambda: A.activation(rstd[0:n, s:s + 1], sd[0:n, s:s + 1], AF.Exp, scale=-0.5), r=["sd"], w=["rstd%d" % s])
                    S.op("dve", lambda: V.scalar_tensor_tensor(ystg[0:n, s, :], x_sb[0:n, s, :], rstd[0:n, s:s + 1], fn_bc[0:n, :], ALU.mult, ALU.mult),
                         r=["x%d" % s, "rstd%d" % s, "fn_bc"], w=["ystg%d" % s, "hid"])
                    dst = y_s[0:16, :] if sample else y_p[tok0 + t0:tok0 + t0 + n, :]
                    S.dma("sp", dst, ystg[0:n, s, :], r=["ystg%d" % s], w=["o_y"])
                assert wblk[0] == NBLK * (p + 1), wblk[0]
                ckp(14 + p)
        except _Stop:
            pass
        S.finish("sp")
        print("build stats: waits", S.nwait, "dmas", S.ndma, "seq", dict(S.seq), flush=True)
    return nc


def _consts():
    c = np.zeros((128, C_N), np.float32)
    c[:, C_ID:C_ID + 128] = np.eye(128)
    p = np.arange(128)
    c[:, C_BD:C_BD + 128] = (p[:, None] // 64 == p[None, :] // 64)
    c[:, C_IH:C_IH + 64] = (p[:, None] % 64 == np.arange(64)[None, :])
    c[:, C_HSEL:C_HSEL + 2] = (p[:, None] // 64 == np.arange(2)[None, :])
    s_ = (p % 64)[:, None]; t_ = np.arange(64)[None, :]
    c[:, C_MS:C_MS + 64] = (s_ < t_)
    c[:, C_MI:C_MI + 64] = (s_ <= t_)
    c[:, C_MST:C_MST + 64] = (s_ > t_)
    c[:, C_RST:C_RST + 256] = (np.arange(256) % 64 != 0)[None, :]
    j = p[:, None].astype(np.float32); i = np.arange(128)[None, :].astype(np.float32)
    BIG = 1e5
    c[:, C_DM:C_DM + 128] = np.where(j > i, -(i + 128 - j), -BIG)
    c[:, C_DM + 128:C_DM + 256] = np.where(j <= i, -(i - j), -BIG)
    sl = np.array(SLOPES, np.float32)[None, :]
    c[:, C_SB:C_SB + 16] = -sl * (127 - j) / SCALE
    c[:, C_ON:C_ON + 128] = 1.0
    return c


def _pv(inp):
    pv = np.zeros((128, PV_N), np.float32)
    f = lambda v, n: np.ascontiguousarray(np.asarray(v, np.float32).reshape(n, 128).T)
    pv[:, PV_GMIX:PV_GMIX + 16] = f(inp["norm_mix"][0], 16)
    pv[:, PV_GFFN:PV_GFFN + 16] = f(inp["norm_ffn"][0], 16)
    pv[:, PV_GPLE:PV_GPLE + 16] = f(inp["norm_ple"][0], 16)
    mu = np.asarray(inp["mu_shift"][0], np.float32)
    for ci in range(28):
        pv[0:RW_M[ci], PV_MU + ci] = mu[RW_COL[ci]:RW_COL[ci] + RW_M[ci]]
    pv[:, PV_W0:PV_W0 + 8] = f(inp["rwkv_w0"][0], 8)
    pv[:, PV_A0:PV_A0 + 8] = f(inp["rwkv_a0"][0], 8)
    pv[:, PV_KK:PV_KK + 8] = f(inp["rwkv_k_k"][0], 8)
    pv[:, PV_KA:PV_KA + 8] = f(inp["rwkv_k_a"][0], 8)
    pv[:, PV_RK:PV_RK + 8] = f(np.asarray(inp["rwkv_r_k"][0]).reshape(-1), 8)
    pv[:, PV_LNW:PV_LNW + 8] = f(inp["rwkv_ln_w"][0], 8)
    pv[:, PV_LNB:PV_LNB + 8] = f(inp["rwkv_ln_b"][0], 8)
    return pv


_NC = [None]


def kernel(**inp):
    A_ = lambda k: np.ascontiguousarray(np.asarray(inp[k], np.float32))
    if _NC[0] is None:
        _NC[0] = build_program()
    nc = _NC[0]
    shared = {
        "w_in": A_("w_in")[0], "w_out": A_("w_out")[0], "w_gate": A_("w_gate")[0], "w_up": A_("w_up")[0],
        "w_down": A_("w_down")[0], "ple_gate": A_("ple_gate")[0], "ple_proj": A_("ple_proj")[0],
        "w2": A_("rwkv_w2")[0], "a2": A_("rwkv_a2")[0], "g2": A_("rwkv_g2")[0],
        "pv": _pv(inp), "cst": _consts(), "fng": A_("final_norm").reshape(1, D),
        "sinks": A_("attn_sinks").reshape(1, 16),
    }
    xp, xs = A_("x_prompt"), A_("x_sample")
    pp, ps = A_("p_prompt"), A_("p_sample")
    swkv, ssh = A_("state_wkv"), A_("state_shift")
    ck, cv = A_("cache_k"), A_("cache_v")
    in_maps = []
    for i in range(8):
        b = slice(16 * i, 16 * i + 16)
        m = dict(shared)
        m.update({
            "xp": xp[i], "xs": np.ascontiguousarray(xs[b, 0]), "pp": pp[0, i], "psm": np.ascontiguousarray(ps[0, b, 0]),
            "swkv": np.ascontiguousarray(np.swapaxes(swkv[0, b], -1, -2)),
            "sshT": np.ascontiguousarray(ssh[0, b].T),
            "ck": np.ascontiguousarray(ck[0, b].reshape(16, 128, 256)), "cv": np.ascontiguousarray(cv[0, b].reshape(16, 128, 256)),
        })
        in_maps.append(m)
    res = run_bass_kernel_spmd(nc, in_maps, core_ids=list(range(8)))
    R = res.results
    cat = lambda k: np.stack([np.asarray(r[k], np.float32) for r in R])
    y_p = cat("y_p")
    y_s = cat("y_s").reshape(128, 1, D)
    wkv_p = np.swapaxes(cat("wkv_p"), -1, -2)[None]
    sh_p = cat("sh_p").reshape(8, RWP)[None]
    k_p = cat("k_p").reshape(8, 128, 4, 64)[None]
    v_p = cat("v_p").reshape(8, 128, 4, 64)[None]
    wkv_s = np.swapaxes(cat("wkv_s").reshape(128, 16, 64, 64), -1, -2)[None]
    sh_s = np.swapaxes(cat("sh_s"), 1, 2).reshape(128, RWP)[None]
    k_s = cat("k_s").reshape(128, 128, 4, 64)[None]
    v_s = cat("v_s").reshape(128, 128, 4, 64)[None]
    c_ = np.ascontiguousarray
    return (c_(y_p), c_(y_s), c_(wkv_p), c_(sh_p), c_(k_p), c_(v_p), c_(wkv_s), c_(sh_s), c_(k_s), c_(v_s))
```
